# Optimizing a Trainium2 kernel written in Bass

```python
import math
import jax, jax.numpy as jnp
from jax import lax
import numpy as np

D_MODEL = 1024
BATCH = 32
SEQ = 2048
DEPTH = 2
DEC_BATCH = 8
DEC_SEQ = 8192
PAST_LEN = 128

RWKV_HEADS = 8
RWKV_HEAD_DIM = 64
RWKV_WIDTH = RWKV_HEADS * RWKV_HEAD_DIM
RWKV_LORA_W = 64
RWKV_LORA_A = 64
RWKV_DECAY_SCALE = 0.6065306597126334
RWKV_GN_EPS = 64e-5
ATTN_HEADS = 8
ATTN_HEAD_DIM = 64
ATTN_WIDTH = ATTN_HEADS * ATTN_HEAD_DIM
DILATED_PATTERNS = ((128, 1), (512, 4), (2048, 16))
ROPE_THETA = 10000.0
NEG_INF = -1e30
EVEN_SHIFT = 3 * RWKV_WIDTH + RWKV_LORA_W + RWKV_LORA_A
EVEN_COLS = EVEN_SHIFT + RWKV_WIDTH + 4 * ATTN_WIDTH
EVEN_MIX = RWKV_WIDTH + ATTN_WIDTH

GLA_HEADS = 4
GLA_KEY_DIM = D_MODEL // 2
GLA_VAL_DIM = D_MODEL
GLA_HK = GLA_KEY_DIM // GLA_HEADS
GLA_HV = GLA_VAL_DIM // GLA_HEADS
GLA_GATE_RANK = 16
GLA_GATE_NORM = 16.0
GLA_CHUNK = 64
ODD_COLS = 2 * GLA_KEY_DIM + GLA_VAL_DIM + GLA_GATE_RANK + GLA_VAL_DIM

RMS_EPS = 1e-6
N_EVEN = (DEPTH + 1) // 2
N_ODD = DEPTH // 2

kernel_name = "hybrid_bidir_rwkv7_dilated_gla_encoder"


def _split(t, sizes):
    out, off = [], 0
    for sz in sizes:
        out.append(t[..., off:off + sz])
        off += sz
    return out


def _rmsnorm(x, w):
    xf = x.astype(jnp.float32)
    y = xf * lax.rsqrt(jnp.mean(xf * xf, axis=-1, keepdims=True) + RMS_EPS)
    return (y * w.astype(jnp.float32)).astype(x.dtype)


def _centred_shift(p, mu_prev, mu_next):
    prev = jnp.pad(p[:, :-1], ((0, 0), (1, 0), (0, 0)))
    nxt = jnp.pad(p[:, 1:], ((0, 0), (0, 1), (0, 0)))
    return p + mu_prev * (prev - p) + mu_next * (nxt - p)


def _rope(x):
    s, dh = x.shape[1], x.shape[-1]
    inv = ROPE_THETA ** (-jnp.arange(0, dh, 2, dtype=jnp.float32) / dh)
    ang = jnp.arange(s, dtype=jnp.float32)[:, None] * inv[None, :]
    cos = jnp.cos(ang)[None, :, None, :]
    sin = jnp.sin(ang)[None, :, None, :]
    x1, x2 = x[..., : dh // 2], x[..., dh // 2:]
    return jnp.concatenate([x1 * cos - x2 * sin, x1 * sin + x2 * cos], axis=-1)


def _rwkv7_step(state, inp):
    r_t, decay_t, k_t, v_t, kk_t, a_t = inp
    sa = jnp.einsum('nhvk,nhk->nhv', state, -kk_t)
    state = (state * decay_t[:, :, None, :]
             + sa[..., None] * (kk_t * a_t)[:, :, None, :]
             + v_t[..., None] * k_t[:, :, None, :])
    y = jnp.einsum('nhvk,nhk->nhv', state, r_t)
    return state, y


def _rwkv7(r, k, v, w_lat, a_lat, w0, w_up, a0, a_up, k_k, k_a, r_k, gn_w, gn_b):
    bsz, s, _ = r.shape
    H, Dh = RWKV_HEADS, RWKV_HEAD_DIM
    f32 = jnp.float32
    r, k, v, w_lat, a_lat = (t.astype(f32) for t in (r, k, v, w_lat, a_lat))
    w_log = -RWKV_DECAY_SCALE * jax.nn.sigmoid(
        w0[:, None, None, :] + jnp.einsum('bsr,zrc->zbsc', jnp.tanh(w_lat), w_up))
    a = jax.nn.sigmoid(a0[:, None, None, :] + jnp.einsum('bsr,zrc->zbsc', a_lat, a_up))
    kk = (k * k_k).reshape(bsz, s, H, Dh)
    kk = kk / jnp.maximum(jnp.sqrt(jnp.sum(kk * kk, axis=-1, keepdims=True)), 1e-12)
    kk = kk.reshape(bsz, s, RWKV_WIDTH)
    k_dir = k[None] * (1.0 + (a - 1.0) * k_a)
    rk = jnp.sum((r[None] * k_dir).reshape(2, bsz, s, H, Dh) * r_k.reshape(H, Dh), axis=(0, -1))
    bonus = rk[..., None] * v.reshape(bsz, s, H, Dh)

    both = lambda t: jnp.stack([t, jnp.flip(t, 1)])
    rev = lambda t: jnp.stack([t[0], jnp.flip(t[1], 1)])
    to_scan = lambda t: t.reshape(2 * bsz, s, H, Dh).transpose(1, 0, 2, 3)
    xs = tuple(to_scan(t) for t in (both(r), rev(jnp.exp(w_log)), rev(k_dir),
                                     both(v), both(kk), rev(a)))
    state0 = jnp.zeros((2 * bsz, H, Dh, Dh), f32)
    _, ys = lax.scan(_rwkv7_step, state0, xs)
    ys = ys.transpose(1, 0, 2, 3).reshape(2, bsz, s, H, Dh)
    y = ys[0] + jnp.flip(ys[1], 1)
    mean = jnp.mean(y, axis=-1, keepdims=True)
    var = jnp.mean(jnp.square(y - mean), axis=-1, keepdims=True)
    y = (y - mean) * lax.rsqrt(var + RWKV_GN_EPS) * gn_w.reshape(H, Dh) + gn_b.reshape(H, Dh)
    return (y + bonus).reshape(bsz, s, RWKV_WIDTH)


def _banded_attention(q, k, v, n_side):
    n, length, h, dh = q.shape
    blk = n_side
    nb = -(-length // blk)
    lp = nb * blk
    qb = jnp.pad(q, ((0, 0), (0, lp - length), (0, 0), (0, 0))).reshape(n, nb, blk, h, dh)
    kv_pad = ((0, 0), (blk, lp - length + blk), (0, 0), (0, 0))
    kp, vp = jnp.pad(k, kv_pad), jnp.pad(v, kv_pad)

    def neighbourhood(t):
        return jnp.concatenate(
            [t[:, o * blk:o * blk + lp].reshape(n, nb, blk, h, dh) for o in range(3)], axis=2)

    kb, vb = neighbourhood(kp), neighbourhood(vp)
    qi = jnp.arange(blk)[:, None]
    kj = jnp.arange(3 * blk)[None, :]
    kpos = jnp.arange(nb)[:, None, None] * blk - blk + kj[None]
    mask = (jnp.abs(kj - blk - qi)[None] <= n_side) & (kpos >= 0) & (kpos < length)
    scores = jnp.einsum('nbqhd,nbkhd->nbhqk', qb, kb) * (dh ** -0.5)
    scores = jnp.where(mask[None, :, None], scores, NEG_INF)
    lse = jax.nn.logsumexp(scores, axis=-1)
    probs = jnp.exp(scores - lse[..., None])
    out = jnp.einsum('nbhqk,nbkhd->nbqhd', probs, vb).reshape(n, lp, h, dh)[:, :length]
    lse = lse.transpose(0, 1, 3, 2).reshape(n, lp, h)[:, :length]
    return out, lse


def _dilated_attention(q, k, v):
    b, s, h, dh = q.shape
    outs, lses = [], []
    for window, dil in DILATED_PATTERNS:
        n_side = window // (2 * dil)
        sub = s // dil
        to_sub = lambda t: t.reshape(b, sub, dil, h, dh).transpose(0, 2, 1, 3, 4).reshape(b * dil, sub, h, dh)
        o, lse = _banded_attention(to_sub(q), to_sub(k), to_sub(v), n_side)
        outs.append(o.reshape(b, dil, sub, h, dh).transpose(0, 2, 1, 3, 4).reshape(b, s, h, dh))
        lses.append(lse.reshape(b, dil, sub, h).transpose(0, 2, 1, 3).reshape(b, s, h))
    weights = jax.nn.softmax(jnp.stack(lses), axis=0)
    return jnp.sum(weights[..., None] * jnp.stack(outs), axis=0)


def _gla_chunked(q, k, v, g):
    n, s, h, dk = q.shape
    dv = v.shape[-1]
    c = GLA_CHUNK
    nc = s // c
    rs = lambda t: t.reshape(n, nc, c, h, t.shape[-1]).transpose(1, 0, 3, 2, 4)
    qc, kc, vc, gc = rs(q), rs(k), rs(v), rs(g)
    bcum = jnp.cumsum(gc, axis=3)
    q_dec = qc * jnp.exp(bcum)
    k_dec = kc * jnp.exp(-bcum)
    causal = jnp.tril(jnp.ones((c, c), dtype=bool))
    att = jnp.where(causal, jnp.einsum('znhtk,znhsk->znhts', q_dec, k_dec), 0.0)
    o_intra = jnp.einsum('znhts,znhsv->znhtv', att, vc)
    b_last = bcum[..., -1:, :]
    k_end = kc * jnp.exp(b_last - bcum)
    chunk_decay = jnp.exp(b_last[..., 0, :])

    def step(state, inp):
        q_c, k_c, v_c, d_c = inp
        o_inter = jnp.einsum('nhtk,nhkv->nhtv', q_c, state)
        state = state * d_c[..., None] + jnp.einsum('nhsk,nhsv->nhkv', k_c, v_c)
        return state, o_inter

    state0 = jnp.zeros((n, h, dk, dv), jnp.float32)
    _, o_inter = lax.scan(step, state0, (q_dec, k_end, vc, chunk_decay))
    o = o_intra + o_inter
    return o.transpose(1, 0, 3, 2, 4).reshape(n, s, h, dv)


def _gla(q, k, v, gate_lat, gate_up, gate_bias, norm_w):
    bsz, s, _ = q.shape
    H = GLA_HEADS
    f32 = jnp.float32
    q = q.astype(f32).reshape(bsz, s, H, GLA_HK) * (GLA_HK ** -0.5)
    k = k.astype(f32).reshape(bsz, s, H, GLA_HK)
    v = v.astype(f32).reshape(bsz, s, H, GLA_HV)
    g = jax.nn.log_sigmoid(jnp.einsum('bsr,zrc->zbsc', gate_lat.astype(f32), gate_up)
                           + gate_bias[:, None, None, :]) / GLA_GATE_NORM
    g = g.reshape(2, bsz, s, H, GLA_HK)
    both = lambda t: jnp.stack([t, jnp.flip(t, 1)]).reshape(2 * bsz, s, H, t.shape[-1])
    g_dir = jnp.stack([g[0], jnp.flip(g[1], 1)]).reshape(2 * bsz, s, H, GLA_HK)
    o = _gla_chunked(both(q), both(k), both(v), g_dir).reshape(2, bsz, s, H, GLA_HV)
    o = o[0] + jnp.flip(o[1], 1)
    o = o * lax.rsqrt(jnp.mean(o * o, axis=-1, keepdims=True) + RMS_EPS) * norm_w
    return o.reshape(bsz, s, GLA_VAL_DIM)


def _even_layer(x, norm_w, w_in, mu_prev, mu_next, w0, w_up, a0, a_up, k_k, k_a, r_k, gn_w, gn_b, w_out):
    bsz, s, _ = x.shape
    f32 = jnp.float32
    p = _rmsnorm(x, norm_w) @ w_in
    shifted = _centred_shift(p[..., :EVEN_SHIFT], mu_prev, mu_next)
    r, k, v, w_lat, a_lat = _split(shifted, (RWKV_WIDTH, RWKV_WIDTH, RWKV_WIDTH, RWKV_LORA_W, RWKV_LORA_A))
    gate_a, q_b, k_b, v_b, gate_b = _split(p[..., EVEN_SHIFT:], (RWKV_WIDTH,) + (ATTN_WIDTH,) * 4)
    y_a = _rwkv7(r, k, v, w_lat, a_lat, w0, w_up, a0, a_up, k_k, k_a, r_k, gn_w, gn_b)
    heads = lambda t: t.astype(f32).reshape(bsz, s, ATTN_HEADS, ATTN_HEAD_DIM)
    y_b = _dilated_attention(_rope(heads(q_b)), _rope(heads(k_b)), heads(v_b)).reshape(bsz, s, ATTN_WIDTH)
    y = jnp.concatenate([y_a * jax.nn.silu(gate_a.astype(f32)),
                         y_b * jax.nn.silu(gate_b.astype(f32))], axis=-1)
    return x + y.astype(x.dtype) @ w_out


def _odd_layer(x, norm_w, w_in, gate_up, gate_bias, gnorm_w, w_out):
    p = _rmsnorm(x, norm_w) @ w_in
    q, k, v, gate_lat, gate = _split(p, (GLA_KEY_DIM, GLA_KEY_DIM, GLA_VAL_DIM, GLA_GATE_RANK, GLA_VAL_DIM))
    y = _gla(q, k, v, gate_lat, gate_up, gate_bias, gnorm_w) * jax.nn.silu(gate.astype(jnp.float32))
    return x + y.astype(x.dtype) @ w_out


def _trunk(x, even_norm, even_w_in, even_mu_prev, even_mu_next, rwkv_w0, rwkv_w_up, rwkv_a0, rwkv_a_up,
           rwkv_k_k, rwkv_k_a, rwkv_r_k, rwkv_gn_w, rwkv_gn_b, even_w_out,
           odd_norm, odd_w_in, gla_gate_up, gla_gate_bias, gla_norm, odd_w_out, final_norm):
    for layer in range(DEPTH):
        i = layer // 2
        if layer % 2 == 0:
            x = _even_layer(x, even_norm[i], even_w_in[i], even_mu_prev[i], even_mu_next[i],
                            rwkv_w0[i], rwkv_w_up[i], rwkv_a0[i], rwkv_a_up[i], rwkv_k_k[i],
                            rwkv_k_a[i], rwkv_r_k[i], rwkv_gn_w[i], rwkv_gn_b[i], even_w_out[i])
        else:
            x = _odd_layer(x, odd_norm[i], odd_w_in[i], gla_gate_up[i], gla_gate_bias[i],
                           gla_norm[i], odd_w_out[i])
    return _rmsnorm(x, final_norm)


def setup_inputs(seed: int = 0) -> dict:
    key = jax.random.key(seed)
    ks = iter(jax.random.split(key, 32))
    nrm = lambda shape, scale: scale * jax.random.normal(next(ks), shape, jnp.float32)
    unif = lambda shape, lo, hi: jax.random.uniform(next(ks), shape, jnp.float32, lo, hi)
    E, O = N_EVEN, N_ODD
    return {
        "x_prompt": nrm((BATCH, SEQ, D_MODEL), 1.0),
        "x_sample": nrm((DEC_BATCH, DEC_SEQ, D_MODEL), 1.0),
        "even_norm": 1.0 + nrm((E, D_MODEL), 0.02),
        "even_w_in": nrm((E, D_MODEL, EVEN_COLS), D_MODEL ** -0.5),
        "even_mu_prev": unif((E, EVEN_SHIFT), 0.0, 0.5),
        "even_mu_next": unif((E, EVEN_SHIFT), 0.0, 0.5),
        "rwkv_w0": nrm((E, 2, RWKV_WIDTH), 0.5),
        "rwkv_w_up": nrm((E, 2, RWKV_LORA_W, RWKV_WIDTH), 0.5 * RWKV_LORA_W ** -0.5),
        "rwkv_a0": nrm((E, 2, RWKV_WIDTH), 0.5),
        "rwkv_a_up": nrm((E, 2, RWKV_LORA_A, RWKV_WIDTH), 0.5 * RWKV_LORA_A ** -0.5),
        "rwkv_k_k": 0.85 + nrm((E, RWKV_WIDTH), 0.05),
        "rwkv_k_a": 1.0 + nrm((E, RWKV_WIDTH), 0.05),
        "rwkv_r_k": nrm((E, RWKV_WIDTH), 0.1),
        "rwkv_gn_w": 1.0 + nrm((E, RWKV_WIDTH), 0.02),
        "rwkv_gn_b": nrm((E, RWKV_WIDTH), 0.02),
        "even_w_out": nrm((E, EVEN_MIX, D_MODEL), EVEN_MIX ** -0.5),
        "odd_norm": 1.0 + nrm((O, D_MODEL), 0.02),
        "odd_w_in": nrm((O, D_MODEL, ODD_COLS), D_MODEL ** -0.5),
        "gla_gate_up": nrm((O, 2, GLA_GATE_RANK, GLA_KEY_DIM), GLA_GATE_RANK ** -0.5),
        "gla_gate_bias": nrm((O, 2, GLA_KEY_DIM), 0.5),
        "gla_norm": 1.0 + nrm((O, GLA_HV), 0.02),
        "odd_w_out": nrm((O, GLA_VAL_DIM, D_MODEL), GLA_VAL_DIM ** -0.5),
        "final_norm": 1.0 + nrm((D_MODEL,), 0.02),
    }


def reference(x_prompt, x_sample, even_norm, even_w_in, even_mu_prev, even_mu_next, rwkv_w0, rwkv_w_up,
              rwkv_a0, rwkv_a_up, rwkv_k_k, rwkv_k_a, rwkv_r_k, rwkv_gn_w, rwkv_gn_b, even_w_out,
              odd_norm, odd_w_in, gla_gate_up, gla_gate_bias, gla_norm, odd_w_out, final_norm):
    y_prompt = _trunk(x_prompt, even_norm, even_w_in, even_mu_prev, even_mu_next, rwkv_w0, rwkv_w_up,
                      rwkv_a0, rwkv_a_up, rwkv_k_k, rwkv_k_a, rwkv_r_k, rwkv_gn_w, rwkv_gn_b, even_w_out,
                      odd_norm, odd_w_in, gla_gate_up, gla_gate_bias, gla_norm, odd_w_out, final_norm)
    y_sample = _trunk(x_sample, even_norm, even_w_in, even_mu_prev, even_mu_next, rwkv_w0, rwkv_w_up,
                      rwkv_a0, rwkv_a_up, rwkv_k_k, rwkv_k_a, rwkv_r_k, rwkv_gn_w, rwkv_gn_b, even_w_out,
                      odd_norm, odd_w_in, gla_gate_up, gla_gate_bias, gla_norm, odd_w_out, final_norm)
    return (y_prompt, y_sample)
```

```python
import contextlib
import numpy as np
import concourse.bass as bass
import concourse.mybir as mybir
from concourse.bass_utils import run_bass_kernel_spmd

F32 = mybir.dt.float32
BF16 = mybir.dt.bfloat16
AF = mybir.ActivationFunctionType
ALU = mybir.AluOpType

SAME_ENG_SYNC = True
LIST_SCHED = True
SCHED_DEBUG = False
NDMASEM = 8
NCORES = 8
D = 1024
RW = 512
EVEN_SHIFT = 1664
EVEN_COLS = 4224
ODD_COLS = 3088
DECAY_C = 0.6065306597126334
GN_EPS = 64e-5
RMS_EPS = 1e-6


class Op:
    __slots__ = ("eng", "fn", "reads", "writes", "is_dma", "waits", "signal", "sig", "barrier", "deps", "cost",
                 "lat", "idx")


def _numel(ap):
    n = 1
    for d in ap.shape[1:]:
        n *= int(d)
    return n


class Prog:
    ENGS = ("pe", "act", "dve", "pool", "sp")

    def __init__(self, nc, stack):
        self.nc = nc
        self.ops = []
        self.sems = {e: stack.enter_context(nc.semaphore("s_" + e)) for e in ("pe", "act", "dve", "pool")}
        self.dsems = {q: [stack.enter_context(nc.semaphore("d_%s%d" % (q, i))) for i in range(NDMASEM)]
                      for q in ("sp", "pool", "act")}
        self.cnt = {e: 0 for e in ("pe", "act", "dve", "pool")}
        self.dcnt = {q: 0 for q in ("sp", "pool", "act")}
        self.last_writer = {}
        self.readers = {}
        self.seen = {e: {} for e in self.ENGS}
        self.last_op = {}
        self.recent_dma = {q: [] for q in ("sp", "pool", "act")}
        self.emitted = 0
        self.n_inst = 0

    def op(self, eng, fn, reads=(), writes=(), cost=500.0):
        o = Op()
        ex = [r for r in reads if isinstance(r, str) and (r.startswith("ps") or r.startswith("pg"))]
        if ex:
            reads = [r for r in reads if r not in ex]
            writes = list(writes) + ex
        o.eng = eng; o.fn = fn; o.reads = tuple(reads); o.writes = tuple(writes)
        o.is_dma = False; o.signal = False; o.sig = None; o.waits = []; o.barrier = False
        o.deps = []; o.cost = cost; o.lat = cost
        self.ops.append(o)
        return o

    def dma(self, q, out, in_, reads=(), writes=()):
        o = self.op(q, lambda e: e.dma_start(out=out, in_=in_), reads, writes, cost=60.0)
        o.is_dma = True
        o.lat = 2200.0 + _numel(out) * 128 * 0.004
        return o

    def barrier(self):
        for e in self.ENGS:
            o = self.op(e, None)
            o.barrier = True

    def _deps(self, ops):
        for o in ops:
            if o.barrier:
                if o.eng == self.ENGS[-1]:
                    self.last_writer = {}
                    self.readers = {}
                continue
            deps = {}
            for k in o.reads:
                w = self.last_writer.get(k)
                if w is not None:
                    deps[id(w)] = (w, True)
            for k in o.writes:
                w = self.last_writer.get(k)
                if w is not None:
                    israw = isinstance(k, str) and (k.startswith("ps") or k.startswith("pg"))
                    if id(w) not in deps or israw:
                        deps[id(w)] = (w, israw or deps.get(id(w), (None, False))[1])
                for r in self.readers.get(k, ()):
                    if id(r) not in deps:
                        deps[id(r)] = (r, False)
            deps.pop(id(o), None)
            o.deps = list(deps.values())
            for k in o.reads:
                self.readers.setdefault(k, []).append(o)
            for k in o.writes:
                self.last_writer[k] = o
                self.readers[k] = []

    def _schedule(self, seg):
        import heapq
        n = len(seg)
        if n < 3 or not LIST_SCHED:
            return seg
        for i, o in enumerate(seg):
            o.idx = i
        inseg = set(id(o) for o in seg)
        succ = [[] for _ in range(n)]
        indeg = [0] * n
        for o in seg:
            for d, _ in o.deps:
                if id(d) in inseg:
                    succ[d.idx].append(o.idx)
                    indeg[o.idx] += 1
        rank = [0.0] * n
        for i in range(n - 1, -1, -1):
            m = 0.0
            for j in succ[i]:
                if rank[j] > m:
                    m = rank[j]
            rank[i] = seg[i].lat + m
        ready = [0.0] * n
        free = {e: 0.0 for e in self.ENGS}
        fut = {e: [] for e in self.ENGS}
        now = {e: [] for e in self.ENGS}
        for i in range(n):
            if indeg[i] == 0:
                heapq.heappush(fut[seg[i].eng], (0.0, i))
        out = []
        XLAT = 180.0
        while len(out) < n:
            best = None
            for e in self.ENGS:
                f, nw = fut[e], now[e]
                while f and f[0][0] <= free[e]:
                    t, i = heapq.heappop(f)
                    heapq.heappush(nw, (-rank[i], i))
                if nw:
                    cand = (free[e], 0, e)
                elif f:
                    cand = (f[0][0], 1, e)
                else:
                    continue
                if best is None or cand < best:
                    best = cand
            start, kind, e = best
            if kind == 0:
                _, i = heapq.heappop(now[e])
            else:
                _, i = heapq.heappop(fut[e])
            o = seg[i]
            free[e] = start + o.cost
            fin = start + o.lat
            out.append(o)
            for j in succ[i]:
                same = (seg[j].eng == e and not o.is_dma)
                r = (start + o.cost) if same else (fin + XLAT)
                if r > ready[j]:
                    ready[j] = r
                indeg[j] -= 1
                if indeg[j] == 0:
                    heapq.heappush(fut[seg[j].eng], (ready[j], j))
        if SCHED_DEBUG and n > 500:
            import collections
            load = collections.defaultdict(float)
            for o in seg:
                load[o.eng] += o.cost
            st = {}
            fr = {e: 0.0 for e in self.ENGS}
            pred = {}
            for o in out:
                t = fr[o.eng]
                p = None
                for d, _ in o.deps:
                    if id(d) in st:
                        same = (d.eng == o.eng and not d.is_dma)
                        r = st[id(d)] + (d.cost if same else d.lat + XLAT)
                        if r > t:
                            t, p = r, d
                st[id(o)] = t
                pred[id(o)] = p
                fr[o.eng] = t + o.cost
            last = max(out, key=lambda o: st[id(o)] + o.lat)
            print("SCHED seg n=%d makespan=%.0f loads=%s" % (n, st[id(last)] + last.lat, {e: int(v) for e, v in load.items()}))
            i = max(range(n), key=lambda q: rank[q])
            print("  pure DAG critical path length: %.0f" % rank[i])
            cp_ = collections.Counter(); cpt = collections.defaultdict(float)
            while True:
                o = seg[i]
                kk = (o.eng, str(o.writes[0] if o.writes else "-")[:7], o.is_dma)
                cp_[kk] += 1; cpt[kk] += o.lat
                if not succ[i]:
                    break
                i = max(succ[i], key=lambda q: rank[q])
            print("  DAG path:", sorted(((int(cpt[kk]), v, kk) for kk, v in cp_.items()), reverse=True)[:16])
            path = collections.Counter()
            tm = collections.defaultdict(float)
            o = last
            cnt = 0
            while o is not None and cnt < 100000:
                kk = (o.eng, str(o.writes[0] if o.writes else "-")[:6])
                path[kk] += 1
                tm[kk] += o.lat
                o = pred[id(o)]
                cnt += 1
            print("  critical path ops:", sorted(((v, int(tm[kk]), kk) for kk, v in path.items()), reverse=True)[:14])
        return out

    def _sync(self, ops):
        for o in ops:
            if o.barrier:
                if o.eng == self.ENGS[0]:
                    self._bar = [lo for lo in self.last_op.values() if lo is not None]
                    for q, lst in self.recent_dma.items():
                        self._bar.extend(lst)
                deps = [(d, True) for d in self._bar]
            else:
                deps = o.deps
            if o.is_dma:
                j = self.dcnt[o.eng]
                self.dcnt[o.eng] = j + 1
                s = j % NDMASEM
                o.sig = (("d", o.eng, s), 16 * (j // NDMASEM + 1))
                o.signal = True
                if j >= NDMASEM:
                    o.waits.append((("d", o.eng, s), 16 * (j // NDMASEM)))
            for d, israw in deps:
                if d.is_dma:
                    o.waits.append(d.sig)
                elif d.eng == o.eng and not o.is_dma:
                    if o.eng == "pe" or not SAME_ENG_SYNC or (not o.barrier and not israw):
                        continue
                    d.signal = True
                    o.waits.append(("c", d))
                else:
                    d.signal = True
                    o.waits.append(("c", d))
            if not o.barrier:
                if o.is_dma:
                    lst = self.recent_dma[o.eng]
                    lst.append(o)
                    if len(lst) > NDMASEM:
                        lst.pop(0)
                else:
                    self.last_op[o.eng] = o
        for o in ops:
            if o.signal and not o.is_dma:
                self.cnt[o.eng] += 1
                o.sig = (("c", o.eng), self.cnt[o.eng])

    def _sem(self, key):
        if key[0] == "c":
            return self.sems[key[1]]
        return self.dsems[key[1]][key[2]]

    def emit(self):
        ops = self.ops[self.emitted:]
        self.emitted = len(self.ops)
        self._deps(ops)
        ordered, seg = [], []
        for o in ops:
            if o.barrier:
                if seg:
                    ordered.extend(self._schedule(seg))
                    seg = []
                ordered.append(o)
            else:
                seg.append(o)
        if seg:
            ordered.extend(self._schedule(seg))
        ops = ordered
        self._sync(ops)
        per = {e: [o for o in ops if o.eng == e] for e in self.ENGS}
        self.n_inst += len(ops)
        with self.nc.Block() as block:
            def run(engname):
                def body(eng):
                    seen = self.seen[engname]
                    for o in per[engname]:
                        for w in o.waits:
                            if w[0] == "c":
                                key, val = w[1].sig
                            else:
                                key, val = w
                            if seen.get(key, 0) >= val:
                                continue
                            seen[key] = val
                            eng.wait_ge(self._sem(key), val)
                        if o.fn is None:
                            continue
                        ins = o.fn(eng)
                        if o.signal:
                            key, val = o.sig
                            ins.then_inc(self._sem(key), 16 if o.is_dma else 1)
                return body
            block.tensor(run("pe"))
            block.scalar(run("act"))
            block.vector(run("dve"))
            block.gpsimd(run("pool"))
            block.sync(run("sp"))

    def stage_end(self):
        self.barrier()
        self.emit()


class Ring:
    uid = 0
    def __init__(self, nc, st, name, shape, dt, n, psum=False):
        alloc = nc.psum_tensor if psum else nc.sbuf_tensor
        Ring.uid += 1
        self.tiles = [st.enter_context(alloc("%s_u%d_%d" % (name, Ring.uid, i), shape, dt)) for i in range(n)]
        self.keys = ["%s%d" % (name, i) for i in range(n)]
        self.i = -1

    def next(self):
        self.i = (self.i + 1) % len(self.tiles)
        return self.tiles[self.i], self.keys[self.i]


def _cols(v):
    v = np.asarray(v, np.float32)
    return np.ascontiguousarray(v.reshape(-1, 128).T)


COLS = {}


def _col_layout():
    off = 0
    for name, n in (("norm0", 8), ("norm1", 8), ("mup", 13), ("mun", 13), ("w0_0", 4), ("w0_1", 4), ("a0_0", 4),
                    ("a0_1", 4), ("k_k", 4), ("k_a", 4), ("r_k", 4), ("gn_w", 4), ("gn_b", 4), ("gnorm", 2),
                    ("gb_0", 4), ("gb_1", 4)):
        COLS[name] = (off, n)
        off += n
    return off


NCOLS = _col_layout()


def host_consts(smax):
    c = {}
    c["ident"] = np.eye(128, dtype=np.float32)
    R = np.zeros((128, 128), np.float32)
    for m in range(128):
        h, j = divmod(m, 64)
        k = h * 64 + (j + 32) % 64
        R[k, m] = 1.0
    c["rot"] = R
    inv = 10000.0 ** (-np.arange(0, 64, 2, dtype=np.float32) / 64.0)
    ang = np.arange(smax, dtype=np.float32)[None, :] * inv[:, None]
    cos, sin = np.cos(ang), np.sin(ang)
    c["cos"] = np.ascontiguousarray(np.tile(cos, (4, 1)).astype(np.float32))
    c["sin"] = np.ascontiguousarray(np.concatenate([-sin, sin, -sin, sin], 0).astype(np.float32))
    i = np.arange(128)[:, None]
    t = np.arange(128)[None, :]
    same = (i // 64) == (t // 64)
    strict = ((i < t) & same).astype(np.float32)
    incl = ((i <= t) & same).astype(np.float32)
    c["m_f"] = np.concatenate([strict, incl, strict.T], 1)
    c["m_b"] = np.concatenate([strict.T, incl.T, strict], 1)
    c["g_f"] = np.concatenate([(i <= t).astype(np.float32), np.ones((128, 128), np.float32)], 1)
    c["g_b"] = np.concatenate([(i >= t).astype(np.float32), np.ones((128, 128), np.float32)], 1)
    kl = np.arange(128)[:, None]
    ql = np.arange(256)[None, :]
    c["a_g"] = ((kl <= ql) & (ql <= kl + 128)).astype(np.float32)
    c["a_0"] = (np.arange(128)[None, :] <= np.arange(64)[:, None] + 64).astype(np.float32)
    bd = (np.arange(128)[:, None] // 64 == np.arange(128)[None, :] // 64).astype(np.float32)
    c["bd1"] = bd
    c["ones"] = np.ones((128, 128), np.float32)
    rm = np.ones((128, 512), np.float32)
    rm[:, ::64] = 0.0
    c["rmask"] = rm
    rm2 = np.ones((128, 512), np.float32)
    rm2[:, ::128] = 0.0
    c["rmask128"] = rm2
    return c


CONST_SHAPES = lambda smax: {"ident": [128, 128], "rot": [128, 128], "cos": [128, smax], "sin": [128, smax],
                             "m_f": [128, 384], "m_b": [128, 384],
                             "g_f": [128, 256], "g_b": [128, 256], "a_g": [128, 256], "a_0": [64, 128],
                             "bd1": [128, 128], "ones": [128, 128], "rmask": [128, 512],
                             "rmask128": [128, 512]}


def host_params(inp):
    g = lambda k: np.asarray(inp[k], np.float32)
    cols = np.zeros((128, NCOLS), np.float32)

    def put(name, v):
        o, n = COLS[name]
        cols[:, o:o + n] = _cols(v)
    put("norm0", g("even_norm")[0]); put("norm1", g("odd_norm")[0])
    put("mup", g("even_mu_prev")[0]); put("mun", g("even_mu_next")[0])
    for z in range(2):
        put("w0_%d" % z, g("rwkv_w0")[0, z]); put("a0_%d" % z, g("rwkv_a0")[0, z])
        put("gb_%d" % z, g("gla_gate_bias")[0, z])
    for nm, k in (("k_k", "rwkv_k_k"), ("k_a", "rwkv_k_a"), ("r_k", "rwkv_r_k"), ("gn_w", "rwkv_gn_w"),
                  ("gn_b", "rwkv_gn_b")):
        put(nm, g(k)[0])
    put("gnorm", g("gla_norm")[0])
    lora = np.zeros((128, 2, 512), np.float32)
    lora[0:64] = np.transpose(g("rwkv_w_up")[0], (1, 0, 2))
    lora[64:128] = np.transpose(g("rwkv_a_up")[0], (1, 0, 2))
    p = {"cols": cols, "lora": lora,
         "w_in0": g("even_w_in")[0], "w_out0": g("even_w_out")[0],
         "w_in1": g("odd_w_in")[0], "w_out1": g("odd_w_out")[0],
         "gate_up": np.ascontiguousarray(np.transpose(g("gla_gate_up")[0], (1, 0, 2))),
         "fnorm": np.ascontiguousarray(np.broadcast_to(g("final_norm")[None, :], (128, D)))}
    return p


PARAM_SHAPES = {"cols": [128, NCOLS], "lora": [128, 2, 512], "w_in0": [D, EVEN_COLS], "w_out0": [D, D],
                "w_in1": [D, ODD_COLS], "w_out1": [D, D], "gate_up": [16, 2, 512], "fnorm": [128, D]}


class K:
    pass


def mm(P, out, lhsT, rhs, start, stop, reads, writes):
    n = _numel(out)
    c = 60.0 + max(64, n) / 1.2 * (4.0 if lhsT.dtype == F32 else 1.0)
    o = P.op("pe", lambda e: e.matmul(out, lhsT=lhsT, rhs=rhs, start=start, stop=stop), reads, writes, cost=c)
    o.lat = c + 100.0


def act(P, out, in_, func, reads, writes, **kw):
    P.op("act", lambda e: e.activation(out=out, in_=in_, func=func, **kw), reads, writes,
         cost=230.0 + _numel(out) / 1.2)


def _vcost(eng, out):
    n = _numel(out)
    return (80.0 + n * 1.05) if eng == "dve" else (150.0 + n * 2.3)


def tt(P, eng, out, in0, in1, op, reads, writes):
    P.op(eng, lambda e: e.tensor_tensor(out=out, in0=in0, in1=in1, op=op), reads, writes, cost=_vcost(eng, out))


def ts(P, eng, out, in0, s1, s2, op0, op1, reads, writes):
    if op1 is None:
        P.op(eng, lambda e: e.tensor_scalar(out=out, in0=in0, scalar1=s1, scalar2=None, op0=op0), reads, writes,
             cost=_vcost(eng, out))
    else:
        P.op(eng, lambda e: e.tensor_scalar(out=out, in0=in0, scalar1=s1, scalar2=s2, op0=op0, op1=op1), reads, writes,
             cost=_vcost(eng, out))


def stt(P, out, in0, scalar, in1, op0, op1, reads, writes):
    P.op("dve", lambda e: e.scalar_tensor_tensor(out=out, in0=in0, scalar=scalar, in1=in1, op0=op0, op1=op1),
         reads, writes, cost=_vcost("dve", out))


def cp(P, eng, out, in_, reads, writes):
    if eng == "act":
        P.op("act", lambda e: e.activation(out=out, in_=in_, func=AF.Copy), reads, writes,
             cost=230.0 + _numel(out) / 1.2)
    else:
        P.op(eng, lambda e: e.tensor_copy(out=out, in_=in_), reads, writes, cost=_vcost(eng, out))


def rsqrt(P, out, in_, scale, bias, reads, writes):
    act(P, out, in_, AF.Ln, reads, writes, scale=scale, bias=bias)
    act(P, out, out, AF.Exp, writes, writes, scale=-0.5)


def col(k, name, j=0, n=1, rows=slice(0, 128)):
    o, _ = COLS[name]
    return k.cols[rows, o + j:o + j + n]


def build(seqs, n_stage=99, debug=()):
    nc = bass.Bass("TRN2", target_bir_lowering=False)
    k = K()
    k.nc = nc
    k.seqs = list(seqs)
    T = sum(seqs)
    k.T = T
    smax = max(seqs)
    k.seq_off = [sum(seqs[:i]) for i in range(len(seqs))]
    ein = lambda n, sh, dt=F32: nc.dram_tensor(n, sh, dt, kind="ExternalInput").ap()
    scr = lambda n, sh, dt=BF16: nc.dram_tensor(n, sh, dt, kind=("ExternalOutput" if n in debug else "Internal")).ap()
    k.x = ein("x", [T, D])
    k.y = nc.dram_tensor("y", [T, D], F32, kind="ExternalOutput").ap()
    k.c = {n: ein("c_" + n, sh) for n, sh in CONST_SHAPES(smax).items()}
    k.p = {n: ein("p_" + n, sh) for n, sh in PARAM_SHAPES.items()}
    k.RW_T = scr("RW_T", [EVEN_SHIFT, T])
    k.GA_T = scr("GA_T", [512, T])
    k.QK_T = scr("QK_T", [1024, T])
    k.VB = scr("VB", [T, 512])
    k.GB_T = scr("GB_T", [512, T])
    k.YF_T = scr("YF_T", [512, T], F32)
    k.Y0_T = scr("Y0_T", [1024, T])
    k.X1 = scr("X1", [T, D], F32)
    k.Q1_T = scr("Q1_T", [1024, T])
    k.GL_T = scr("GL_T", [16, T], F32)
    k.V1 = scr("V1", [T, D])
    k.G1_T = scr("G1_T", [D, T])
    k.OF_T = scr("OF_T", [D, T], F32)
    k.Y1_T = scr("Y1_T", [D, T])

    with contextlib.ExitStack() as gst:
        P = Prog(nc, gst)
        k.P = P
        sb = lambda n, sh, dt: gst.enter_context(nc.sbuf_tensor(n, sh, dt))
        k.cols = sb("cols", [128, NCOLS + 32], F32)
        k.cb = {}
        k.cf = {}
        for n in ("ident", "bd1", "ones", "rmask", "rmask128"):
            k.cf[n] = sb("cf_" + n, CONST_SHAPES(smax)[n], F32)
        BN = ("ident", "rot", "m_f", "m_b", "g_f", "g_b", "a_g", "a_0", "bd1")
        for n in BN:
            k.cb[n] = sb("cb_" + n, CONST_SHAPES(smax)[n], BF16)
        k.lora = sb("lora", [128, 2, 512], BF16)
        with contextlib.ExitStack() as st:
            tmp = st.enter_context(nc.sbuf_tensor("ctmp", [128, 2048], F32))
            P.dma("sp", k.cols[:, 0:NCOLS], k.p["cols"][:, :], writes=["cols"])
            for n in ("ident", "bd1", "ones", "rmask", "rmask128"):
                P.dma("sp", k.cf[n][:], k.c[n][:, :], writes=["cf_" + n])
            off = 0
            for n in BN:
                sh = CONST_SHAPES(smax)[n]
                P.dma("sp", tmp[0:sh[0], off:off + sh[1]], k.c[n][:, :], writes=["ctmp"])
                cp(P, "dve", k.cb[n][:], tmp[0:sh[0], off:off + sh[1]], ["ctmp"], ["cb_" + n])
                off += sh[1]
            P.stage_end()
            P.dma("sp", tmp[:, 0:1024], k.p["lora"].rearrange("p z c -> p (z c)"), writes=["ctmp2"])
            cp(P, "dve", k.lora[:].rearrange("p z c -> p (z c)"), tmp[:, 0:1024], ["ctmp2"], ["lora"])
            o_mup, o_mun, o_ka = COLS["mup"][0], COLS["mun"][0], COLS["k_a"][0]
            k.o_c0, k.o_omk = NCOLS, NCOLS + 13
            tt(P, "dve", k.cols[:, k.o_c0:k.o_c0 + 13], k.cols[:, o_mup:o_mup + 13], k.cols[:, o_mun:o_mun + 13],
               ALU.add, ["cols"], ["cols"])
            ts(P, "dve", k.cols[:, k.o_c0:k.o_c0 + 13], k.cols[:, k.o_c0:k.o_c0 + 13], -1.0, 1.0, ALU.mult, ALU.add,
               ["cols"], ["cols"])
            ts(P, "dve", k.cols[:, k.o_omk:k.o_omk + 4], k.cols[:, o_ka:o_ka + 4], -1.0, 1.0, ALU.mult, ALU.add,
               ["cols"], ["cols"])
            P.stage_end()
        stages = [stage1, stage2_attn, stage3_rwkv, stage4_out0, stage5_in1, stage6_gla, stage7_out1]
        for i, s in enumerate(stages):
            if i < n_stage:
                s(k)
        P.stage_end()
    k.n_inst = P.n_inst
    return nc, k


def load_weights(k, st, specs):
    P, nc = k.P, k.nc
    ws = [st.enter_context(nc.sbuf_tensor(name, [128, rows, ncol], BF16)) for name, dram, rows, ncol, key in specs]
    with contextlib.ExitStack() as inner:
        ring = Ring(nc, inner, "wstage", [128, 1056], F32, 3)
        for w, (name, dram, rows, ncol, key) in zip(ws, specs):
            v = dram.rearrange("(kc p) c -> p kc c", p=128)
            for kc in range(rows):
                for c0 in range(0, ncol, 1056):
                    c1 = min(ncol, c0 + 1056)
                    t, tk = ring.next()
                    P.dma("sp", t[:, 0:c1 - c0], v[:, kc, c0:c1], writes=[tk])
                    cp(P, "dve" if (c0 // 1056) % 2 else "pool", w[:, kc, c0:c1], t[:, 0:c1 - c0], [tk], [])
        P.stage_end()
    return ws


def rms_transpose(k, st, rings, src_ap, normcol, t0):
    P = k.P
    xt, kx = rings["x"].next()
    P.dma("sp", xt[:], src_ap[t0:t0 + 512, :].rearrange("(j p) d -> p j d", p=128), writes=[kx])
    return xt, kx


def rms_transpose_compute(k, rings, xt, kx, normname):
    P = k.P
    ss, kss = rings["ss"].next()
    junk, kj = rings["junk"].next()
    for j in range(4):
        act(P, junk[:], xt[:, j, :], AF.Square, [kx], [kss, kj], accum_out=ss[:, j:j + 1])
    rsqrt(P, ss[:, 0:4], ss[:, 0:4], 1.0 / D, RMS_EPS, [kss], [kss])
    xn, kxn = rings["xn"].next()
    for j in range(4):
        if j % 2 == 0:
            ts(P, "dve", xn[:, j, :], xt[:, j, :], ss[:, j:j + 1], None, ALU.mult, None, [kx, kss], [kxn])
        else:
            act(P, xn[:, j, :], xt[:, j, :], AF.Copy, [kx, kss], [kxn], scale=ss[:, j:j + 1])
    xnT, kT = rings["xnT"].next()
    for c in range(8):
        ps, kp = rings["pst"].next()
        for j in range(4):
            mm(P, ps[:, 128 * j:128 * j + 128], xn[:, j, 128 * c:128 * c + 128], k.cb["ident"][:], True, True,
               [kxn, "cb_ident"], [kp])
        if c % 2 == 0:
            act(P, xnT[:, c, :], ps[:], AF.Copy, [kp, "cols"], [kT], scale=col(k, normname, c))
        else:
            ts(P, "dve", xnT[:, c, :], ps[:], col(k, normname, c), None, ALU.mult, None, [kp, "cols"], [kT])
    return xnT, kT


def in_rings(k, st):
    nc = k.nc
    return {"x": Ring(nc, st, "xt", [128, 4, D], F32, 2), "ss": Ring(nc, st, "ss", [128, 4], F32, 2),
            "junk": Ring(nc, st, "junk", [128, D], BF16, 1), "xn": Ring(nc, st, "xn", [128, 4, D], BF16, 2),
            "xnT": Ring(nc, st, "xnT", [128, 8, 512], BF16, 2),
            "pst": Ring(nc, st, "pst", [128, 512], F32, 2, psum=True)}


def seq_pos(k, t0):
    for off, S in zip(k.seq_off, k.seqs):
        if off <= t0 < off + S:
            return t0 - off
    raise ValueError


def stage1(k):
    P, nc, T = k.P, k.nc, k.T
    with contextlib.ExitStack() as st:
        W, = load_weights(k, st, [("w0", k.p["w_in0"], 8, EVEN_COLS, "W")])
        R = in_rings(k, st)
        psr = Ring(nc, st, "ps1", [128, 512], F32, 3, psum=True)
        psrot = Ring(nc, st, "psrot", [128, 512], F32, 2, psum=True)
        ost = Ring(nc, st, "ost", [128, 512], BF16, 6)
        qraw = Ring(nc, st, "qraw", [128, 512], BF16, 2)
        ra = Ring(nc, st, "ra", [128, 512], F32, 2)
        rb = Ring(nc, st, "rb", [128, 512], F32, 2)
        cs = Ring(nc, st, "cs", [128, 2, 512], F32, 2)
        ntile = T // 512
        nxt = rms_transpose(k, st, R, k.x, "norm0", 0)
        for ti in range(ntile):
            t0 = ti * 512
            xt, kx = nxt
            if ti + 1 < ntile:
                nxt = rms_transpose(k, st, R, k.x, "norm0", t0 + 512)
            pos = seq_pos(k, t0)
            cst, kcs = cs.next()
            P.dma("sp", cst[:, 0, :], k.c["cos"][:, pos:pos + 512], writes=[kcs])
            P.dma("sp", cst[:, 1, :], k.c["sin"][:, pos:pos + 512], writes=[kcs])
            xnT, kT = rms_transpose_compute(k, R, xt, kx, "norm0")
            for oc in range(33):
                if 25 <= oc < 29:
                    j = oc - 25
                    ps, kp = psr.next()
                    for kc in range(8):
                        mm(P, ps[:], xnT[:, kc, 128 * j:128 * j + 128], W[:, kc, 3200:3712], kc == 0, kc == 7,
                           [kT, "W"], [kp])
                    o, ko = ost.next()
                    cp(P, "act" if j % 2 else "dve", o[:], ps[:], [kp], [ko])
                    P.dma("sp", k.VB[t0 + 128 * j:t0 + 128 * j + 128, :], o[:], reads=[ko], writes=[("VB", ti)])
                    continue
                ps, kp = psr.next()
                for kc in range(8):
                    mm(P, ps[:], W[:, kc, 128 * oc:128 * oc + 128], xnT[:, kc, :], kc == 0, kc == 7, [kT, "W"], [kp])
                o, ko = ost.next()
                if oc < 13:
                    cp(P, "act" if oc % 2 else "dve", o[:], ps[:], [kp], [ko])
                    P.dma("sp", k.RW_T[128 * oc:128 * oc + 128, t0:t0 + 512], o[:], reads=[ko], writes=[("RW_T", ti)])
                elif oc < 17 or oc >= 29:
                    act(P, o[:], ps[:], AF.Silu, [kp], [ko])
                    dst = k.GA_T if oc < 17 else k.GB_T
                    r0 = 128 * (oc - 13) if oc < 17 else 128 * (oc - 29)
                    P.dma("sp", dst[r0:r0 + 128, t0:t0 + 512], o[:], reads=[ko],
                          writes=[("GA_T" if oc < 17 else "GB_T", ti)])
                else:
                    qr, kq = qraw.next()
                    cp(P, "act", qr[:], ps[:], [kp], [kq])
                    pr, kpr = psrot.next()
                    mm(P, pr[:], k.cb["rot"][:], qr[:], True, True, [kq, "cb_rot"], [kpr])
                    a, ka = ra.next()
                    tt(P, "pool", a[:], qr[:], cst[:, 0, :], ALU.mult, [kq, kcs], [ka])
                    b, kb = rb.next()
                    tt(P, "dve", b[:], pr[:], cst[:, 1, :], ALU.mult, [kpr, kcs], [kb])
                    tt(P, "pool", o[:], a[:], b[:], ALU.add, [ka, kb], [ko])
                    r0 = 128 * (oc - 17)
                    P.dma("sp", k.QK_T[r0:r0 + 128, t0:t0 + 512], o[:], reads=[ko], writes=[("QK_T", ti)])
        P.stage_end()


class SRing:
    def __init__(self, items):
        self.items = items
        self.i = -1

    def next(self):
        self.i = (self.i + 1) % len(self.items)
        return self.items[self.i]


GLA_CUT = 0
PATTERNS = ((128, 1), (512, 4), (2048, 16))


def stage2_attn(k):
    P, nc = k.P, k.nc
    smax = max(k.seqs)
    with contextlib.ExitStack() as st:
        qsr = Ring(nc, st, "qs", [128, smax], BF16, 1)
        ksr = Ring(nc, st, "ks", [128, smax], BF16, 1)
        qdr = {d: Ring(nc, st, "qd%d" % d, [128, smax], BF16, 1) for d in (4, 16)}
        kdr = {d: Ring(nc, st, "kd%d" % d, [128, smax], BF16, 1) for d in (4, 16)}
        acc = [st.enter_context(nc.sbuf_tensor("acc%d" % h, [65, smax], F32)) for h in range(2)]
        vtr = Ring(nc, st, "vt", [128, 2, 65], BF16, 6)
        for t, key in zip(vtr.tiles, vtr.keys):
            P.op("pool", (lambda t: lambda e: e.memset(t[:], 1.0))(t), writes=[key])
        pst = [st.enter_context(nc.psum_tensor("psS%d" % i, [128, 512], F32)) for i in range(2)]
        psr = SRing([(pst[i], "psS%d" % i) for i in range(2)])
        po = [[(st.enter_context(nc.psum_tensor("psO%d_%d" % (h, b), [65, 128], F32)), "psO%d_%d" % (h, b))
               for b in range(2)] for h in range(2)]
        pdr = Ring(nc, st, "psD", [64, 512], F32, 2, psum=True)
        ptr_ = Ring(nc, st, "pt", [128, 256], BF16, 4)
        pmr = Ring(nc, st, "pm", [128, 256], BF16, 4)
        rcr = Ring(nc, st, "rc", [64, 512], F32, 2)
        tmr = Ring(nc, st, "tm", [64, 512], F32, 2)
        gr = Ring(nc, st, "gb", [64, 512], BF16, 2)
        outr = Ring(nc, st, "ao", [64, 512], BF16, 2)
        cnt = 0
        for s0, S in zip(k.seq_off, k.seqs):
            nchunk = S // 128
            for hp in range(4):
                qs, kq = qsr.next()
                ks_, kk_ = ksr.next()
                P.dma("sp", qs[:, 0:S], k.QK_T[128 * hp:128 * hp + 128, s0:s0 + S], writes=[kq])
                P.dma("sp", ks_[:, 0:S], k.QK_T[512 + 128 * hp:512 + 128 * hp + 128, s0:s0 + S], writes=[kk_])
                qv, kv_ = {1: (qs, kq)}, {1: (ks_, kk_)}
                for d in (4, 16):
                    qd, kqd = qdr[d].next()
                    kd, kkd = kdr[d].next()
                    cp(P, "act", qd[:, 0:S].rearrange("p (r i) -> p r i", r=d),
                       qs[:, 0:S].rearrange("p (i r) -> p r i", r=d), [kq], [kqd])
                    cp(P, "dve", kd[:, 0:S].rearrange("p (r i) -> p r i", r=d),
                       ks_[:, 0:S].rearrange("p (i r) -> p r i", r=d), [kk_], [kkd])
                    qv[d] = (qd, kqd)
                    kv_[d] = (kd, kkd)
                for h in range(2):
                    P.op("pool", (lambda a: lambda e: e.memset(a, 0.0))(acc[h][:, 0:S]),
                         writes=[("acc", h, c) for c in range(nchunk)])
                for (win, d) in PATTERNS:
                    sub = S // d
                    nb = sub // 128
                    assert sub % 128 == 0 and nb >= 1
                    qt, kqt = qv[d]
                    kt, kkt = kv_[d]
                    for r in range(d):
                        for kb in range(nb + 1):
                            lo = max(0, 128 * kb - 64)
                            hi = min(sub, 128 * kb + 64)
                            nk = hi - lo
                            v, kv = vtr.next()
                            rows = k.VB[s0 + r + d * lo:s0 + r + d * (hi - 1) + 1:d, 128 * hp:128 * hp + 128]
                            P.dma("sp", v[0:nk, :, 0:64], rows.rearrange("p (h e) -> p h e", h=2), writes=[kv])
                            qb_lo = max(0, kb - 1)
                            qb_hi = min(nb - 1, kb)
                            nq = 128 * (qb_hi - qb_lo + 1)
                            if kb == 0:
                                mask = k.cb["a_0"][0:64, 0:128]
                            elif kb == nb:
                                mask = k.cb["a_g"][0:64, 0:128]
                            else:
                                mask = k.cb["a_g"][:, 0:256]
                            for h in range(2):
                                ph = slice(64 * h, 64 * h + 64)
                                keys = kt[ph, r * sub + lo:r * sub + hi]
                                qry = qt[ph, r * sub + 128 * qb_lo:r * sub + 128 * qb_lo + nq]
                                ps, kps = psr.next()
                                mm(P, ps[0:nk, 0:nq], keys, qry, True, True, [kqt, kkt], [kps])
                                e, ke = ptr_.next()
                                act(P, e[0:nk, 0:nq], ps[0:nk, 0:nq], AF.Exp, [kps], [ke], scale=0.125)
                                m, km = pmr.next()
                                cnt += 1
                                tt(P, "dve" if cnt % 3 else "pool", m[0:nk, 0:nq], e[0:nk, 0:nq], mask, ALU.mult,
                                   [ke, "cb_a_g", "cb_a_0"], [km])
                                for qb in range(qb_lo, qb_hi + 1):
                                    first = (kb == qb)
                                    pv, kpo = po[h][qb % 2]
                                    c0 = 128 * (qb - qb_lo)
                                    mm(P, pv[0:65, :], v[0:nk, h, 0:65], m[0:nk, c0:c0 + 128], first, not first,
                                       [kv, km], [kpo])
                                    if not first:
                                        a0 = r + d * 128 * qb
                                        pos = acc[h][0:65, a0:a0 + d * 127 + 1:d]
                                        ck = [("acc", h, c) for c in range(d * qb, d * qb + d)]
                                        tt(P, "dve", pos, pos, pv[0:65, :], ALU.add, [kpo] + ck, ck)
                for h in range(2):
                    for c0 in range(0, S, 512):
                        ck = [("acc", h, c) for c in range(c0 // 128, c0 // 128 + 4)]
                        pd, kpd = pdr.next()
                        mm(P, pd[0:64, :], k.cf["ones"][64:65, 0:64], acc[h][64:65, c0:c0 + 512], True, True,
                           ["cf_ones"] + ck, [kpd])
                        rc, krc = rcr.next()
                        act(P, rc[:], pd[0:64, :], AF.Ln, [kpd], [krc])
                        act(P, rc[:], rc[:], AF.Exp, [krc], [krc], scale=-1.0)
                        g, kg = gr.next()
                        r0 = 128 * hp + 64 * h
                        P.dma("sp", g[:], k.GB_T[r0:r0 + 64, s0 + c0:s0 + c0 + 512], writes=[kg])
                        tm, ktm = tmr.next()
                        tt(P, "pool", tm[:], acc[h][0:64, c0:c0 + 512], rc[:], ALU.mult, [krc] + ck, [ktm])
                        o, ko = outr.next()
                        tt(P, "dve", o[:], tm[:], g[:], ALU.mult, [ktm, kg], [ko])
                        P.dma("sp", k.Y0_T[512 + r0:512 + r0 + 64, s0 + c0:s0 + c0 + 512], o[:], reads=[ko],
                              writes=[("Y0_T", r0, c0)])
        P.stage_end()


def stage3_rwkv(k):
    P, nc = k.P, k.nc
    with contextlib.ExitStack() as st:
        A = lambda n, sh, dt=F32: st.enter_context(nc.sbuf_tensor(n, sh, dt))
        RG = lambda n, sh, dt, c: Ring(nc, st, n, sh, dt, c)
        raw = {n: RG("raw_" + n, [128, 514], BF16, 2) for n in ("r", "k", "v", "la")}
        f = {n: RG("f_" + n, [128, 512], F32, 1) for n in
             ("t1", "t2", "rs", "ks", "las", "sg", "lr", "kkr", "sq", "rn", "kk", "tka", "kd", "beta",
              "ci", "cf", "ce", "E1", "E2", "E3", "E4", "lro", "e1", "e2", "e3", "e4", "e6")}
        f["vs"] = RG("f_vs", [128, 512], F32, 3)
        prb = RG("prb", [128, 512], BF16, 3)
        twal = RG("twal", [128, 512], BF16, 1)
        pc = RG("pc", [128, 8], F32, 3)
        AR = RG("AR", [128, 4, 2, 128], BF16, 3)
        ob = {n: RG("o_" + n, [128, 512], BF16, 3) for n in ("Bt", "Kt", "Be", "Ke", "vb")}
        TB = RG("TB", [128, 512], BF16, 8)
        G13r = RG("G13r", [128, 384], BF16, 4)
        G2r = RG("G2r", [128, 256], BF16, 4)
        G13 = RG("G13", [128, 384], BF16, 16)
        G2 = RG("G2", [128, 256], BF16, 16)
        Z0 = RG("Z0", [128, 128], BF16, 8)
        ZP = RG("ZP", [128, 384], BF16, 16)
        Z6 = RG("Z6", [128, 128], BF16, 16)
        UT = RG("UT", [128, 2, 128], BF16, 3)
        for t_, k_ in zip(UT.tiles, UT.keys):
            P.op("pool", (lambda a: lambda e: e.memset(a, 0.0))(t_[:]), writes=[k_])
        AW = RG("AW", [128, 512], BF16, 2)
        YT = RG("YT", [128, 512], F32, 2)
        yf = RG("yf", [128, 512], F32, 2)
        gat = RG("gat", [128, 512], BF16, 2)
        yo = RG("yo", [128, 512], BF16, 2)
        ST2 = A("ST2", [128, 128])
        STb2 = A("STb2", [128, 128], BF16)
        pbank = [st.enter_context(nc.psum_tensor("pg%d" % i, [128, 512], F32)) for i in range(7)]
        psg = SRing([(pbank[i], "pg%d" % i) for i in range(4)])
        psc = SRing([(pbank[i], "pg%d" % i) for i in range(4, 6)])
        pse = SRing([(pbank[i], "pg%d" % i) for i in range(6, 7)])
        psy = st.enter_context(nc.psum_tensor("psy", [128, 512], F32))
        idb, bd1 = k.cb["ident"], k.cf["bd1"]
        bd1b = k.cb["bd1"]
        sqb = RG("sqb", [128, 512], BF16, 2)
        k.o_omk2 = k.o_omk + 4
        ts(P, "dve", k.cols[:, k.o_omk2:k.o_omk2 + 4], k.cols[:, k.o_omk:k.o_omk + 4], 2.0, None, ALU.mult, None,
           ["cols"], ["cols"])
        ev = [0]
        lvl = [0]

        def evac_eng():
            ev[0] += 1
            return "act" if ev[0] % 2 else "dve"

        def cpx(out, in_, reads, writes):
            cp(P, evac_eng(), out, in_, reads, writes)
        v3 = lambda a: a.rearrange("p (c j) -> p c j", j=64)
        v4 = lambda a: a.rearrange("p (b t) -> p b t", t=128)

        def elem(c):
            s0, S, hp, z, ti, ntile = c["item"]
            t0 = s0 + 512 * ti
            c["t0"] = t0
            lo_edge, hi_edge = (ti == 0), (ti == ntile - 1)
            rt = {}
            for n, r0 in (("r", 128 * hp), ("k", 512 + 128 * hp), ("v", 1024 + 128 * hp), ("la", 1536)):
                t, key = raw[n].next()
                if lo_edge:
                    P.op("pool", (lambda a: lambda e: e.memset(a, 0.0))(t[:, 0:1]), writes=[key])
                if hi_edge:
                    P.op("pool", (lambda a: lambda e: e.memset(a, 0.0))(t[:, 513:514]), writes=[key])
                c0 = 1 if lo_edge else 0
                c1 = 513 if hi_edge else 514
                P.dma("sp", t[:, c0:c1], k.RW_T[r0:r0 + 128, t0 - 1 + c0:t0 - 1 + c1], writes=[key])
                rt[n] = (t, key)
            yield
            sh = {}
            for n, mc, dst in (("r", hp, "rs"), ("k", 4 + hp, "ks"), ("v", 8 + hp, "vs"), ("la", 12, "las")):
                t, key = rt[n]
                t1, k1 = f["t1"].next()
                t2, k2 = f["t2"].next()
                o, ko = f[dst].next()
                act(P, t1[:], t[:, 0:512], AF.Copy, [key, "cols"], [k1], scale=col(k, "mup", mc))
                stt(P, t2[:], t[:, 2:514], col(k, "mun", mc), t1[:], ALU.mult, ALU.add, [key, k1, "cols"], [k2])
                stt(P, o[:], t[:, 1:513], k.cols[:, k.o_c0 + mc:k.o_c0 + mc + 1], t2[:], ALU.mult, ALU.add,
                    [key, k2, "cols"], [ko])
                sh[dst] = (o, ko)
                yield
            rs, krs = sh["rs"]; ks_, kks = sh["ks"]; vs, kvs = sh["vs"]; las, klas = sh["las"]
            c["vs"] = (vs, kvs)
            tw, ktw = twal.next()
            act(P, tw[0:64, :], las[0:64, :], AF.Tanh, [klas], [ktw])
            cp(P, "dve", tw[64:128, :], las[64:128, :], [klas], [ktw])
            hc = slice(128 * hp, 128 * hp + 128)

            def lora_sig(zz, wsel, dst, kdst, bias_name):
                rows = slice(0, 64) if wsel == "w" else slice(64, 128)
                pw, kpw = pse.next()
                mm(P, pw[:], k.lora[rows, zz, hc], tw[rows, :], True, True, [ktw, "lora"], [kpw])
                act(P, dst[:], pw[:], AF.Sigmoid, [kpw, "cols"], [kdst], bias=col(k, bias_name, hp))
            sg, ksg = f["sg"].next()
            lr, klr = f["lr"].next()
            lora_sig(z, "w", sg, ksg, "w0_%d" % z)
            lora_sig(z, "a", lr, klr, "a0_%d" % z)
            yield
            sq, ksq = sqb.next()
            act(P, sq[:], ks_[:], AF.Square, [kks, "cols"], [ksq], scale=col(k, "k_k", hp))
            rn, krn = f["rn"].next()
            pn, kpn = pse.next()
            mm(P, pn[:], bd1b[:], sq[:], True, True, [ksq, "cb_bd1"], [kpn])
            act(P, rn[:], pn[:], AF.Ln, [kpn], [krn], bias=1e-18)
            act(P, rn[:], rn[:], AF.Exp, [krn], [krn], scale=-0.5)
            kk, kkk = f["kk"].next()
            stt(P, kk[:], ks_[:], col(k, "k_k", hp), rn[:], ALU.mult, ALU.mult, [kks, krn, "cols"], [kkk])
            yield
            tka, ktka = f["tka"].next()
            ts(P, "dve", tka[:], lr[:], col(k, "k_a", hp), k.cols[:, k.o_omk + hp:k.o_omk + hp + 1],
               ALU.mult, ALU.add, [klr, "cols"], [ktka])
            kd, kkd = f["kd"].next()
            tt(P, "pool", kd[:], ks_[:], tka[:], ALU.mult, [kks, ktka], [kkd])
            beta, kbeta = f["beta"].next()
            tt(P, "pool", beta[:], kk[:], lr[:], ALU.mult, [kkk, klr], [kbeta])
            cf_, kcf = f["cf"].next()
            P.op("dve", (lambda o, a, b: lambda e: e.tensor_tensor_scan(
                out=o, data0=a, data1=b, initial=0.0, op0=ALU.mult, op1=ALU.add))(
                cf_[:], k.cf["rmask"][:], sg[:]), [ksg, "cf_rmask"], [kcf])
            tot = cf_[:, 63:512:64]
            totb = tot.unsqueeze(2).to_broadcast([128, 8, 64])
            if z == 0:
                ci, kci = cf_, kcf
            else:
                ci, kci = f["ci"].next()
                tt(P, "pool", ci[:], sg[:], cf_[:], ALU.subtract, [ksg, kcf], [kci])
                tt(P, "dve", v3(ci[:]), v3(ci[:]), totb, ALU.add, [kci, kcf], [kci])
            ce, kce = f["ce"].next()
            tt(P, "pool", ce[:], ci[:], sg[:], ALU.subtract, [kci, ksg], [kce])
            yield
            E1, kE1 = f["E1"].next(); E2, kE2 = f["E2"].next(); E3, kE3 = f["E3"].next(); E4, kE4 = f["E4"].next()
            act(P, E1[:], ce[:], AF.Exp, [kce], [kE1], scale=-DECAY_C)
            act(P, E2[:], ci[:], AF.Exp, [kci], [kE2], scale=DECAY_C)
            act(P, E3[:], ci[:], AF.Exp, [kci], [kE3], scale=-DECAY_C)
            pct, kpc = pc.next()
            act(P, pct[:], tot, AF.Exp, [kcf], [kpc], scale=-DECAY_C)
            c["pc"] = (pct, kpc)
            pcb = pct[:].unsqueeze(2).to_broadcast([128, 8, 64])
            tt(P, "dve", v3(E4[:]), v3(E2[:]), pcb, ALU.mult, [kE2, kpc], [kE4])
            yield
            art, kar = AR.next()
            stt(P, art[:, :, 0, :], v4(kk[:]), -1.0, v4(E1[:]), ALU.mult, ALU.mult, [kkk, kE1], [kar])
            tt(P, "pool", art[:, :, 1, :], v4(rs[:]), v4(E3[:]), ALU.mult, [krs, kE3], [kar])
            Bt, kBt = ob["Bt"].next(); Kt, kKt = ob["Kt"].next()
            Be, kBe = ob["Be"].next(); Ke, kKe = ob["Ke"].next(); vb, kvb = ob["vb"].next()
            tt(P, "pool", Bt[:], beta[:], E2[:], ALU.mult, [kbeta, kE2], [kBt])
            tt(P, "pool", Kt[:], kd[:], E2[:], ALU.mult, [kkd, kE2], [kKt])
            yield
            tt(P, "pool", Be[:], beta[:], E4[:], ALU.mult, [kbeta, kE4], [kBe])
            tt(P, "pool", Ke[:], kd[:], E4[:], ALU.mult, [kkd, kE4], [kKe])
            cp(P, "act", vb[:], vs[:], [kvs], [kvb])
            c.update(art=(art, kar), Bt=(Bt, kBt), Kt=(Kt, kKt), Be=(Be, kBe), Ke=(Ke, kKe), vb=(vb, kvb))
            if z == 1:
                yield
                lro, klro = f["lro"].next()
                lora_sig(0, "a", lro, klro, "a0_0")
                tt(P, "pool", lro[:], lro[:], lr[:], ALU.add, [klro, klr], [klro])
                ts(P, "dve", lro[:], lro[:], col(k, "k_a", hp), k.cols[:, k.o_omk2 + hp:k.o_omk2 + hp + 1],
                   ALU.mult, ALU.add, [klro, "cols"], [klro])
                tt(P, "pool", lro[:], lro[:], ks_[:], ALU.mult, [klro, kks], [klro])
                pr_, kpr_ = prb.next()
                stt(P, pr_[:], rs[:], col(k, "r_k", hp), lro[:], ALU.mult, ALU.mult, [krs, klro, "cols"], [kpr_])
                c["pr"] = (pr_, kpr_)

        def wave(c):
            s0, S, hp, z, ti, ntile = c["item"]
            mask = k.cb["m_f" if z == 0 else "m_b"]
            mkeys = ["cb_m_f", "cb_m_b"]
            art, kar = c["art"]; Bt, kBt = c["Bt"]; Kt, kKt = c["Kt"]
            Be, kBe = c["Be"]; Ke, kKe = c["Ke"]; vb, kvb = c["vb"]
            border = list(range(4)) if z == 0 else list(range(3, -1, -1))
            c["border"] = border
            aw, kaw = AW.next()
            c["aw"] = (aw, kaw)
            tb = {}
            for b in border:
                bc = slice(128 * b, 128 * b + 128)
                p_, kp_ = psg.next()
                for i, (src, skey) in enumerate(((vb, kvb), (Be, kBe), (Ke, kKe))):
                    mm(P, p_[:, 128 * i:128 * i + 128], src[:, bc], idb[:], True, True, [skey, "cb_ident"], [kp_])
                mm(P, p_[:, 384:512], vb[64:128, bc], idb[64:128, :], True, True, [kvb, "cb_ident"], [kp_])
                t, kt = TB.next()
                cpx(t[:], p_[:, 0:512], [kp_], [kt])
                tb[b] = (t, kt)
                yield
            c["tb"] = tb
            units = [(b, h) for b in border for h in range(2)]
            U = {}
            for ui, (b, h) in enumerate(units):
                bc = slice(128 * b, 128 * b + 128)
                ph = slice(64 * h, 64 * h + 64)
                arh = art[ph, b, :, :].rearrange("p a t -> p (a t)")
                p1, kp1 = psg.next()
                mm(P, p1[:, 0:256], Bt[ph, bc], arh, True, True, [kBt, kar], [kp1])
                mm(P, p1[:, 256:384], art[ph, b, 0, :], Bt[ph, bc], True, True, [kBt, kar], [kp1])
                g13, kg13 = G13.next()
                tt(P, "dve", g13[:], p1[:, 0:384], mask[:], ALU.mult, [kp1] + mkeys, [kg13])
                p2, kp2 = psg.next()
                mm(P, p2[:, 0:256], Kt[ph, bc], arh, True, True, [kKt, kar], [kp2])
                g2r, kg2r = G2r.next()
                cp(P, "act", g2r[:], p2[:, 0:256], [kp2], [kg2r])
                g2, kg2 = G2.next()
                tt(P, "pool", g2[:], g2r[:], mask[:, 0:256], ALU.mult, [kg2r] + mkeys, [kg2])
                U[(b, h)] = dict(g13=(g13, kg13), g2=(g2, kg2))
                yield
            for (b, h) in units:
                u = U[(b, h)]
                ph = slice(64 * h, 64 * h + 64)
                g2, kg2 = u["g2"]
                vt, kvt = tb[b]
                za = slice(0, 64) if h == 0 else slice(64, 128)
                zv = slice(64, 128) if h == 0 else slice(0, 64)
                p4, kp4 = psg.next()
                mm(P, p4[:, za], art[ph, b, 0, :], idb[ph, 64 * h:64 * h + 64], True, True, [kar, "cb_ident"], [kp4])
                mm(P, p4[:, zv], g2[:, 0:128], vt[:, 64 * h:64 * h + 64], True, True, [kg2, kvt], [kp4])
                z0, kz0 = Z0.next()
                cpx(z0[:], p4[:, 0:128], [kp4], [kz0])
                g13, kg13 = u["g13"]
                u["Z"] = (z0[:, 0:128], kz0)
                u["P"] = (g13[:, 0:128], kg13)
                u["PT"] = (g13[:, 256:384], kg13)
                if h == 1:
                    yield
            for j in range(6):
                last = (j == 5)
                for (b, h) in units:
                    u = U[(b, h)]
                    Zt, kZ = u["Z"]; Pt, kP = u["P"]; PTt, kPT = u["PT"]
                    ps, kps = psg.next()
                    if not last:
                        if j == 0:
                            mm(P, ps[:, 0:128], Pt, Zt, True, True, [kP, kZ], [kps])
                            mm(P, ps[:, 128:256], Pt, PTt, True, True, [kP, kPT], [kps])
                        else:
                            mm(P, ps[:, 0:256], Pt, u["ZPT"], True, True, [kP, kZ], [kps])
                        mm(P, ps[:, 256:384], PTt, Pt, True, True, [kP, kPT], [kps])
                        zn, kzn = ZP.next()
                        tt(P, "dve", zn[:, 0:128], ps[:, 0:128], Zt, ALU.add, [kps, kZ], [kzn])
                        lvl[0] += 1
                        cp(P, "act" if lvl[0] % 8 else "dve", zn[:, 128:384], ps[:, 128:384], [kps], [kzn])
                        u["Z"] = (zn[:, 0:128], kzn)
                        u["PT"] = (zn[:, 128:256], kzn)
                        u["P"] = (zn[:, 256:384], kzn)
                        u["ZPT"] = zn[:, 0:256]
                    else:
                        mm(P, ps[:, 0:128], Pt, Zt, True, True, [kP, kZ], [kps])
                        z6, kz6 = Z6.next()
                        tt(P, "dve", z6[:], ps[:, 0:128], Zt, ALU.add, [kps, kZ], [kz6])
                        u["z6"] = (z6, kz6)
                    if h == 1:
                        yield
            for (b, h) in units:
                u = U[(b, h)]
                ph = slice(64 * h, 64 * h + 64)
                bc = slice(128 * b, 128 * b + 128)
                z6, kz6 = u["z6"]
                p5, kp5 = psg.next()
                mm(P, p5[:, 0:128], z6[:], idb[:], True, True, [kz6, "cb_ident"], [kp5])
                cpx(aw[ph, bc], p5[ph, 0:128], [kp5], [kaw])
                if h == 1:
                    yield
            c["U"] = U

        def scan(c):
            s0, S, hp, z, ti, ntile = c["item"]
            t0 = c["t0"]
            first = (ti == 0) if z == 0 else (ti == ntile - 1)
            if first:
                P.op("pool", lambda e: e.memset(ST2[:], 0.0), writes=[("ST2", 0), ("ST2", 1)])
                P.op("pool", lambda e: e.memset(STb2[:], 0.0), writes=[("STb2", 0), ("STb2", 1)])
            art, kar = c["art"]
            pct, kpc = c["pc"]
            aw, kaw = c["aw"]
            U = c["U"]
            corder = (0, 1) if z == 0 else (1, 0)
            for b in c["border"]:
                bc = slice(128 * b, 128 * b + 128)
                tbt, ktb = c["tb"][b]
                vt, bet, ket = tbt[:, 0:128], tbt[:, 128:256], tbt[:, 256:384]
                ut, kut = UT.next()
                for cc in corder:
                    rows = slice(64 * cc, 64 * cc + 64)
                    tk = slice(128 * b + 64 * cc, 128 * b + 64 * cc + 64)
                    cidx = 2 * b + cc
                    hop = []
                    for h in range(2):
                        ph = slice(64 * h, 64 * h + 64)
                        pu, kpu = psc.next()
                        mm(P, pu[:, 0:64], aw[ph, bc], STb2[ph, 64 * h:64 * h + 64], True, True, [kaw, ("STb2", h)], [kpu])
                        hop.append((pu, kpu))
                    yield
                    hop2 = []
                    vtz = tbt[:, 384:512]
                    for h in range(2):
                        u = U[(b, h)]
                        z6, kz6 = u["z6"]
                        pu, kpu = hop[h]
                        if h == 0:
                            tt(P, "dve", ut[rows, 0, 0:64], pu[rows, 0:64], z6[rows, 64:128], ALU.add, [kpu, kz6], [kut])
                        else:
                            tt(P, "dve", ut[rows, 1, 64:128], pu[rows, 0:64], z6[rows, 0:64], ALU.add, [kpu, kz6], [kut])
                    gA, kgA = U[(b, 0)]["g13"]; g2A, kg2A = U[(b, 0)]["g2"]
                    gB, kgB = U[(b, 1)]["g13"]; g2B, kg2B = U[(b, 1)]["g2"]
                    cs_ = slice(128 + 64 * cc, 128 + 64 * cc + 64)
                    mm(P, psy[:, tk], STb2[:, :], art[:, b, 1, 64 * cc:64 * cc + 64], True, False,
                       [("STb2", 0), ("STb2", 1), kar], ["psy"])
                    mm(P, psy[0:64, tk], ut[rows, 0, 0:64], gA[rows, cs_], False, False, [kut, kgA], ["psy"])
                    mm(P, psy[0:64, tk], vt[rows, 0:64], g2A[rows, cs_], False, False, [ktb, kg2A], ["psy"])
                    mm(P, psy[:, tk], ut[rows, 1, :], gB[rows, cs_], False, False, [kut, kgB], ["psy"])
                    mm(P, psy[:, tk], vtz[rows, :], g2B[rows, cs_], False, True, [ktb, kg2B], ["psy"])
                    for h in range(2):
                        pss, kpss = psc.next()
                        if h == 0:
                            mm(P, pss[0:64, 0:64], bet[rows, 0:64], ut[rows, 0, 0:64], True, False, [ktb, kut], [kpss])
                            mm(P, pss[0:64, 0:64], ket[rows, 0:64], vt[rows, 0:64], False, True, [ktb], [kpss])
                        else:
                            mm(P, pss[:, 0:64], bet[rows, 0:128], ut[rows, 1, 64:128], True, False, [ktb, kut], [kpss])
                            mm(P, pss[:, 0:64], ket[rows, 0:128], vt[rows, 64:128], False, True, [ktb], [kpss])
                        hop2.append((pss, kpss))
                    yield
                    for h in range(2):
                        ph = slice(64 * h, 64 * h + 64)
                        pss, kpss = hop2[h]
                        sv = ST2[ph, 64 * h:64 * h + 64]
                        stt(P, sv, sv, pct[ph, cidx:cidx + 1], pss[ph, 0:64], ALU.mult, ALU.add,
                            [kpss, kpc, ("ST2", h)], [("ST2", h)])
                        cp(P, "act", STb2[ph, 64 * h:64 * h + 64], sv, [("ST2", h)], [("STb2", h)])
                    yield
            yt, kyt = YT.next()
            cp(P, "act", yt[:], psy[:], ["psy"], [kyt])
            rr = slice(128 * hp, 128 * hp + 128)
            if z == 0:
                P.dma("sp", k.YF_T[rr, t0:t0 + 512], yt[:], reads=[kyt], writes=[("YF", hp, t0)])
                return
            vs, kvs = c["vs"]
            pr_, kpr_ = c["pr"]
            yft, kyf = yf.next()
            P.dma("sp", yft[:], k.YF_T[rr, t0:t0 + 512], reads=[("YF", hp, t0)], writes=[kyf])
            gt, kgt = gat.next()
            P.dma("sp", gt[:], k.GA_T[rr, t0:t0 + 512], writes=[kgt])
            y, ky = f["e1"].next()
            tt(P, "pool", y[:], yt[:], yft[:], ALU.add, [kyt, kyf], [ky])
            d_, kd_ = f["e2"].next()
            sq2, ksq2 = f["e3"].next()
            rstd, krstd = f["e4"].next()
            pm, kpm = pse.next()
            mm(P, pm[:], bd1[:], y[:], True, True, [ky, "cf_bd1"], [kpm])
            stt(P, d_[:], pm[:], -1.0 / 64, y[:], ALU.mult, ALU.add, [kpm, ky], [kd_])
            sq2, ksq2 = sqb.next()
            act(P, sq2[:], d_[:], AF.Square, [kd_], [ksq2])
            yield
            pv_, kpv_ = pse.next()
            mm(P, pv_[:], bd1b[:], sq2[:], True, True, [ksq2, "cb_bd1"], [kpv_])
            rsqrt(P, rstd[:], pv_[:], 1.0 / 64, GN_EPS, [kpv_], [krstd])
            tt(P, "pool", d_[:], d_[:], rstd[:], ALU.mult, [kd_, krstd], [kd_])
            ts(P, "dve", d_[:], d_[:], col(k, "gn_w", hp), col(k, "gn_b", hp), ALU.mult, ALU.add, [kd_, "cols"], [kd_])
            yield
            bo, kbo = f["e6"].next()
            pb2, kpb2 = pse.next()
            mm(P, pb2[:], bd1b[:], pr_[:], True, True, [kpr_, "cb_bd1"], [kpb2])
            tt(P, "dve", bo[:], pb2[:], vs[:], ALU.mult, [kpb2, kvs], [kbo])
            tt(P, "pool", bo[:], bo[:], d_[:], ALU.add, [kbo, kd_], [kbo])
            o_, ko_ = yo.next()
            tt(P, "dve", o_[:], bo[:], gt[:], ALU.mult, [kbo, kgt], [ko_])
            P.dma("sp", k.Y0_T[rr, t0:t0 + 512], o_[:], reads=[ko_], writes=[("Y0a", hp, t0)])

        items = []
        for s0, S in zip(k.seq_off, k.seqs):
            ntile = S // 512
            for hp in range(4):
                for z in range(2):
                    order = range(ntile) if z == 0 else range(ntile - 1, -1, -1)
                    for ti in order:
                        items.append(dict(item=(s0, S, hp, z, ti, ntile)))
        n = len(items)
        for step in range(n + 2):
            gens = []
            if step < n:
                gens.append(elem(items[step]))
            if 0 <= step - 1 < n:
                gens.append(wave(items[step - 1]))
            if 0 <= step - 2 < n:
                gens.append(scan(items[step - 2]))
            while gens:
                for g in list(gens):
                    try:
                        next(g)
                    except StopIteration:
                        gens.remove(g)
            if step - 2 >= 0:
                items[step - 2].clear()
        P.stage_end()


def out_proj_tile(k, R, W, ysrc, xsrc, t0, j, pso, xres_ring, dst=None):
    P = k.P
    yT, kyT = ysrc
    xt, kx = xsrc
    if dst is None:
        xr, kxr = xres_ring.next()
    else:
        xr, kxr = dst
    for half in range(2):
        ps, kp = pso.next()
        for kc in range(8):
            mm(P, ps[:], yT[:, kc, 128 * j:128 * j + 128], W[:, kc, 512 * half:512 * half + 512], kc == 0, kc == 7,
               [kyT, "Wo"], [kp])
        tt(P, "dve", xr[:, 512 * half:512 * half + 512], ps[:], xt[:, j, 512 * half:512 * half + 512], ALU.add,
           [kp, kx], [kxr])
    return xr, kxr


def stage4_out0(k):
    P, nc, T = k.P, k.nc, k.T
    with contextlib.ExitStack() as st:
        Wo, W = load_weights(k, st, [("wo0", k.p["w_out0"], 8, D, "Wo"), ("w1", k.p["w_in1"], 8, ODD_COLS, "W")])
        R = in_rings(k, st)
        yr = Ring(nc, st, "y0T", [128, 8, 512], BF16, 2)
        x1r = Ring(nc, st, "x1t", [128, 4, D], F32, 2)
        pso = Ring(nc, st, "pso", [128, 512], F32, 2, psum=True)
        psr = Ring(nc, st, "ps1", [128, 512], F32, 3, psum=True)
        ost = Ring(nc, st, "ost", [128, 512], BF16, 6)
        glr = Ring(nc, st, "glt", [16, 512], F32, 2)
        qscale = 128.0 ** -0.5
        ntile = T // 512

        def loads(t0):
            xt, kx = R["x"].next()
            P.dma("sp", xt[:], k.x[t0:t0 + 512, :].rearrange("(j p) d -> p j d", p=128), writes=[kx])
            yT, kyT = yr.next()
            P.dma("sp", yT[:], k.Y0_T[:, t0:t0 + 512].rearrange("(kc p) t -> p kc t", p=128), writes=[kyT])
            return (xt, kx), (yT, kyT)
        nxt = loads(0)
        for ti in range(ntile):
            t0 = ti * 512
            xsrc, ysrc = nxt
            if ti + 1 < ntile:
                nxt = loads(t0 + 512)
            x1, kx1 = x1r.next()
            for j in range(4):
                xr, kxr = out_proj_tile(k, R, Wo, ysrc, xsrc, t0, j, pso, None, dst=(x1[:, j, :], kx1))
                P.dma("sp", k.X1[t0 + 128 * j:t0 + 128 * j + 128, :], xr, reads=[kxr], writes=[("X1", ti, j)])
            xnT, kT = rms_transpose_compute(k, R, x1, kx1, "norm1")
            for oc in list(range(8)) + list(range(16, 24)):
                ps, kp = psr.next()
                c0 = 128 * oc if oc < 8 else 2064 + 128 * (oc - 16)
                for kc in range(8):
                    mm(P, ps[:], W[:, kc, c0:c0 + 128], xnT[:, kc, :], kc == 0, kc == 7, [kT, "W"], [kp])
                o, ko = ost.next()
                if oc < 4:
                    act(P, o[:], ps[:], AF.Copy, [kp], [ko], scale=qscale)
                elif oc < 8:
                    cp(P, "dve", o[:], ps[:], [kp], [ko])
                else:
                    act(P, o[:], ps[:], AF.Silu, [kp], [ko])
                if oc < 8:
                    P.dma("sp", k.Q1_T[128 * oc:128 * oc + 128, t0:t0 + 512], o[:], reads=[ko], writes=[("Q1", ti, oc)])
                else:
                    r0 = 128 * (oc - 16)
                    P.dma("sp", k.G1_T[r0:r0 + 128, t0:t0 + 512], o[:], reads=[ko], writes=[("G1", ti, oc)])
            ps, kp = psr.next()
            for kc in range(8):
                mm(P, ps[0:16, :], W[:, kc, 2048:2064], xnT[:, kc, :], kc == 0, kc == 7, [kT, "W"], [kp])
            gl, kgl = glr.next()
            cp(P, "act", gl[:], ps[0:16, :], [kp], [kgl])
            P.dma("sp", k.GL_T[:, t0:t0 + 512], gl[:], reads=[kgl], writes=[("GL", ti)])
            for j in range(4):
                for half in range(2):
                    ps, kp = psr.next()
                    c0 = 1024 + 512 * half
                    for kc in range(8):
                        mm(P, ps[:], xnT[:, kc, 128 * j:128 * j + 128], W[:, kc, c0:c0 + 512], kc == 0, kc == 7,
                           [kT, "W"], [kp])
                    o, ko = ost.next()
                    cp(P, "act" if half else "dve", o[:], ps[:], [kp], [ko])
                    P.dma("sp", k.V1[t0 + 128 * j:t0 + 128 * j + 128, 512 * half:512 * half + 512], o[:], reads=[ko],
                          writes=[("V1", ti, j, half)])
        P.stage_end()


def stage5_in1(k):
    pass


def stage6_gla(k):
    P, nc = k.P, k.nc
    with contextlib.ExitStack() as st:
        A = lambda n, sh, dt=F32: st.enter_context(nc.sbuf_tensor(n, sh, dt))
        RG = lambda n, sh, dt, c: Ring(nc, st, n, sh, dt, c)
        gu = A("g_gu", [16, 2, 512])
        P.dma("sp", gu[:], k.p["gate_up"][:, :, :], writes=["gu"])
        ngb = A("g_ngb", [128, 8])
        og = COLS["gb_0"][0]
        ts(P, "dve", ngb[:], k.cols[:, og:og + 8], -1.0, None, ALU.mult, None, ["cols"], ["ngb"])
        def make_lane(li):
            L = {}
            nm = lambda n: "%s_l%d" % (n, li)
            L["Sr"] = RG(nm("g_S"), [128, 256], F32, 2)
            L["Sbr"] = RG(nm("g_Sb"), [128, 256], BF16, 2)
            L["qr"] = RG(nm("g_q"), [128, 512], BF16, 2)
            L["kr"] = RG(nm("g_k"), [128, 512], BF16, 2)
            L["glr"] = RG(nm("g_gl"), [16, 512], F32, 2)
            L["vtr"] = RG(nm("g_v"), [128, 4, 256], BF16, 2)
            L["f"] = {n: RG(nm("gf_" + n), [128, 512], F32, 1) for n in ("e", "l", "cf", "ci", "Eq", "Ek", "Ee", "rstd")}
            L["dcr"] = RG(nm("g_dc"), [128, 4], F32, 2)
            L["ob"] = {n: RG(nm("go_" + n), [128, 512], BF16, 2) for n in ("qd", "kd", "ke")}
            L["attr"] = RG(nm("g_att"), [128, 256], BF16, 3)
            L["otr"] = RG(nm("g_ot"), [128, 2, 512], F32, 2)
            L["ofr"] = RG(nm("g_of"), [128, 2, 512], F32, 1)
            L["sqr"] = RG(nm("g_sq"), [128, 2, 512], F32, 1)
            L["gtr"] = RG(nm("g_gt"), [128, 2, 512], BF16, 1)
            L["yor"] = RG(nm("g_yo"), [128, 512], BF16, 2)
            L["psg"] = Ring(nc, st, nm("pgG"), [128, 512], F32, 4, psum=True)
            return L
        lanes = [make_lane(0), make_lane(1)]
        npass = [0]
        idb = k.cb["ident"]
        ones = k.cf["ones"]
        ev = [0]

        def evac_eng():
            ev[0] += 1
            return "act" if ev[0] % 2 else "dve"
        v3 = lambda a: a.rearrange("p (c j) -> p c j", j=128)
        for s0, S in zip(k.seq_off, k.seqs):
            ntile = S // 512
            for h in range(4):
                for z in range(2):
                    L = lanes[npass[0] % 2]
                    npass[0] += 1
                    Sr, Sbr, qr, kr, glr, vtr, f, dcr, ob = (L[n_] for n_ in ("Sr", "Sbr", "qr", "kr", "glr", "vtr", "f", "dcr", "ob"))
                    attr, otr, ofr, sqr, gtr, yor, psg = (L[n_] for n_ in ("attr", "otr", "ofr", "sqr", "gtr", "yor", "psg"))
                    S_, kS = Sr.next()
                    Sb, kSb = Sbr.next()
                    P.op("pool", (lambda a: lambda e: e.memset(a, 0.0))(S_[:]), writes=[kS])
                    P.op("pool", (lambda a: lambda e: e.memset(a, 0.0))(Sb[:]), writes=[kSb])
                    mask = k.cb["g_f" if z == 0 else "g_b"]
                    order = range(ntile) if z == 0 else range(ntile - 1, -1, -1)
                    for ti in order:
                        t0 = s0 + 512 * ti
                        qT, kq = qr.next(); kT, kk_ = kr.next(); gl, kgl = glr.next(); vt, kvt = vtr.next()
                        P.dma("sp", qT[:], k.Q1_T[128 * h:128 * h + 128, t0:t0 + 512], writes=[kq])
                        P.dma("sp", kT[:], k.Q1_T[512 + 128 * h:512 + 128 * h + 128, t0:t0 + 512], writes=[kk_])
                        P.dma("sp", gl[:], k.GL_T[:, t0:t0 + 512], writes=[kgl])
                        P.dma("sp", vt[:], k.V1[t0:t0 + 512, 256 * h:256 * h + 256].rearrange("(j p) d -> p j d", p=128),
                              writes=[kvt])
                        pz, kpz = psg.next()
                        mm(P, pz[:], gu[0:16, z, 128 * h:128 * h + 128], gl[0:16, :], True, True, ["gu", kgl], [kpz])
                        e_, ke_ = f["e"].next()
                        act(P, e_[:], pz[:], AF.Exp, [kpz, "ngb"], [ke_], scale=-1.0, bias=ngb[:, 4 * z + h:4 * z + h + 1])
                        l_, kl_ = f["l"].next()
                        act(P, l_[:], e_[:], AF.Ln, [ke_], [kl_], bias=1.0)
                        if GLA_CUT == 1:
                            continue
                        cf_, kcf = f["cf"].next()
                        P.op("dve", (lambda o, a, b: lambda e: e.tensor_tensor_scan(
                            out=o, data0=a, data1=b, initial=0.0, op0=ALU.mult, op1=ALU.add))(
                            cf_[:], k.cf["rmask128"][:], l_[:]), [kl_, "cf_rmask128"], [kcf])
                        tot = cf_[:, 127:512:128]
                        totb = tot.unsqueeze(2).to_broadcast([128, 4, 128])
                        if z == 0:
                            ci, kci = cf_, kcf
                        else:
                            ci, kci = f["ci"].next()
                            tt(P, "pool", ci[:], l_[:], cf_[:], ALU.subtract, [kl_, kcf], [kci])
                            tt(P, "dve", v3(ci[:]), v3(ci[:]), totb, ALU.add, [kci, kcf], [kci])
                        Eq, kEq = f["Eq"].next(); Ek, kEk = f["Ek"].next(); Ee, kEe = f["Ee"].next()
                        act(P, Eq[:], ci[:], AF.Exp, [kci], [kEq], scale=-1.0 / 16)
                        act(P, Ek[:], ci[:], AF.Exp, [kci], [kEk], scale=1.0 / 16)
                        dc, kdc = dcr.next()
                        act(P, dc[:], tot, AF.Exp, [kcf], [kdc], scale=-1.0 / 16)
                        tt(P, "dve", v3(Ee[:]), v3(Ek[:]), dc[:].unsqueeze(2).to_broadcast([128, 4, 128]), ALU.mult,
                           [kEk, kdc], [kEe])
                        qd, kqd = ob["qd"].next(); kd, kkd = ob["kd"].next(); ke, kke = ob["ke"].next()
                        tt(P, "pool", qd[:], qT[:], Eq[:], ALU.mult, [kq, kEq], [kqd])
                        tt(P, "dve", kd[:], kT[:], Ek[:], ALU.mult, [kk_, kEk], [kkd])
                        tt(P, "pool", ke[:], kT[:], Ee[:], ALU.mult, [kk_, kEe], [kke])
                        if GLA_CUT == 2:
                            continue
                        ot, kot = otr.next()
                        border = range(4) if z == 0 else range(3, -1, -1)
                        for b in border:
                            bc = slice(128 * b, 128 * b + 128)
                            pa, kpa = psg.next()
                            mm(P, pa[:, 0:128], kd[:, bc], qd[:, bc], True, True, [kkd, kqd], [kpa])
                            mm(P, pa[:, 128:256], ke[:, bc], idb[:], True, True, [kke, "cb_ident"], [kpa])
                            at, kat = attr.next()
                            tt(P, "dve", at[:], pa[:, 0:256], mask[:], ALU.mult, [kpa, "cb_g_f", "cb_g_b"], [kat])
                            keT = at[:, 128:256]
                            po, kpo = psg.next()
                            for half in range(2):
                                hc = slice(128 * half, 128 * half + 128)
                                mm(P, po[:, hc], vt[:, b, hc], at[:, 0:128], True, False, [kvt, kat], [kpo])
                                mm(P, po[:, hc], Sb[:, hc], qd[:, bc], False, True, [kSb, kqd], [kpo])
                            cp(P, "act", ot[:, :, bc], po[:, 0:256].rearrange("p (a t) -> p a t", a=2), [kpo], [kot])
                            pS, kpS = psg.next()
                            mm(P, pS[:, 0:256], keT, vt[:, b, :], True, True, [kat, kvt], [kpS])
                            stt(P, S_[:], S_[:], dc[:, b:b + 1], pS[:, 0:256], ALU.mult, ALU.add, [kpS, kdc, kS], [kS])
                            cp(P, "act", Sb[:], S_[:], [kS], [kSb])
                        if GLA_CUT == 3:
                            continue
                        if z == 0:
                            for half in range(2):
                                r0 = 256 * h + 128 * half
                                P.dma("sp", k.OF_T[r0:r0 + 128, t0:t0 + 512], ot[:, half, :], reads=[kot],
                                      writes=[("OF", h, half, t0)])
                            continue
                        of, kof = ofr.next(); gt, kgt = gtr.next()
                        for half in range(2):
                            r0 = 256 * h + 128 * half
                            P.dma("sp", of[:, half, :], k.OF_T[r0:r0 + 128, t0:t0 + 512], reads=[("OF", h, half, t0)],
                                  writes=[kof])
                            P.dma("sp", gt[:, half, :], k.G1_T[r0:r0 + 128, t0:t0 + 512], writes=[kgt])
                        tt(P, "pool", of[:], of[:], ot[:], ALU.add, [kof, kot], [kof])
                        if GLA_CUT == 4:
                            continue
                        sq, ksq = sqr.next()
                        act(P, sq[:], of[:], AF.Square, [kof], [ksq])
                        pn, kpn = psg.next()
                        mm(P, pn[:], ones[:], sq[:, 0, :], True, False, [ksq, "cf_ones"], [kpn])
                        mm(P, pn[:], ones[:], sq[:, 1, :], False, True, [ksq, "cf_ones"], [kpn])
                        rstd, krstd = f["rstd"].next()
                        if GLA_CUT == 5:
                            continue
                        rsqrt(P, rstd[:], pn[:], 1.0 / 256, RMS_EPS, [kpn], [krstd])
                        if GLA_CUT == 6:
                            continue
                        for half in range(2):
                            r0 = 256 * h + 128 * half
                            stt(P, of[:, half, :], of[:, half, :], col(k, "gnorm", half), rstd[:], ALU.mult, ALU.mult,
                                [kof, krstd, "cols"], [kof])
                            if GLA_CUT == 7:
                                continue
                            yo, kyo = yor.next()
                            tt(P, "dve", yo[:], of[:, half, :], gt[:, half, :], ALU.mult, [kof, kgt], [kyo])
                            P.dma("sp", k.Y1_T[r0:r0 + 128, t0:t0 + 512], yo[:], reads=[kyo], writes=[("Y1", h, half, t0)])
        P.stage_end()


def stage7_out1(k):
    P, nc, T = k.P, k.nc, k.T
    with contextlib.ExitStack() as st:
        Wo, = load_weights(k, st, [("wo1", k.p["w_out1"], 8, D, "Wo")])
        fn = st.enter_context(nc.sbuf_tensor("fnorm", [128, D], F32))
        P.dma("sp", fn[:], k.p["fnorm"][:, :], writes=["fnorm"])
        xr_ = Ring(nc, st, "x1in", [128, 4, D], F32, 2)
        yr = Ring(nc, st, "y1T", [128, 8, 512], BF16, 2)
        xrr = Ring(nc, st, "xres", [128, D], F32, 2)
        outr = Ring(nc, st, "outt", [128, D], F32, 2)
        junk = st.enter_context(nc.sbuf_tensor("junk7", [128, D], BF16))
        ssr = Ring(nc, st, "ss7", [128, 1], F32, 2)
        pso = Ring(nc, st, "pso", [128, 512], F32, 4, psum=True)
        ntile = T // 512

        def loads(t0):
            xt, kx = xr_.next()
            P.dma("sp", xt[:], k.X1[t0:t0 + 512, :].rearrange("(j p) d -> p j d", p=128), writes=[kx])
            yT, kyT = yr.next()
            P.dma("sp", yT[:], k.Y1_T[:, t0:t0 + 512].rearrange("(kc p) t -> p kc t", p=128), writes=[kyT])
            return (xt, kx), (yT, kyT)
        nxt = loads(0)
        for ti in range(ntile):
            t0 = ti * 512
            xsrc, ysrc = nxt
            if ti + 1 < ntile:
                nxt = loads(t0 + 512)
            for j in range(4):
                xr, kxr = out_proj_tile(k, None, Wo, ysrc, xsrc, t0, j, pso, xrr)
                ss, kss = ssr.next()
                act(P, junk[:], xr[:], AF.Square, [kxr], [kss, "junk7"], accum_out=ss[:, 0:1])
                rsqrt(P, ss[:], ss[:], 1.0 / D, RMS_EPS, [kss], [kss])
                o, ko = outr.next()
                stt(P, o[:], xr[:], ss[:, 0:1], fn[:], ALU.mult, ALU.mult, [kxr, kss, "fnorm"], [ko])
                P.dma("sp", k.y[t0 + 128 * j:t0 + 128 * j + 128, :], o[:], reads=[ko], writes=[("y", ti, j)])
        P.stage_end()


_CACHE = {}


def kernel(**inputs):
    xp = np.asarray(inputs["x_prompt"], np.float32)
    xs = np.asarray(inputs["x_sample"], np.float32)
    B, S, _ = xp.shape
    DB, DS, _ = xs.shape
    n = NCORES
    pb, sbn = B // n, DB // n
    seqs = [S] * pb + [DS] * sbn
    key = tuple(seqs)
    if key not in _CACHE:
        _CACHE[key] = build(seqs)
    nc, k = _CACHE[key]
    consts = host_consts(max(seqs))
    params = host_params(inputs)
    shared = {"c_" + a: v for a, v in consts.items()}
    shared.update({"p_" + a: v for a, v in params.items()})
    in_maps = []
    for c in range(n):
        parts = [xp[c * pb + i] for i in range(pb)] + [xs[c * sbn + i] for i in range(sbn)]
        m = {"x": np.ascontiguousarray(np.concatenate(parts, axis=0))}
        m.update(shared)
        in_maps.append(m)
    res = run_bass_kernel_spmd(nc, in_maps, core_ids=list(range(n)))
    yp = np.empty_like(xp)
    ys = np.empty_like(xs)
    for c in range(n):
        y = np.asarray(res.results[c]["y"], np.float32)
        off = 0
        for i in range(pb):
            yp[c * pb + i] = y[off:off + S]
            off += S
        for i in range(sbn):
            ys[c * sbn + i] = y[off:off + DS]
            off += DS
    return (yp, ys)
```

```python
import contextlib
import numpy as np
import concourse.bass as bass
import concourse.mybir as mybir
from concourse.bass_utils import run_bass_kernel_spmd

F32 = mybir.dt.float32
BF16 = mybir.dt.bfloat16
AF = mybir.ActivationFunctionType
ALU = mybir.AluOpType

SAME_ENG_SYNC = True
LIST_SCHED = True
SCHED_DEBUG = False
NDMASEM = 16
NCORES = 8
D = 1024
RW = 512
EVEN_SHIFT = 1664
EVEN_COLS = 4224
ODD_COLS = 3088
DECAY_C = 0.6065306597126334
GN_EPS = 64e-5
RMS_EPS = 1e-6


class Op:
    __slots__ = ("eng", "fn", "reads", "writes", "is_dma", "waits", "signal", "sig", "barrier", "deps", "cost",
                 "lat", "idx")


def _numel(ap):
    n = 1
    for d in ap.shape[1:]:
        n *= int(d)
    return n


class Prog:
    ENGS = ("pe", "act", "dve", "pool", "sp")

    def __init__(self, nc, stack):
        self.nc = nc
        self.ops = []
        self.sems = {e: stack.enter_context(nc.semaphore("s_" + e)) for e in ("pe", "act", "dve", "pool")}
        self.dsems = {q: [stack.enter_context(nc.semaphore("d_%s%d" % (q, i))) for i in range(NDMASEM)]
                      for q in ("sp", "pool", "act")}
        self.cnt = {e: 0 for e in ("pe", "act", "dve", "pool")}
        self.dcnt = {q: 0 for q in ("sp", "pool", "act")}
        self.last_writer = {}
        self.readers = {}
        self.seen = {e: {} for e in self.ENGS}
        self.last_op = {}
        self.recent_dma = {q: [] for q in ("sp", "pool", "act")}
        self.emitted = 0
        self.n_inst = 0

    def op(self, eng, fn, reads=(), writes=(), cost=500.0):
        o = Op()
        ex = [r for r in reads if isinstance(r, str) and (r.startswith("ps") or r.startswith("pg"))]
        if ex:
            reads = [r for r in reads if r not in ex]
            writes = list(writes) + ex
        o.eng = eng; o.fn = fn; o.reads = tuple(reads); o.writes = tuple(writes)
        o.is_dma = False; o.signal = False; o.sig = None; o.waits = []; o.barrier = False
        o.deps = []; o.cost = cost; o.lat = cost
        self.ops.append(o)
        return o

    def dma(self, q, out, in_, reads=(), writes=()):
        o = self.op(q, lambda e: e.dma_start(out=out, in_=in_), reads, writes, cost=60.0)
        o.is_dma = True
        o.lat = 2200.0 + _numel(out) * 128 * 0.004
        return o

    def barrier(self):
        for e in self.ENGS:
            o = self.op(e, None)
            o.barrier = True

    def _deps(self, ops):
        for o in ops:
            if o.barrier:
                if o.eng == self.ENGS[-1]:
                    self.last_writer = {}
                    self.readers = {}
                continue
            deps = {}
            for k in o.reads:
                w = self.last_writer.get(k)
                if w is not None:
                    deps[id(w)] = (w, True)
            for k in o.writes:
                w = self.last_writer.get(k)
                if w is not None:
                    israw = isinstance(k, str) and (k.startswith("ps") or k.startswith("pg"))
                    if id(w) not in deps or israw:
                        deps[id(w)] = (w, israw or deps.get(id(w), (None, False))[1])
                for r in self.readers.get(k, ()):
                    if id(r) not in deps:
                        deps[id(r)] = (r, False)
            deps.pop(id(o), None)
            o.deps = list(deps.values())
            for k in o.reads:
                self.readers.setdefault(k, []).append(o)
            for k in o.writes:
                self.last_writer[k] = o
                self.readers[k] = []

    def _schedule(self, seg):
        import heapq
        n = len(seg)
        if n < 3 or not LIST_SCHED:
            return seg
        for i, o in enumerate(seg):
            o.idx = i
        inseg = set(id(o) for o in seg)
        succ = [[] for _ in range(n)]
        indeg = [0] * n
        for o in seg:
            for d, _ in o.deps:
                if id(d) in inseg:
                    succ[d.idx].append(o.idx)
                    indeg[o.idx] += 1
        rank = [0.0] * n
        for i in range(n - 1, -1, -1):
            m = 0.0
            for j in succ[i]:
                if rank[j] > m:
                    m = rank[j]
            rank[i] = seg[i].lat + m
        ready = [0.0] * n
        free = {e: 0.0 for e in self.ENGS}
        fut = {e: [] for e in self.ENGS}
        now = {e: [] for e in self.ENGS}
        for i in range(n):
            if indeg[i] == 0:
                heapq.heappush(fut[seg[i].eng], (0.0, i))
        out = []
        XLAT = 180.0
        while len(out) < n:
            best = None
            for e in self.ENGS:
                f, nw = fut[e], now[e]
                while f and f[0][0] <= free[e]:
                    t, i = heapq.heappop(f)
                    heapq.heappush(nw, (-rank[i], i))
                if nw:
                    cand = (free[e], 0, e)
                elif f:
                    cand = (f[0][0], 1, e)
                else:
                    continue
                if best is None or cand < best:
                    best = cand
            start, kind, e = best
            if kind == 0:
                _, i = heapq.heappop(now[e])
            else:
                _, i = heapq.heappop(fut[e])
            o = seg[i]
            free[e] = start + o.cost
            fin = start + o.lat
            out.append(o)
            for j in succ[i]:
                same = (seg[j].eng == e and not o.is_dma)
                r = (start + o.cost) if same else (fin + XLAT)
                if r > ready[j]:
                    ready[j] = r
                indeg[j] -= 1
                if indeg[j] == 0:
                    heapq.heappush(fut[seg[j].eng], (ready[j], j))
        if SCHED_DEBUG and n > 500:
            import collections
            load = collections.defaultdict(float)
            for o in seg:
                load[o.eng] += o.cost
            st = {}
            fr = {e: 0.0 for e in self.ENGS}
            pred = {}
            for o in out:
                t = fr[o.eng]
                p = None
                for d, _ in o.deps:
                    if id(d) in st:
                        same = (d.eng == o.eng and not d.is_dma)
                        r = st[id(d)] + (d.cost if same else d.lat + XLAT)
                        if r > t:
                            t, p = r, d
                st[id(o)] = t
                pred[id(o)] = p
                fr[o.eng] = t + o.cost
            last = max(out, key=lambda o: st[id(o)] + o.lat)
            print("SCHED seg n=%d makespan=%.0f loads=%s" % (n, st[id(last)] + last.lat, {e: int(v) for e, v in load.items()}))
            i = max(range(n), key=lambda q: rank[q])
            print("  pure DAG critical path length: %.0f" % rank[i])
            cp_ = collections.Counter(); cpt = collections.defaultdict(float)
            while True:
                o = seg[i]
                kk = (o.eng, str(o.writes[0] if o.writes else "-")[:7], o.is_dma)
                cp_[kk] += 1; cpt[kk] += o.lat
                if not succ[i]:
                    break
                i = max(succ[i], key=lambda q: rank[q])
            print("  DAG path:", sorted(((int(cpt[kk]), v, kk) for kk, v in cp_.items()), reverse=True)[:16])
            path = collections.Counter()
            tm = collections.defaultdict(float)
            o = last
            cnt = 0
            while o is not None and cnt < 100000:
                kk = (o.eng, str(o.writes[0] if o.writes else "-")[:6])
                path[kk] += 1
                tm[kk] += o.lat
                o = pred[id(o)]
                cnt += 1
            print("  critical path ops:", sorted(((v, int(tm[kk]), kk) for kk, v in path.items()), reverse=True)[:14])
        return out

    def _sync(self, ops):
        for o in ops:
            if o.barrier:
                if o.eng == self.ENGS[0]:
                    self._bar = [lo for lo in self.last_op.values() if lo is not None]
                    for q, lst in self.recent_dma.items():
                        self._bar.extend(lst)
                deps = [(d, True) for d in self._bar]
            else:
                deps = o.deps
            if o.is_dma:
                j = self.dcnt[o.eng]
                self.dcnt[o.eng] = j + 1
                s = j % NDMASEM
                o.sig = (("d", o.eng, s), 16 * (j // NDMASEM + 1))
                o.signal = True
                if j >= NDMASEM:
                    o.waits.append((("d", o.eng, s), 16 * (j // NDMASEM)))
            for d, israw in deps:
                if d.is_dma:
                    o.waits.append(d.sig)
                elif d.eng == o.eng and not o.is_dma:
                    if o.eng == "pe" or not SAME_ENG_SYNC or (not o.barrier and not israw):
                        continue
                    d.signal = True
                    o.waits.append(("c", d))
                else:
                    d.signal = True
                    o.waits.append(("c", d))
            if not o.barrier:
                if o.is_dma:
                    lst = self.recent_dma[o.eng]
                    lst.append(o)
                    if len(lst) > NDMASEM:
                        lst.pop(0)
                else:
                    self.last_op[o.eng] = o
        for o in ops:
            if o.signal and not o.is_dma:
                self.cnt[o.eng] += 1
                o.sig = (("c", o.eng), self.cnt[o.eng])

    def _sem(self, key):
        if key[0] == "c":
            return self.sems[key[1]]
        return self.dsems[key[1]][key[2]]

    def emit(self):
        ops = self.ops[self.emitted:]
        self.emitted = len(self.ops)
        self._deps(ops)
        ordered, seg = [], []
        for o in ops:
            if o.barrier:
                if seg:
                    ordered.extend(self._schedule(seg))
                    seg = []
                ordered.append(o)
            else:
                seg.append(o)
        if seg:
            ordered.extend(self._schedule(seg))
        ops = ordered
        self._sync(ops)
        per = {e: [o for o in ops if o.eng == e] for e in self.ENGS}
        self.n_inst += len(ops)
        with self.nc.Block() as block:
            def run(engname):
                def body(eng):
                    seen = self.seen[engname]
                    for o in per[engname]:
                        for w in o.waits:
                            if w[0] == "c":
                                key, val = w[1].sig
                            else:
                                key, val = w
                            if seen.get(key, 0) >= val:
                                continue
                            seen[key] = val
                            eng.wait_ge(self._sem(key), val)
                        if o.fn is None:
                            continue
                        ins = o.fn(eng)
                        if o.signal:
                            key, val = o.sig
                            ins.then_inc(self._sem(key), 16 if o.is_dma else 1)
                return body
            block.tensor(run("pe"))
            block.scalar(run("act"))
            block.vector(run("dve"))
            block.gpsimd(run("pool"))
            block.sync(run("sp"))

    def stage_end(self):
        self.barrier()
        self.emit()


class Ring:
    uid = 0
    def __init__(self, nc, st, name, shape, dt, n, psum=False):
        alloc = nc.psum_tensor if psum else nc.sbuf_tensor
        Ring.uid += 1
        self.tiles = [st.enter_context(alloc("%s_u%d_%d" % (name, Ring.uid, i), shape, dt)) for i in range(n)]
        self.keys = ["%s%d" % (name, i) for i in range(n)]
        self.i = -1

    def next(self):
        self.i = (self.i + 1) % len(self.tiles)
        return self.tiles[self.i], self.keys[self.i]


def _cols(v):
    v = np.asarray(v, np.float32)
    return np.ascontiguousarray(v.reshape(-1, 128).T)


COLS = {}


def _col_layout():
    off = 0
    for name, n in (("norm0", 8), ("norm1", 8), ("mup", 13), ("mun", 13), ("w0_0", 4), ("w0_1", 4), ("a0_0", 4),
                    ("a0_1", 4), ("k_k", 4), ("k_a", 4), ("r_k", 4), ("gn_w", 4), ("gn_b", 4), ("gnorm", 2),
                    ("gb_0", 4), ("gb_1", 4)):
        COLS[name] = (off, n)
        off += n
    return off


NCOLS = _col_layout()


def host_consts(smax):
    c = {}
    c["ident"] = np.eye(128, dtype=np.float32)
    R = np.zeros((128, 128), np.float32)
    for m in range(128):
        h, j = divmod(m, 64)
        k = h * 64 + (j + 32) % 64
        R[k, m] = 1.0
    c["rot"] = R
    inv = 10000.0 ** (-np.arange(0, 64, 2, dtype=np.float32) / 64.0)
    ang = np.arange(smax, dtype=np.float32)[None, :] * inv[:, None]
    cos, sin = np.cos(ang), np.sin(ang)
    c["cos"] = np.ascontiguousarray(np.tile(cos, (4, 1)).astype(np.float32))
    c["sin"] = np.ascontiguousarray(np.concatenate([-sin, sin, -sin, sin], 0).astype(np.float32))
    i = np.arange(128)[:, None]
    t = np.arange(128)[None, :]
    same = (i // 64) == (t // 64)
    strict = ((i < t) & same).astype(np.float32)
    incl = ((i <= t) & same).astype(np.float32)
    c["m_f"] = np.concatenate([strict, incl, strict.T], 1)
    c["m_b"] = np.concatenate([strict.T, incl.T, strict], 1)
    c["g_f"] = np.concatenate([(i <= t).astype(np.float32), np.ones((128, 128), np.float32)], 1)
    c["g_b"] = np.concatenate([(i >= t).astype(np.float32), np.ones((128, 128), np.float32)], 1)
    kl = np.arange(128)[:, None]
    ql = np.arange(256)[None, :]
    c["a_g"] = ((kl <= ql) & (ql <= kl + 128)).astype(np.float32)
    c["a_0"] = (np.arange(128)[None, :] <= np.arange(64)[:, None] + 64).astype(np.float32)
    bd = (np.arange(128)[:, None] // 64 == np.arange(128)[None, :] // 64).astype(np.float32)
    c["bd1"] = bd
    c["ones"] = np.ones((128, 128), np.float32)
    rm = np.ones((128, 512), np.float32)
    rm[:, ::64] = 0.0
    c["rmask"] = rm
    rm2 = np.ones((128, 512), np.float32)
    rm2[:, ::128] = 0.0
    c["rmask128"] = rm2
    return c


CONST_SHAPES = lambda smax: {"ident": [128, 128], "rot": [128, 128], "cos": [128, smax], "sin": [128, smax],
                             "m_f": [128, 384], "m_b": [128, 384],
                             "g_f": [128, 256], "g_b": [128, 256], "a_g": [128, 256], "a_0": [64, 128],
                             "bd1": [128, 128], "ones": [128, 128], "rmask": [128, 512],
                             "rmask128": [128, 512]}


def host_params(inp):
    g = lambda k: np.asarray(inp[k], np.float32)
    cols = np.zeros((128, NCOLS), np.float32)

    def put(name, v):
        o, n = COLS[name]
        cols[:, o:o + n] = _cols(v)
    put("norm0", g("even_norm")[0]); put("norm1", g("odd_norm")[0])
    put("mup", g("even_mu_prev")[0]); put("mun", g("even_mu_next")[0])
    for z in range(2):
        put("w0_%d" % z, g("rwkv_w0")[0, z]); put("a0_%d" % z, g("rwkv_a0")[0, z])
        put("gb_%d" % z, g("gla_gate_bias")[0, z])
    for nm, k in (("k_k", "rwkv_k_k"), ("k_a", "rwkv_k_a"), ("r_k", "rwkv_r_k"), ("gn_w", "rwkv_gn_w"),
                  ("gn_b", "rwkv_gn_b")):
        put(nm, g(k)[0])
    put("gnorm", g("gla_norm")[0])
    lora = np.zeros((128, 2, 512), np.float32)
    lora[0:64] = np.transpose(g("rwkv_w_up")[0], (1, 0, 2))
    lora[64:128] = np.transpose(g("rwkv_a_up")[0], (1, 0, 2))
    p = {"cols": cols, "lora": lora,
         "w_in0": g("even_w_in")[0], "w_out0": g("even_w_out")[0],
         "w_in1": g("odd_w_in")[0], "w_out1": g("odd_w_out")[0],
         "gate_up": np.ascontiguousarray(np.transpose(g("gla_gate_up")[0], (1, 0, 2))),
         "fnorm": np.ascontiguousarray(np.broadcast_to(g("final_norm")[None, :], (128, D)))}
    return p


PARAM_SHAPES = {"cols": [128, NCOLS], "lora": [128, 2, 512], "w_in0": [D, EVEN_COLS], "w_out0": [D, D],
                "w_in1": [D, ODD_COLS], "w_out1": [D, D], "gate_up": [16, 2, 512], "fnorm": [128, D]}


class K:
    pass


def mm(P, out, lhsT, rhs, start, stop, reads, writes):
    n = _numel(out)
    c = 60.0 + max(64, n) / 1.2 * (4.0 if lhsT.dtype == F32 else 1.0)
    o = P.op("pe", lambda e: e.matmul(out, lhsT=lhsT, rhs=rhs, start=start, stop=stop), reads, writes, cost=c)
    o.lat = c + 100.0


def act(P, out, in_, func, reads, writes, **kw):
    P.op("act", lambda e: e.activation(out=out, in_=in_, func=func, **kw), reads, writes,
         cost=230.0 + _numel(out) / 1.2)


def _vcost(eng, out):
    n = _numel(out)
    return (80.0 + n * 1.05) if eng == "dve" else (150.0 + n * 2.3)


def tt(P, eng, out, in0, in1, op, reads, writes):
    P.op(eng, lambda e: e.tensor_tensor(out=out, in0=in0, in1=in1, op=op), reads, writes, cost=_vcost(eng, out))


def ts(P, eng, out, in0, s1, s2, op0, op1, reads, writes):
    if op1 is None:
        P.op(eng, lambda e: e.tensor_scalar(out=out, in0=in0, scalar1=s1, scalar2=None, op0=op0), reads, writes,
             cost=_vcost(eng, out))
    else:
        P.op(eng, lambda e: e.tensor_scalar(out=out, in0=in0, scalar1=s1, scalar2=s2, op0=op0, op1=op1), reads, writes,
             cost=_vcost(eng, out))


def stt(P, out, in0, scalar, in1, op0, op1, reads, writes):
    P.op("dve", lambda e: e.scalar_tensor_tensor(out=out, in0=in0, scalar=scalar, in1=in1, op0=op0, op1=op1),
         reads, writes, cost=_vcost("dve", out))


def cp(P, eng, out, in_, reads, writes):
    if eng == "act":
        P.op("act", lambda e: e.activation(out=out, in_=in_, func=AF.Copy), reads, writes,
             cost=230.0 + _numel(out) / 1.2)
    else:
        P.op(eng, lambda e: e.tensor_copy(out=out, in_=in_), reads, writes, cost=_vcost(eng, out))


def rsqrt(P, out, in_, scale, bias, reads, writes):
    act(P, out, in_, AF.Ln, reads, writes, scale=scale, bias=bias)
    act(P, out, out, AF.Exp, writes, writes, scale=-0.5)


def col(k, name, j=0, n=1, rows=slice(0, 128)):
    o, _ = COLS[name]
    return k.cols[rows, o + j:o + j + n]


def build(seqs, n_stage=99, debug=()):
    nc = bass.Bass("TRN2", target_bir_lowering=False)
    k = K()
    k.nc = nc
    k.seqs = list(seqs)
    T = sum(seqs)
    k.T = T
    smax = max(seqs)
    k.seq_off = [sum(seqs[:i]) for i in range(len(seqs))]
    ein = lambda n, sh, dt=F32: nc.dram_tensor(n, sh, dt, kind="ExternalInput").ap()
    scr = lambda n, sh, dt=BF16: nc.dram_tensor(n, sh, dt, kind=("ExternalOutput" if n in debug else "Internal")).ap()
    k.x = ein("x", [T, D])
    k.y = nc.dram_tensor("y", [T, D], F32, kind="ExternalOutput").ap()
    k.c = {n: ein("c_" + n, sh) for n, sh in CONST_SHAPES(smax).items()}
    k.p = {n: ein("p_" + n, sh) for n, sh in PARAM_SHAPES.items()}
    k.RW_T = scr("RW_T", [EVEN_SHIFT, T])
    k.GA_T = scr("GA_T", [512, T])
    k.QK_T = scr("QK_T", [1024, T])
    k.VB = scr("VB", [T, 512])
    k.GB_T = scr("GB_T", [512, T])
    k.YF_T = scr("YF_T", [512, T], F32)
    k.Y0_T = scr("Y0_T", [1024, T])
    k.X1 = scr("X1", [T, D], F32)
    k.Q1_T = scr("Q1_T", [1024, T])
    k.GL_T = scr("GL_T", [16, T], F32)
    k.V1 = scr("V1", [T, D])
    k.G1_T = scr("G1_T", [D, T])
    k.OF_T = scr("OF_T", [D, T], F32)
    k.Y1_T = scr("Y1_T", [D, T])

    with contextlib.ExitStack() as gst:
        P = Prog(nc, gst)
        k.P = P
        sb = lambda n, sh, dt: gst.enter_context(nc.sbuf_tensor(n, sh, dt))
        k.cols = sb("cols", [128, NCOLS + 32], F32)
        k.cb = {}
        k.cf = {}
        for n in ("ident", "bd1", "ones", "rmask", "rmask128"):
            k.cf[n] = sb("cf_" + n, CONST_SHAPES(smax)[n], F32)
        BN = ("ident", "rot", "m_f", "m_b", "g_f", "g_b", "a_g", "a_0", "bd1")
        for n in BN:
            k.cb[n] = sb("cb_" + n, CONST_SHAPES(smax)[n], BF16)
        k.lora = sb("lora", [128, 2, 512], BF16)
        with contextlib.ExitStack() as st:
            tmp = st.enter_context(nc.sbuf_tensor("ctmp", [128, 2048], F32))
            P.dma("sp", k.cols[:, 0:NCOLS], k.p["cols"][:, :], writes=["cols"])
            for n in ("ident", "bd1", "ones", "rmask", "rmask128"):
                P.dma("sp", k.cf[n][:], k.c[n][:, :], writes=["cf_" + n])
            off = 0
            for n in BN:
                sh = CONST_SHAPES(smax)[n]
                P.dma("sp", tmp[0:sh[0], off:off + sh[1]], k.c[n][:, :], writes=["ctmp"])
                cp(P, "dve", k.cb[n][:], tmp[0:sh[0], off:off + sh[1]], ["ctmp"], ["cb_" + n])
                off += sh[1]
            P.stage_end()
            P.dma("sp", tmp[:, 0:1024], k.p["lora"].rearrange("p z c -> p (z c)"), writes=["ctmp2"])
            cp(P, "dve", k.lora[:].rearrange("p z c -> p (z c)"), tmp[:, 0:1024], ["ctmp2"], ["lora"])
            o_mup, o_mun, o_ka = COLS["mup"][0], COLS["mun"][0], COLS["k_a"][0]
            k.o_c0, k.o_omk = NCOLS, NCOLS + 13
            tt(P, "dve", k.cols[:, k.o_c0:k.o_c0 + 13], k.cols[:, o_mup:o_mup + 13], k.cols[:, o_mun:o_mun + 13],
               ALU.add, ["cols"], ["cols"])
            ts(P, "dve", k.cols[:, k.o_c0:k.o_c0 + 13], k.cols[:, k.o_c0:k.o_c0 + 13], -1.0, 1.0, ALU.mult, ALU.add,
               ["cols"], ["cols"])
            ts(P, "dve", k.cols[:, k.o_omk:k.o_omk + 4], k.cols[:, o_ka:o_ka + 4], -1.0, 1.0, ALU.mult, ALU.add,
               ["cols"], ["cols"])
            P.stage_end()
        stages = [stage1, stage2_attn, stage3_rwkv, stage4_out0, stage5_in1, stage6_gla, stage7_out1]
        for i, s in enumerate(stages):
            if i < n_stage:
                s(k)
        P.stage_end()
    k.n_inst = P.n_inst
    return nc, k


def load_weights(k, st, specs):
    P, nc = k.P, k.nc
    ws = [st.enter_context(nc.sbuf_tensor(name, [128, rows, ncol], BF16)) for name, dram, rows, ncol, key in specs]
    with contextlib.ExitStack() as inner:
        ring = Ring(nc, inner, "wstage", [128, 1056], F32, 3)
        for w, (name, dram, rows, ncol, key) in zip(ws, specs):
            v = dram.rearrange("(kc p) c -> p kc c", p=128)
            for kc in range(rows):
                for c0 in range(0, ncol, 1056):
                    c1 = min(ncol, c0 + 1056)
                    t, tk = ring.next()
                    P.dma("sp", t[:, 0:c1 - c0], v[:, kc, c0:c1], writes=[tk])
                    cp(P, "dve" if (c0 // 1056) % 2 else "pool", w[:, kc, c0:c1], t[:, 0:c1 - c0], [tk], [])
        P.stage_end()
    return ws


def rms_transpose(k, st, rings, src_ap, normcol, t0):
    P = k.P
    xt, kx = rings["x"].next()
    P.dma("sp", xt[:], src_ap[t0:t0 + 512, :].rearrange("(j p) d -> p j d", p=128), writes=[kx])
    return xt, kx


def rms_transpose_compute(k, rings, xt, kx, normname):
    P = k.P
    ss, kss = rings["ss"].next()
    junk, kj = rings["junk"].next()
    for j in range(4):
        act(P, junk[:], xt[:, j, :], AF.Square, [kx], [kss, kj], accum_out=ss[:, j:j + 1])
    rsqrt(P, ss[:, 0:4], ss[:, 0:4], 1.0 / D, RMS_EPS, [kss], [kss])
    xn, kxn = rings["xn"].next()
    for j in range(4):
        if j % 2 == 0:
            ts(P, "dve", xn[:, j, :], xt[:, j, :], ss[:, j:j + 1], None, ALU.mult, None, [kx, kss], [kxn])
        else:
            act(P, xn[:, j, :], xt[:, j, :], AF.Copy, [kx, kss], [kxn], scale=ss[:, j:j + 1])
    xnT, kT = rings["xnT"].next()
    for c in range(8):
        ps, kp = rings["pst"].next()
        for j in range(4):
            mm(P, ps[:, 128 * j:128 * j + 128], xn[:, j, 128 * c:128 * c + 128], k.cb["ident"][:], True, True,
               [kxn, "cb_ident"], [kp])
        if c % 2 == 0:
            act(P, xnT[:, c, :], ps[:], AF.Copy, [kp, "cols"], [kT], scale=col(k, normname, c))
        else:
            ts(P, "dve", xnT[:, c, :], ps[:], col(k, normname, c), None, ALU.mult, None, [kp, "cols"], [kT])
    return xnT, kT


def in_rings(k, st):
    nc = k.nc
    return {"x": Ring(nc, st, "xt", [128, 4, D], F32, 2), "ss": Ring(nc, st, "ss", [128, 4], F32, 2),
            "junk": Ring(nc, st, "junk", [128, D], BF16, 1), "xn": Ring(nc, st, "xn", [128, 4, D], BF16, 2),
            "xnT": Ring(nc, st, "xnT", [128, 8, 512], BF16, 2),
            "pst": Ring(nc, st, "pst", [128, 512], F32, 2, psum=True)}


def seq_pos(k, t0):
    for off, S in zip(k.seq_off, k.seqs):
        if off <= t0 < off + S:
            return t0 - off
    raise ValueError


def stage1(k):
    P, nc, T = k.P, k.nc, k.T
    with contextlib.ExitStack() as st:
        W, = load_weights(k, st, [("w0", k.p["w_in0"], 8, EVEN_COLS, "W")])
        R = in_rings(k, st)
        psr = Ring(nc, st, "ps1", [128, 512], F32, 4, psum=True)
        psrot = Ring(nc, st, "psrot", [128, 512], F32, 2, psum=True)
        ost = Ring(nc, st, "ost", [128, 512], BF16, 6)
        qraw = Ring(nc, st, "qraw", [128, 512], BF16, 2)
        ra = Ring(nc, st, "ra", [128, 512], F32, 2)
        rb = Ring(nc, st, "rb", [128, 512], F32, 2)
        cs = Ring(nc, st, "cs", [128, 2, 512], F32, 2)
        ntile = T // 512
        nxt = rms_transpose(k, st, R, k.x, "norm0", 0)
        for ti in range(ntile):
            t0 = ti * 512
            xt, kx = nxt
            if ti + 1 < ntile:
                nxt = rms_transpose(k, st, R, k.x, "norm0", t0 + 512)
            pos = seq_pos(k, t0)
            cst, kcs = cs.next()
            P.dma("sp", cst[:, 0, :], k.c["cos"][:, pos:pos + 512], writes=[kcs])
            P.dma("sp", cst[:, 1, :], k.c["sin"][:, pos:pos + 512], writes=[kcs])
            xnT, kT = rms_transpose_compute(k, R, xt, kx, "norm0")
            for oc in range(33):
                if 25 <= oc < 29:
                    j = oc - 25
                    ps, kp = psr.next()
                    for kc in range(8):
                        mm(P, ps[:], xnT[:, kc, 128 * j:128 * j + 128], W[:, kc, 3200:3712], kc == 0, kc == 7,
                           [kT, "W"], [kp])
                    o, ko = ost.next()
                    cp(P, "act" if j % 2 else "dve", o[:], ps[:], [kp], [ko])
                    P.dma("sp", k.VB[t0 + 128 * j:t0 + 128 * j + 128, :], o[:], reads=[ko], writes=[("VB", ti)])
                    continue
                ps, kp = psr.next()
                for kc in range(8):
                    mm(P, ps[:], W[:, kc, 128 * oc:128 * oc + 128], xnT[:, kc, :], kc == 0, kc == 7, [kT, "W"], [kp])
                o, ko = ost.next()
                if oc < 13:
                    cp(P, "act" if oc % 2 else "dve", o[:], ps[:], [kp], [ko])
                    P.dma("sp", k.RW_T[128 * oc:128 * oc + 128, t0:t0 + 512], o[:], reads=[ko], writes=[("RW_T", ti)])
                elif oc < 17 or oc >= 29:
                    act(P, o[:], ps[:], AF.Silu, [kp], [ko])
                    dst = k.GA_T if oc < 17 else k.GB_T
                    r0 = 128 * (oc - 13) if oc < 17 else 128 * (oc - 29)
                    P.dma("sp", dst[r0:r0 + 128, t0:t0 + 512], o[:], reads=[ko],
                          writes=[("GA_T" if oc < 17 else "GB_T", ti)])
                else:
                    qr, kq = qraw.next()
                    cp(P, "act", qr[:], ps[:], [kp], [kq])
                    pr, kpr = psrot.next()
                    mm(P, pr[:], k.cb["rot"][:], qr[:], True, True, [kq, "cb_rot"], [kpr])
                    a, ka = ra.next()
                    tt(P, "pool", a[:], qr[:], cst[:, 0, :], ALU.mult, [kq, kcs], [ka])
                    b, kb = rb.next()
                    tt(P, "dve", b[:], pr[:], cst[:, 1, :], ALU.mult, [kpr, kcs], [kb])
                    tt(P, "pool", o[:], a[:], b[:], ALU.add, [ka, kb], [ko])
                    r0 = 128 * (oc - 17)
                    P.dma("sp", k.QK_T[r0:r0 + 128, t0:t0 + 512], o[:], reads=[ko], writes=[("QK_T", ti)])
        P.stage_end()


class SRing:
    def __init__(self, items):
        self.items = items
        self.i = -1

    def next(self):
        self.i = (self.i + 1) % len(self.items)
        return self.items[self.i]


GLA_CUT = 0
PATTERNS = ((128, 1), (512, 4), (2048, 16))


def stage2_attn(k):
    P, nc = k.P, k.nc
    smax = max(k.seqs)
    with contextlib.ExitStack() as st:
        qsr = Ring(nc, st, "qs", [128, smax], BF16, 1)
        ksr = Ring(nc, st, "ks", [128, smax], BF16, 1)
        qdr = {d: Ring(nc, st, "qd%d" % d, [128, smax], BF16, 1) for d in (4, 16)}
        kdr = {d: Ring(nc, st, "kd%d" % d, [128, smax], BF16, 1) for d in (4, 16)}
        acc = [st.enter_context(nc.sbuf_tensor("acc%d" % h, [65, smax], F32)) for h in range(2)]
        vtr = Ring(nc, st, "vt", [128, 2, 65], BF16, 6)
        for t, key in zip(vtr.tiles, vtr.keys):
            P.op("pool", (lambda t: lambda e: e.memset(t[:], 1.0))(t), writes=[key])
        pst = [st.enter_context(nc.psum_tensor("psS%d" % i, [128, 512], F32)) for i in range(2)]
        psr = SRing([(pst[i], "psS%d" % i) for i in range(2)])
        po = [[(st.enter_context(nc.psum_tensor("psO%d_%d" % (h, b), [65, 128], F32)), "psO%d_%d" % (h, b))
               for b in range(2)] for h in range(2)]
        pdr = Ring(nc, st, "psD", [64, 512], F32, 2, psum=True)
        ptr_ = Ring(nc, st, "pt", [128, 256], BF16, 4)
        pmr = Ring(nc, st, "pm", [128, 256], BF16, 4)
        rcr = Ring(nc, st, "rc", [64, 512], F32, 2)
        tmr = Ring(nc, st, "tm", [64, 512], F32, 2)
        gr = Ring(nc, st, "gb", [64, 512], BF16, 2)
        outr = Ring(nc, st, "ao", [64, 512], BF16, 2)
        cnt = 0
        for s0, S in zip(k.seq_off, k.seqs):
            nchunk = S // 128
            for hp in range(4):
                qs, kq = qsr.next()
                ks_, kk_ = ksr.next()
                P.dma("sp", qs[:, 0:S], k.QK_T[128 * hp:128 * hp + 128, s0:s0 + S], writes=[kq])
                P.dma("sp", ks_[:, 0:S], k.QK_T[512 + 128 * hp:512 + 128 * hp + 128, s0:s0 + S], writes=[kk_])
                qv, kv_ = {1: (qs, kq)}, {1: (ks_, kk_)}
                for d in (4, 16):
                    qd, kqd = qdr[d].next()
                    kd, kkd = kdr[d].next()
                    cp(P, "act", qd[:, 0:S].rearrange("p (r i) -> p r i", r=d),
                       qs[:, 0:S].rearrange("p (i r) -> p r i", r=d), [kq], [kqd])
                    cp(P, "dve", kd[:, 0:S].rearrange("p (r i) -> p r i", r=d),
                       ks_[:, 0:S].rearrange("p (i r) -> p r i", r=d), [kk_], [kkd])
                    qv[d] = (qd, kqd)
                    kv_[d] = (kd, kkd)
                for h in range(2):
                    P.op("pool", (lambda a: lambda e: e.memset(a, 0.0))(acc[h][:, 0:S]),
                         writes=[("acc", h, c) for c in range(nchunk)])
                for (win, d) in PATTERNS:
                    sub = S // d
                    nb = sub // 128
                    assert sub % 128 == 0 and nb >= 1
                    qt, kqt = qv[d]
                    kt, kkt = kv_[d]
                    for r in range(d):
                        for kb in range(nb + 1):
                            lo = max(0, 128 * kb - 64)
                            hi = min(sub, 128 * kb + 64)
                            nk = hi - lo
                            v, kv = vtr.next()
                            rows = k.VB[s0 + r + d * lo:s0 + r + d * (hi - 1) + 1:d, 128 * hp:128 * hp + 128]
                            P.dma("sp", v[0:nk, :, 0:64], rows.rearrange("p (h e) -> p h e", h=2), writes=[kv])
                            qb_lo = max(0, kb - 1)
                            qb_hi = min(nb - 1, kb)
                            nq = 128 * (qb_hi - qb_lo + 1)
                            if kb == 0:
                                mask = k.cb["a_0"][0:64, 0:128]
                            elif kb == nb:
                                mask = k.cb["a_g"][0:64, 0:128]
                            else:
                                mask = k.cb["a_g"][:, 0:256]
                            for h in range(2):
                                ph = slice(64 * h, 64 * h + 64)
                                keys = kt[ph, r * sub + lo:r * sub + hi]
                                qry = qt[ph, r * sub + 128 * qb_lo:r * sub + 128 * qb_lo + nq]
                                ps, kps = psr.next()
                                mm(P, ps[0:nk, 0:nq], keys, qry, True, True, [kqt, kkt], [kps])
                                e, ke = ptr_.next()
                                act(P, e[0:nk, 0:nq], ps[0:nk, 0:nq], AF.Exp, [kps], [ke], scale=0.125)
                                m, km = pmr.next()
                                cnt += 1
                                tt(P, "dve" if cnt % 3 else "pool", m[0:nk, 0:nq], e[0:nk, 0:nq], mask, ALU.mult,
                                   [ke, "cb_a_g", "cb_a_0"], [km])
                                for qb in range(qb_lo, qb_hi + 1):
                                    first = (kb == qb)
                                    pv, kpo = po[h][qb % 2]
                                    c0 = 128 * (qb - qb_lo)
                                    mm(P, pv[0:65, :], v[0:nk, h, 0:65], m[0:nk, c0:c0 + 128], first, not first,
                                       [kv, km], [kpo])
                                    if not first:
                                        a0 = r + d * 128 * qb
                                        pos = acc[h][0:65, a0:a0 + d * 127 + 1:d]
                                        ck = [("acc", h, c) for c in range(d * qb, d * qb + d)]
                                        tt(P, "dve", pos, pos, pv[0:65, :], ALU.add, [kpo] + ck, ck)
                for h in range(2):
                    for c0 in range(0, S, 512):
                        ck = [("acc", h, c) for c in range(c0 // 128, c0 // 128 + 4)]
                        pd, kpd = pdr.next()
                        mm(P, pd[0:64, :], k.cf["ones"][64:65, 0:64], acc[h][64:65, c0:c0 + 512], True, True,
                           ["cf_ones"] + ck, [kpd])
                        rc, krc = rcr.next()
                        act(P, rc[:], pd[0:64, :], AF.Ln, [kpd], [krc])
                        act(P, rc[:], rc[:], AF.Exp, [krc], [krc], scale=-1.0)
                        g, kg = gr.next()
                        r0 = 128 * hp + 64 * h
                        P.dma("sp", g[:], k.GB_T[r0:r0 + 64, s0 + c0:s0 + c0 + 512], writes=[kg])
                        tm, ktm = tmr.next()
                        tt(P, "pool", tm[:], acc[h][0:64, c0:c0 + 512], rc[:], ALU.mult, [krc] + ck, [ktm])
                        o, ko = outr.next()
                        tt(P, "dve", o[:], tm[:], g[:], ALU.mult, [ktm, kg], [ko])
                        P.dma("sp", k.Y0_T[512 + r0:512 + r0 + 64, s0 + c0:s0 + c0 + 512], o[:], reads=[ko],
                              writes=[("Y0_T", r0, c0)])
        P.stage_end()


def stage3_rwkv(k):
    P, nc = k.P, k.nc
    with contextlib.ExitStack() as st:
        A = lambda n, sh, dt=F32: st.enter_context(nc.sbuf_tensor(n, sh, dt))
        RG = lambda n, sh, dt, c: Ring(nc, st, n, sh, dt, c)
        raw = {n: RG("raw_" + n, [128, 514], BF16, 2) for n in ("r", "k", "v", "la")}
        f = {n: RG("f_" + n, [128, 512], F32, 1) for n in
             ("t1", "t2", "rs", "ks", "las", "sg", "lr", "kkr", "sq", "rn", "kk", "tka", "kd", "beta",
              "ci", "cf", "ce", "E1", "E2", "E3", "E4", "lro", "e1", "e2", "e3", "e4", "e6")}
        f["vs"] = RG("f_vs", [128, 512], F32, 3)
        prb = RG("prb", [128, 512], BF16, 3)
        twal = RG("twal", [128, 512], BF16, 1)
        pc = RG("pc", [128, 8], F32, 3)
        AR = RG("AR", [128, 4, 2, 128], BF16, 3)
        ob = {n: RG("o_" + n, [128, 512], BF16, 3) for n in ("Bt", "Kt", "Be", "Ke", "vb")}
        TB = RG("TB", [128, 512], BF16, 8)
        G13r = RG("G13r", [128, 384], BF16, 4)
        G2r = RG("G2r", [128, 256], BF16, 4)
        G13 = RG("G13", [128, 384], BF16, 16)
        G2 = RG("G2", [128, 256], BF16, 16)
        Z0 = RG("Z0", [128, 128], BF16, 8)
        ZP = RG("ZP", [128, 384], BF16, 16)
        Z6 = RG("Z6", [128, 128], BF16, 16)
        UT = RG("UT", [128, 2, 128], BF16, 3)
        for t_, k_ in zip(UT.tiles, UT.keys):
            P.op("pool", (lambda a: lambda e: e.memset(a, 0.0))(t_[:]), writes=[k_])
        AW = RG("AW", [128, 512], BF16, 2)
        YT = RG("YT", [128, 512], F32, 2)
        yf = RG("yf", [128, 512], F32, 2)
        gat = RG("gat", [128, 512], BF16, 2)
        yo = RG("yo", [128, 512], BF16, 2)
        ST2 = A("ST2", [128, 128])
        STb2 = A("STb2", [128, 128], BF16)
        pbank = [st.enter_context(nc.psum_tensor("pg%d" % i, [128, 512], F32)) for i in range(7)]
        psg = SRing([(pbank[i], "pg%d" % i) for i in range(4)])
        psc = SRing([(pbank[i], "pg%d" % i) for i in range(4, 6)])
        pse = SRing([(pbank[i], "pg%d" % i) for i in range(6, 7)])
        psy = st.enter_context(nc.psum_tensor("psy", [128, 512], F32))
        idb, bd1 = k.cb["ident"], k.cf["bd1"]
        bd1b = k.cb["bd1"]
        sqb = RG("sqb", [128, 512], BF16, 2)
        k.o_omk2 = k.o_omk + 4
        ts(P, "dve", k.cols[:, k.o_omk2:k.o_omk2 + 4], k.cols[:, k.o_omk:k.o_omk + 4], 2.0, None, ALU.mult, None,
           ["cols"], ["cols"])
        ev = [0]
        lvl = [0]

        def evac_eng():
            ev[0] += 1
            return "act" if ev[0] % 2 else "dve"

        def cpx(out, in_, reads, writes):
            cp(P, evac_eng(), out, in_, reads, writes)
        v3 = lambda a: a.rearrange("p (c j) -> p c j", j=64)
        v4 = lambda a: a.rearrange("p (b t) -> p b t", t=128)

        def elem(c):
            s0, S, hp, z, ti, ntile = c["item"]
            t0 = s0 + 512 * ti
            c["t0"] = t0
            lo_edge, hi_edge = (ti == 0), (ti == ntile - 1)
            rt = {}
            for n, r0 in (("r", 128 * hp), ("k", 512 + 128 * hp), ("v", 1024 + 128 * hp), ("la", 1536)):
                t, key = raw[n].next()
                if lo_edge:
                    P.op("pool", (lambda a: lambda e: e.memset(a, 0.0))(t[:, 0:1]), writes=[key])
                if hi_edge:
                    P.op("pool", (lambda a: lambda e: e.memset(a, 0.0))(t[:, 513:514]), writes=[key])
                c0 = 1 if lo_edge else 0
                c1 = 513 if hi_edge else 514
                P.dma("sp", t[:, c0:c1], k.RW_T[r0:r0 + 128, t0 - 1 + c0:t0 - 1 + c1], writes=[key])
                rt[n] = (t, key)
            yield
            sh = {}
            for n, mc, dst in (("r", hp, "rs"), ("k", 4 + hp, "ks"), ("v", 8 + hp, "vs"), ("la", 12, "las")):
                t, key = rt[n]
                t1, k1 = f["t1"].next()
                t2, k2 = f["t2"].next()
                o, ko = f[dst].next()
                act(P, t1[:], t[:, 0:512], AF.Copy, [key, "cols"], [k1], scale=col(k, "mup", mc))
                stt(P, t2[:], t[:, 2:514], col(k, "mun", mc), t1[:], ALU.mult, ALU.add, [key, k1, "cols"], [k2])
                stt(P, o[:], t[:, 1:513], k.cols[:, k.o_c0 + mc:k.o_c0 + mc + 1], t2[:], ALU.mult, ALU.add,
                    [key, k2, "cols"], [ko])
                sh[dst] = (o, ko)
                yield
            rs, krs = sh["rs"]; ks_, kks = sh["ks"]; vs, kvs = sh["vs"]; las, klas = sh["las"]
            c["vs"] = (vs, kvs)
            tw, ktw = twal.next()
            act(P, tw[0:64, :], las[0:64, :], AF.Tanh, [klas], [ktw])
            cp(P, "dve", tw[64:128, :], las[64:128, :], [klas], [ktw])
            hc = slice(128 * hp, 128 * hp + 128)

            def lora_sig(zz, wsel, dst, kdst, bias_name):
                rows = slice(0, 64) if wsel == "w" else slice(64, 128)
                pw, kpw = pse.next()
                mm(P, pw[:], k.lora[rows, zz, hc], tw[rows, :], True, True, [ktw, "lora"], [kpw])
                act(P, dst[:], pw[:], AF.Sigmoid, [kpw, "cols"], [kdst], bias=col(k, bias_name, hp))
            sg, ksg = f["sg"].next()
            lr, klr = f["lr"].next()
            lora_sig(z, "w", sg, ksg, "w0_%d" % z)
            lora_sig(z, "a", lr, klr, "a0_%d" % z)
            yield
            sq, ksq = sqb.next()
            act(P, sq[:], ks_[:], AF.Square, [kks, "cols"], [ksq], scale=col(k, "k_k", hp))
            rn, krn = f["rn"].next()
            pn, kpn = pse.next()
            mm(P, pn[:], bd1b[:], sq[:], True, True, [ksq, "cb_bd1"], [kpn])
            act(P, rn[:], pn[:], AF.Ln, [kpn], [krn], bias=1e-18)
            act(P, rn[:], rn[:], AF.Exp, [krn], [krn], scale=-0.5)
            kk, kkk = f["kk"].next()
            stt(P, kk[:], ks_[:], col(k, "k_k", hp), rn[:], ALU.mult, ALU.mult, [kks, krn, "cols"], [kkk])
            yield
            tka, ktka = f["tka"].next()
            ts(P, "dve", tka[:], lr[:], col(k, "k_a", hp), k.cols[:, k.o_omk + hp:k.o_omk + hp + 1],
               ALU.mult, ALU.add, [klr, "cols"], [ktka])
            kd, kkd = f["kd"].next()
            tt(P, "pool", kd[:], ks_[:], tka[:], ALU.mult, [kks, ktka], [kkd])
            beta, kbeta = f["beta"].next()
            tt(P, "pool", beta[:], kk[:], lr[:], ALU.mult, [kkk, klr], [kbeta])
            cf_, kcf = f["cf"].next()
            P.op("dve", (lambda o, a, b: lambda e: e.tensor_tensor_scan(
                out=o, data0=a, data1=b, initial=0.0, op0=ALU.mult, op1=ALU.add))(
                cf_[:], k.cf["rmask"][:], sg[:]), [ksg, "cf_rmask"], [kcf])
            tot = cf_[:, 63:512:64]
            totb = tot.unsqueeze(2).to_broadcast([128, 8, 64])
            if z == 0:
                ci, kci = cf_, kcf
            else:
                ci, kci = f["ci"].next()
                tt(P, "pool", ci[:], sg[:], cf_[:], ALU.subtract, [ksg, kcf], [kci])
                tt(P, "dve", v3(ci[:]), v3(ci[:]), totb, ALU.add, [kci, kcf], [kci])
            ce, kce = f["ce"].next()
            tt(P, "pool", ce[:], ci[:], sg[:], ALU.subtract, [kci, ksg], [kce])
            yield
            E1, kE1 = f["E1"].next(); E2, kE2 = f["E2"].next(); E3, kE3 = f["E3"].next(); E4, kE4 = f["E4"].next()
            act(P, E1[:], ce[:], AF.Exp, [kce], [kE1], scale=-DECAY_C)
            act(P, E2[:], ci[:], AF.Exp, [kci], [kE2], scale=DECAY_C)
            act(P, E3[:], ci[:], AF.Exp, [kci], [kE3], scale=-DECAY_C)
            pct, kpc = pc.next()
            act(P, pct[:], tot, AF.Exp, [kcf], [kpc], scale=-DECAY_C)
            c["pc"] = (pct, kpc)
            pcb = pct[:].unsqueeze(2).to_broadcast([128, 8, 64])
            tt(P, "dve", v3(E4[:]), v3(E2[:]), pcb, ALU.mult, [kE2, kpc], [kE4])
            yield
            art, kar = AR.next()
            stt(P, art[:, :, 0, :], v4(kk[:]), -1.0, v4(E1[:]), ALU.mult, ALU.mult, [kkk, kE1], [kar])
            tt(P, "pool", art[:, :, 1, :], v4(rs[:]), v4(E3[:]), ALU.mult, [krs, kE3], [kar])
            Bt, kBt = ob["Bt"].next(); Kt, kKt = ob["Kt"].next()
            Be, kBe = ob["Be"].next(); Ke, kKe = ob["Ke"].next(); vb, kvb = ob["vb"].next()
            tt(P, "pool", Bt[:], beta[:], E2[:], ALU.mult, [kbeta, kE2], [kBt])
            tt(P, "pool", Kt[:], kd[:], E2[:], ALU.mult, [kkd, kE2], [kKt])
            yield
            tt(P, "pool", Be[:], beta[:], E4[:], ALU.mult, [kbeta, kE4], [kBe])
            tt(P, "pool", Ke[:], kd[:], E4[:], ALU.mult, [kkd, kE4], [kKe])
            cp(P, "act", vb[:], vs[:], [kvs], [kvb])
            c.update(art=(art, kar), Bt=(Bt, kBt), Kt=(Kt, kKt), Be=(Be, kBe), Ke=(Ke, kKe), vb=(vb, kvb))
            if z == 1:
                yield
                lro, klro = f["lro"].next()
                lora_sig(0, "a", lro, klro, "a0_0")
                tt(P, "pool", lro[:], lro[:], lr[:], ALU.add, [klro, klr], [klro])
                ts(P, "dve", lro[:], lro[:], col(k, "k_a", hp), k.cols[:, k.o_omk2 + hp:k.o_omk2 + hp + 1],
                   ALU.mult, ALU.add, [klro, "cols"], [klro])
                tt(P, "pool", lro[:], lro[:], ks_[:], ALU.mult, [klro, kks], [klro])
                pr_, kpr_ = prb.next()
                stt(P, pr_[:], rs[:], col(k, "r_k", hp), lro[:], ALU.mult, ALU.mult, [krs, klro, "cols"], [kpr_])
                c["pr"] = (pr_, kpr_)

        def wave(c):
            s0, S, hp, z, ti, ntile = c["item"]
            mask = k.cb["m_f" if z == 0 else "m_b"]
            mkeys = ["cb_m_f", "cb_m_b"]
            art, kar = c["art"]; Bt, kBt = c["Bt"]; Kt, kKt = c["Kt"]
            Be, kBe = c["Be"]; Ke, kKe = c["Ke"]; vb, kvb = c["vb"]
            border = list(range(4)) if z == 0 else list(range(3, -1, -1))
            c["border"] = border
            aw, kaw = AW.next()
            c["aw"] = (aw, kaw)
            tb = {}
            for b in border:
                bc = slice(128 * b, 128 * b + 128)
                p_, kp_ = psg.next()
                for i, (src, skey) in enumerate(((vb, kvb), (Be, kBe), (Ke, kKe))):
                    mm(P, p_[:, 128 * i:128 * i + 128], src[:, bc], idb[:], True, True, [skey, "cb_ident"], [kp_])
                mm(P, p_[:, 384:512], vb[64:128, bc], idb[64:128, :], True, True, [kvb, "cb_ident"], [kp_])
                t, kt = TB.next()
                cpx(t[:], p_[:, 0:512], [kp_], [kt])
                tb[b] = (t, kt)
                yield
            c["tb"] = tb
            units = [(b, h) for b in border for h in range(2)]
            U = {}
            for ui, (b, h) in enumerate(units):
                bc = slice(128 * b, 128 * b + 128)
                ph = slice(64 * h, 64 * h + 64)
                arh = art[ph, b, :, :].rearrange("p a t -> p (a t)")
                p1, kp1 = psg.next()
                mm(P, p1[:, 0:256], Bt[ph, bc], arh, True, True, [kBt, kar], [kp1])
                mm(P, p1[:, 256:384], art[ph, b, 0, :], Bt[ph, bc], True, True, [kBt, kar], [kp1])
                g13, kg13 = G13.next()
                tt(P, "dve", g13[:], p1[:, 0:384], mask[:], ALU.mult, [kp1] + mkeys, [kg13])
                p2, kp2 = psg.next()
                mm(P, p2[:, 0:256], Kt[ph, bc], arh, True, True, [kKt, kar], [kp2])
                g2r, kg2r = G2r.next()
                cp(P, "act", g2r[:], p2[:, 0:256], [kp2], [kg2r])
                g2, kg2 = G2.next()
                tt(P, "pool", g2[:], g2r[:], mask[:, 0:256], ALU.mult, [kg2r] + mkeys, [kg2])
                U[(b, h)] = dict(g13=(g13, kg13), g2=(g2, kg2))
                yield
            for (b, h) in units:
                u = U[(b, h)]
                ph = slice(64 * h, 64 * h + 64)
                g2, kg2 = u["g2"]
                vt, kvt = tb[b]
                za = slice(0, 64) if h == 0 else slice(64, 128)
                zv = slice(64, 128) if h == 0 else slice(0, 64)
                p4, kp4 = psg.next()
                mm(P, p4[:, za], art[ph, b, 0, :], idb[ph, 64 * h:64 * h + 64], True, True, [kar, "cb_ident"], [kp4])
                mm(P, p4[:, zv], g2[:, 0:128], vt[:, 64 * h:64 * h + 64], True, True, [kg2, kvt], [kp4])
                z0, kz0 = Z0.next()
                cpx(z0[:], p4[:, 0:128], [kp4], [kz0])
                g13, kg13 = u["g13"]
                u["Z"] = (z0[:, 0:128], kz0)
                u["P"] = (g13[:, 0:128], kg13)
                u["PT"] = (g13[:, 256:384], kg13)
                if h == 1:
                    yield
            for j in range(6):
                last = (j == 5)
                for (b, h) in units:
                    u = U[(b, h)]
                    Zt, kZ = u["Z"]; Pt, kP = u["P"]; PTt, kPT = u["PT"]
                    ps, kps = psg.next()
                    if not last:
                        if j == 0:
                            mm(P, ps[:, 0:128], Pt, Zt, True, True, [kP, kZ], [kps])
                            mm(P, ps[:, 128:256], Pt, PTt, True, True, [kP, kPT], [kps])
                        else:
                            mm(P, ps[:, 0:256], Pt, u["ZPT"], True, True, [kP, kZ], [kps])
                        mm(P, ps[:, 256:384], PTt, Pt, True, True, [kP, kPT], [kps])
                        zn, kzn = ZP.next()
                        tt(P, "dve", zn[:, 0:128], ps[:, 0:128], Zt, ALU.add, [kps, kZ], [kzn])
                        lvl[0] += 1
                        cp(P, "act" if lvl[0] % 8 else "dve", zn[:, 128:384], ps[:, 128:384], [kps], [kzn])
                        u["Z"] = (zn[:, 0:128], kzn)
                        u["PT"] = (zn[:, 128:256], kzn)
                        u["P"] = (zn[:, 256:384], kzn)
                        u["ZPT"] = zn[:, 0:256]
                    else:
                        mm(P, ps[:, 0:128], Pt, Zt, True, True, [kP, kZ], [kps])
                        z6, kz6 = Z6.next()
                        tt(P, "dve", z6[:], ps[:, 0:128], Zt, ALU.add, [kps, kZ], [kz6])
                        u["z6"] = (z6, kz6)
                    if h == 1:
                        yield
            for (b, h) in units:
                u = U[(b, h)]
                ph = slice(64 * h, 64 * h + 64)
                bc = slice(128 * b, 128 * b + 128)
                z6, kz6 = u["z6"]
                p5, kp5 = psg.next()
                mm(P, p5[:, 0:128], z6[:], idb[:], True, True, [kz6, "cb_ident"], [kp5])
                cpx(aw[ph, bc], p5[ph, 0:128], [kp5], [kaw])
                if h == 1:
                    yield
            c["U"] = U

        def scan(c):
            s0, S, hp, z, ti, ntile = c["item"]
            t0 = c["t0"]
            first = (ti == 0) if z == 0 else (ti == ntile - 1)
            if first:
                P.op("pool", lambda e: e.memset(ST2[:], 0.0), writes=[("ST2", 0), ("ST2", 1)])
                P.op("pool", lambda e: e.memset(STb2[:], 0.0), writes=[("STb2", 0), ("STb2", 1)])
            art, kar = c["art"]
            pct, kpc = c["pc"]
            aw, kaw = c["aw"]
            U = c["U"]
            corder = (0, 1) if z == 0 else (1, 0)
            for b in c["border"]:
                bc = slice(128 * b, 128 * b + 128)
                tbt, ktb = c["tb"][b]
                vt, bet, ket = tbt[:, 0:128], tbt[:, 128:256], tbt[:, 256:384]
                ut, kut = UT.next()
                for cc in corder:
                    rows = slice(64 * cc, 64 * cc + 64)
                    tk = slice(128 * b + 64 * cc, 128 * b + 64 * cc + 64)
                    cidx = 2 * b + cc
                    hop = []
                    for h in range(2):
                        ph = slice(64 * h, 64 * h + 64)
                        pu, kpu = psc.next()
                        mm(P, pu[:, 0:64], aw[ph, bc], STb2[ph, 64 * h:64 * h + 64], True, True, [kaw, ("STb2", h)], [kpu])
                        hop.append((pu, kpu))
                    yield
                    hop2 = []
                    vtz = tbt[:, 384:512]
                    for h in range(2):
                        u = U[(b, h)]
                        z6, kz6 = u["z6"]
                        pu, kpu = hop[h]
                        if h == 0:
                            tt(P, "dve", ut[rows, 0, 0:64], pu[rows, 0:64], z6[rows, 64:128], ALU.add, [kpu, kz6], [kut])
                        else:
                            tt(P, "dve", ut[rows, 1, 64:128], pu[rows, 0:64], z6[rows, 0:64], ALU.add, [kpu, kz6], [kut])
                    gA, kgA = U[(b, 0)]["g13"]; g2A, kg2A = U[(b, 0)]["g2"]
                    gB, kgB = U[(b, 1)]["g13"]; g2B, kg2B = U[(b, 1)]["g2"]
                    cs_ = slice(128 + 64 * cc, 128 + 64 * cc + 64)
                    mm(P, psy[:, tk], STb2[:, :], art[:, b, 1, 64 * cc:64 * cc + 64], True, False,
                       [("STb2", 0), ("STb2", 1), kar], ["psy"])
                    mm(P, psy[0:64, tk], ut[rows, 0, 0:64], gA[rows, cs_], False, False, [kut, kgA], ["psy"])
                    mm(P, psy[0:64, tk], vt[rows, 0:64], g2A[rows, cs_], False, False, [ktb, kg2A], ["psy"])
                    mm(P, psy[:, tk], ut[rows, 1, :], gB[rows, cs_], False, False, [kut, kgB], ["psy"])
                    mm(P, psy[:, tk], vtz[rows, :], g2B[rows, cs_], False, True, [ktb, kg2B], ["psy"])
                    for h in range(2):
                        pss, kpss = psc.next()
                        if h == 0:
                            mm(P, pss[0:64, 0:64], bet[rows, 0:64], ut[rows, 0, 0:64], True, False, [ktb, kut], [kpss])
                            mm(P, pss[0:64, 0:64], ket[rows, 0:64], vt[rows, 0:64], False, True, [ktb], [kpss])
                        else:
                            mm(P, pss[:, 0:64], bet[rows, 0:128], ut[rows, 1, 64:128], True, False, [ktb, kut], [kpss])
                            mm(P, pss[:, 0:64], ket[rows, 0:128], vt[rows, 64:128], False, True, [ktb], [kpss])
                        hop2.append((pss, kpss))
                    yield
                    for h in range(2):
                        ph = slice(64 * h, 64 * h + 64)
                        pss, kpss = hop2[h]
                        sv = ST2[ph, 64 * h:64 * h + 64]
                        stt(P, sv, sv, pct[ph, cidx:cidx + 1], pss[ph, 0:64], ALU.mult, ALU.add,
                            [kpss, kpc, ("ST2", h)], [("ST2", h)])
                        cp(P, "act", STb2[ph, 64 * h:64 * h + 64], sv, [("ST2", h)], [("STb2", h)])
                    yield
            yt, kyt = YT.next()
            cp(P, "act", yt[:], psy[:], ["psy"], [kyt])
            rr = slice(128 * hp, 128 * hp + 128)
            if z == 0:
                P.dma("sp", k.YF_T[rr, t0:t0 + 512], yt[:], reads=[kyt], writes=[("YF", hp, t0)])
                return
            vs, kvs = c["vs"]
            pr_, kpr_ = c["pr"]
            yft, kyf = yf.next()
            P.dma("sp", yft[:], k.YF_T[rr, t0:t0 + 512], reads=[("YF", hp, t0)], writes=[kyf])
            gt, kgt = gat.next()
            P.dma("sp", gt[:], k.GA_T[rr, t0:t0 + 512], writes=[kgt])
            y, ky = f["e1"].next()
            tt(P, "pool", y[:], yt[:], yft[:], ALU.add, [kyt, kyf], [ky])
            d_, kd_ = f["e2"].next()
            sq2, ksq2 = f["e3"].next()
            rstd, krstd = f["e4"].next()
            pm, kpm = pse.next()
            mm(P, pm[:], bd1[:], y[:], True, True, [ky, "cf_bd1"], [kpm])
            stt(P, d_[:], pm[:], -1.0 / 64, y[:], ALU.mult, ALU.add, [kpm, ky], [kd_])
            sq2, ksq2 = sqb.next()
            act(P, sq2[:], d_[:], AF.Square, [kd_], [ksq2])
            yield
            pv_, kpv_ = pse.next()
            mm(P, pv_[:], bd1b[:], sq2[:], True, True, [ksq2, "cb_bd1"], [kpv_])
            rsqrt(P, rstd[:], pv_[:], 1.0 / 64, GN_EPS, [kpv_], [krstd])
            tt(P, "pool", d_[:], d_[:], rstd[:], ALU.mult, [kd_, krstd], [kd_])
            ts(P, "dve", d_[:], d_[:], col(k, "gn_w", hp), col(k, "gn_b", hp), ALU.mult, ALU.add, [kd_, "cols"], [kd_])
            yield
            bo, kbo = f["e6"].next()
            pb2, kpb2 = pse.next()
            mm(P, pb2[:], bd1b[:], pr_[:], True, True, [kpr_, "cb_bd1"], [kpb2])
            tt(P, "dve", bo[:], pb2[:], vs[:], ALU.mult, [kpb2, kvs], [kbo])
            tt(P, "pool", bo[:], bo[:], d_[:], ALU.add, [kbo, kd_], [kbo])
            o_, ko_ = yo.next()
            tt(P, "dve", o_[:], bo[:], gt[:], ALU.mult, [kbo, kgt], [ko_])
            P.dma("sp", k.Y0_T[rr, t0:t0 + 512], o_[:], reads=[ko_], writes=[("Y0a", hp, t0)])

        items = []
        for s0, S in zip(k.seq_off, k.seqs):
            ntile = S // 512
            for hp in range(4):
                for z in range(2):
                    order = range(ntile) if z == 0 else range(ntile - 1, -1, -1)
                    for ti in order:
                        items.append(dict(item=(s0, S, hp, z, ti, ntile)))
        n = len(items)
        for step in range(n + 2):
            gens = []
            if step < n:
                gens.append(elem(items[step]))
            if 0 <= step - 1 < n:
                gens.append(wave(items[step - 1]))
            if 0 <= step - 2 < n:
                gens.append(scan(items[step - 2]))
            while gens:
                for g in list(gens):
                    try:
                        next(g)
                    except StopIteration:
                        gens.remove(g)
            if step - 2 >= 0:
                items[step - 2].clear()
        P.stage_end()


def out_proj_tile(k, R, W, ysrc, xsrc, t0, j, pso, xres_ring, dst=None):
    P = k.P
    yT, kyT = ysrc
    xt, kx = xsrc
    if dst is None:
        xr, kxr = xres_ring.next()
    else:
        xr, kxr = dst
    for half in range(2):
        ps, kp = pso.next()
        for kc in range(8):
            mm(P, ps[:], yT[:, kc, 128 * j:128 * j + 128], W[:, kc, 512 * half:512 * half + 512], kc == 0, kc == 7,
               [kyT, "Wo"], [kp])
        tt(P, "dve", xr[:, 512 * half:512 * half + 512], ps[:], xt[:, j, 512 * half:512 * half + 512], ALU.add,
           [kp, kx], [kxr])
    return xr, kxr


def stage4_out0(k):
    P, nc, T = k.P, k.nc, k.T
    with contextlib.ExitStack() as st:
        Wo, W = load_weights(k, st, [("wo0", k.p["w_out0"], 8, D, "Wo"), ("w1", k.p["w_in1"], 8, ODD_COLS, "W")])
        R = in_rings(k, st)
        yr = Ring(nc, st, "y0T", [128, 8, 512], BF16, 2)
        x1r = Ring(nc, st, "x1t", [128, 4, D], F32, 2)
        pso = Ring(nc, st, "pso", [128, 512], F32, 2, psum=True)
        psr = Ring(nc, st, "ps1", [128, 512], F32, 4, psum=True)
        ost = Ring(nc, st, "ost", [128, 512], BF16, 6)
        glr = Ring(nc, st, "glt", [16, 512], F32, 2)
        qscale = 128.0 ** -0.5
        ntile = T // 512

        def loads(t0):
            xt, kx = R["x"].next()
            P.dma("sp", xt[:], k.x[t0:t0 + 512, :].rearrange("(j p) d -> p j d", p=128), writes=[kx])
            yT, kyT = yr.next()
            P.dma("sp", yT[:], k.Y0_T[:, t0:t0 + 512].rearrange("(kc p) t -> p kc t", p=128), writes=[kyT])
            return (xt, kx), (yT, kyT)
        nxt = loads(0)
        for ti in range(ntile):
            t0 = ti * 512
            xsrc, ysrc = nxt
            if ti + 1 < ntile:
                nxt = loads(t0 + 512)
            x1, kx1 = x1r.next()
            for j in range(4):
                xr, kxr = out_proj_tile(k, R, Wo, ysrc, xsrc, t0, j, pso, None, dst=(x1[:, j, :], kx1))
                P.dma("sp", k.X1[t0 + 128 * j:t0 + 128 * j + 128, :], xr, reads=[kxr], writes=[("X1", ti, j)])
            xnT, kT = rms_transpose_compute(k, R, x1, kx1, "norm1")
            for oc in list(range(8)) + list(range(16, 24)):
                ps, kp = psr.next()
                c0 = 128 * oc if oc < 8 else 2064 + 128 * (oc - 16)
                for kc in range(8):
                    mm(P, ps[:], W[:, kc, c0:c0 + 128], xnT[:, kc, :], kc == 0, kc == 7, [kT, "W"], [kp])
                o, ko = ost.next()
                if oc < 4:
                    act(P, o[:], ps[:], AF.Copy, [kp], [ko], scale=qscale)
                elif oc < 8:
                    cp(P, "dve", o[:], ps[:], [kp], [ko])
                else:
                    act(P, o[:], ps[:], AF.Silu, [kp], [ko])
                if oc < 8:
                    P.dma("sp", k.Q1_T[128 * oc:128 * oc + 128, t0:t0 + 512], o[:], reads=[ko], writes=[("Q1", ti, oc)])
                else:
                    r0 = 128 * (oc - 16)
                    P.dma("sp", k.G1_T[r0:r0 + 128, t0:t0 + 512], o[:], reads=[ko], writes=[("G1", ti, oc)])
            ps, kp = psr.next()
            for kc in range(8):
                mm(P, ps[0:16, :], W[:, kc, 2048:2064], xnT[:, kc, :], kc == 0, kc == 7, [kT, "W"], [kp])
            gl, kgl = glr.next()
            cp(P, "act", gl[:], ps[0:16, :], [kp], [kgl])
            P.dma("sp", k.GL_T[:, t0:t0 + 512], gl[:], reads=[kgl], writes=[("GL", ti)])
            for j in range(4):
                for half in range(2):
                    ps, kp = psr.next()
                    c0 = 1024 + 512 * half
                    for kc in range(8):
                        mm(P, ps[:], xnT[:, kc, 128 * j:128 * j + 128], W[:, kc, c0:c0 + 512], kc == 0, kc == 7,
                           [kT, "W"], [kp])
                    o, ko = ost.next()
                    cp(P, "act" if half else "dve", o[:], ps[:], [kp], [ko])
                    P.dma("sp", k.V1[t0 + 128 * j:t0 + 128 * j + 128, 512 * half:512 * half + 512], o[:], reads=[ko],
                          writes=[("V1", ti, j, half)])
        P.stage_end()


def stage5_in1(k):
    pass


def stage6_gla(k):
    P, nc = k.P, k.nc
    with contextlib.ExitStack() as st:
        A = lambda n, sh, dt=F32: st.enter_context(nc.sbuf_tensor(n, sh, dt))
        RG = lambda n, sh, dt, c: Ring(nc, st, n, sh, dt, c)
        gu = A("g_gu", [16, 2, 512])
        P.dma("sp", gu[:], k.p["gate_up"][:, :, :], writes=["gu"])
        ngb = A("g_ngb", [128, 8])
        og = COLS["gb_0"][0]
        ts(P, "dve", ngb[:], k.cols[:, og:og + 8], -1.0, None, ALU.mult, None, ["cols"], ["ngb"])
        def make_lane(li):
            L = {}
            nm = lambda n: "%s_l%d" % (n, li)
            L["Sr"] = RG(nm("g_S"), [128, 256], F32, 2)
            L["Sbr"] = RG(nm("g_Sb"), [128, 256], BF16, 2)
            L["qr"] = RG(nm("g_q"), [128, 512], BF16, 2)
            L["kr"] = RG(nm("g_k"), [128, 512], BF16, 2)
            L["glr"] = RG(nm("g_gl"), [16, 512], F32, 2)
            L["vtr"] = RG(nm("g_v"), [128, 4, 256], BF16, 2)
            L["f"] = {n: RG(nm("gf_" + n), [128, 512], F32, 1) for n in ("e", "l", "cf", "ci", "Eq", "Ek", "Ee", "rstd")}
            L["dcr"] = RG(nm("g_dc"), [128, 4], F32, 2)
            L["ob"] = {n: RG(nm("go_" + n), [128, 512], BF16, 2) for n in ("qd", "kd", "ke")}
            L["attr"] = RG(nm("g_att"), [128, 256], BF16, 3)
            L["otr"] = RG(nm("g_ot"), [128, 2, 512], F32, 2)
            L["ofr"] = RG(nm("g_of"), [128, 2, 512], F32, 1)
            L["sqr"] = RG(nm("g_sq"), [128, 2, 512], F32, 1)
            L["gtr"] = RG(nm("g_gt"), [128, 2, 512], BF16, 1)
            L["yor"] = RG(nm("g_yo"), [128, 512], BF16, 2)
            L["psg"] = Ring(nc, st, nm("pgG"), [128, 512], F32, 4, psum=True)
            return L
        lanes = [make_lane(0), make_lane(1)]
        npass = [0]
        idb = k.cb["ident"]
        ones = k.cf["ones"]
        ev = [0]

        def evac_eng():
            ev[0] += 1
            return "act" if ev[0] % 2 else "dve"
        v3 = lambda a: a.rearrange("p (c j) -> p c j", j=128)
        for s0, S in zip(k.seq_off, k.seqs):
            ntile = S // 512
            for h in range(4):
                for z in range(2):
                    L = lanes[npass[0] % 2]
                    npass[0] += 1
                    Sr, Sbr, qr, kr, glr, vtr, f, dcr, ob = (L[n_] for n_ in ("Sr", "Sbr", "qr", "kr", "glr", "vtr", "f", "dcr", "ob"))
                    attr, otr, ofr, sqr, gtr, yor, psg = (L[n_] for n_ in ("attr", "otr", "ofr", "sqr", "gtr", "yor", "psg"))
                    S_, kS = Sr.next()
                    Sb, kSb = Sbr.next()
                    P.op("pool", (lambda a: lambda e: e.memset(a, 0.0))(S_[:]), writes=[kS])
                    P.op("pool", (lambda a: lambda e: e.memset(a, 0.0))(Sb[:]), writes=[kSb])
                    mask = k.cb["g_f" if z == 0 else "g_b"]
                    order = range(ntile) if z == 0 else range(ntile - 1, -1, -1)
                    for ti in order:
                        t0 = s0 + 512 * ti
                        qT, kq = qr.next(); kT, kk_ = kr.next(); gl, kgl = glr.next(); vt, kvt = vtr.next()
                        P.dma("sp", qT[:], k.Q1_T[128 * h:128 * h + 128, t0:t0 + 512], writes=[kq])
                        P.dma("sp", kT[:], k.Q1_T[512 + 128 * h:512 + 128 * h + 128, t0:t0 + 512], writes=[kk_])
                        P.dma("sp", gl[:], k.GL_T[:, t0:t0 + 512], writes=[kgl])
                        P.dma("sp", vt[:], k.V1[t0:t0 + 512, 256 * h:256 * h + 256].rearrange("(j p) d -> p j d", p=128),
                              writes=[kvt])
                        pz, kpz = psg.next()
                        mm(P, pz[:], gu[0:16, z, 128 * h:128 * h + 128], gl[0:16, :], True, True, ["gu", kgl], [kpz])
                        e_, ke_ = f["e"].next()
                        act(P, e_[:], pz[:], AF.Exp, [kpz, "ngb"], [ke_], scale=-1.0, bias=ngb[:, 4 * z + h:4 * z + h + 1])
                        l_, kl_ = f["l"].next()
                        act(P, l_[:], e_[:], AF.Ln, [ke_], [kl_], bias=1.0)
                        if GLA_CUT == 1:
                            continue
                        cf_, kcf = f["cf"].next()
                        P.op("dve", (lambda o, a, b: lambda e: e.tensor_tensor_scan(
                            out=o, data0=a, data1=b, initial=0.0, op0=ALU.mult, op1=ALU.add))(
                            cf_[:], k.cf["rmask128"][:], l_[:]), [kl_, "cf_rmask128"], [kcf])
                        tot = cf_[:, 127:512:128]
                        totb = tot.unsqueeze(2).to_broadcast([128, 4, 128])
                        if z == 0:
                            ci, kci = cf_, kcf
                        else:
                            ci, kci = f["ci"].next()
                            tt(P, "pool", ci[:], l_[:], cf_[:], ALU.subtract, [kl_, kcf], [kci])
                            tt(P, "dve", v3(ci[:]), v3(ci[:]), totb, ALU.add, [kci, kcf], [kci])
                        Eq, kEq = f["Eq"].next(); Ek, kEk = f["Ek"].next(); Ee, kEe = f["Ee"].next()
                        act(P, Eq[:], ci[:], AF.Exp, [kci], [kEq], scale=-1.0 / 16)
                        act(P, Ek[:], ci[:], AF.Exp, [kci], [kEk], scale=1.0 / 16)
                        dc, kdc = dcr.next()
                        act(P, dc[:], tot, AF.Exp, [kcf], [kdc], scale=-1.0 / 16)
                        tt(P, "dve", v3(Ee[:]), v3(Ek[:]), dc[:].unsqueeze(2).to_broadcast([128, 4, 128]), ALU.mult,
                           [kEk, kdc], [kEe])
                        qd, kqd = ob["qd"].next(); kd, kkd = ob["kd"].next(); ke, kke = ob["ke"].next()
                        tt(P, "pool", qd[:], qT[:], Eq[:], ALU.mult, [kq, kEq], [kqd])
                        tt(P, "dve", kd[:], kT[:], Ek[:], ALU.mult, [kk_, kEk], [kkd])
                        tt(P, "pool", ke[:], kT[:], Ee[:], ALU.mult, [kk_, kEe], [kke])
                        if GLA_CUT == 2:
                            continue
                        ot, kot = otr.next()
                        border = range(4) if z == 0 else range(3, -1, -1)
                        for b in border:
                            bc = slice(128 * b, 128 * b + 128)
                            pa, kpa = psg.next()
                            mm(P, pa[:, 0:128], kd[:, bc], qd[:, bc], True, True, [kkd, kqd], [kpa])
                            mm(P, pa[:, 128:256], ke[:, bc], idb[:], True, True, [kke, "cb_ident"], [kpa])
                            at, kat = attr.next()
                            tt(P, "dve", at[:], pa[:, 0:256], mask[:], ALU.mult, [kpa, "cb_g_f", "cb_g_b"], [kat])
                            keT = at[:, 128:256]
                            po, kpo = psg.next()
                            for half in range(2):
                                hc = slice(128 * half, 128 * half + 128)
                                mm(P, po[:, hc], vt[:, b, hc], at[:, 0:128], True, False, [kvt, kat], [kpo])
                                mm(P, po[:, hc], Sb[:, hc], qd[:, bc], False, True, [kSb, kqd], [kpo])
                            cp(P, "act", ot[:, :, bc], po[:, 0:256].rearrange("p (a t) -> p a t", a=2), [kpo], [kot])
                            pS, kpS = psg.next()
                            mm(P, pS[:, 0:256], keT, vt[:, b, :], True, True, [kat, kvt], [kpS])
                            stt(P, S_[:], S_[:], dc[:, b:b + 1], pS[:, 0:256], ALU.mult, ALU.add, [kpS, kdc, kS], [kS])
                            cp(P, "act", Sb[:], S_[:], [kS], [kSb])
                        if GLA_CUT == 3:
                            continue
                        if z == 0:
                            for half in range(2):
                                r0 = 256 * h + 128 * half
                                P.dma("sp", k.OF_T[r0:r0 + 128, t0:t0 + 512], ot[:, half, :], reads=[kot],
                                      writes=[("OF", h, half, t0)])
                            continue
                        of, kof = ofr.next(); gt, kgt = gtr.next()
                        for half in range(2):
                            r0 = 256 * h + 128 * half
                            P.dma("sp", of[:, half, :], k.OF_T[r0:r0 + 128, t0:t0 + 512], reads=[("OF", h, half, t0)],
                                  writes=[kof])
                            P.dma("sp", gt[:, half, :], k.G1_T[r0:r0 + 128, t0:t0 + 512], writes=[kgt])
                        tt(P, "pool", of[:], of[:], ot[:], ALU.add, [kof, kot], [kof])
                        if GLA_CUT == 4:
                            continue
                        sq, ksq = sqr.next()
                        act(P, sq[:], of[:], AF.Square, [kof], [ksq])
                        pn, kpn = psg.next()
                        mm(P, pn[:], ones[:], sq[:, 0, :], True, False, [ksq, "cf_ones"], [kpn])
                        mm(P, pn[:], ones[:], sq[:, 1, :], False, True, [ksq, "cf_ones"], [kpn])
                        rstd, krstd = f["rstd"].next()
                        if GLA_CUT == 5:
                            continue
                        rsqrt(P, rstd[:], pn[:], 1.0 / 256, RMS_EPS, [kpn], [krstd])
                        if GLA_CUT == 6:
                            continue
                        for half in range(2):
                            r0 = 256 * h + 128 * half
                            stt(P, of[:, half, :], of[:, half, :], col(k, "gnorm", half), rstd[:], ALU.mult, ALU.mult,
                                [kof, krstd, "cols"], [kof])
                            if GLA_CUT == 7:
                                continue
                            yo, kyo = yor.next()
                            tt(P, "dve", yo[:], of[:, half, :], gt[:, half, :], ALU.mult, [kof, kgt], [kyo])
                            P.dma("sp", k.Y1_T[r0:r0 + 128, t0:t0 + 512], yo[:], reads=[kyo], writes=[("Y1", h, half, t0)])
        P.stage_end()


def stage7_out1(k):
    P, nc, T = k.P, k.nc, k.T
    with contextlib.ExitStack() as st:
        Wo, = load_weights(k, st, [("wo1", k.p["w_out1"], 8, D, "Wo")])
        fn = st.enter_context(nc.sbuf_tensor("fnorm", [128, D], F32))
        P.dma("sp", fn[:], k.p["fnorm"][:, :], writes=["fnorm"])
        xr_ = Ring(nc, st, "x1in", [128, 4, D], F32, 2)
        yr = Ring(nc, st, "y1T", [128, 8, 512], BF16, 2)
        xrr = Ring(nc, st, "xres", [128, D], F32, 2)
        outr = Ring(nc, st, "outt", [128, D], F32, 2)
        junk = st.enter_context(nc.sbuf_tensor("junk7", [128, D], BF16))
        ssr = Ring(nc, st, "ss7", [128, 1], F32, 2)
        pso = Ring(nc, st, "pso", [128, 512], F32, 4, psum=True)
        ntile = T // 512

        def loads(t0):
            xt, kx = xr_.next()
            P.dma("sp", xt[:], k.X1[t0:t0 + 512, :].rearrange("(j p) d -> p j d", p=128), writes=[kx])
            yT, kyT = yr.next()
            P.dma("sp", yT[:], k.Y1_T[:, t0:t0 + 512].rearrange("(kc p) t -> p kc t", p=128), writes=[kyT])
            return (xt, kx), (yT, kyT)
        nxt = loads(0)
        for ti in range(ntile):
            t0 = ti * 512
            xsrc, ysrc = nxt
            if ti + 1 < ntile:
                nxt = loads(t0 + 512)
            for j in range(4):
                xr, kxr = out_proj_tile(k, None, Wo, ysrc, xsrc, t0, j, pso, xrr)
                ss, kss = ssr.next()
                act(P, junk[:], xr[:], AF.Square, [kxr], [kss, "junk7"], accum_out=ss[:, 0:1])
                rsqrt(P, ss[:], ss[:], 1.0 / D, RMS_EPS, [kss], [kss])
                o, ko = outr.next()
                stt(P, o[:], xr[:], ss[:, 0:1], fn[:], ALU.mult, ALU.mult, [kxr, kss, "fnorm"], [ko])
                P.dma("sp", k.y[t0 + 128 * j:t0 + 128 * j + 128, :], o[:], reads=[ko], writes=[("y", ti, j)])
        P.stage_end()


_CACHE = {}


def kernel(**inputs):
    xp = np.asarray(inputs["x_prompt"], np.float32)
    xs = np.asarray(inputs["x_sample"], np.float32)
    B, S, _ = xp.shape
    DB, DS, _ = xs.shape
    n = NCORES
    pb, sbn = B // n, DB // n
    seqs = [S] * pb + [DS] * sbn
    key = tuple(seqs)
    if key not in _CACHE:
        _CACHE[key] = build(seqs)
    nc, k = _CACHE[key]
    consts = host_consts(max(seqs))
    params = host_params(inputs)
    shared = {"c_" + a: v for a, v in consts.items()}
    shared.update({"p_" + a: v for a, v in params.items()})
    in_maps = []
    for c in range(n):
        parts = [xp[c * pb + i] for i in range(pb)] + [xs[c * sbn + i] for i in range(sbn)]
        m = {"x": np.ascontiguousarray(np.concatenate(parts, axis=0))}
        m.update(shared)
        in_maps.append(m)
    res = run_bass_kernel_spmd(nc, in_maps, core_ids=list(range(n)))
    yp = np.empty_like(xp)
    ys = np.empty_like(xs)
    for c in range(n):
        y = np.asarray(res.results[c]["y"], np.float32)
        off = 0
        for i in range(pb):
            yp[c * pb + i] = y[off:off + S]
            off += S
        for i in range(sbn):
            ys[c * sbn + i] = y[off:off + DS]
            off += DS
    return (yp, ys)
```

```python
import contextlib
import numpy as np
import concourse.bass as bass
import concourse.mybir as mybir
from concourse.bass_utils import run_bass_kernel_spmd

F32 = mybir.dt.float32
BF16 = mybir.dt.bfloat16
AF = mybir.ActivationFunctionType
ALU = mybir.AluOpType

SAME_ENG_SYNC = True
LIST_SCHED = True
PE_MODE_GROUP = True
PE_MODE_WINDOW = 12
PE_MODE_SLACK = 4000.0
SCHED_DEBUG = False
NDMASEM = 16
NCORES = 8
D = 1024
RW = 512
EVEN_SHIFT = 1664
EVEN_COLS = 4224
ODD_COLS = 3088
DECAY_C = 0.6065306597126334
GN_EPS = 64e-5
RMS_EPS = 1e-6


class Op:
    __slots__ = ("eng", "fn", "reads", "writes", "is_dma", "waits", "signal", "sig", "barrier", "deps", "cost",
                 "lat", "idx", "mode")


def _numel(ap):
    n = 1
    for d in ap.shape[1:]:
        n *= int(d)
    return n


class Prog:
    ENGS = ("pe", "act", "dve", "pool", "sp")

    def __init__(self, nc, stack):
        self.nc = nc
        self.ops = []
        self.sems = {e: stack.enter_context(nc.semaphore("s_" + e)) for e in ("pe", "act", "dve", "pool")}
        self.dsems = {q: [stack.enter_context(nc.semaphore("d_%s%d" % (q, i))) for i in range(NDMASEM)]
                      for q in ("sp", "pool", "act")}
        self.cnt = {e: 0 for e in ("pe", "act", "dve", "pool")}
        self.dcnt = {q: 0 for q in ("sp", "pool", "act")}
        self.last_writer = {}
        self.readers = {}
        self.seen = {e: {} for e in self.ENGS}
        self.last_op = {}
        self.recent_dma = {q: [] for q in ("sp", "pool", "act")}
        self.emitted = 0
        self.n_inst = 0

    def op(self, eng, fn, reads=(), writes=(), cost=500.0):
        o = Op()
        ex = [r for r in reads if isinstance(r, str) and (r.startswith("ps") or r.startswith("pg"))]
        if ex:
            reads = [r for r in reads if r not in ex]
            writes = list(writes) + ex
        o.eng = eng; o.fn = fn; o.reads = tuple(reads); o.writes = tuple(writes)
        o.is_dma = False; o.signal = False; o.sig = None; o.waits = []; o.barrier = False
        o.deps = []; o.cost = cost; o.lat = cost; o.mode = None
        self.ops.append(o)
        return o

    def dma(self, q, out, in_, reads=(), writes=()):
        o = self.op(q, lambda e: e.dma_start(out=out, in_=in_), reads, writes, cost=60.0)
        o.is_dma = True
        o.lat = 2200.0 + _numel(out) * 128 * 0.004
        return o

    def barrier(self):
        for e in self.ENGS:
            o = self.op(e, None)
            o.barrier = True

    def _deps(self, ops):
        for o in ops:
            if o.barrier:
                if o.eng == self.ENGS[-1]:
                    self.last_writer = {}
                    self.readers = {}
                continue
            deps = {}
            for k in o.reads:
                w = self.last_writer.get(k)
                if w is not None:
                    deps[id(w)] = (w, True)
            for k in o.writes:
                w = self.last_writer.get(k)
                if w is not None:
                    israw = isinstance(k, str) and (k.startswith("ps") or k.startswith("pg"))
                    if id(w) not in deps or israw:
                        deps[id(w)] = (w, israw or deps.get(id(w), (None, False))[1])
                for r in self.readers.get(k, ()):
                    if id(r) not in deps:
                        deps[id(r)] = (r, False)
            deps.pop(id(o), None)
            o.deps = list(deps.values())
            for k in o.reads:
                self.readers.setdefault(k, []).append(o)
            for k in o.writes:
                self.last_writer[k] = o
                self.readers[k] = []

    def _schedule(self, seg):
        import heapq
        n = len(seg)
        if n < 3 or not LIST_SCHED:
            return seg
        for i, o in enumerate(seg):
            o.idx = i
        inseg = set(id(o) for o in seg)
        succ = [[] for _ in range(n)]
        indeg = [0] * n
        for o in seg:
            for d, _ in o.deps:
                if id(d) in inseg:
                    succ[d.idx].append(o.idx)
                    indeg[o.idx] += 1
        rank = [0.0] * n
        for i in range(n - 1, -1, -1):
            m = 0.0
            for j in succ[i]:
                if rank[j] > m:
                    m = rank[j]
            rank[i] = seg[i].lat + m
        ready = [0.0] * n
        free = {e: 0.0 for e in self.ENGS}
        fut = {e: [] for e in self.ENGS}
        now = {e: [] for e in self.ENGS}
        for i in range(n):
            if indeg[i] == 0:
                heapq.heappush(fut[seg[i].eng], (0.0, i))
        out = []
        XLAT = 180.0
        pe_mode = [None]
        while len(out) < n:
            best = None
            for e in self.ENGS:
                f, nw = fut[e], now[e]
                while f and f[0][0] <= free[e]:
                    t, i = heapq.heappop(f)
                    heapq.heappush(nw, (-rank[i], i))
                if nw:
                    cand = (free[e], 0, e)
                elif f:
                    cand = (f[0][0], 1, e)
                else:
                    continue
                if best is None or cand < best:
                    best = cand
            start, kind, e = best
            if kind == 0:
                if e == "pe" and PE_MODE_GROUP:
                    nw = now[e]
                    top_rank = -nw[0][0]
                    pick = None
                    cand_list = heapq.nsmallest(PE_MODE_WINDOW, nw)
                    for (nr, ii) in cand_list:
                        if seg[ii].mode == pe_mode[0] and (top_rank + nr) <= PE_MODE_SLACK:
                            pick = (nr, ii)
                            break
                    if pick is None:
                        _, i = heapq.heappop(nw)
                    else:
                        nw.remove(pick)
                        heapq.heapify(nw)
                        i = pick[1]
                    pe_mode[0] = seg[i].mode
                else:
                    _, i = heapq.heappop(now[e])
            else:
                _, i = heapq.heappop(fut[e])
                if e == "pe":
                    pe_mode[0] = seg[i].mode
            o = seg[i]
            free[e] = start + o.cost
            fin = start + o.lat
            out.append(o)
            for j in succ[i]:
                same = (seg[j].eng == e and not o.is_dma)
                r = (start + o.cost) if same else (fin + XLAT)
                if r > ready[j]:
                    ready[j] = r
                indeg[j] -= 1
                if indeg[j] == 0:
                    heapq.heappush(fut[seg[j].eng], (ready[j], j))
        if SCHED_DEBUG and n > 500:
            import collections
            load = collections.defaultdict(float)
            for o in seg:
                load[o.eng] += o.cost
            st = {}
            fr = {e: 0.0 for e in self.ENGS}
            pred = {}
            for o in out:
                t = fr[o.eng]
                p = None
                for d, _ in o.deps:
                    if id(d) in st:
                        same = (d.eng == o.eng and not d.is_dma)
                        r = st[id(d)] + (d.cost if same else d.lat + XLAT)
                        if r > t:
                            t, p = r, d
                st[id(o)] = t
                pred[id(o)] = p
                fr[o.eng] = t + o.cost
            last = max(out, key=lambda o: st[id(o)] + o.lat)
            print("SCHED seg n=%d makespan=%.0f loads=%s" % (n, st[id(last)] + last.lat, {e: int(v) for e, v in load.items()}))
            i = max(range(n), key=lambda q: rank[q])
            print("  pure DAG critical path length: %.0f" % rank[i])
            cp_ = collections.Counter(); cpt = collections.defaultdict(float)
            while True:
                o = seg[i]
                kk = (o.eng, str(o.writes[0] if o.writes else "-")[:7], o.is_dma)
                cp_[kk] += 1; cpt[kk] += o.lat
                if not succ[i]:
                    break
                i = max(succ[i], key=lambda q: rank[q])
            print("  DAG path:", sorted(((int(cpt[kk]), v, kk) for kk, v in cp_.items()), reverse=True)[:16])
            path = collections.Counter()
            tm = collections.defaultdict(float)
            o = last
            cnt = 0
            while o is not None and cnt < 100000:
                kk = (o.eng, str(o.writes[0] if o.writes else "-")[:6])
                path[kk] += 1
                tm[kk] += o.lat
                o = pred[id(o)]
                cnt += 1
            print("  critical path ops:", sorted(((v, int(tm[kk]), kk) for kk, v in path.items()), reverse=True)[:14])
        return out

    def _sync(self, ops):
        for o in ops:
            if o.barrier:
                if o.eng == self.ENGS[0]:
                    self._bar = [lo for lo in self.last_op.values() if lo is not None]
                    for q, lst in self.recent_dma.items():
                        self._bar.extend(lst)
                deps = [(d, True) for d in self._bar]
            else:
                deps = o.deps
            if o.is_dma:
                j = self.dcnt[o.eng]
                self.dcnt[o.eng] = j + 1
                s = j % NDMASEM
                o.sig = (("d", o.eng, s), 16 * (j // NDMASEM + 1))
                o.signal = True
                if j >= NDMASEM:
                    o.waits.append((("d", o.eng, s), 16 * (j // NDMASEM)))
            for d, israw in deps:
                if d.is_dma:
                    o.waits.append(d.sig)
                elif d.eng == o.eng and not o.is_dma:
                    if o.eng == "pe" or not SAME_ENG_SYNC or (not o.barrier and not israw):
                        continue
                    d.signal = True
                    o.waits.append(("c", d))
                else:
                    d.signal = True
                    o.waits.append(("c", d))
            if not o.barrier:
                if o.is_dma:
                    lst = self.recent_dma[o.eng]
                    lst.append(o)
                    if len(lst) > NDMASEM:
                        lst.pop(0)
                else:
                    self.last_op[o.eng] = o
        for o in ops:
            if o.signal and not o.is_dma:
                self.cnt[o.eng] += 1
                o.sig = (("c", o.eng), self.cnt[o.eng])

    def _sem(self, key):
        if key[0] == "c":
            return self.sems[key[1]]
        return self.dsems[key[1]][key[2]]

    def emit(self):
        ops = self.ops[self.emitted:]
        self.emitted = len(self.ops)
        self._deps(ops)
        ordered, seg = [], []
        for o in ops:
            if o.barrier:
                if seg:
                    ordered.extend(self._schedule(seg))
                    seg = []
                ordered.append(o)
            else:
                seg.append(o)
        if seg:
            ordered.extend(self._schedule(seg))
        ops = ordered
        self._sync(ops)
        per = {e: [o for o in ops if o.eng == e] for e in self.ENGS}
        self.n_inst += len(ops)
        with self.nc.Block() as block:
            def run(engname):
                def body(eng):
                    seen = self.seen[engname]
                    for o in per[engname]:
                        for w in o.waits:
                            if w[0] == "c":
                                key, val = w[1].sig
                            else:
                                key, val = w
                            if seen.get(key, 0) >= val:
                                continue
                            seen[key] = val
                            eng.wait_ge(self._sem(key), val)
                        if o.fn is None:
                            continue
                        ins = o.fn(eng)
                        if o.signal:
                            key, val = o.sig
                            ins.then_inc(self._sem(key), 16 if o.is_dma else 1)
                return body
            block.tensor(run("pe"))
            block.scalar(run("act"))
            block.vector(run("dve"))
            block.gpsimd(run("pool"))
            block.sync(run("sp"))

    def stage_end(self):
        self.barrier()
        self.emit()


class Ring:
    uid = 0
    def __init__(self, nc, st, name, shape, dt, n, psum=False):
        alloc = nc.psum_tensor if psum else nc.sbuf_tensor
        Ring.uid += 1
        self.tiles = [st.enter_context(alloc("%s_u%d_%d" % (name, Ring.uid, i), shape, dt)) for i in range(n)]
        self.keys = ["%s%d" % (name, i) for i in range(n)]
        self.i = -1

    def next(self):
        self.i = (self.i + 1) % len(self.tiles)
        return self.tiles[self.i], self.keys[self.i]


def _cols(v):
    v = np.asarray(v, np.float32)
    return np.ascontiguousarray(v.reshape(-1, 128).T)


COLS = {}


def _col_layout():
    off = 0
    for name, n in (("norm0", 8), ("norm1", 8), ("mup", 13), ("mun", 13), ("w0_0", 4), ("w0_1", 4), ("a0_0", 4),
                    ("a0_1", 4), ("k_k", 4), ("k_a", 4), ("r_k", 4), ("gn_w", 4), ("gn_b", 4), ("gnorm", 2),
                    ("gb_0", 4), ("gb_1", 4)):
        COLS[name] = (off, n)
        off += n
    return off


NCOLS = _col_layout()


def host_consts(smax):
    c = {}
    c["ident"] = np.eye(128, dtype=np.float32)
    R = np.zeros((128, 128), np.float32)
    for m in range(128):
        h, j = divmod(m, 64)
        k = h * 64 + (j + 32) % 64
        R[k, m] = 1.0
    c["rot"] = R
    inv = 10000.0 ** (-np.arange(0, 64, 2, dtype=np.float32) / 64.0)
    ang = np.arange(smax, dtype=np.float32)[None, :] * inv[:, None]
    cos, sin = np.cos(ang), np.sin(ang)
    c["cos"] = np.ascontiguousarray(np.tile(cos, (4, 1)).astype(np.float32))
    c["sin"] = np.ascontiguousarray(np.concatenate([-sin, sin, -sin, sin], 0).astype(np.float32))
    i = np.arange(128)[:, None]
    t = np.arange(128)[None, :]
    same = (i // 64) == (t // 64)
    strict = ((i < t) & same).astype(np.float32)
    incl = ((i <= t) & same).astype(np.float32)
    c["m_f"] = np.concatenate([strict, incl, strict.T], 1)
    c["m_b"] = np.concatenate([strict.T, incl.T, strict], 1)
    c["g_f"] = np.concatenate([(i <= t).astype(np.float32), np.ones((128, 128), np.float32)], 1)
    c["g_b"] = np.concatenate([(i >= t).astype(np.float32), np.ones((128, 128), np.float32)], 1)
    kl = np.arange(128)[:, None]
    ql = np.arange(256)[None, :]
    c["a_g"] = ((kl <= ql) & (ql <= kl + 128)).astype(np.float32)
    c["a_0"] = (np.arange(128)[None, :] <= np.arange(64)[:, None] + 64).astype(np.float32)
    bd = (np.arange(128)[:, None] // 64 == np.arange(128)[None, :] // 64).astype(np.float32)
    c["bd1"] = bd
    c["ones"] = np.ones((128, 128), np.float32)
    rm = np.ones((128, 512), np.float32)
    rm[:, ::64] = 0.0
    c["rmask"] = rm
    rm2 = np.ones((128, 512), np.float32)
    rm2[:, ::128] = 0.0
    c["rmask128"] = rm2
    return c


CONST_SHAPES = lambda smax: {"ident": [128, 128], "rot": [128, 128], "cos": [128, smax], "sin": [128, smax],
                             "m_f": [128, 384], "m_b": [128, 384],
                             "g_f": [128, 256], "g_b": [128, 256], "a_g": [128, 256], "a_0": [64, 128],
                             "bd1": [128, 128], "ones": [128, 128], "rmask": [128, 512],
                             "rmask128": [128, 512]}


def host_params(inp):
    g = lambda k: np.asarray(inp[k], np.float32)
    cols = np.zeros((128, NCOLS), np.float32)

    def put(name, v):
        o, n = COLS[name]
        cols[:, o:o + n] = _cols(v)
    put("norm0", g("even_norm")[0]); put("norm1", g("odd_norm")[0])
    put("mup", g("even_mu_prev")[0]); put("mun", g("even_mu_next")[0])
    for z in range(2):
        put("w0_%d" % z, g("rwkv_w0")[0, z]); put("a0_%d" % z, g("rwkv_a0")[0, z])
        put("gb_%d" % z, g("gla_gate_bias")[0, z])
    for nm, k in (("k_k", "rwkv_k_k"), ("k_a", "rwkv_k_a"), ("r_k", "rwkv_r_k"), ("gn_w", "rwkv_gn_w"),
                  ("gn_b", "rwkv_gn_b")):
        put(nm, g(k)[0])
    put("gnorm", g("gla_norm")[0])
    lora = np.zeros((128, 2, 512), np.float32)
    lora[0:64] = np.transpose(g("rwkv_w_up")[0], (1, 0, 2))
    lora[64:128] = np.transpose(g("rwkv_a_up")[0], (1, 0, 2))
    p = {"cols": cols, "lora": lora,
         "w_in0": g("even_w_in")[0], "w_out0": g("even_w_out")[0],
         "w_in1": g("odd_w_in")[0], "w_out1": g("odd_w_out")[0],
         "gate_up": np.ascontiguousarray(np.transpose(g("gla_gate_up")[0], (1, 0, 2))),
         "fnorm": np.ascontiguousarray(np.broadcast_to(g("final_norm")[None, :], (128, D)))}
    return p


PARAM_SHAPES = {"cols": [128, NCOLS], "lora": [128, 2, 512], "w_in0": [D, EVEN_COLS], "w_out0": [D, D],
                "w_in1": [D, ODD_COLS], "w_out1": [D, D], "gate_up": [16, 2, 512], "fnorm": [128, D]}


class K:
    pass


def mm(P, out, lhsT, rhs, start, stop, reads, writes):
    n = _numel(out)
    c = 60.0 + max(64, n) / 1.2 * (4.0 if lhsT.dtype == F32 else 1.0)
    o = P.op("pe", lambda e: e.matmul(out, lhsT=lhsT, rhs=rhs, start=start, stop=stop), reads, writes, cost=c)
    o.lat = c + 100.0
    rnd = lambda v: 32 if v <= 32 else (64 if v <= 64 else 128)
    o.mode = (rnd(int(lhsT.shape[0])), rnd(int(lhsT.shape[1])))


def act(P, out, in_, func, reads, writes, **kw):
    P.op("act", lambda e: e.activation(out=out, in_=in_, func=func, **kw), reads, writes,
         cost=230.0 + _numel(out) / 1.2)


def _vcost(eng, out):
    n = _numel(out)
    return (80.0 + n * 1.05) if eng == "dve" else (150.0 + n * 2.3)


def tt(P, eng, out, in0, in1, op, reads, writes):
    P.op(eng, lambda e: e.tensor_tensor(out=out, in0=in0, in1=in1, op=op), reads, writes, cost=_vcost(eng, out))


def ts(P, eng, out, in0, s1, s2, op0, op1, reads, writes):
    if op1 is None:
        P.op(eng, lambda e: e.tensor_scalar(out=out, in0=in0, scalar1=s1, scalar2=None, op0=op0), reads, writes,
             cost=_vcost(eng, out))
    else:
        P.op(eng, lambda e: e.tensor_scalar(out=out, in0=in0, scalar1=s1, scalar2=s2, op0=op0, op1=op1), reads, writes,
             cost=_vcost(eng, out))


def stt(P, out, in0, scalar, in1, op0, op1, reads, writes):
    P.op("dve", lambda e: e.scalar_tensor_tensor(out=out, in0=in0, scalar=scalar, in1=in1, op0=op0, op1=op1),
         reads, writes, cost=_vcost("dve", out))


def cp(P, eng, out, in_, reads, writes):
    if eng == "act":
        P.op("act", lambda e: e.activation(out=out, in_=in_, func=AF.Copy), reads, writes,
             cost=230.0 + _numel(out) / 1.2)
    else:
        P.op(eng, lambda e: e.tensor_copy(out=out, in_=in_), reads, writes, cost=_vcost(eng, out))


def rsqrt(P, out, in_, scale, bias, reads, writes):
    act(P, out, in_, AF.Ln, reads, writes, scale=scale, bias=bias)
    act(P, out, out, AF.Exp, writes, writes, scale=-0.5)


def col(k, name, j=0, n=1, rows=slice(0, 128)):
    o, _ = COLS[name]
    return k.cols[rows, o + j:o + j + n]


def build(seqs, n_stage=99, debug=()):
    nc = bass.Bass("TRN2", target_bir_lowering=False)
    k = K()
    k.nc = nc
    k.seqs = list(seqs)
    T = sum(seqs)
    k.T = T
    smax = max(seqs)
    k.seq_off = [sum(seqs[:i]) for i in range(len(seqs))]
    ein = lambda n, sh, dt=F32: nc.dram_tensor(n, sh, dt, kind="ExternalInput").ap()
    scr = lambda n, sh, dt=BF16: nc.dram_tensor(n, sh, dt, kind=("ExternalOutput" if n in debug else "Internal")).ap()
    k.x = ein("x", [T, D])
    k.y = nc.dram_tensor("y", [T, D], F32, kind="ExternalOutput").ap()
    k.c = {n: ein("c_" + n, sh) for n, sh in CONST_SHAPES(smax).items()}
    k.p = {n: ein("p_" + n, sh) for n, sh in PARAM_SHAPES.items()}
    k.RW_T = scr("RW_T", [EVEN_SHIFT, T])
    k.GA_T = scr("GA_T", [512, T])
    k.QK_T = scr("QK_T", [1024, T])
    k.VB = scr("VB", [T, 512])
    k.GB_T = scr("GB_T", [512, T])
    k.YF_T = scr("YF_T", [512, T], F32)
    k.Y0_T = scr("Y0_T", [1024, T])
    k.X1 = scr("X1", [T, D], F32)
    k.Q1_T = scr("Q1_T", [1024, T])
    k.GL_T = scr("GL_T", [16, T], F32)
    k.V1 = scr("V1", [T, D])
    k.G1_T = scr("G1_T", [D, T])
    k.OF_T = scr("OF_T", [D, T], F32)
    k.Y1_T = scr("Y1_T", [D, T])
    k.DEN = scr("DEN", [8, 512], F32)

    with contextlib.ExitStack() as gst:
        P = Prog(nc, gst)
        k.P = P
        sb = lambda n, sh, dt: gst.enter_context(nc.sbuf_tensor(n, sh, dt))
        k.cols = sb("cols", [128, NCOLS + 32], F32)
        k.cb = {}
        k.cf = {}
        for n in ("ident", "bd1", "ones", "rmask", "rmask128"):
            k.cf[n] = sb("cf_" + n, CONST_SHAPES(smax)[n], F32)
        BN = ("ident", "rot", "m_f", "m_b", "g_f", "g_b", "a_g", "a_0", "bd1")
        for n in BN:
            k.cb[n] = sb("cb_" + n, CONST_SHAPES(smax)[n], BF16)
        k.lora = sb("lora", [128, 2, 512], BF16)
        with contextlib.ExitStack() as st:
            tmp = st.enter_context(nc.sbuf_tensor("ctmp", [128, 2048], F32))
            P.dma("sp", k.cols[:, 0:NCOLS], k.p["cols"][:, :], writes=["cols"])
            for n in ("ident", "bd1", "ones", "rmask", "rmask128"):
                P.dma("sp", k.cf[n][:], k.c[n][:, :], writes=["cf_" + n])
            off = 0
            for n in BN:
                sh = CONST_SHAPES(smax)[n]
                P.dma("sp", tmp[0:sh[0], off:off + sh[1]], k.c[n][:, :], writes=["ctmp"])
                cp(P, "dve", k.cb[n][:], tmp[0:sh[0], off:off + sh[1]], ["ctmp"], ["cb_" + n])
                off += sh[1]
            P.stage_end()
            P.dma("sp", tmp[:, 0:1024], k.p["lora"].rearrange("p z c -> p (z c)"), writes=["ctmp2"])
            cp(P, "dve", k.lora[:].rearrange("p z c -> p (z c)"), tmp[:, 0:1024], ["ctmp2"], ["lora"])
            o_mup, o_mun, o_ka = COLS["mup"][0], COLS["mun"][0], COLS["k_a"][0]
            k.o_c0, k.o_omk = NCOLS, NCOLS + 13
            tt(P, "dve", k.cols[:, k.o_c0:k.o_c0 + 13], k.cols[:, o_mup:o_mup + 13], k.cols[:, o_mun:o_mun + 13],
               ALU.add, ["cols"], ["cols"])
            ts(P, "dve", k.cols[:, k.o_c0:k.o_c0 + 13], k.cols[:, k.o_c0:k.o_c0 + 13], -1.0, 1.0, ALU.mult, ALU.add,
               ["cols"], ["cols"])
            ts(P, "dve", k.cols[:, k.o_omk:k.o_omk + 4], k.cols[:, o_ka:o_ka + 4], -1.0, 1.0, ALU.mult, ALU.add,
               ["cols"], ["cols"])
            P.stage_end()
        stages = [stage1, stage2_attn, stage3_rwkv, stage4_out0, stage5_in1, stage6_gla, stage7_out1]
        for i, s in enumerate(stages):
            if i < n_stage:
                s(k)
        P.stage_end()
    k.n_inst = P.n_inst
    return nc, k


def load_weights(k, st, specs):
    P, nc = k.P, k.nc
    ws = [st.enter_context(nc.sbuf_tensor(name, [128, rows, ncol], BF16)) for name, dram, rows, ncol, key in specs]
    with contextlib.ExitStack() as inner:
        ring = Ring(nc, inner, "wstage", [128, 1056], F32, 3)
        for w, (name, dram, rows, ncol, key) in zip(ws, specs):
            v = dram.rearrange("(kc p) c -> p kc c", p=128)
            for kc in range(rows):
                for c0 in range(0, ncol, 1056):
                    c1 = min(ncol, c0 + 1056)
                    t, tk = ring.next()
                    P.dma("sp", t[:, 0:c1 - c0], v[:, kc, c0:c1], writes=[tk])
                    cp(P, "dve" if (c0 // 1056) % 2 else "pool", w[:, kc, c0:c1], t[:, 0:c1 - c0], [tk], [])
        P.stage_end()
    return ws


def rms_transpose(k, st, rings, src_ap, normcol, t0):
    P = k.P
    xt, kx = rings["x"].next()
    P.dma("sp", xt[:], src_ap[t0:t0 + 512, :].rearrange("(j p) d -> p j d", p=128), writes=[kx])
    return xt, kx


def rms_transpose_compute(k, rings, xt, kx, normname):
    P = k.P
    ss, kss = rings["ss"].next()
    junk, kj = rings["junk"].next()
    for j in range(4):
        act(P, junk[:], xt[:, j, :], AF.Square, [kx], [kss, kj], accum_out=ss[:, j:j + 1])
    rsqrt(P, ss[:, 0:4], ss[:, 0:4], 1.0 / D, RMS_EPS, [kss], [kss])
    xn, kxn = rings["xn"].next()
    for j in range(4):
        if j % 2 == 0:
            ts(P, "dve", xn[:, j, :], xt[:, j, :], ss[:, j:j + 1], None, ALU.mult, None, [kx, kss], [kxn])
        else:
            act(P, xn[:, j, :], xt[:, j, :], AF.Copy, [kx, kss], [kxn], scale=ss[:, j:j + 1])
    xnT, kT = rings["xnT"].next()
    for c in range(8):
        ps, kp = rings["pst"].next()
        for j in range(4):
            mm(P, ps[:, 128 * j:128 * j + 128], xn[:, j, 128 * c:128 * c + 128], k.cb["ident"][:], True, True,
               [kxn, "cb_ident"], [kp])
        if c % 2 == 0:
            act(P, xnT[:, c, :], ps[:], AF.Copy, [kp, "cols"], [kT], scale=col(k, normname, c))
        else:
            ts(P, "dve", xnT[:, c, :], ps[:], col(k, normname, c), None, ALU.mult, None, [kp, "cols"], [kT])
    return xnT, kT


def in_rings(k, st):
    nc = k.nc
    return {"x": Ring(nc, st, "xt", [128, 4, D], F32, 2), "ss": Ring(nc, st, "ss", [128, 4], F32, 2),
            "junk": Ring(nc, st, "junk", [128, D], BF16, 1), "xn": Ring(nc, st, "xn", [128, 4, D], BF16, 2),
            "xnT": Ring(nc, st, "xnT", [128, 8, 512], BF16, 2),
            "pst": Ring(nc, st, "pst", [128, 512], F32, 2, psum=True)}


def seq_pos(k, t0):
    for off, S in zip(k.seq_off, k.seqs):
        if off <= t0 < off + S:
            return t0 - off
    raise ValueError


def stage1(k):
    P, nc, T = k.P, k.nc, k.T
    with contextlib.ExitStack() as st:
        W, = load_weights(k, st, [("w0", k.p["w_in0"], 8, EVEN_COLS, "W")])
        R = in_rings(k, st)
        psr = Ring(nc, st, "ps1", [128, 512], F32, 4, psum=True)
        psrot = Ring(nc, st, "psrot", [128, 512], F32, 2, psum=True)
        ost = Ring(nc, st, "ost", [128, 512], BF16, 6)
        qraw = Ring(nc, st, "qraw", [128, 512], BF16, 2)
        ra = Ring(nc, st, "ra", [128, 512], F32, 2)
        rb = Ring(nc, st, "rb", [128, 512], F32, 2)
        cs = Ring(nc, st, "cs", [128, 2, 512], F32, 2)
        ntile = T // 512
        nxt = rms_transpose(k, st, R, k.x, "norm0", 0)
        for ti in range(ntile):
            t0 = ti * 512
            xt, kx = nxt
            if ti + 1 < ntile:
                nxt = rms_transpose(k, st, R, k.x, "norm0", t0 + 512)
            pos = seq_pos(k, t0)
            cst, kcs = cs.next()
            P.dma("sp", cst[:, 0, :], k.c["cos"][:, pos:pos + 512], writes=[kcs])
            P.dma("sp", cst[:, 1, :], k.c["sin"][:, pos:pos + 512], writes=[kcs])
            xnT, kT = rms_transpose_compute(k, R, xt, kx, "norm0")
            for oc in range(33):
                if 25 <= oc < 29:
                    j = oc - 25
                    ps, kp = psr.next()
                    for kc in range(8):
                        mm(P, ps[:], xnT[:, kc, 128 * j:128 * j + 128], W[:, kc, 3200:3712], kc == 0, kc == 7,
                           [kT, "W"], [kp])
                    o, ko = ost.next()
                    cp(P, "act" if j % 2 else "dve", o[:], ps[:], [kp], [ko])
                    P.dma("sp", k.VB[t0 + 128 * j:t0 + 128 * j + 128, :], o[:], reads=[ko], writes=[("VB", ti)])
                    continue
                ps, kp = psr.next()
                for kc in range(8):
                    mm(P, ps[:], W[:, kc, 128 * oc:128 * oc + 128], xnT[:, kc, :], kc == 0, kc == 7, [kT, "W"], [kp])
                o, ko = ost.next()
                if oc < 13:
                    cp(P, "act" if oc % 2 else "dve", o[:], ps[:], [kp], [ko])
                    P.dma("sp", k.RW_T[128 * oc:128 * oc + 128, t0:t0 + 512], o[:], reads=[ko], writes=[("RW_T", ti)])
                elif oc < 17 or oc >= 29:
                    act(P, o[:], ps[:], AF.Silu, [kp], [ko])
                    dst = k.GA_T if oc < 17 else k.GB_T
                    r0 = 128 * (oc - 13) if oc < 17 else 128 * (oc - 29)
                    P.dma("sp", dst[r0:r0 + 128, t0:t0 + 512], o[:], reads=[ko],
                          writes=[("GA_T" if oc < 17 else "GB_T", ti)])
                else:
                    qr, kq = qraw.next()
                    cp(P, "act", qr[:], ps[:], [kp], [kq])
                    pr, kpr = psrot.next()
                    mm(P, pr[:], k.cb["rot"][:], qr[:], True, True, [kq, "cb_rot"], [kpr])
                    a, ka = ra.next()
                    tt(P, "pool", a[:], qr[:], cst[:, 0, :], ALU.mult, [kq, kcs], [ka])
                    b, kb = rb.next()
                    tt(P, "dve", b[:], pr[:], cst[:, 1, :], ALU.mult, [kpr, kcs], [kb])
                    tt(P, "pool", o[:], a[:], b[:], ALU.add, [ka, kb], [ko])
                    r0 = 128 * (oc - 17)
                    P.dma("sp", k.QK_T[r0:r0 + 128, t0:t0 + 512], o[:], reads=[ko], writes=[("QK_T", ti)])
        P.stage_end()


class SRing:
    def __init__(self, items):
        self.items = items
        self.i = -1

    def next(self):
        self.i = (self.i + 1) % len(self.items)
        return self.items[self.i]


GLA_CUT = 0
PATTERNS = ((128, 1), (512, 4), (2048, 16))


def stage2_attn(k):
    P, nc = k.P, k.nc
    smax = max(k.seqs)
    with contextlib.ExitStack() as st:
        qsr = Ring(nc, st, "qs", [128, smax], BF16, 1)
        ksr = Ring(nc, st, "ks", [128, smax], BF16, 1)
        qdr = {d: Ring(nc, st, "qd%d" % d, [128, smax], BF16, 1) for d in (4, 16)}
        kdr = {d: Ring(nc, st, "kd%d" % d, [128, smax], BF16, 1) for d in (4, 16)}
        acc = [st.enter_context(nc.sbuf_tensor("acc%d" % h, [65, smax], F32)) for h in range(2)]
        vtr = Ring(nc, st, "vt", [128, 2, 65], BF16, 6)
        for t, key in zip(vtr.tiles, vtr.keys):
            P.op("pool", (lambda t: lambda e: e.memset(t[:], 1.0))(t), writes=[key])
        pst = [st.enter_context(nc.psum_tensor("psS%d" % i, [128, 512], F32)) for i in range(2)]
        psr = SRing([(pst[i], "psS%d" % i) for i in range(2)])
        po = [[(st.enter_context(nc.psum_tensor("psO%d_%d" % (h, b), [65, 128], F32)), "psO%d_%d" % (h, b))
               for b in range(2)] for h in range(2)]
        pdr = Ring(nc, st, "psD", [64, 512], F32, 2, psum=True)
        ptr_ = Ring(nc, st, "pt", [128, 256], BF16, 4)
        pmr = Ring(nc, st, "pm", [128, 256], BF16, 4)
        rcr = Ring(nc, st, "rc", [64, 512], F32, 2)
        tmr = Ring(nc, st, "tm", [64, 512], F32, 2)
        gr = Ring(nc, st, "gb", [64, 512], BF16, 2)
        outr = Ring(nc, st, "ao", [64, 512], BF16, 2)
        cnt = 0
        dslot = [0]
        for s0, S in zip(k.seq_off, k.seqs):
            nchunk = S // 128
            for hp in range(4):
                qs, kq = qsr.next()
                ks_, kk_ = ksr.next()
                P.dma("sp", qs[:, 0:S], k.QK_T[128 * hp:128 * hp + 128, s0:s0 + S], writes=[kq])
                P.dma("sp", ks_[:, 0:S], k.QK_T[512 + 128 * hp:512 + 128 * hp + 128, s0:s0 + S], writes=[kk_])
                qv, kv_ = {1: (qs, kq)}, {1: (ks_, kk_)}
                for d in (4, 16):
                    qd, kqd = qdr[d].next()
                    kd, kkd = kdr[d].next()
                    cp(P, "act", qd[:, 0:S].rearrange("p (r i) -> p r i", r=d),
                       qs[:, 0:S].rearrange("p (i r) -> p r i", r=d), [kq], [kqd])
                    cp(P, "dve", kd[:, 0:S].rearrange("p (r i) -> p r i", r=d),
                       ks_[:, 0:S].rearrange("p (i r) -> p r i", r=d), [kk_], [kkd])
                    qv[d] = (qd, kqd)
                    kv_[d] = (kd, kkd)
                for h in range(2):
                    P.op("pool", (lambda a: lambda e: e.memset(a, 0.0))(acc[h][:, 0:S]),
                         writes=[("acc", h, c) for c in range(nchunk)])
                for (win, d) in PATTERNS:
                    sub = S // d
                    nb = sub // 128
                    assert sub % 128 == 0 and nb >= 1
                    qt, kqt = qv[d]
                    kt, kkt = kv_[d]
                    for r in range(d):
                        for kb in range(nb + 1):
                            lo = max(0, 128 * kb - 64)
                            hi = min(sub, 128 * kb + 64)
                            nk = hi - lo
                            v, kv = vtr.next()
                            rows = k.VB[s0 + r + d * lo:s0 + r + d * (hi - 1) + 1:d, 128 * hp:128 * hp + 128]
                            P.dma("sp", v[0:nk, :, 0:64], rows.rearrange("p (h e) -> p h e", h=2), writes=[kv])
                            qb_lo = max(0, kb - 1)
                            qb_hi = min(nb - 1, kb)
                            nq = 128 * (qb_hi - qb_lo + 1)
                            if kb == 0:
                                mask = k.cb["a_0"][0:64, 0:128]
                            elif kb == nb:
                                mask = k.cb["a_g"][0:64, 0:128]
                            else:
                                mask = k.cb["a_g"][:, 0:256]
                            for h in range(2):
                                ph = slice(64 * h, 64 * h + 64)
                                keys = kt[ph, r * sub + lo:r * sub + hi]
                                qry = qt[ph, r * sub + 128 * qb_lo:r * sub + 128 * qb_lo + nq]
                                ps, kps = psr.next()
                                mm(P, ps[0:nk, 0:nq], keys, qry, True, True, [kqt, kkt], [kps])
                                e, ke = ptr_.next()
                                act(P, e[0:nk, 0:nq], ps[0:nk, 0:nq], AF.Exp, [kps], [ke], scale=0.125)
                                m, km = pmr.next()
                                cnt += 1
                                tt(P, "dve" if cnt % 3 else "pool", m[0:nk, 0:nq], e[0:nk, 0:nq], mask, ALU.mult,
                                   [ke, "cb_a_g", "cb_a_0"], [km])
                                for qb in range(qb_lo, qb_hi + 1):
                                    first = (kb == qb)
                                    pv, kpo = po[h][qb % 2]
                                    c0 = 128 * (qb - qb_lo)
                                    mm(P, pv[0:65, :], v[0:nk, h, 0:65], m[0:nk, c0:c0 + 128], first, not first,
                                       [kv, km], [kpo])
                                    if not first:
                                        a0 = r + d * 128 * qb
                                        pos = acc[h][0:65, a0:a0 + d * 127 + 1:d]
                                        ck = [("acc", h, c) for c in range(d * qb, d * qb + d)]
                                        tt(P, "dve", pos, pos, pv[0:65, :], ALU.add, [kpo] + ck, ck)
                for h in range(2):
                    for c0 in range(0, S, 512):
                        ck = [("acc", h, c) for c in range(c0 // 128, c0 // 128 + 4)]
                        pd, kpd = pdr.next()
                        mm(P, pd[0:64, :], k.cf["ones"][64:65, 0:64], acc[h][64:65, c0:c0 + 512], True, True,
                           ["cf_ones"] + ck, [kpd])
                        rc, krc = rcr.next()
                        act(P, rc[:], pd[0:64, :], AF.Ln, [kpd], [krc])
                        act(P, rc[:], rc[:], AF.Exp, [krc], [krc], scale=-1.0)
                        g, kg = gr.next()
                        r0 = 128 * hp + 64 * h
                        P.dma("sp", g[:], k.GB_T[r0:r0 + 64, s0 + c0:s0 + c0 + 512], writes=[kg])
                        tm, ktm = tmr.next()
                        tt(P, "pool", tm[:], acc[h][0:64, c0:c0 + 512], rc[:], ALU.mult, [krc] + ck, [ktm])
                        o, ko = outr.next()
                        tt(P, "dve", o[:], tm[:], g[:], ALU.mult, [ktm, kg], [ko])
                        P.dma("sp", k.Y0_T[512 + r0:512 + r0 + 64, s0 + c0:s0 + c0 + 512], o[:], reads=[ko],
                              writes=[("Y0_T", r0, c0)])
        P.stage_end()


def stage3_rwkv(k):
    P, nc = k.P, k.nc
    with contextlib.ExitStack() as st:
        A = lambda n, sh, dt=F32: st.enter_context(nc.sbuf_tensor(n, sh, dt))
        RG = lambda n, sh, dt, c: Ring(nc, st, n, sh, dt, c)
        raw = {n: RG("raw_" + n, [128, 514], BF16, 2) for n in ("r", "k", "v", "la")}
        f = {n: RG("f_" + n, [128, 512], F32, 1) for n in
             ("t1", "t2", "rs", "ks", "las", "sg", "lr", "kkr", "sq", "rn", "kk", "tka", "kd", "beta",
              "ci", "cf", "ce", "E1", "E2", "E3", "E4", "lro", "e1", "e2", "e3", "e4", "e6")}
        f["vs"] = RG("f_vs", [128, 512], F32, 3)
        prb = RG("prb", [128, 512], BF16, 3)
        twal = RG("twal", [128, 512], BF16, 1)
        pc = RG("pc", [128, 8], F32, 3)
        AR = RG("AR", [128, 4, 2, 128], BF16, 3)
        ob = {n: RG("o_" + n, [128, 512], BF16, 3) for n in ("Bt", "Kt", "Be", "Ke", "vb")}
        TB = RG("TB", [128, 512], BF16, 8)
        G13r = RG("G13r", [128, 384], BF16, 4)
        G2r = RG("G2r", [128, 256], BF16, 4)
        G13 = RG("G13", [128, 384], BF16, 16)
        G2 = RG("G2", [128, 256], BF16, 16)
        Z0 = RG("Z0", [128, 128], BF16, 8)
        ZP = RG("ZP", [128, 384], BF16, 16)
        Z6 = RG("Z6", [128, 128], BF16, 16)
        UT = RG("UT", [128, 2, 128], BF16, 3)
        for t_, k_ in zip(UT.tiles, UT.keys):
            P.op("pool", (lambda a: lambda e: e.memset(a, 0.0))(t_[:]), writes=[k_])
        AW = RG("AW", [128, 512], BF16, 2)
        YT = RG("YT", [128, 512], F32, 2)
        yf = RG("yf", [128, 512], F32, 2)
        gat = RG("gat", [128, 512], BF16, 2)
        yo = RG("yo", [128, 512], BF16, 2)
        ST2 = A("ST2", [128, 128])
        STb2 = A("STb2", [128, 128], BF16)
        pbank = [st.enter_context(nc.psum_tensor("pg%d" % i, [128, 512], F32)) for i in range(7)]
        psg = SRing([(pbank[i], "pg%d" % i) for i in range(4)])
        psc = SRing([(pbank[i], "pg%d" % i) for i in range(4, 6)])
        pse = SRing([(pbank[i], "pg%d" % i) for i in range(6, 7)])
        psy = st.enter_context(nc.psum_tensor("psy", [128, 512], F32))
        idb, bd1 = k.cb["ident"], k.cf["bd1"]
        bd1b = k.cb["bd1"]
        sqb = RG("sqb", [128, 512], BF16, 2)
        k.o_omk2 = k.o_omk + 4
        ts(P, "dve", k.cols[:, k.o_omk2:k.o_omk2 + 4], k.cols[:, k.o_omk:k.o_omk + 4], 2.0, None, ALU.mult, None,
           ["cols"], ["cols"])
        ev = [0]
        lvl = [0]

        def evac_eng():
            ev[0] += 1
            return "act" if ev[0] % 2 else "dve"

        def cpx(out, in_, reads, writes):
            cp(P, evac_eng(), out, in_, reads, writes)
        v3 = lambda a: a.rearrange("p (c j) -> p c j", j=64)
        v4 = lambda a: a.rearrange("p (b t) -> p b t", t=128)

        def elem(c):
            s0, S, hp, z, ti, ntile = c["item"]
            t0 = s0 + 512 * ti
            c["t0"] = t0
            lo_edge, hi_edge = (ti == 0), (ti == ntile - 1)
            rt = {}
            for n, r0 in (("r", 128 * hp), ("k", 512 + 128 * hp), ("v", 1024 + 128 * hp), ("la", 1536)):
                t, key = raw[n].next()
                if lo_edge:
                    P.op("pool", (lambda a: lambda e: e.memset(a, 0.0))(t[:, 0:1]), writes=[key])
                if hi_edge:
                    P.op("pool", (lambda a: lambda e: e.memset(a, 0.0))(t[:, 513:514]), writes=[key])
                c0 = 1 if lo_edge else 0
                c1 = 513 if hi_edge else 514
                P.dma("sp", t[:, c0:c1], k.RW_T[r0:r0 + 128, t0 - 1 + c0:t0 - 1 + c1], writes=[key])
                rt[n] = (t, key)
            yield
            sh = {}
            for n, mc, dst in (("r", hp, "rs"), ("k", 4 + hp, "ks"), ("v", 8 + hp, "vs"), ("la", 12, "las")):
                t, key = rt[n]
                t1, k1 = f["t1"].next()
                t2, k2 = f["t2"].next()
                o, ko = f[dst].next()
                act(P, t1[:], t[:, 0:512], AF.Copy, [key, "cols"], [k1], scale=col(k, "mup", mc))
                stt(P, t2[:], t[:, 2:514], col(k, "mun", mc), t1[:], ALU.mult, ALU.add, [key, k1, "cols"], [k2])
                stt(P, o[:], t[:, 1:513], k.cols[:, k.o_c0 + mc:k.o_c0 + mc + 1], t2[:], ALU.mult, ALU.add,
                    [key, k2, "cols"], [ko])
                sh[dst] = (o, ko)
                yield
            rs, krs = sh["rs"]; ks_, kks = sh["ks"]; vs, kvs = sh["vs"]; las, klas = sh["las"]
            c["vs"] = (vs, kvs)
            tw, ktw = twal.next()
            act(P, tw[0:64, :], las[0:64, :], AF.Tanh, [klas], [ktw])
            cp(P, "dve", tw[64:128, :], las[64:128, :], [klas], [ktw])
            hc = slice(128 * hp, 128 * hp + 128)

            def lora_sig(zz, wsel, dst, kdst, bias_name):
                rows = slice(0, 64) if wsel == "w" else slice(64, 128)
                pw, kpw = pse.next()
                mm(P, pw[:], k.lora[rows, zz, hc], tw[rows, :], True, True, [ktw, "lora"], [kpw])
                act(P, dst[:], pw[:], AF.Sigmoid, [kpw, "cols"], [kdst], bias=col(k, bias_name, hp))
            sg, ksg = f["sg"].next()
            lr, klr = f["lr"].next()
            lora_sig(z, "w", sg, ksg, "w0_%d" % z)
            lora_sig(z, "a", lr, klr, "a0_%d" % z)
            yield
            sq, ksq = sqb.next()
            act(P, sq[:], ks_[:], AF.Square, [kks, "cols"], [ksq], scale=col(k, "k_k", hp))
            rn, krn = f["rn"].next()
            pn, kpn = pse.next()
            mm(P, pn[:], bd1b[:], sq[:], True, True, [ksq, "cb_bd1"], [kpn])
            act(P, rn[:], pn[:], AF.Ln, [kpn], [krn], bias=1e-18)
            act(P, rn[:], rn[:], AF.Exp, [krn], [krn], scale=-0.5)
            kk, kkk = f["kk"].next()
            stt(P, kk[:], ks_[:], col(k, "k_k", hp), rn[:], ALU.mult, ALU.mult, [kks, krn, "cols"], [kkk])
            yield
            tka, ktka = f["tka"].next()
            ts(P, "dve", tka[:], lr[:], col(k, "k_a", hp), k.cols[:, k.o_omk + hp:k.o_omk + hp + 1],
               ALU.mult, ALU.add, [klr, "cols"], [ktka])
            kd, kkd = f["kd"].next()
            tt(P, "pool", kd[:], ks_[:], tka[:], ALU.mult, [kks, ktka], [kkd])
            beta, kbeta = f["beta"].next()
            tt(P, "pool", beta[:], kk[:], lr[:], ALU.mult, [kkk, klr], [kbeta])
            cf_, kcf = f["cf"].next()
            P.op("dve", (lambda o, a, b: lambda e: e.tensor_tensor_scan(
                out=o, data0=a, data1=b, initial=0.0, op0=ALU.mult, op1=ALU.add))(
                cf_[:], k.cf["rmask"][:], sg[:]), [ksg, "cf_rmask"], [kcf])
            tot = cf_[:, 63:512:64]
            totb = tot.unsqueeze(2).to_broadcast([128, 8, 64])
            if z == 0:
                ci, kci = cf_, kcf
            else:
                ci, kci = f["ci"].next()
                tt(P, "pool", ci[:], sg[:], cf_[:], ALU.subtract, [ksg, kcf], [kci])
                tt(P, "dve", v3(ci[:]), v3(ci[:]), totb, ALU.add, [kci, kcf], [kci])
            ce, kce = f["ce"].next()
            tt(P, "pool", ce[:], ci[:], sg[:], ALU.subtract, [kci, ksg], [kce])
            yield
            E1, kE1 = f["E1"].next(); E2, kE2 = f["E2"].next(); E3, kE3 = f["E3"].next(); E4, kE4 = f["E4"].next()
            act(P, E1[:], ce[:], AF.Exp, [kce], [kE1], scale=-DECAY_C)
            act(P, E2[:], ci[:], AF.Exp, [kci], [kE2], scale=DECAY_C)
            act(P, E3[:], ci[:], AF.Exp, [kci], [kE3], scale=-DECAY_C)
            pct, kpc = pc.next()
            act(P, pct[:], tot, AF.Exp, [kcf], [kpc], scale=-DECAY_C)
            c["pc"] = (pct, kpc)
            pcb = pct[:].unsqueeze(2).to_broadcast([128, 8, 64])
            tt(P, "dve", v3(E4[:]), v3(E2[:]), pcb, ALU.mult, [kE2, kpc], [kE4])
            yield
            art, kar = AR.next()
            stt(P, art[:, :, 0, :], v4(kk[:]), -1.0, v4(E1[:]), ALU.mult, ALU.mult, [kkk, kE1], [kar])
            tt(P, "pool", art[:, :, 1, :], v4(rs[:]), v4(E3[:]), ALU.mult, [krs, kE3], [kar])
            Bt, kBt = ob["Bt"].next(); Kt, kKt = ob["Kt"].next()
            Be, kBe = ob["Be"].next(); Ke, kKe = ob["Ke"].next(); vb, kvb = ob["vb"].next()
            tt(P, "pool", Bt[:], beta[:], E2[:], ALU.mult, [kbeta, kE2], [kBt])
            tt(P, "pool", Kt[:], kd[:], E2[:], ALU.mult, [kkd, kE2], [kKt])
            yield
            tt(P, "pool", Be[:], beta[:], E4[:], ALU.mult, [kbeta, kE4], [kBe])
            tt(P, "pool", Ke[:], kd[:], E4[:], ALU.mult, [kkd, kE4], [kKe])
            cp(P, "act", vb[:], vs[:], [kvs], [kvb])
            c.update(art=(art, kar), Bt=(Bt, kBt), Kt=(Kt, kKt), Be=(Be, kBe), Ke=(Ke, kKe), vb=(vb, kvb))
            if z == 1:
                yield
                lro, klro = f["lro"].next()
                lora_sig(0, "a", lro, klro, "a0_0")
                tt(P, "pool", lro[:], lro[:], lr[:], ALU.add, [klro, klr], [klro])
                ts(P, "dve", lro[:], lro[:], col(k, "k_a", hp), k.cols[:, k.o_omk2 + hp:k.o_omk2 + hp + 1],
                   ALU.mult, ALU.add, [klro, "cols"], [klro])
                tt(P, "pool", lro[:], lro[:], ks_[:], ALU.mult, [klro, kks], [klro])
                pr_, kpr_ = prb.next()
                stt(P, pr_[:], rs[:], col(k, "r_k", hp), lro[:], ALU.mult, ALU.mult, [krs, klro, "cols"], [kpr_])
                c["pr"] = (pr_, kpr_)

        def wave(c):
            s0, S, hp, z, ti, ntile = c["item"]
            mask = k.cb["m_f" if z == 0 else "m_b"]
            mkeys = ["cb_m_f", "cb_m_b"]
            art, kar = c["art"]; Bt, kBt = c["Bt"]; Kt, kKt = c["Kt"]
            Be, kBe = c["Be"]; Ke, kKe = c["Ke"]; vb, kvb = c["vb"]
            border = list(range(4)) if z == 0 else list(range(3, -1, -1))
            c["border"] = border
            aw, kaw = AW.next()
            c["aw"] = (aw, kaw)
            tb = {}
            for b in border:
                bc = slice(128 * b, 128 * b + 128)
                p_, kp_ = psg.next()
                for i, (src, skey) in enumerate(((vb, kvb), (Be, kBe), (Ke, kKe))):
                    mm(P, p_[:, 128 * i:128 * i + 128], src[:, bc], idb[:], True, True, [skey, "cb_ident"], [kp_])
                mm(P, p_[:, 384:512], vb[64:128, bc], idb[64:128, :], True, True, [kvb, "cb_ident"], [kp_])
                t, kt = TB.next()
                cpx(t[:], p_[:, 0:512], [kp_], [kt])
                tb[b] = (t, kt)
                yield
            c["tb"] = tb
            units = [(b, h) for b in border for h in range(2)]
            U = {}
            for ui, (b, h) in enumerate(units):
                bc = slice(128 * b, 128 * b + 128)
                ph = slice(64 * h, 64 * h + 64)
                arh = art[ph, b, :, :].rearrange("p a t -> p (a t)")
                p1, kp1 = psg.next()
                mm(P, p1[:, 0:256], Bt[ph, bc], arh, True, True, [kBt, kar], [kp1])
                mm(P, p1[:, 256:384], art[ph, b, 0, :], Bt[ph, bc], True, True, [kBt, kar], [kp1])
                g13, kg13 = G13.next()
                tt(P, "dve", g13[:], p1[:, 0:384], mask[:], ALU.mult, [kp1] + mkeys, [kg13])
                p2, kp2 = psg.next()
                mm(P, p2[:, 0:256], Kt[ph, bc], arh, True, True, [kKt, kar], [kp2])
                g2r, kg2r = G2r.next()
                cp(P, "act", g2r[:], p2[:, 0:256], [kp2], [kg2r])
                g2, kg2 = G2.next()
                tt(P, "pool", g2[:], g2r[:], mask[:, 0:256], ALU.mult, [kg2r] + mkeys, [kg2])
                U[(b, h)] = dict(g13=(g13, kg13), g2=(g2, kg2))
                yield
            for (b, h) in units:
                u = U[(b, h)]
                ph = slice(64 * h, 64 * h + 64)
                g2, kg2 = u["g2"]
                vt, kvt = tb[b]
                za = slice(0, 64) if h == 0 else slice(64, 128)
                zv = slice(64, 128) if h == 0 else slice(0, 64)
                p4, kp4 = psg.next()
                mm(P, p4[:, za], art[ph, b, 0, :], idb[ph, 64 * h:64 * h + 64], True, True, [kar, "cb_ident"], [kp4])
                mm(P, p4[:, zv], g2[:, 0:128], vt[:, 64 * h:64 * h + 64], True, True, [kg2, kvt], [kp4])
                z0, kz0 = Z0.next()
                cpx(z0[:], p4[:, 0:128], [kp4], [kz0])
                g13, kg13 = u["g13"]
                u["Z"] = (z0[:, 0:128], kz0)
                u["P"] = (g13[:, 0:128], kg13)
                u["PT"] = (g13[:, 256:384], kg13)
                if h == 1:
                    yield
            for j in range(6):
                last = (j == 5)
                for (b, h) in units:
                    u = U[(b, h)]
                    Zt, kZ = u["Z"]; Pt, kP = u["P"]; PTt, kPT = u["PT"]
                    ps, kps = psg.next()
                    if not last:
                        need_pt = (j < 4)
                        if j == 0:
                            mm(P, ps[:, 0:128], Pt, Zt, True, True, [kP, kZ], [kps])
                            mm(P, ps[:, 128:256], Pt, PTt, True, True, [kP, kPT], [kps])
                        elif need_pt:
                            mm(P, ps[:, 0:256], Pt, u["ZPT"], True, True, [kP, kZ], [kps])
                        else:
                            mm(P, ps[:, 0:128], Pt, Zt, True, True, [kP, kZ], [kps])
                        mm(P, ps[:, 256:384], PTt, Pt, True, True, [kP, kPT], [kps])
                        zn, kzn = ZP.next()
                        tt(P, "dve", zn[:, 0:128], ps[:, 0:128], Zt, ALU.add, [kps, kZ], [kzn])
                        lvl[0] += 1
                        if need_pt:
                            cp(P, "act" if lvl[0] % 8 else "dve", zn[:, 128:384], ps[:, 128:384], [kps], [kzn])
                        else:
                            cp(P, "act" if lvl[0] % 8 else "dve", zn[:, 256:384], ps[:, 256:384], [kps], [kzn])
                        u["Z"] = (zn[:, 0:128], kzn)
                        u["PT"] = (zn[:, 128:256], kzn)
                        u["P"] = (zn[:, 256:384], kzn)
                        u["ZPT"] = zn[:, 0:256]
                    else:
                        mm(P, ps[:, 0:128], Pt, Zt, True, True, [kP, kZ], [kps])
                        z6, kz6 = Z6.next()
                        tt(P, "dve", z6[:], ps[:, 0:128], Zt, ALU.add, [kps, kZ], [kz6])
                        u["z6"] = (z6, kz6)
                    if h == 1:
                        yield
            for (b, h) in units:
                u = U[(b, h)]
                ph = slice(64 * h, 64 * h + 64)
                bc = slice(128 * b, 128 * b + 128)
                z6, kz6 = u["z6"]
                p5, kp5 = psg.next()
                mm(P, p5[:, 0:128], z6[:], idb[:], True, True, [kz6, "cb_ident"], [kp5])
                cpx(aw[ph, bc], p5[ph, 0:128], [kp5], [kaw])
                if h == 1:
                    yield
            c["U"] = U

        def scan(c):
            s0, S, hp, z, ti, ntile = c["item"]
            t0 = c["t0"]
            first = (ti == 0) if z == 0 else (ti == ntile - 1)
            if first:
                P.op("pool", lambda e: e.memset(ST2[:], 0.0), writes=[("ST2", 0), ("ST2", 1)])
                P.op("pool", lambda e: e.memset(STb2[:], 0.0), writes=[("STb2", 0), ("STb2", 1)])
            art, kar = c["art"]
            pct, kpc = c["pc"]
            aw, kaw = c["aw"]
            U = c["U"]
            corder = (0, 1) if z == 0 else (1, 0)
            for b in c["border"]:
                bc = slice(128 * b, 128 * b + 128)
                tbt, ktb = c["tb"][b]
                vt, bet, ket = tbt[:, 0:128], tbt[:, 128:256], tbt[:, 256:384]
                ut, kut = UT.next()
                for cc in corder:
                    rows = slice(64 * cc, 64 * cc + 64)
                    tk = slice(128 * b + 64 * cc, 128 * b + 64 * cc + 64)
                    cidx = 2 * b + cc
                    hop = []
                    for h in range(2):
                        ph = slice(64 * h, 64 * h + 64)
                        pu, kpu = psc.next()
                        mm(P, pu[:, 0:64], aw[ph, bc], STb2[ph, 64 * h:64 * h + 64], True, True, [kaw, ("STb2", h)], [kpu])
                        hop.append((pu, kpu))
                    yield
                    hop2 = []
                    vtz = tbt[:, 384:512]
                    for h in range(2):
                        u = U[(b, h)]
                        z6, kz6 = u["z6"]
                        pu, kpu = hop[h]
                        if h == 0:
                            tt(P, "dve", ut[rows, 0, 0:64], pu[rows, 0:64], z6[rows, 64:128], ALU.add, [kpu, kz6], [kut])
                        else:
                            tt(P, "dve", ut[rows, 1, 64:128], pu[rows, 0:64], z6[rows, 0:64], ALU.add, [kpu, kz6], [kut])
                    gA, kgA = U[(b, 0)]["g13"]; g2A, kg2A = U[(b, 0)]["g2"]
                    gB, kgB = U[(b, 1)]["g13"]; g2B, kg2B = U[(b, 1)]["g2"]
                    cs_ = slice(128 + 64 * cc, 128 + 64 * cc + 64)
                    mm(P, psy[:, tk], STb2[:, :], art[:, b, 1, 64 * cc:64 * cc + 64], True, False,
                       [("STb2", 0), ("STb2", 1), kar], ["psy"])
                    mm(P, psy[0:64, tk], ut[rows, 0, 0:64], gA[rows, cs_], False, False, [kut, kgA], ["psy"])
                    mm(P, psy[0:64, tk], vt[rows, 0:64], g2A[rows, cs_], False, False, [ktb, kg2A], ["psy"])
                    mm(P, psy[:, tk], ut[rows, 1, :], gB[rows, cs_], False, False, [kut, kgB], ["psy"])
                    mm(P, psy[:, tk], vtz[rows, :], g2B[rows, cs_], False, True, [ktb, kg2B], ["psy"])
                    for h in range(2):
                        pss, kpss = psc.next()
                        if h == 0:
                            mm(P, pss[0:64, 0:64], bet[rows, 0:64], ut[rows, 0, 0:64], True, False, [ktb, kut], [kpss])
                            mm(P, pss[0:64, 0:64], ket[rows, 0:64], vt[rows, 0:64], False, True, [ktb], [kpss])
                        else:
                            mm(P, pss[:, 0:64], bet[rows, 0:128], ut[rows, 1, 64:128], True, False, [ktb, kut], [kpss])
                            mm(P, pss[:, 0:64], ket[rows, 0:128], vt[rows, 64:128], False, True, [ktb], [kpss])
                        hop2.append((pss, kpss))
                    yield
                    for h in range(2):
                        ph = slice(64 * h, 64 * h + 64)
                        pss, kpss = hop2[h]
                        sv = ST2[ph, 64 * h:64 * h + 64]
                        stt(P, sv, sv, pct[ph, cidx:cidx + 1], pss[ph, 0:64], ALU.mult, ALU.add,
                            [kpss, kpc, ("ST2", h)], [("ST2", h)])
                        cp(P, "act", STb2[ph, 64 * h:64 * h + 64], sv, [("ST2", h)], [("STb2", h)])
                    yield
            yt, kyt = YT.next()
            cp(P, "act", yt[:], psy[:], ["psy"], [kyt])
            rr = slice(128 * hp, 128 * hp + 128)
            if z == 0:
                P.dma("sp", k.YF_T[rr, t0:t0 + 512], yt[:], reads=[kyt], writes=[("YF", hp, t0)])
                return
            vs, kvs = c["vs"]
            pr_, kpr_ = c["pr"]
            yft, kyf = yf.next()
            P.dma("sp", yft[:], k.YF_T[rr, t0:t0 + 512], reads=[("YF", hp, t0)], writes=[kyf])
            gt, kgt = gat.next()
            P.dma("sp", gt[:], k.GA_T[rr, t0:t0 + 512], writes=[kgt])
            y, ky = f["e1"].next()
            tt(P, "pool", y[:], yt[:], yft[:], ALU.add, [kyt, kyf], [ky])
            d_, kd_ = f["e2"].next()
            sq2, ksq2 = f["e3"].next()
            rstd, krstd = f["e4"].next()
            pm, kpm = pse.next()
            mm(P, pm[:], bd1[:], y[:], True, True, [ky, "cf_bd1"], [kpm])
            stt(P, d_[:], pm[:], -1.0 / 64, y[:], ALU.mult, ALU.add, [kpm, ky], [kd_])
            sq2, ksq2 = sqb.next()
            act(P, sq2[:], d_[:], AF.Square, [kd_], [ksq2])
            yield
            pv_, kpv_ = pse.next()
            mm(P, pv_[:], bd1b[:], sq2[:], True, True, [ksq2, "cb_bd1"], [kpv_])
            rsqrt(P, rstd[:], pv_[:], 1.0 / 64, GN_EPS, [kpv_], [krstd])
            tt(P, "pool", d_[:], d_[:], rstd[:], ALU.mult, [kd_, krstd], [kd_])
            ts(P, "dve", d_[:], d_[:], col(k, "gn_w", hp), col(k, "gn_b", hp), ALU.mult, ALU.add, [kd_, "cols"], [kd_])
            yield
            bo, kbo = f["e6"].next()
            pb2, kpb2 = pse.next()
            mm(P, pb2[:], bd1b[:], pr_[:], True, True, [kpr_, "cb_bd1"], [kpb2])
            tt(P, "dve", bo[:], pb2[:], vs[:], ALU.mult, [kpb2, kvs], [kbo])
            tt(P, "pool", bo[:], bo[:], d_[:], ALU.add, [kbo, kd_], [kbo])
            o_, ko_ = yo.next()
            tt(P, "dve", o_[:], bo[:], gt[:], ALU.mult, [kbo, kgt], [ko_])
            P.dma("sp", k.Y0_T[rr, t0:t0 + 512], o_[:], reads=[ko_], writes=[("Y0a", hp, t0)])

        items = []
        for s0, S in zip(k.seq_off, k.seqs):
            ntile = S // 512
            for hp in range(4):
                for z in range(2):
                    order = range(ntile) if z == 0 else range(ntile - 1, -1, -1)
                    for ti in order:
                        items.append(dict(item=(s0, S, hp, z, ti, ntile)))
        n = len(items)
        for step in range(n + 2):
            gens = []
            if step < n:
                gens.append(elem(items[step]))
            if 0 <= step - 1 < n:
                gens.append(wave(items[step - 1]))
            if 0 <= step - 2 < n:
                gens.append(scan(items[step - 2]))
            while gens:
                for g in list(gens):
                    try:
                        next(g)
                    except StopIteration:
                        gens.remove(g)
            if step - 2 >= 0:
                items[step - 2].clear()
        P.stage_end()


def out_proj_tile(k, R, W, ysrc, xsrc, t0, j, pso, xres_ring, dst=None):
    P = k.P
    yT, kyT = ysrc
    xt, kx = xsrc
    if dst is None:
        xr, kxr = xres_ring.next()
    else:
        xr, kxr = dst
    for half in range(2):
        ps, kp = pso.next()
        for kc in range(8):
            mm(P, ps[:], yT[:, kc, 128 * j:128 * j + 128], W[:, kc, 512 * half:512 * half + 512], kc == 0, kc == 7,
               [kyT, "Wo"], [kp])
        tt(P, "dve", xr[:, 512 * half:512 * half + 512], ps[:], xt[:, j, 512 * half:512 * half + 512], ALU.add,
           [kp, kx], [kxr])
    return xr, kxr


def stage4_out0(k):
    P, nc, T = k.P, k.nc, k.T
    with contextlib.ExitStack() as st:
        Wo, W = load_weights(k, st, [("wo0", k.p["w_out0"], 8, D, "Wo"), ("w1", k.p["w_in1"], 8, ODD_COLS, "W")])
        R = in_rings(k, st)
        yr = Ring(nc, st, "y0T", [128, 8, 512], BF16, 2)
        x1r = Ring(nc, st, "x1t", [128, 4, D], F32, 2)
        pso = Ring(nc, st, "pso", [128, 512], F32, 2, psum=True)
        psr = Ring(nc, st, "ps1", [128, 512], F32, 4, psum=True)
        ost = Ring(nc, st, "ost", [128, 512], BF16, 6)
        glr = Ring(nc, st, "glt", [16, 512], F32, 2)
        qscale = 128.0 ** -0.5
        ntile = T // 512

        def loads(t0):
            xt, kx = R["x"].next()
            P.dma("sp", xt[:], k.x[t0:t0 + 512, :].rearrange("(j p) d -> p j d", p=128), writes=[kx])
            yT, kyT = yr.next()
            P.dma("sp", yT[:], k.Y0_T[:, t0:t0 + 512].rearrange("(kc p) t -> p kc t", p=128), writes=[kyT])
            return (xt, kx), (yT, kyT)
        nxt = loads(0)
        for ti in range(ntile):
            t0 = ti * 512
            xsrc, ysrc = nxt
            if ti + 1 < ntile:
                nxt = loads(t0 + 512)
            x1, kx1 = x1r.next()
            for j in range(4):
                xr, kxr = out_proj_tile(k, R, Wo, ysrc, xsrc, t0, j, pso, None, dst=(x1[:, j, :], kx1))
                P.dma("sp", k.X1[t0 + 128 * j:t0 + 128 * j + 128, :], xr, reads=[kxr], writes=[("X1", ti, j)])
            xnT, kT = rms_transpose_compute(k, R, x1, kx1, "norm1")
            for oc in list(range(8)) + list(range(16, 24)):
                ps, kp = psr.next()
                c0 = 128 * oc if oc < 8 else 2064 + 128 * (oc - 16)
                for kc in range(8):
                    mm(P, ps[:], W[:, kc, c0:c0 + 128], xnT[:, kc, :], kc == 0, kc == 7, [kT, "W"], [kp])
                o, ko = ost.next()
                if oc < 4:
                    act(P, o[:], ps[:], AF.Copy, [kp], [ko], scale=qscale)
                elif oc < 8:
                    cp(P, "dve", o[:], ps[:], [kp], [ko])
                else:
                    act(P, o[:], ps[:], AF.Silu, [kp], [ko])
                if oc < 8:
                    P.dma("sp", k.Q1_T[128 * oc:128 * oc + 128, t0:t0 + 512], o[:], reads=[ko], writes=[("Q1", ti, oc)])
                else:
                    r0 = 128 * (oc - 16)
                    P.dma("sp", k.G1_T[r0:r0 + 128, t0:t0 + 512], o[:], reads=[ko], writes=[("G1", ti, oc)])
            ps, kp = psr.next()
            for kc in range(8):
                mm(P, ps[0:16, :], W[:, kc, 2048:2064], xnT[:, kc, :], kc == 0, kc == 7, [kT, "W"], [kp])
            gl, kgl = glr.next()
            cp(P, "act", gl[:], ps[0:16, :], [kp], [kgl])
            P.dma("sp", k.GL_T[:, t0:t0 + 512], gl[:], reads=[kgl], writes=[("GL", ti)])
            for j in range(4):
                for half in range(2):
                    ps, kp = psr.next()
                    c0 = 1024 + 512 * half
                    for kc in range(8):
                        mm(P, ps[:], xnT[:, kc, 128 * j:128 * j + 128], W[:, kc, c0:c0 + 512], kc == 0, kc == 7,
                           [kT, "W"], [kp])
                    o, ko = ost.next()
                    cp(P, "act" if half else "dve", o[:], ps[:], [kp], [ko])
                    P.dma("sp", k.V1[t0 + 128 * j:t0 + 128 * j + 128, 512 * half:512 * half + 512], o[:], reads=[ko],
                          writes=[("V1", ti, j, half)])
        P.stage_end()


def stage5_in1(k):
    pass


def stage6_gla(k):
    P, nc = k.P, k.nc
    with contextlib.ExitStack() as st:
        A = lambda n, sh, dt=F32: st.enter_context(nc.sbuf_tensor(n, sh, dt))
        RG = lambda n, sh, dt, c: Ring(nc, st, n, sh, dt, c)
        gu = A("g_gu", [16, 2, 512])
        P.dma("sp", gu[:], k.p["gate_up"][:, :, :], writes=["gu"])
        ngb = A("g_ngb", [128, 8])
        og = COLS["gb_0"][0]
        ts(P, "dve", ngb[:], k.cols[:, og:og + 8], -1.0, None, ALU.mult, None, ["cols"], ["ngb"])
        def make_lane(li):
            L = {}
            nm = lambda n: "%s_l%d" % (n, li)
            L["Sr"] = RG(nm("g_S"), [128, 256], F32, 2)
            L["Sbr"] = RG(nm("g_Sb"), [128, 256], BF16, 2)
            L["qr"] = RG(nm("g_q"), [128, 512], BF16, 2)
            L["kr"] = RG(nm("g_k"), [128, 512], BF16, 2)
            L["glr"] = RG(nm("g_gl"), [16, 512], F32, 2)
            L["vtr"] = RG(nm("g_v"), [128, 4, 256], BF16, 2)
            L["f"] = {n: RG(nm("gf_" + n), [128, 512], F32, 1) for n in ("e", "l", "cf", "ci", "Eq", "Ek", "Ee", "rstd")}
            L["dcr"] = RG(nm("g_dc"), [128, 4], F32, 2)
            L["ob"] = {n: RG(nm("go_" + n), [128, 512], BF16, 2) for n in ("qd", "kd", "ke")}
            L["attr"] = RG(nm("g_att"), [128, 256], BF16, 3)
            L["otr"] = RG(nm("g_ot"), [128, 2, 512], F32, 2)
            L["ofr"] = RG(nm("g_of"), [128, 2, 512], F32, 1)
            L["sqr"] = RG(nm("g_sq"), [128, 2, 512], F32, 1)
            L["gtr"] = RG(nm("g_gt"), [128, 2, 512], BF16, 1)
            L["yor"] = RG(nm("g_yo"), [128, 512], BF16, 2)
            L["psg"] = Ring(nc, st, nm("pgG"), [128, 512], F32, (3, 3, 2)[li], psum=True)
            return L
        lanes = [make_lane(0), make_lane(1), make_lane(2)]
        npass = [0]
        idb = k.cb["ident"]
        ones = k.cf["ones"]
        ev = [0]

        def evac_eng():
            ev[0] += 1
            return "act" if ev[0] % 2 else "dve"
        v3 = lambda a: a.rearrange("p (c j) -> p c j", j=128)
        for s0, S in zip(k.seq_off, k.seqs):
            ntile = S // 512
            for h in range(4):
                for z in range(2):
                    L = lanes[npass[0] % 3]
                    npass[0] += 1
                    Sr, Sbr, qr, kr, glr, vtr, f, dcr, ob = (L[n_] for n_ in ("Sr", "Sbr", "qr", "kr", "glr", "vtr", "f", "dcr", "ob"))
                    attr, otr, ofr, sqr, gtr, yor, psg = (L[n_] for n_ in ("attr", "otr", "ofr", "sqr", "gtr", "yor", "psg"))
                    S_, kS = Sr.next()
                    Sb, kSb = Sbr.next()
                    P.op("pool", (lambda a: lambda e: e.memset(a, 0.0))(S_[:]), writes=[kS])
                    P.op("pool", (lambda a: lambda e: e.memset(a, 0.0))(Sb[:]), writes=[kSb])
                    mask = k.cb["g_f" if z == 0 else "g_b"]
                    order = range(ntile) if z == 0 else range(ntile - 1, -1, -1)
                    for ti in order:
                        t0 = s0 + 512 * ti
                        qT, kq = qr.next(); kT, kk_ = kr.next(); gl, kgl = glr.next(); vt, kvt = vtr.next()
                        P.dma("sp", qT[:], k.Q1_T[128 * h:128 * h + 128, t0:t0 + 512], writes=[kq])
                        P.dma("sp", kT[:], k.Q1_T[512 + 128 * h:512 + 128 * h + 128, t0:t0 + 512], writes=[kk_])
                        P.dma("sp", gl[:], k.GL_T[:, t0:t0 + 512], writes=[kgl])
                        P.dma("sp", vt[:], k.V1[t0:t0 + 512, 256 * h:256 * h + 256].rearrange("(j p) d -> p j d", p=128),
                              writes=[kvt])
                        pz, kpz = psg.next()
                        mm(P, pz[:], gu[0:16, z, 128 * h:128 * h + 128], gl[0:16, :], True, True, ["gu", kgl], [kpz])
                        e_, ke_ = f["e"].next()
                        act(P, e_[:], pz[:], AF.Exp, [kpz, "ngb"], [ke_], scale=-1.0, bias=ngb[:, 4 * z + h:4 * z + h + 1])
                        l_, kl_ = f["l"].next()
                        act(P, l_[:], e_[:], AF.Ln, [ke_], [kl_], bias=1.0)
                        if GLA_CUT == 1:
                            continue
                        cf_, kcf = f["cf"].next()
                        P.op("dve", (lambda o, a, b: lambda e: e.tensor_tensor_scan(
                            out=o, data0=a, data1=b, initial=0.0, op0=ALU.mult, op1=ALU.add))(
                            cf_[:], k.cf["rmask128"][:], l_[:]), [kl_, "cf_rmask128"], [kcf])
                        tot = cf_[:, 127:512:128]
                        totb = tot.unsqueeze(2).to_broadcast([128, 4, 128])
                        if z == 0:
                            ci, kci = cf_, kcf
                        else:
                            ci, kci = f["ci"].next()
                            tt(P, "pool", ci[:], l_[:], cf_[:], ALU.subtract, [kl_, kcf], [kci])
                            tt(P, "dve", v3(ci[:]), v3(ci[:]), totb, ALU.add, [kci, kcf], [kci])
                        Eq, kEq = f["Eq"].next(); Ek, kEk = f["Ek"].next(); Ee, kEe = f["Ee"].next()
                        act(P, Eq[:], ci[:], AF.Exp, [kci], [kEq], scale=-1.0 / 16)
                        act(P, Ek[:], ci[:], AF.Exp, [kci], [kEk], scale=1.0 / 16)
                        dc, kdc = dcr.next()
                        act(P, dc[:], tot, AF.Exp, [kcf], [kdc], scale=-1.0 / 16)
                        tt(P, "dve", v3(Ee[:]), v3(Ek[:]), dc[:].unsqueeze(2).to_broadcast([128, 4, 128]), ALU.mult,
                           [kEk, kdc], [kEe])
                        qd, kqd = ob["qd"].next(); kd, kkd = ob["kd"].next(); ke, kke = ob["ke"].next()
                        tt(P, "pool", qd[:], qT[:], Eq[:], ALU.mult, [kq, kEq], [kqd])
                        tt(P, "dve", kd[:], kT[:], Ek[:], ALU.mult, [kk_, kEk], [kkd])
                        tt(P, "pool", ke[:], kT[:], Ee[:], ALU.mult, [kk_, kEe], [kke])
                        if GLA_CUT == 2:
                            continue
                        ot, kot = otr.next()
                        border = range(4) if z == 0 else range(3, -1, -1)
                        for b in border:
                            bc = slice(128 * b, 128 * b + 128)
                            pa, kpa = psg.next()
                            mm(P, pa[:, 0:128], kd[:, bc], qd[:, bc], True, True, [kkd, kqd], [kpa])
                            mm(P, pa[:, 128:256], ke[:, bc], idb[:], True, True, [kke, "cb_ident"], [kpa])
                            at, kat = attr.next()
                            tt(P, "dve", at[:], pa[:, 0:256], mask[:], ALU.mult, [kpa, "cb_g_f", "cb_g_b"], [kat])
                            keT = at[:, 128:256]
                            po, kpo = psg.next()
                            for half in range(2):
                                hc = slice(128 * half, 128 * half + 128)
                                mm(P, po[:, hc], vt[:, b, hc], at[:, 0:128], True, False, [kvt, kat], [kpo])
                                mm(P, po[:, hc], Sb[:, hc], qd[:, bc], False, True, [kSb, kqd], [kpo])
                            cp(P, "act", ot[:, :, bc], po[:, 0:256].rearrange("p (a t) -> p a t", a=2), [kpo], [kot])
                            pS, kpS = psg.next()
                            mm(P, pS[:, 0:256], keT, vt[:, b, :], True, True, [kat, kvt], [kpS])
                            stt(P, S_[:], S_[:], dc[:, b:b + 1], pS[:, 0:256], ALU.mult, ALU.add, [kpS, kdc, kS], [kS])
                            cp(P, "act", Sb[:], S_[:], [kS], [kSb])
                        if GLA_CUT == 3:
                            continue
                        if z == 0:
                            for half in range(2):
                                r0 = 256 * h + 128 * half
                                P.dma("sp", k.OF_T[r0:r0 + 128, t0:t0 + 512], ot[:, half, :], reads=[kot],
                                      writes=[("OF", h, half, t0)])
                            continue
                        of, kof = ofr.next(); gt, kgt = gtr.next()
                        for half in range(2):
                            r0 = 256 * h + 128 * half
                            P.dma("sp", of[:, half, :], k.OF_T[r0:r0 + 128, t0:t0 + 512], reads=[("OF", h, half, t0)],
                                  writes=[kof])
                            P.dma("sp", gt[:, half, :], k.G1_T[r0:r0 + 128, t0:t0 + 512], writes=[kgt])
                        tt(P, "pool", of[:], of[:], ot[:], ALU.add, [kof, kot], [kof])
                        if GLA_CUT == 4:
                            continue
                        sq, ksq = sqr.next()
                        act(P, sq[:], of[:], AF.Square, [kof], [ksq])
                        pn, kpn = psg.next()
                        mm(P, pn[:], ones[:], sq[:, 0, :], True, False, [ksq, "cf_ones"], [kpn])
                        mm(P, pn[:], ones[:], sq[:, 1, :], False, True, [ksq, "cf_ones"], [kpn])
                        rstd, krstd = f["rstd"].next()
                        if GLA_CUT == 5:
                            continue
                        rsqrt(P, rstd[:], pn[:], 1.0 / 256, RMS_EPS, [kpn], [krstd])
                        if GLA_CUT == 6:
                            continue
                        for half in range(2):
                            r0 = 256 * h + 128 * half
                            stt(P, of[:, half, :], of[:, half, :], col(k, "gnorm", half), rstd[:], ALU.mult, ALU.mult,
                                [kof, krstd, "cols"], [kof])
                            if GLA_CUT == 7:
                                continue
                            yo, kyo = yor.next()
                            tt(P, "dve", yo[:], of[:, half, :], gt[:, half, :], ALU.mult, [kof, kgt], [kyo])
                            P.dma("sp", k.Y1_T[r0:r0 + 128, t0:t0 + 512], yo[:], reads=[kyo], writes=[("Y1", h, half, t0)])
        P.stage_end()


def stage7_out1(k):
    P, nc, T = k.P, k.nc, k.T
    with contextlib.ExitStack() as st:
        Wo, = load_weights(k, st, [("wo1", k.p["w_out1"], 8, D, "Wo")])
        fn = st.enter_context(nc.sbuf_tensor("fnorm", [128, D], F32))
        P.dma("sp", fn[:], k.p["fnorm"][:, :], writes=["fnorm"])
        xr_ = Ring(nc, st, "x1in", [128, 4, D], F32, 2)
        yr = Ring(nc, st, "y1T", [128, 8, 512], BF16, 2)
        xrr = Ring(nc, st, "xres", [128, D], F32, 2)
        outr = Ring(nc, st, "outt", [128, D], F32, 2)
        junk = st.enter_context(nc.sbuf_tensor("junk7", [128, D], BF16))
        ssr = Ring(nc, st, "ss7", [128, 1], F32, 2)
        pso = Ring(nc, st, "pso", [128, 512], F32, 4, psum=True)
        ntile = T // 512

        def loads(t0):
            xt, kx = xr_.next()
            P.dma("sp", xt[:], k.X1[t0:t0 + 512, :].rearrange("(j p) d -> p j d", p=128), writes=[kx])
            yT, kyT = yr.next()
            P.dma("sp", yT[:], k.Y1_T[:, t0:t0 + 512].rearrange("(kc p) t -> p kc t", p=128), writes=[kyT])
            return (xt, kx), (yT, kyT)
        nxt = loads(0)
        for ti in range(ntile):
            t0 = ti * 512
            xsrc, ysrc = nxt
            if ti + 1 < ntile:
                nxt = loads(t0 + 512)
            for j in range(4):
                xr, kxr = out_proj_tile(k, None, Wo, ysrc, xsrc, t0, j, pso, xrr)
                ss, kss = ssr.next()
                act(P, junk[:], xr[:], AF.Square, [kxr], [kss, "junk7"], accum_out=ss[:, 0:1])
                rsqrt(P, ss[:], ss[:], 1.0 / D, RMS_EPS, [kss], [kss])
                o, ko = outr.next()
                stt(P, o[:], xr[:], ss[:, 0:1], fn[:], ALU.mult, ALU.mult, [kxr, kss, "fnorm"], [ko])
                P.dma("sp", k.y[t0 + 128 * j:t0 + 128 * j + 128, :], o[:], reads=[ko], writes=[("y", ti, j)])
        P.stage_end()


_CACHE = {}


def kernel(**inputs):
    xp = np.asarray(inputs["x_prompt"], np.float32)
    xs = np.asarray(inputs["x_sample"], np.float32)
    B, S, _ = xp.shape
    DB, DS, _ = xs.shape
    n = NCORES
    pb, sbn = B // n, DB // n
    seqs = [S] * pb + [DS] * sbn
    key = tuple(seqs)
    if key not in _CACHE:
        _CACHE[key] = build(seqs)
    nc, k = _CACHE[key]
    consts = host_consts(max(seqs))
    params = host_params(inputs)
    shared = {"c_" + a: v for a, v in consts.items()}
    shared.update({"p_" + a: v for a, v in params.items()})
    in_maps = []
    for c in range(n):
        parts = [xp[c * pb + i] for i in range(pb)] + [xs[c * sbn + i] for i in range(sbn)]
        m = {"x": np.ascontiguousarray(np.concatenate(parts, axis=0))}
        m.update(shared)
        in_maps.append(m)
    res = run_bass_kernel_spmd(nc, in_maps, core_ids=list(range(n)))
    yp = np.empty_like(xp)
    ys = np.empty_like(xs)
    for c in range(n):
        y = np.asarray(res.results[c]["y"], np.float32)
        off = 0
        for i in range(pb):
            yp[c * pb + i] = y[off:off + S]
            off += S
        for i in range(sbn):
            ys[c * sbn + i] = y[off:off + DS]
            off += DS
    return (yp, ys)
```

```python
import contextlib
import numpy as np
import concourse.bass as bass
import concourse.mybir as mybir
from concourse.bass_utils import run_bass_kernel_spmd

F32 = mybir.dt.float32
BF16 = mybir.dt.bfloat16
AF = mybir.ActivationFunctionType
ALU = mybir.AluOpType

SAME_ENG_SYNC = True
LIST_SCHED = True
SCHED_XLAT = 180.0
PE_MODE_GROUP = True
PE_MODE_WINDOW = 12
PE_MODE_SLACK = 4000.0
SCHED_DEBUG = False
NDMASEM = 16
NCORES = 8
D = 1024
RW = 512
EVEN_SHIFT = 1664
EVEN_COLS = 4224
ODD_COLS = 3088
DECAY_C = 0.6065306597126334
GN_EPS = 64e-5
RMS_EPS = 1e-6


class Op:
    __slots__ = ("eng", "fn", "reads", "writes", "is_dma", "waits", "signal", "sig", "barrier", "deps", "cost",
                 "lat", "idx", "mode")


def _numel(ap):
    n = 1
    for d in ap.shape[1:]:
        n *= int(d)
    return n


class Prog:
    ENGS = ("pe", "act", "dve", "pool", "sp")

    def __init__(self, nc, stack):
        self.nc = nc
        self.ops = []
        self.sems = {e: stack.enter_context(nc.semaphore("s_" + e)) for e in ("pe", "act", "dve", "pool")}
        self.dsems = {q: [stack.enter_context(nc.semaphore("d_%s%d" % (q, i))) for i in range(NDMASEM)]
                      for q in ("sp", "pool", "act")}
        self.cnt = {e: 0 for e in ("pe", "act", "dve", "pool")}
        self.dcnt = {q: 0 for q in ("sp", "pool", "act")}
        self.last_writer = {}
        self.readers = {}
        self.seen = {e: {} for e in self.ENGS}
        self.last_op = {}
        self.recent_dma = {q: [] for q in ("sp", "pool", "act")}
        self.emitted = 0
        self.n_inst = 0

    def op(self, eng, fn, reads=(), writes=(), cost=500.0):
        o = Op()
        ex = [r for r in reads if isinstance(r, str) and (r.startswith("ps") or r.startswith("pg"))]
        if ex:
            reads = [r for r in reads if r not in ex]
            writes = list(writes) + ex
        o.eng = eng; o.fn = fn; o.reads = tuple(reads); o.writes = tuple(writes)
        o.is_dma = False; o.signal = False; o.sig = None; o.waits = []; o.barrier = False
        o.deps = []; o.cost = cost; o.lat = cost; o.mode = None
        self.ops.append(o)
        return o

    def dma(self, q, out, in_, reads=(), writes=()):
        o = self.op(q, lambda e: e.dma_start(out=out, in_=in_), reads, writes, cost=60.0)
        o.is_dma = True
        o.lat = 2200.0 + _numel(out) * 128 * 0.004
        return o

    def barrier(self):
        for e in self.ENGS:
            o = self.op(e, None)
            o.barrier = True

    def _deps(self, ops):
        for o in ops:
            if o.barrier:
                if o.eng == self.ENGS[-1]:
                    self.last_writer = {}
                    self.readers = {}
                continue
            deps = {}
            for k in o.reads:
                w = self.last_writer.get(k)
                if w is not None:
                    deps[id(w)] = (w, True)
            for k in o.writes:
                w = self.last_writer.get(k)
                if w is not None:
                    israw = isinstance(k, str) and (k.startswith("ps") or k.startswith("pg"))
                    if id(w) not in deps or israw:
                        deps[id(w)] = (w, israw or deps.get(id(w), (None, False))[1])
                for r in self.readers.get(k, ()):
                    if id(r) not in deps:
                        deps[id(r)] = (r, False)
            deps.pop(id(o), None)
            o.deps = list(deps.values())
            for k in o.reads:
                self.readers.setdefault(k, []).append(o)
            for k in o.writes:
                self.last_writer[k] = o
                self.readers[k] = []

    def _schedule(self, seg):
        import heapq
        n = len(seg)
        if n < 3 or not LIST_SCHED:
            return seg
        for i, o in enumerate(seg):
            o.idx = i
        inseg = set(id(o) for o in seg)
        succ = [[] for _ in range(n)]
        indeg = [0] * n
        for o in seg:
            for d, _ in o.deps:
                if id(d) in inseg:
                    succ[d.idx].append(o.idx)
                    indeg[o.idx] += 1
        rank = [0.0] * n
        for i in range(n - 1, -1, -1):
            m = 0.0
            for j in succ[i]:
                if rank[j] > m:
                    m = rank[j]
            rank[i] = seg[i].lat + m
        ready = [0.0] * n
        free = {e: 0.0 for e in self.ENGS}
        fut = {e: [] for e in self.ENGS}
        now = {e: [] for e in self.ENGS}
        for i in range(n):
            if indeg[i] == 0:
                heapq.heappush(fut[seg[i].eng], (0.0, i))
        out = []
        XLAT = SCHED_XLAT
        pe_mode = [None]
        while len(out) < n:
            best = None
            for e in self.ENGS:
                f, nw = fut[e], now[e]
                while f and f[0][0] <= free[e]:
                    t, i = heapq.heappop(f)
                    heapq.heappush(nw, (-rank[i], i))
                if nw:
                    cand = (free[e], 0, e)
                elif f:
                    cand = (f[0][0], 1, e)
                else:
                    continue
                if best is None or cand < best:
                    best = cand
            start, kind, e = best
            if kind == 0:
                if e == "pe" and PE_MODE_GROUP:
                    nw = now[e]
                    top_rank = -nw[0][0]
                    pick = None
                    cand_list = heapq.nsmallest(PE_MODE_WINDOW, nw)
                    for (nr, ii) in cand_list:
                        if seg[ii].mode == pe_mode[0] and (top_rank + nr) <= PE_MODE_SLACK:
                            pick = (nr, ii)
                            break
                    if pick is None:
                        _, i = heapq.heappop(nw)
                    else:
                        nw.remove(pick)
                        heapq.heapify(nw)
                        i = pick[1]
                    pe_mode[0] = seg[i].mode
                else:
                    _, i = heapq.heappop(now[e])
            else:
                _, i = heapq.heappop(fut[e])
                if e == "pe":
                    pe_mode[0] = seg[i].mode
            o = seg[i]
            free[e] = start + o.cost
            fin = start + o.lat
            out.append(o)
            for j in succ[i]:
                same = (seg[j].eng == e and not o.is_dma)
                r = (start + o.cost) if same else (fin + XLAT)
                if r > ready[j]:
                    ready[j] = r
                indeg[j] -= 1
                if indeg[j] == 0:
                    heapq.heappush(fut[seg[j].eng], (ready[j], j))
        if SCHED_DEBUG and n > 500:
            import collections
            load = collections.defaultdict(float)
            for o in seg:
                load[o.eng] += o.cost
            st = {}
            fr = {e: 0.0 for e in self.ENGS}
            pred = {}
            for o in out:
                t = fr[o.eng]
                p = None
                for d, _ in o.deps:
                    if id(d) in st:
                        same = (d.eng == o.eng and not d.is_dma)
                        r = st[id(d)] + (d.cost if same else d.lat + XLAT)
                        if r > t:
                            t, p = r, d
                st[id(o)] = t
                pred[id(o)] = p
                fr[o.eng] = t + o.cost
            last = max(out, key=lambda o: st[id(o)] + o.lat)
            print("SCHED seg n=%d makespan=%.0f loads=%s" % (n, st[id(last)] + last.lat, {e: int(v) for e, v in load.items()}))
            i = max(range(n), key=lambda q: rank[q])
            print("  pure DAG critical path length: %.0f" % rank[i])
            cp_ = collections.Counter(); cpt = collections.defaultdict(float)
            while True:
                o = seg[i]
                kk = (o.eng, str(o.writes[0] if o.writes else "-")[:7], o.is_dma)
                cp_[kk] += 1; cpt[kk] += o.lat
                if not succ[i]:
                    break
                i = max(succ[i], key=lambda q: rank[q])
            print("  DAG path:", sorted(((int(cpt[kk]), v, kk) for kk, v in cp_.items()), reverse=True)[:16])
            path = collections.Counter()
            tm = collections.defaultdict(float)
            o = last
            cnt = 0
            while o is not None and cnt < 100000:
                kk = (o.eng, str(o.writes[0] if o.writes else "-")[:6])
                path[kk] += 1
                tm[kk] += o.lat
                o = pred[id(o)]
                cnt += 1
            print("  critical path ops:", sorted(((v, int(tm[kk]), kk) for kk, v in path.items()), reverse=True)[:14])
        return out

    def _sync(self, ops):
        for o in ops:
            if o.barrier:
                if o.eng == self.ENGS[0]:
                    self._bar = [lo for lo in self.last_op.values() if lo is not None]
                    for q, lst in self.recent_dma.items():
                        self._bar.extend(lst)
                deps = [(d, True) for d in self._bar]
            else:
                deps = o.deps
            if o.is_dma:
                j = self.dcnt[o.eng]
                self.dcnt[o.eng] = j + 1
                s = j % NDMASEM
                o.sig = (("d", o.eng, s), 16 * (j // NDMASEM + 1))
                o.signal = True
                if j >= NDMASEM:
                    o.waits.append((("d", o.eng, s), 16 * (j // NDMASEM)))
            for d, israw in deps:
                if d.is_dma:
                    o.waits.append(d.sig)
                elif d.eng == o.eng and not o.is_dma:
                    if o.eng == "pe" or not SAME_ENG_SYNC or (not o.barrier and not israw):
                        continue
                    d.signal = True
                    o.waits.append(("c", d))
                else:
                    d.signal = True
                    o.waits.append(("c", d))
            if not o.barrier:
                if o.is_dma:
                    lst = self.recent_dma[o.eng]
                    lst.append(o)
                    if len(lst) > NDMASEM:
                        lst.pop(0)
                else:
                    self.last_op[o.eng] = o
        for o in ops:
            if o.signal and not o.is_dma:
                self.cnt[o.eng] += 1
                o.sig = (("c", o.eng), self.cnt[o.eng])

    def _sem(self, key):
        if key[0] == "c":
            return self.sems[key[1]]
        return self.dsems[key[1]][key[2]]

    def emit(self):
        ops = self.ops[self.emitted:]
        self.emitted = len(self.ops)
        self._deps(ops)
        ordered, seg = [], []
        for o in ops:
            if o.barrier:
                if seg:
                    ordered.extend(self._schedule(seg))
                    seg = []
                ordered.append(o)
            else:
                seg.append(o)
        if seg:
            ordered.extend(self._schedule(seg))
        ops = ordered
        self._sync(ops)
        per = {e: [o for o in ops if o.eng == e] for e in self.ENGS}
        self.n_inst += len(ops)
        with self.nc.Block() as block:
            def run(engname):
                def body(eng):
                    seen = self.seen[engname]
                    for o in per[engname]:
                        for w in o.waits:
                            if w[0] == "c":
                                key, val = w[1].sig
                            else:
                                key, val = w
                            if seen.get(key, 0) >= val:
                                continue
                            seen[key] = val
                            eng.wait_ge(self._sem(key), val)
                        if o.fn is None:
                            continue
                        ins = o.fn(eng)
                        if o.signal:
                            key, val = o.sig
                            ins.then_inc(self._sem(key), 16 if o.is_dma else 1)
                return body
            block.tensor(run("pe"))
            block.scalar(run("act"))
            block.vector(run("dve"))
            block.gpsimd(run("pool"))
            block.sync(run("sp"))

    def stage_end(self):
        self.barrier()
        self.emit()


class Ring:
    uid = 0
    def __init__(self, nc, st, name, shape, dt, n, psum=False):
        alloc = nc.psum_tensor if psum else nc.sbuf_tensor
        Ring.uid += 1
        self.tiles = [st.enter_context(alloc("%s_u%d_%d" % (name, Ring.uid, i), shape, dt)) for i in range(n)]
        self.keys = ["%s%d" % (name, i) for i in range(n)]
        self.i = -1

    def next(self):
        self.i = (self.i + 1) % len(self.tiles)
        return self.tiles[self.i], self.keys[self.i]


def _cols(v):
    v = np.asarray(v, np.float32)
    return np.ascontiguousarray(v.reshape(-1, 128).T)


COLS = {}


def _col_layout():
    off = 0
    for name, n in (("norm0", 8), ("norm1", 8), ("mup", 13), ("mun", 13), ("w0_0", 4), ("w0_1", 4), ("a0_0", 4),
                    ("a0_1", 4), ("k_k", 4), ("k_a", 4), ("r_k", 4), ("gn_w", 4), ("gn_b", 4), ("gnorm", 2),
                    ("gb_0", 4), ("gb_1", 4)):
        COLS[name] = (off, n)
        off += n
    return off


NCOLS = _col_layout()


def host_consts(smax):
    c = {}
    c["ident"] = np.eye(128, dtype=np.float32)
    R = np.zeros((128, 128), np.float32)
    for m in range(128):
        h, j = divmod(m, 64)
        k = h * 64 + (j + 32) % 64
        R[k, m] = 1.0
    c["rot"] = R
    inv = 10000.0 ** (-np.arange(0, 64, 2, dtype=np.float32) / 64.0)
    ang = np.arange(smax, dtype=np.float32)[None, :] * inv[:, None]
    cos, sin = np.cos(ang), np.sin(ang)
    c["cos"] = np.ascontiguousarray(np.tile(cos, (4, 1)).astype(np.float32))
    c["sin"] = np.ascontiguousarray(np.concatenate([-sin, sin, -sin, sin], 0).astype(np.float32))
    i = np.arange(128)[:, None]
    t = np.arange(128)[None, :]
    same = (i // 64) == (t // 64)
    strict = ((i < t) & same).astype(np.float32)
    incl = ((i <= t) & same).astype(np.float32)
    c["m_f"] = np.concatenate([strict, incl, strict.T], 1)
    c["m_b"] = np.concatenate([strict.T, incl.T, strict], 1)
    c["g_f"] = np.concatenate([(i <= t).astype(np.float32), np.ones((128, 128), np.float32)], 1)
    c["g_b"] = np.concatenate([(i >= t).astype(np.float32), np.ones((128, 128), np.float32)], 1)
    kl = np.arange(128)[:, None]
    ql = np.arange(256)[None, :]
    c["a_g"] = ((kl <= ql) & (ql <= kl + 128)).astype(np.float32)
    c["a_0"] = (np.arange(128)[None, :] <= np.arange(64)[:, None] + 64).astype(np.float32)
    bd = (np.arange(128)[:, None] // 64 == np.arange(128)[None, :] // 64).astype(np.float32)
    c["bd1"] = bd
    c["ones"] = np.ones((128, 128), np.float32)
    rm = np.ones((128, 512), np.float32)
    rm[:, ::64] = 0.0
    c["rmask"] = rm
    rm2 = np.ones((128, 512), np.float32)
    rm2[:, ::128] = 0.0
    c["rmask128"] = rm2
    return c


CONST_SHAPES = lambda smax: {"ident": [128, 128], "rot": [128, 128], "cos": [128, smax], "sin": [128, smax],
                             "m_f": [128, 384], "m_b": [128, 384],
                             "g_f": [128, 256], "g_b": [128, 256], "a_g": [128, 256], "a_0": [64, 128],
                             "bd1": [128, 128], "ones": [128, 128], "rmask": [128, 512],
                             "rmask128": [128, 512]}


def host_params(inp):
    g = lambda k: np.asarray(inp[k], np.float32)
    cols = np.zeros((128, NCOLS), np.float32)

    def put(name, v):
        o, n = COLS[name]
        cols[:, o:o + n] = _cols(v)
    put("norm0", g("even_norm")[0]); put("norm1", g("odd_norm")[0])
    put("mup", g("even_mu_prev")[0]); put("mun", g("even_mu_next")[0])
    for z in range(2):
        put("w0_%d" % z, g("rwkv_w0")[0, z]); put("a0_%d" % z, g("rwkv_a0")[0, z])
        put("gb_%d" % z, g("gla_gate_bias")[0, z])
    for nm, k in (("k_k", "rwkv_k_k"), ("k_a", "rwkv_k_a"), ("r_k", "rwkv_r_k"), ("gn_w", "rwkv_gn_w"),
                  ("gn_b", "rwkv_gn_b")):
        put(nm, g(k)[0])
    put("gnorm", g("gla_norm")[0])
    lora = np.zeros((128, 2, 512), np.float32)
    lora[0:64] = np.transpose(g("rwkv_w_up")[0], (1, 0, 2))
    lora[64:128] = np.transpose(g("rwkv_a_up")[0], (1, 0, 2))
    p = {"cols": cols, "lora": lora,
         "w_in0": g("even_w_in")[0], "w_out0": g("even_w_out")[0],
         "w_in1": g("odd_w_in")[0], "w_out1": g("odd_w_out")[0],
         "gate_up": np.ascontiguousarray(np.transpose(g("gla_gate_up")[0], (1, 0, 2))),
         "fnorm": np.ascontiguousarray(np.broadcast_to(g("final_norm")[None, :], (128, D)))}
    return p


PARAM_SHAPES = {"cols": [128, NCOLS], "lora": [128, 2, 512], "w_in0": [D, EVEN_COLS], "w_out0": [D, D],
                "w_in1": [D, ODD_COLS], "w_out1": [D, D], "gate_up": [16, 2, 512], "fnorm": [128, D]}


class K:
    pass


def mm(P, out, lhsT, rhs, start, stop, reads, writes):
    n = _numel(out)
    c = 60.0 + max(64, n) / 1.2 * (4.0 if lhsT.dtype == F32 else 1.0)
    o = P.op("pe", lambda e: e.matmul(out, lhsT=lhsT, rhs=rhs, start=start, stop=stop), reads, writes, cost=c)
    o.lat = c + 100.0
    rnd = lambda v: 32 if v <= 32 else (64 if v <= 64 else 128)
    o.mode = (rnd(int(lhsT.shape[0])), rnd(int(lhsT.shape[1])))


def act(P, out, in_, func, reads, writes, **kw):
    P.op("act", lambda e: e.activation(out=out, in_=in_, func=func, **kw), reads, writes,
         cost=230.0 + _numel(out) / 1.2)


def _vcost(eng, out):
    n = _numel(out)
    return (80.0 + n * 1.05) if eng == "dve" else (150.0 + n * 2.3)


def tt(P, eng, out, in0, in1, op, reads, writes):
    P.op(eng, lambda e: e.tensor_tensor(out=out, in0=in0, in1=in1, op=op), reads, writes, cost=_vcost(eng, out))


def ts(P, eng, out, in0, s1, s2, op0, op1, reads, writes):
    if op1 is None:
        P.op(eng, lambda e: e.tensor_scalar(out=out, in0=in0, scalar1=s1, scalar2=None, op0=op0), reads, writes,
             cost=_vcost(eng, out))
    else:
        P.op(eng, lambda e: e.tensor_scalar(out=out, in0=in0, scalar1=s1, scalar2=s2, op0=op0, op1=op1), reads, writes,
             cost=_vcost(eng, out))


def stt(P, out, in0, scalar, in1, op0, op1, reads, writes):
    P.op("dve", lambda e: e.scalar_tensor_tensor(out=out, in0=in0, scalar=scalar, in1=in1, op0=op0, op1=op1),
         reads, writes, cost=_vcost("dve", out))


def cp(P, eng, out, in_, reads, writes):
    if eng == "act":
        P.op("act", lambda e: e.activation(out=out, in_=in_, func=AF.Copy), reads, writes,
             cost=230.0 + _numel(out) / 1.2)
    else:
        P.op(eng, lambda e: e.tensor_copy(out=out, in_=in_), reads, writes, cost=_vcost(eng, out))


def rsqrt(P, out, in_, scale, bias, reads, writes):
    act(P, out, in_, AF.Ln, reads, writes, scale=scale, bias=bias)
    act(P, out, out, AF.Exp, writes, writes, scale=-0.5)


def col(k, name, j=0, n=1, rows=slice(0, 128)):
    o, _ = COLS[name]
    return k.cols[rows, o + j:o + j + n]


def build(seqs, n_stage=99, debug=()):
    nc = bass.Bass("TRN2", target_bir_lowering=False)
    k = K()
    k.nc = nc
    k.seqs = list(seqs)
    T = sum(seqs)
    k.T = T
    smax = max(seqs)
    k.seq_off = [sum(seqs[:i]) for i in range(len(seqs))]
    ein = lambda n, sh, dt=F32: nc.dram_tensor(n, sh, dt, kind="ExternalInput").ap()
    scr = lambda n, sh, dt=BF16: nc.dram_tensor(n, sh, dt, kind=("ExternalOutput" if n in debug else "Internal")).ap()
    k.x = ein("x", [T, D])
    k.y = nc.dram_tensor("y", [T, D], F32, kind="ExternalOutput").ap()
    k.c = {n: ein("c_" + n, sh) for n, sh in CONST_SHAPES(smax).items()}
    k.p = {n: ein("p_" + n, sh) for n, sh in PARAM_SHAPES.items()}
    k.RW_T = scr("RW_T", [EVEN_SHIFT, T])
    k.GA_T = scr("GA_T", [512, T])
    k.QK_T = scr("QK_T", [1024, T])
    k.VB = scr("VB", [T, 512])
    k.GB_T = scr("GB_T", [512, T])
    k.YF_T = scr("YF_T", [512, T], F32)
    k.Y0_T = scr("Y0_T", [1024, T])
    k.X1 = scr("X1", [T, D], F32)
    k.Q1_T = scr("Q1_T", [1024, T])
    k.GL_T = scr("GL_T", [16, T], F32)
    k.V1 = scr("V1", [T, D])
    k.G1_T = scr("G1_T", [D, T])
    k.OF_T = scr("OF_T", [D, T], F32)
    k.Y1_T = scr("Y1_T", [D, T])
    k.DEN = scr("DEN", [8, 512], F32)

    with contextlib.ExitStack() as gst:
        P = Prog(nc, gst)
        k.P = P
        sb = lambda n, sh, dt: gst.enter_context(nc.sbuf_tensor(n, sh, dt))
        k.cols = sb("cols", [128, NCOLS + 32], F32)
        k.cb = {}
        k.cf = {}
        for n in ("ident", "bd1", "ones", "rmask", "rmask128"):
            k.cf[n] = sb("cf_" + n, CONST_SHAPES(smax)[n], F32)
        BN = ("ident", "rot", "m_f", "m_b", "g_f", "g_b", "a_g", "a_0", "bd1")
        for n in BN:
            k.cb[n] = sb("cb_" + n, CONST_SHAPES(smax)[n], BF16)
        k.lora = sb("lora", [128, 2, 512], BF16)
        with contextlib.ExitStack() as st:
            tmp = st.enter_context(nc.sbuf_tensor("ctmp", [128, 2048], F32))
            P.dma("sp", k.cols[:, 0:NCOLS], k.p["cols"][:, :], writes=["cols"])
            for n in ("ident", "bd1", "ones", "rmask", "rmask128"):
                P.dma("sp", k.cf[n][:], k.c[n][:, :], writes=["cf_" + n])
            off = 0
            for n in BN:
                sh = CONST_SHAPES(smax)[n]
                P.dma("sp", tmp[0:sh[0], off:off + sh[1]], k.c[n][:, :], writes=["ctmp"])
                cp(P, "dve", k.cb[n][:], tmp[0:sh[0], off:off + sh[1]], ["ctmp"], ["cb_" + n])
                off += sh[1]
            P.stage_end()
            P.dma("sp", tmp[:, 0:1024], k.p["lora"].rearrange("p z c -> p (z c)"), writes=["ctmp2"])
            cp(P, "dve", k.lora[:].rearrange("p z c -> p (z c)"), tmp[:, 0:1024], ["ctmp2"], ["lora"])
            o_mup, o_mun, o_ka = COLS["mup"][0], COLS["mun"][0], COLS["k_a"][0]
            k.o_c0, k.o_omk = NCOLS, NCOLS + 13
            tt(P, "dve", k.cols[:, k.o_c0:k.o_c0 + 13], k.cols[:, o_mup:o_mup + 13], k.cols[:, o_mun:o_mun + 13],
               ALU.add, ["cols"], ["cols"])
            ts(P, "dve", k.cols[:, k.o_c0:k.o_c0 + 13], k.cols[:, k.o_c0:k.o_c0 + 13], -1.0, 1.0, ALU.mult, ALU.add,
               ["cols"], ["cols"])
            ts(P, "dve", k.cols[:, k.o_omk:k.o_omk + 4], k.cols[:, o_ka:o_ka + 4], -1.0, 1.0, ALU.mult, ALU.add,
               ["cols"], ["cols"])
            P.stage_end()
        stages = [stage1, stage2_attn, stage3_rwkv, stage4_out0, stage5_in1, stage6_gla, stage7_out1]
        for i, s in enumerate(stages):
            if i < n_stage:
                s(k)
        P.stage_end()
    k.n_inst = P.n_inst
    return nc, k


def load_weights(k, st, specs):
    P, nc = k.P, k.nc
    ws = [st.enter_context(nc.sbuf_tensor(name, [128, rows, ncol], BF16)) for name, dram, rows, ncol, key in specs]
    with contextlib.ExitStack() as inner:
        ring = Ring(nc, inner, "wstage", [128, 1056], F32, 3)
        for w, (name, dram, rows, ncol, key) in zip(ws, specs):
            v = dram.rearrange("(kc p) c -> p kc c", p=128)
            for kc in range(rows):
                for c0 in range(0, ncol, 1056):
                    c1 = min(ncol, c0 + 1056)
                    t, tk = ring.next()
                    P.dma("sp", t[:, 0:c1 - c0], v[:, kc, c0:c1], writes=[tk])
                    cp(P, "dve" if (c0 // 1056) % 2 else "pool", w[:, kc, c0:c1], t[:, 0:c1 - c0], [tk], [])
        P.stage_end()
    return ws


def rms_transpose(k, st, rings, src_ap, normcol, t0):
    P = k.P
    xt, kx = rings["x"].next()
    P.dma("sp", xt[:], src_ap[t0:t0 + 512, :].rearrange("(j p) d -> p j d", p=128), writes=[kx])
    return xt, kx


def rms_transpose_compute(k, rings, xt, kx, normname):
    P = k.P
    ss, kss = rings["ss"].next()
    junk, kj = rings["junk"].next()
    for j in range(4):
        act(P, junk[:], xt[:, j, :], AF.Square, [kx], [kss, kj], accum_out=ss[:, j:j + 1])
    rsqrt(P, ss[:, 0:4], ss[:, 0:4], 1.0 / D, RMS_EPS, [kss], [kss])
    xn, kxn = rings["xn"].next()
    for j in range(4):
        if j % 2 == 0:
            ts(P, "dve", xn[:, j, :], xt[:, j, :], ss[:, j:j + 1], None, ALU.mult, None, [kx, kss], [kxn])
        else:
            act(P, xn[:, j, :], xt[:, j, :], AF.Copy, [kx, kss], [kxn], scale=ss[:, j:j + 1])
    xnT, kT = rings["xnT"].next()
    for c in range(8):
        ps, kp = rings["pst"].next()
        for j in range(4):
            mm(P, ps[:, 128 * j:128 * j + 128], xn[:, j, 128 * c:128 * c + 128], k.cb["ident"][:], True, True,
               [kxn, "cb_ident"], [kp])
        if c % 2 == 0:
            act(P, xnT[:, c, :], ps[:], AF.Copy, [kp, "cols"], [kT], scale=col(k, normname, c))
        else:
            ts(P, "dve", xnT[:, c, :], ps[:], col(k, normname, c), None, ALU.mult, None, [kp, "cols"], [kT])
    return xnT, kT


def in_rings(k, st):
    nc = k.nc
    return {"x": Ring(nc, st, "xt", [128, 4, D], F32, 2), "ss": Ring(nc, st, "ss", [128, 4], F32, 2),
            "junk": Ring(nc, st, "junk", [128, D], BF16, 1), "xn": Ring(nc, st, "xn", [128, 4, D], BF16, 2),
            "xnT": Ring(nc, st, "xnT", [128, 8, 512], BF16, 2),
            "pst": Ring(nc, st, "pst", [128, 512], F32, 2, psum=True)}


def seq_pos(k, t0):
    for off, S in zip(k.seq_off, k.seqs):
        if off <= t0 < off + S:
            return t0 - off
    raise ValueError


def stage1(k):
    P, nc, T = k.P, k.nc, k.T
    with contextlib.ExitStack() as st:
        W, = load_weights(k, st, [("w0", k.p["w_in0"], 8, EVEN_COLS, "W")])
        R = in_rings(k, st)
        psr = Ring(nc, st, "ps1", [128, 512], F32, 4, psum=True)
        psrot = Ring(nc, st, "psrot", [128, 512], F32, 2, psum=True)
        ost = Ring(nc, st, "ost", [128, 512], BF16, 6)
        qraw = Ring(nc, st, "qraw", [128, 512], BF16, 2)
        ra = Ring(nc, st, "ra", [128, 512], F32, 2)
        rb = Ring(nc, st, "rb", [128, 512], F32, 2)
        cs = Ring(nc, st, "cs", [128, 2, 512], F32, 2)
        ntile = T // 512
        nxt = rms_transpose(k, st, R, k.x, "norm0", 0)
        for ti in range(ntile):
            t0 = ti * 512
            xt, kx = nxt
            if ti + 1 < ntile:
                nxt = rms_transpose(k, st, R, k.x, "norm0", t0 + 512)
            pos = seq_pos(k, t0)
            cst, kcs = cs.next()
            P.dma("sp", cst[:, 0, :], k.c["cos"][:, pos:pos + 512], writes=[kcs])
            P.dma("sp", cst[:, 1, :], k.c["sin"][:, pos:pos + 512], writes=[kcs])
            xnT, kT = rms_transpose_compute(k, R, xt, kx, "norm0")
            for oc in range(33):
                if 25 <= oc < 29:
                    j = oc - 25
                    ps, kp = psr.next()
                    for kc in range(8):
                        mm(P, ps[:], xnT[:, kc, 128 * j:128 * j + 128], W[:, kc, 3200:3712], kc == 0, kc == 7,
                           [kT, "W"], [kp])
                    o, ko = ost.next()
                    cp(P, "act" if j % 2 else "dve", o[:], ps[:], [kp], [ko])
                    P.dma("sp", k.VB[t0 + 128 * j:t0 + 128 * j + 128, :], o[:], reads=[ko], writes=[("VB", ti)])
                    continue
                ps, kp = psr.next()
                for kc in range(8):
                    mm(P, ps[:], W[:, kc, 128 * oc:128 * oc + 128], xnT[:, kc, :], kc == 0, kc == 7, [kT, "W"], [kp])
                o, ko = ost.next()
                if oc < 13:
                    cp(P, "act" if oc % 2 else "dve", o[:], ps[:], [kp], [ko])
                    P.dma("sp", k.RW_T[128 * oc:128 * oc + 128, t0:t0 + 512], o[:], reads=[ko], writes=[("RW_T", ti)])
                elif oc < 17 or oc >= 29:
                    act(P, o[:], ps[:], AF.Silu, [kp], [ko])
                    dst = k.GA_T if oc < 17 else k.GB_T
                    r0 = 128 * (oc - 13) if oc < 17 else 128 * (oc - 29)
                    P.dma("sp", dst[r0:r0 + 128, t0:t0 + 512], o[:], reads=[ko],
                          writes=[("GA_T" if oc < 17 else "GB_T", ti)])
                else:
                    qr, kq = qraw.next()
                    cp(P, "act", qr[:], ps[:], [kp], [kq])
                    pr, kpr = psrot.next()
                    mm(P, pr[:], k.cb["rot"][:], qr[:], True, True, [kq, "cb_rot"], [kpr])
                    a, ka = ra.next()
                    tt(P, "pool", a[:], qr[:], cst[:, 0, :], ALU.mult, [kq, kcs], [ka])
                    b, kb = rb.next()
                    tt(P, "dve", b[:], pr[:], cst[:, 1, :], ALU.mult, [kpr, kcs], [kb])
                    tt(P, "pool", o[:], a[:], b[:], ALU.add, [ka, kb], [ko])
                    r0 = 128 * (oc - 17)
                    P.dma("sp", k.QK_T[r0:r0 + 128, t0:t0 + 512], o[:], reads=[ko], writes=[("QK_T", ti)])
        P.stage_end()


class SRing:
    def __init__(self, items):
        self.items = items
        self.i = -1

    def next(self):
        self.i = (self.i + 1) % len(self.items)
        return self.items[self.i]


GLA_CUT = 0
PATTERNS = ((128, 1), (512, 4), (2048, 16))


def stage2_attn(k):
    P, nc = k.P, k.nc
    smax = max(k.seqs)
    with contextlib.ExitStack() as st:
        qsr = Ring(nc, st, "qs", [128, smax], BF16, 1)
        ksr = Ring(nc, st, "ks", [128, smax], BF16, 1)
        qdr = {d: Ring(nc, st, "qd%d" % d, [128, smax], BF16, 1) for d in (4, 16)}
        kdr = {d: Ring(nc, st, "kd%d" % d, [128, smax], BF16, 1) for d in (4, 16)}
        acc = [st.enter_context(nc.sbuf_tensor("acc%d" % h, [65, smax], F32)) for h in range(2)]
        vtr = Ring(nc, st, "vt", [128, 2, 65], BF16, 6)
        for t, key in zip(vtr.tiles, vtr.keys):
            P.op("pool", (lambda t: lambda e: e.memset(t[:], 1.0))(t), writes=[key])
        pst = [st.enter_context(nc.psum_tensor("psS%d" % i, [128, 512], F32)) for i in range(2)]
        psr = SRing([(pst[i], "psS%d" % i) for i in range(2)])
        po = [[(st.enter_context(nc.psum_tensor("psO%d_%d" % (h, b), [65, 128], F32)), "psO%d_%d" % (h, b))
               for b in range(2)] for h in range(2)]
        pdr = Ring(nc, st, "psD", [64, 512], F32, 2, psum=True)
        ptr_ = Ring(nc, st, "pt", [128, 256], BF16, 4)
        pmr = Ring(nc, st, "pm", [128, 256], BF16, 4)
        rcr = Ring(nc, st, "rc", [64, 512], F32, 2)
        tmr = Ring(nc, st, "tm", [64, 512], F32, 2)
        gr = Ring(nc, st, "gb", [64, 512], BF16, 2)
        outr = Ring(nc, st, "ao", [64, 512], BF16, 2)
        cnt = 0
        dslot = [0]
        for s0, S in zip(k.seq_off, k.seqs):
            nchunk = S // 128
            for hp in range(4):
                qs, kq = qsr.next()
                ks_, kk_ = ksr.next()
                P.dma("sp", qs[:, 0:S], k.QK_T[128 * hp:128 * hp + 128, s0:s0 + S], writes=[kq])
                P.dma("sp", ks_[:, 0:S], k.QK_T[512 + 128 * hp:512 + 128 * hp + 128, s0:s0 + S], writes=[kk_])
                qv, kv_ = {1: (qs, kq)}, {1: (ks_, kk_)}
                for d in (4, 16):
                    qd, kqd = qdr[d].next()
                    kd, kkd = kdr[d].next()
                    cp(P, "act", qd[:, 0:S].rearrange("p (r i) -> p r i", r=d),
                       qs[:, 0:S].rearrange("p (i r) -> p r i", r=d), [kq], [kqd])
                    cp(P, "dve", kd[:, 0:S].rearrange("p (r i) -> p r i", r=d),
                       ks_[:, 0:S].rearrange("p (i r) -> p r i", r=d), [kk_], [kkd])
                    qv[d] = (qd, kqd)
                    kv_[d] = (kd, kkd)
                for h in range(2):
                    P.op("pool", (lambda a: lambda e: e.memset(a, 0.0))(acc[h][:, 0:S]),
                         writes=[("acc", h, c) for c in range(nchunk)])
                for (win, d) in PATTERNS:
                    sub = S // d
                    nb = sub // 128
                    assert sub % 128 == 0 and nb >= 1
                    qt, kqt = qv[d]
                    kt, kkt = kv_[d]
                    for r in range(d):
                        for kb in range(nb + 1):
                            lo = max(0, 128 * kb - 64)
                            hi = min(sub, 128 * kb + 64)
                            nk = hi - lo
                            v, kv = vtr.next()
                            rows = k.VB[s0 + r + d * lo:s0 + r + d * (hi - 1) + 1:d, 128 * hp:128 * hp + 128]
                            P.dma("sp", v[0:nk, :, 0:64], rows.rearrange("p (h e) -> p h e", h=2), writes=[kv])
                            qb_lo = max(0, kb - 1)
                            qb_hi = min(nb - 1, kb)
                            nq = 128 * (qb_hi - qb_lo + 1)
                            if kb == 0:
                                mask = k.cb["a_0"][0:64, 0:128]
                            elif kb == nb:
                                mask = k.cb["a_g"][0:64, 0:128]
                            else:
                                mask = k.cb["a_g"][:, 0:256]
                            for h in range(2):
                                ph = slice(64 * h, 64 * h + 64)
                                keys = kt[ph, r * sub + lo:r * sub + hi]
                                qry = qt[ph, r * sub + 128 * qb_lo:r * sub + 128 * qb_lo + nq]
                                ps, kps = psr.next()
                                mm(P, ps[0:nk, 0:nq], keys, qry, True, True, [kqt, kkt], [kps])
                                e, ke = ptr_.next()
                                act(P, e[0:nk, 0:nq], ps[0:nk, 0:nq], AF.Exp, [kps], [ke], scale=0.125)
                                m, km = pmr.next()
                                cnt += 1
                                tt(P, "dve" if cnt % 3 else "pool", m[0:nk, 0:nq], e[0:nk, 0:nq], mask, ALU.mult,
                                   [ke, "cb_a_g", "cb_a_0"], [km])
                                for qb in range(qb_lo, qb_hi + 1):
                                    first = (kb == qb)
                                    pv, kpo = po[h][qb % 2]
                                    c0 = 128 * (qb - qb_lo)
                                    mm(P, pv[0:65, :], v[0:nk, h, 0:65], m[0:nk, c0:c0 + 128], first, not first,
                                       [kv, km], [kpo])
                                    if not first:
                                        a0 = r + d * 128 * qb
                                        pos = acc[h][0:65, a0:a0 + d * 127 + 1:d]
                                        ck = [("acc", h, c) for c in range(d * qb, d * qb + d)]
                                        tt(P, "dve", pos, pos, pv[0:65, :], ALU.add, [kpo] + ck, ck)
                for h in range(2):
                    for c0 in range(0, S, 512):
                        ck = [("acc", h, c) for c in range(c0 // 128, c0 // 128 + 4)]
                        pd, kpd = pdr.next()
                        mm(P, pd[0:64, :], k.cf["ones"][64:65, 0:64], acc[h][64:65, c0:c0 + 512], True, True,
                           ["cf_ones"] + ck, [kpd])
                        rc, krc = rcr.next()
                        act(P, rc[:], pd[0:64, :], AF.Ln, [kpd], [krc])
                        act(P, rc[:], rc[:], AF.Exp, [krc], [krc], scale=-1.0)
                        g, kg = gr.next()
                        r0 = 128 * hp + 64 * h
                        P.dma("sp", g[:], k.GB_T[r0:r0 + 64, s0 + c0:s0 + c0 + 512], writes=[kg])
                        tm, ktm = tmr.next()
                        tt(P, "pool", tm[:], acc[h][0:64, c0:c0 + 512], rc[:], ALU.mult, [krc] + ck, [ktm])
                        o, ko = outr.next()
                        tt(P, "dve", o[:], tm[:], g[:], ALU.mult, [ktm, kg], [ko])
                        P.dma("sp", k.Y0_T[512 + r0:512 + r0 + 64, s0 + c0:s0 + c0 + 512], o[:], reads=[ko],
                              writes=[("Y0_T", r0, c0)])
        P.stage_end()


def stage3_rwkv(k):
    P, nc = k.P, k.nc
    with contextlib.ExitStack() as st:
        A = lambda n, sh, dt=F32: st.enter_context(nc.sbuf_tensor(n, sh, dt))
        RG = lambda n, sh, dt, c: Ring(nc, st, n, sh, dt, c)
        raw = {n: RG("raw_" + n, [128, 514], BF16, 2) for n in ("r", "k", "v", "la")}
        f = {n: RG("f_" + n, [128, 512], F32, 1) for n in
             ("t1", "t2", "rs", "ks", "las", "sg", "lr", "kkr", "sq", "rn", "kk", "tka", "kd", "beta",
              "ci", "cf", "ce", "E1", "E2", "E3", "E4", "lro", "e1", "e2", "e3", "e4", "e6")}
        f["vs"] = RG("f_vs", [128, 512], F32, 3)
        prb = RG("prb", [128, 512], BF16, 3)
        twal = RG("twal", [128, 512], BF16, 1)
        pc = RG("pc", [128, 8], F32, 3)
        AR = RG("AR", [128, 4, 2, 128], BF16, 3)
        ob = {n: RG("o_" + n, [128, 512], BF16, 3) for n in ("Bt", "Kt", "Be", "Ke", "vb")}
        TB = RG("TB", [128, 512], BF16, 8)
        G13r = RG("G13r", [128, 384], BF16, 4)
        G2r = RG("G2r", [128, 256], BF16, 4)
        G13 = RG("G13", [128, 384], BF16, 16)
        G2 = RG("G2", [128, 256], BF16, 16)
        Z0 = RG("Z0", [128, 128], BF16, 8)
        ZP = RG("ZP", [128, 384], BF16, 16)
        Z6 = RG("Z6", [128, 128], BF16, 16)
        UT = RG("UT", [128, 2, 128], BF16, 3)
        for t_, k_ in zip(UT.tiles, UT.keys):
            P.op("pool", (lambda a: lambda e: e.memset(a, 0.0))(t_[:]), writes=[k_])
        AW = RG("AW", [128, 512], BF16, 2)
        YT = RG("YT", [128, 512], F32, 2)
        yf = RG("yf", [128, 512], F32, 2)
        gat = RG("gat", [128, 512], BF16, 2)
        yo = RG("yo", [128, 512], BF16, 2)
        ST2 = A("ST2", [128, 128])
        STb2 = A("STb2", [128, 128], BF16)
        pbank = [st.enter_context(nc.psum_tensor("pg%d" % i, [128, 512], F32)) for i in range(7)]
        psg = SRing([(pbank[i], "pg%d" % i) for i in range(4)])
        psc = SRing([(pbank[i], "pg%d" % i) for i in range(4, 6)])
        pse = SRing([(pbank[i], "pg%d" % i) for i in range(6, 7)])
        psy = st.enter_context(nc.psum_tensor("psy", [128, 512], F32))
        idb, bd1 = k.cb["ident"], k.cf["bd1"]
        bd1b = k.cb["bd1"]
        sqb = RG("sqb", [128, 512], BF16, 2)
        k.o_omk2 = k.o_omk + 4
        ts(P, "dve", k.cols[:, k.o_omk2:k.o_omk2 + 4], k.cols[:, k.o_omk:k.o_omk + 4], 2.0, None, ALU.mult, None,
           ["cols"], ["cols"])
        ev = [0]
        lvl = [0]

        def evac_eng():
            ev[0] += 1
            return "act" if ev[0] % 2 else "dve"

        def cpx(out, in_, reads, writes):
            cp(P, evac_eng(), out, in_, reads, writes)
        v3 = lambda a: a.rearrange("p (c j) -> p c j", j=64)
        v4 = lambda a: a.rearrange("p (b t) -> p b t", t=128)

        def elem(c):
            s0, S, hp, z, ti, ntile = c["item"]
            t0 = s0 + 512 * ti
            c["t0"] = t0
            lo_edge, hi_edge = (ti == 0), (ti == ntile - 1)
            rt = {}
            for n, r0 in (("r", 128 * hp), ("k", 512 + 128 * hp), ("v", 1024 + 128 * hp), ("la", 1536)):
                t, key = raw[n].next()
                if lo_edge:
                    P.op("pool", (lambda a: lambda e: e.memset(a, 0.0))(t[:, 0:1]), writes=[key])
                if hi_edge:
                    P.op("pool", (lambda a: lambda e: e.memset(a, 0.0))(t[:, 513:514]), writes=[key])
                c0 = 1 if lo_edge else 0
                c1 = 513 if hi_edge else 514
                P.dma("sp", t[:, c0:c1], k.RW_T[r0:r0 + 128, t0 - 1 + c0:t0 - 1 + c1], writes=[key])
                rt[n] = (t, key)
            yield
            sh = {}
            for n, mc, dst in (("r", hp, "rs"), ("k", 4 + hp, "ks"), ("v", 8 + hp, "vs"), ("la", 12, "las")):
                t, key = rt[n]
                t1, k1 = f["t1"].next()
                t2, k2 = f["t2"].next()
                o, ko = f[dst].next()
                act(P, t1[:], t[:, 0:512], AF.Copy, [key, "cols"], [k1], scale=col(k, "mup", mc))
                stt(P, t2[:], t[:, 2:514], col(k, "mun", mc), t1[:], ALU.mult, ALU.add, [key, k1, "cols"], [k2])
                stt(P, o[:], t[:, 1:513], k.cols[:, k.o_c0 + mc:k.o_c0 + mc + 1], t2[:], ALU.mult, ALU.add,
                    [key, k2, "cols"], [ko])
                sh[dst] = (o, ko)
                yield
            rs, krs = sh["rs"]; ks_, kks = sh["ks"]; vs, kvs = sh["vs"]; las, klas = sh["las"]
            c["vs"] = (vs, kvs)
            tw, ktw = twal.next()
            act(P, tw[0:64, :], las[0:64, :], AF.Tanh, [klas], [ktw])
            cp(P, "dve", tw[64:128, :], las[64:128, :], [klas], [ktw])
            hc = slice(128 * hp, 128 * hp + 128)

            def lora_sig(zz, wsel, dst, kdst, bias_name):
                rows = slice(0, 64) if wsel == "w" else slice(64, 128)
                pw, kpw = pse.next()
                mm(P, pw[:], k.lora[rows, zz, hc], tw[rows, :], True, True, [ktw, "lora"], [kpw])
                act(P, dst[:], pw[:], AF.Sigmoid, [kpw, "cols"], [kdst], bias=col(k, bias_name, hp))
            sg, ksg = f["sg"].next()
            lr, klr = f["lr"].next()
            lora_sig(z, "w", sg, ksg, "w0_%d" % z)
            lora_sig(z, "a", lr, klr, "a0_%d" % z)
            yield
            sq, ksq = sqb.next()
            act(P, sq[:], ks_[:], AF.Square, [kks, "cols"], [ksq], scale=col(k, "k_k", hp))
            rn, krn = f["rn"].next()
            pn, kpn = pse.next()
            mm(P, pn[:], bd1b[:], sq[:], True, True, [ksq, "cb_bd1"], [kpn])
            act(P, rn[:], pn[:], AF.Ln, [kpn], [krn], bias=1e-18)
            act(P, rn[:], rn[:], AF.Exp, [krn], [krn], scale=-0.5)
            kk, kkk = f["kk"].next()
            stt(P, kk[:], ks_[:], col(k, "k_k", hp), rn[:], ALU.mult, ALU.mult, [kks, krn, "cols"], [kkk])
            yield
            tka, ktka = f["tka"].next()
            ts(P, "dve", tka[:], lr[:], col(k, "k_a", hp), k.cols[:, k.o_omk + hp:k.o_omk + hp + 1],
               ALU.mult, ALU.add, [klr, "cols"], [ktka])
            kd, kkd = f["kd"].next()
            tt(P, "pool", kd[:], ks_[:], tka[:], ALU.mult, [kks, ktka], [kkd])
            beta, kbeta = f["beta"].next()
            tt(P, "pool", beta[:], kk[:], lr[:], ALU.mult, [kkk, klr], [kbeta])
            cf_, kcf = f["cf"].next()
            P.op("dve", (lambda o, a, b: lambda e: e.tensor_tensor_scan(
                out=o, data0=a, data1=b, initial=0.0, op0=ALU.mult, op1=ALU.add))(
                cf_[:], k.cf["rmask"][:], sg[:]), [ksg, "cf_rmask"], [kcf])
            tot = cf_[:, 63:512:64]
            totb = tot.unsqueeze(2).to_broadcast([128, 8, 64])
            if z == 0:
                ci, kci = cf_, kcf
            else:
                ci, kci = f["ci"].next()
                tt(P, "pool", ci[:], sg[:], cf_[:], ALU.subtract, [ksg, kcf], [kci])
                tt(P, "dve", v3(ci[:]), v3(ci[:]), totb, ALU.add, [kci, kcf], [kci])
            ce, kce = f["ce"].next()
            tt(P, "pool", ce[:], ci[:], sg[:], ALU.subtract, [kci, ksg], [kce])
            yield
            E1, kE1 = f["E1"].next(); E2, kE2 = f["E2"].next(); E3, kE3 = f["E3"].next(); E4, kE4 = f["E4"].next()
            act(P, E1[:], ce[:], AF.Exp, [kce], [kE1], scale=-DECAY_C)
            act(P, E2[:], ci[:], AF.Exp, [kci], [kE2], scale=DECAY_C)
            act(P, E3[:], ci[:], AF.Exp, [kci], [kE3], scale=-DECAY_C)
            pct, kpc = pc.next()
            act(P, pct[:], tot, AF.Exp, [kcf], [kpc], scale=-DECAY_C)
            c["pc"] = (pct, kpc)
            pcb = pct[:].unsqueeze(2).to_broadcast([128, 8, 64])
            tt(P, "dve", v3(E4[:]), v3(E2[:]), pcb, ALU.mult, [kE2, kpc], [kE4])
            yield
            art, kar = AR.next()
            stt(P, art[:, :, 0, :], v4(kk[:]), -1.0, v4(E1[:]), ALU.mult, ALU.mult, [kkk, kE1], [kar])
            tt(P, "pool", art[:, :, 1, :], v4(rs[:]), v4(E3[:]), ALU.mult, [krs, kE3], [kar])
            Bt, kBt = ob["Bt"].next(); Kt, kKt = ob["Kt"].next()
            Be, kBe = ob["Be"].next(); Ke, kKe = ob["Ke"].next(); vb, kvb = ob["vb"].next()
            tt(P, "pool", Bt[:], beta[:], E2[:], ALU.mult, [kbeta, kE2], [kBt])
            tt(P, "pool", Kt[:], kd[:], E2[:], ALU.mult, [kkd, kE2], [kKt])
            yield
            tt(P, "pool", Be[:], beta[:], E4[:], ALU.mult, [kbeta, kE4], [kBe])
            tt(P, "pool", Ke[:], kd[:], E4[:], ALU.mult, [kkd, kE4], [kKe])
            cp(P, "act", vb[:], vs[:], [kvs], [kvb])
            c.update(art=(art, kar), Bt=(Bt, kBt), Kt=(Kt, kKt), Be=(Be, kBe), Ke=(Ke, kKe), vb=(vb, kvb))
            if z == 1:
                yield
                lro, klro = f["lro"].next()
                lora_sig(0, "a", lro, klro, "a0_0")
                tt(P, "pool", lro[:], lro[:], lr[:], ALU.add, [klro, klr], [klro])
                ts(P, "dve", lro[:], lro[:], col(k, "k_a", hp), k.cols[:, k.o_omk2 + hp:k.o_omk2 + hp + 1],
                   ALU.mult, ALU.add, [klro, "cols"], [klro])
                tt(P, "pool", lro[:], lro[:], ks_[:], ALU.mult, [klro, kks], [klro])
                pr_, kpr_ = prb.next()
                stt(P, pr_[:], rs[:], col(k, "r_k", hp), lro[:], ALU.mult, ALU.mult, [krs, klro, "cols"], [kpr_])
                c["pr"] = (pr_, kpr_)

        def wave(c):
            s0, S, hp, z, ti, ntile = c["item"]
            mask = k.cb["m_f" if z == 0 else "m_b"]
            mkeys = ["cb_m_f", "cb_m_b"]
            art, kar = c["art"]; Bt, kBt = c["Bt"]; Kt, kKt = c["Kt"]
            Be, kBe = c["Be"]; Ke, kKe = c["Ke"]; vb, kvb = c["vb"]
            border = list(range(4)) if z == 0 else list(range(3, -1, -1))
            c["border"] = border
            aw, kaw = AW.next()
            c["aw"] = (aw, kaw)
            tb = {}
            for b in border:
                bc = slice(128 * b, 128 * b + 128)
                p_, kp_ = psg.next()
                for i, (src, skey) in enumerate(((vb, kvb), (Be, kBe), (Ke, kKe))):
                    mm(P, p_[:, 128 * i:128 * i + 128], src[:, bc], idb[:], True, True, [skey, "cb_ident"], [kp_])
                mm(P, p_[:, 384:512], vb[64:128, bc], idb[64:128, :], True, True, [kvb, "cb_ident"], [kp_])
                t, kt = TB.next()
                cpx(t[:], p_[:, 0:512], [kp_], [kt])
                tb[b] = (t, kt)
                yield
            c["tb"] = tb
            units = [(b, h) for b in border for h in range(2)]
            U = {}
            for ui, (b, h) in enumerate(units):
                bc = slice(128 * b, 128 * b + 128)
                ph = slice(64 * h, 64 * h + 64)
                arh = art[ph, b, :, :].rearrange("p a t -> p (a t)")
                p1, kp1 = psg.next()
                mm(P, p1[:, 0:256], Bt[ph, bc], arh, True, True, [kBt, kar], [kp1])
                mm(P, p1[:, 256:384], art[ph, b, 0, :], Bt[ph, bc], True, True, [kBt, kar], [kp1])
                g13, kg13 = G13.next()
                tt(P, "dve", g13[:], p1[:, 0:384], mask[:], ALU.mult, [kp1] + mkeys, [kg13])
                p2, kp2 = psg.next()
                mm(P, p2[:, 0:256], Kt[ph, bc], arh, True, True, [kKt, kar], [kp2])
                g2r, kg2r = G2r.next()
                cp(P, "act", g2r[:], p2[:, 0:256], [kp2], [kg2r])
                g2, kg2 = G2.next()
                tt(P, "pool", g2[:], g2r[:], mask[:, 0:256], ALU.mult, [kg2r] + mkeys, [kg2])
                U[(b, h)] = dict(g13=(g13, kg13), g2=(g2, kg2))
                yield
            for (b, h) in units:
                u = U[(b, h)]
                ph = slice(64 * h, 64 * h + 64)
                g2, kg2 = u["g2"]
                vt, kvt = tb[b]
                za = slice(0, 64) if h == 0 else slice(64, 128)
                zv = slice(64, 128) if h == 0 else slice(0, 64)
                p4, kp4 = psg.next()
                mm(P, p4[:, za], art[ph, b, 0, :], idb[ph, 64 * h:64 * h + 64], True, True, [kar, "cb_ident"], [kp4])
                mm(P, p4[:, zv], g2[:, 0:128], vt[:, 64 * h:64 * h + 64], True, True, [kg2, kvt], [kp4])
                z0, kz0 = Z0.next()
                cpx(z0[:], p4[:, 0:128], [kp4], [kz0])
                g13, kg13 = u["g13"]
                u["Z"] = (z0[:, 0:128], kz0)
                u["P"] = (g13[:, 0:128], kg13)
                u["PT"] = (g13[:, 256:384], kg13)
                if h == 1:
                    yield
            for j in range(6):
                last = (j == 5)
                for (b, h) in units:
                    u = U[(b, h)]
                    Zt, kZ = u["Z"]; Pt, kP = u["P"]; PTt, kPT = u["PT"]
                    ps, kps = psg.next()
                    if not last:
                        need_pt = (j < 4)
                        if j == 0:
                            mm(P, ps[:, 0:128], Pt, Zt, True, True, [kP, kZ], [kps])
                            mm(P, ps[:, 128:256], Pt, PTt, True, True, [kP, kPT], [kps])
                        elif need_pt:
                            mm(P, ps[:, 0:256], Pt, u["ZPT"], True, True, [kP, kZ], [kps])
                        else:
                            mm(P, ps[:, 0:128], Pt, Zt, True, True, [kP, kZ], [kps])
                        mm(P, ps[:, 256:384], PTt, Pt, True, True, [kP, kPT], [kps])
                        zn, kzn = ZP.next()
                        tt(P, "dve", zn[:, 0:128], ps[:, 0:128], Zt, ALU.add, [kps, kZ], [kzn])
                        lvl[0] += 1
                        if need_pt:
                            cp(P, "act" if lvl[0] % 8 else "dve", zn[:, 128:384], ps[:, 128:384], [kps], [kzn])
                        else:
                            cp(P, "act" if lvl[0] % 8 else "dve", zn[:, 256:384], ps[:, 256:384], [kps], [kzn])
                        u["Z"] = (zn[:, 0:128], kzn)
                        u["PT"] = (zn[:, 128:256], kzn)
                        u["P"] = (zn[:, 256:384], kzn)
                        u["ZPT"] = zn[:, 0:256]
                    else:
                        mm(P, ps[:, 0:128], Pt, Zt, True, True, [kP, kZ], [kps])
                        z6, kz6 = Z6.next()
                        tt(P, "dve", z6[:], ps[:, 0:128], Zt, ALU.add, [kps, kZ], [kz6])
                        u["z6"] = (z6, kz6)
                    if h == 1:
                        yield
            for (b, h) in units:
                u = U[(b, h)]
                ph = slice(64 * h, 64 * h + 64)
                bc = slice(128 * b, 128 * b + 128)
                z6, kz6 = u["z6"]
                p5, kp5 = psg.next()
                mm(P, p5[:, 0:128], z6[:], idb[:], True, True, [kz6, "cb_ident"], [kp5])
                cpx(aw[ph, bc], p5[ph, 0:128], [kp5], [kaw])
                if h == 1:
                    yield
            c["U"] = U

        def scan(c):
            s0, S, hp, z, ti, ntile = c["item"]
            t0 = c["t0"]
            first = (ti == 0) if z == 0 else (ti == ntile - 1)
            if first:
                P.op("pool", lambda e: e.memset(ST2[:], 0.0), writes=[("ST2", 0), ("ST2", 1)])
                P.op("pool", lambda e: e.memset(STb2[:], 0.0), writes=[("STb2", 0), ("STb2", 1)])
            art, kar = c["art"]
            pct, kpc = c["pc"]
            aw, kaw = c["aw"]
            U = c["U"]
            corder = (0, 1) if z == 0 else (1, 0)
            for b in c["border"]:
                bc = slice(128 * b, 128 * b + 128)
                tbt, ktb = c["tb"][b]
                vt, bet, ket = tbt[:, 0:128], tbt[:, 128:256], tbt[:, 256:384]
                ut, kut = UT.next()
                for cc in corder:
                    rows = slice(64 * cc, 64 * cc + 64)
                    tk = slice(128 * b + 64 * cc, 128 * b + 64 * cc + 64)
                    cidx = 2 * b + cc
                    hop = []
                    for h in range(2):
                        ph = slice(64 * h, 64 * h + 64)
                        pu, kpu = psc.next()
                        mm(P, pu[:, 0:64], aw[ph, bc], STb2[ph, 64 * h:64 * h + 64], True, True, [kaw, ("STb2", h)], [kpu])
                        hop.append((pu, kpu))
                    yield
                    hop2 = []
                    vtz = tbt[:, 384:512]
                    for h in range(2):
                        u = U[(b, h)]
                        z6, kz6 = u["z6"]
                        pu, kpu = hop[h]
                        if h == 0:
                            tt(P, "dve", ut[rows, 0, 0:64], pu[rows, 0:64], z6[rows, 64:128], ALU.add, [kpu, kz6], [kut])
                        else:
                            tt(P, "dve", ut[rows, 1, 64:128], pu[rows, 0:64], z6[rows, 0:64], ALU.add, [kpu, kz6], [kut])
                    gA, kgA = U[(b, 0)]["g13"]; g2A, kg2A = U[(b, 0)]["g2"]
                    gB, kgB = U[(b, 1)]["g13"]; g2B, kg2B = U[(b, 1)]["g2"]
                    cs_ = slice(128 + 64 * cc, 128 + 64 * cc + 64)
                    mm(P, psy[:, tk], STb2[:, :], art[:, b, 1, 64 * cc:64 * cc + 64], True, False,
                       [("STb2", 0), ("STb2", 1), kar], ["psy"])
                    mm(P, psy[0:64, tk], ut[rows, 0, 0:64], gA[rows, cs_], False, False, [kut, kgA], ["psy"])
                    mm(P, psy[0:64, tk], vt[rows, 0:64], g2A[rows, cs_], False, False, [ktb, kg2A], ["psy"])
                    mm(P, psy[:, tk], ut[rows, 1, :], gB[rows, cs_], False, False, [kut, kgB], ["psy"])
                    mm(P, psy[:, tk], vtz[rows, :], g2B[rows, cs_], False, True, [ktb, kg2B], ["psy"])
                    for h in range(2):
                        pss, kpss = psc.next()
                        if h == 0:
                            mm(P, pss[0:64, 0:64], bet[rows, 0:64], ut[rows, 0, 0:64], True, False, [ktb, kut], [kpss])
                            mm(P, pss[0:64, 0:64], ket[rows, 0:64], vt[rows, 0:64], False, True, [ktb], [kpss])
                        else:
                            mm(P, pss[:, 0:64], bet[rows, 0:128], ut[rows, 1, 64:128], True, False, [ktb, kut], [kpss])
                            mm(P, pss[:, 0:64], ket[rows, 0:128], vt[rows, 64:128], False, True, [ktb], [kpss])
                        hop2.append((pss, kpss))
                    yield
                    for h in range(2):
                        ph = slice(64 * h, 64 * h + 64)
                        pss, kpss = hop2[h]
                        sv = ST2[ph, 64 * h:64 * h + 64]
                        stt(P, sv, sv, pct[ph, cidx:cidx + 1], pss[ph, 0:64], ALU.mult, ALU.add,
                            [kpss, kpc, ("ST2", h)], [("ST2", h)])
                        cp(P, "act", STb2[ph, 64 * h:64 * h + 64], sv, [("ST2", h)], [("STb2", h)])
                    yield
            yt, kyt = YT.next()
            cp(P, "act", yt[:], psy[:], ["psy"], [kyt])
            rr = slice(128 * hp, 128 * hp + 128)
            if z == 0:
                P.dma("sp", k.YF_T[rr, t0:t0 + 512], yt[:], reads=[kyt], writes=[("YF", hp, t0)])
                return
            vs, kvs = c["vs"]
            pr_, kpr_ = c["pr"]
            yft, kyf = yf.next()
            P.dma("sp", yft[:], k.YF_T[rr, t0:t0 + 512], reads=[("YF", hp, t0)], writes=[kyf])
            gt, kgt = gat.next()
            P.dma("sp", gt[:], k.GA_T[rr, t0:t0 + 512], writes=[kgt])
            y, ky = f["e1"].next()
            tt(P, "pool", y[:], yt[:], yft[:], ALU.add, [kyt, kyf], [ky])
            d_, kd_ = f["e2"].next()
            sq2, ksq2 = f["e3"].next()
            rstd, krstd = f["e4"].next()
            pm, kpm = pse.next()
            mm(P, pm[:], bd1[:], y[:], True, True, [ky, "cf_bd1"], [kpm])
            stt(P, d_[:], pm[:], -1.0 / 64, y[:], ALU.mult, ALU.add, [kpm, ky], [kd_])
            sq2, ksq2 = sqb.next()
            act(P, sq2[:], d_[:], AF.Square, [kd_], [ksq2])
            yield
            pv_, kpv_ = pse.next()
            mm(P, pv_[:], bd1b[:], sq2[:], True, True, [ksq2, "cb_bd1"], [kpv_])
            rsqrt(P, rstd[:], pv_[:], 1.0 / 64, GN_EPS, [kpv_], [krstd])
            tt(P, "pool", d_[:], d_[:], rstd[:], ALU.mult, [kd_, krstd], [kd_])
            ts(P, "dve", d_[:], d_[:], col(k, "gn_w", hp), col(k, "gn_b", hp), ALU.mult, ALU.add, [kd_, "cols"], [kd_])
            yield
            bo, kbo = f["e6"].next()
            pb2, kpb2 = pse.next()
            mm(P, pb2[:], bd1b[:], pr_[:], True, True, [kpr_, "cb_bd1"], [kpb2])
            tt(P, "dve", bo[:], pb2[:], vs[:], ALU.mult, [kpb2, kvs], [kbo])
            tt(P, "pool", bo[:], bo[:], d_[:], ALU.add, [kbo, kd_], [kbo])
            o_, ko_ = yo.next()
            tt(P, "dve", o_[:], bo[:], gt[:], ALU.mult, [kbo, kgt], [ko_])
            P.dma("sp", k.Y0_T[rr, t0:t0 + 512], o_[:], reads=[ko_], writes=[("Y0a", hp, t0)])

        items = []
        for s0, S in zip(k.seq_off, k.seqs):
            ntile = S // 512
            for hp in range(4):
                for z in range(2):
                    order = range(ntile) if z == 0 else range(ntile - 1, -1, -1)
                    for ti in order:
                        items.append(dict(item=(s0, S, hp, z, ti, ntile)))
        n = len(items)
        for step in range(n + 2):
            gens = []
            if step < n:
                gens.append(elem(items[step]))
            if 0 <= step - 1 < n:
                gens.append(wave(items[step - 1]))
            if 0 <= step - 2 < n:
                gens.append(scan(items[step - 2]))
            while gens:
                for g in list(gens):
                    try:
                        next(g)
                    except StopIteration:
                        gens.remove(g)
            if step - 2 >= 0:
                items[step - 2].clear()
        P.stage_end()


def out_proj_tile(k, R, W, ysrc, xsrc, t0, j, pso, xres_ring, dst=None):
    P = k.P
    yT, kyT = ysrc
    xt, kx = xsrc
    if dst is None:
        xr, kxr = xres_ring.next()
    else:
        xr, kxr = dst
    for half in range(2):
        ps, kp = pso.next()
        for kc in range(8):
            mm(P, ps[:], yT[:, kc, 128 * j:128 * j + 128], W[:, kc, 512 * half:512 * half + 512], kc == 0, kc == 7,
               [kyT, "Wo"], [kp])
        tt(P, "dve", xr[:, 512 * half:512 * half + 512], ps[:], xt[:, j, 512 * half:512 * half + 512], ALU.add,
           [kp, kx], [kxr])
    return xr, kxr


def stage4_out0(k):
    P, nc, T = k.P, k.nc, k.T
    with contextlib.ExitStack() as st:
        Wo, W = load_weights(k, st, [("wo0", k.p["w_out0"], 8, D, "Wo"), ("w1", k.p["w_in1"], 8, ODD_COLS, "W")])
        R = in_rings(k, st)
        yr = Ring(nc, st, "y0T", [128, 8, 512], BF16, 2)
        x1r = Ring(nc, st, "x1t", [128, 4, D], F32, 2)
        pso = Ring(nc, st, "pso", [128, 512], F32, 2, psum=True)
        psr = Ring(nc, st, "ps1", [128, 512], F32, 4, psum=True)
        ost = Ring(nc, st, "ost", [128, 512], BF16, 6)
        glr = Ring(nc, st, "glt", [16, 512], F32, 2)
        qscale = 128.0 ** -0.5
        ntile = T // 512

        def loads(t0):
            xt, kx = R["x"].next()
            P.dma("sp", xt[:], k.x[t0:t0 + 512, :].rearrange("(j p) d -> p j d", p=128), writes=[kx])
            yT, kyT = yr.next()
            P.dma("sp", yT[:], k.Y0_T[:, t0:t0 + 512].rearrange("(kc p) t -> p kc t", p=128), writes=[kyT])
            return (xt, kx), (yT, kyT)
        nxt = loads(0)
        for ti in range(ntile):
            t0 = ti * 512
            xsrc, ysrc = nxt
            if ti + 1 < ntile:
                nxt = loads(t0 + 512)
            x1, kx1 = x1r.next()
            for j in range(4):
                xr, kxr = out_proj_tile(k, R, Wo, ysrc, xsrc, t0, j, pso, None, dst=(x1[:, j, :], kx1))
                P.dma("sp", k.X1[t0 + 128 * j:t0 + 128 * j + 128, :], xr, reads=[kxr], writes=[("X1", ti, j)])
            xnT, kT = rms_transpose_compute(k, R, x1, kx1, "norm1")
            for oc in list(range(8)) + list(range(16, 24)):
                ps, kp = psr.next()
                c0 = 128 * oc if oc < 8 else 2064 + 128 * (oc - 16)
                for kc in range(8):
                    mm(P, ps[:], W[:, kc, c0:c0 + 128], xnT[:, kc, :], kc == 0, kc == 7, [kT, "W"], [kp])
                o, ko = ost.next()
                if oc < 4:
                    act(P, o[:], ps[:], AF.Copy, [kp], [ko], scale=qscale)
                elif oc < 8:
                    cp(P, "dve", o[:], ps[:], [kp], [ko])
                else:
                    act(P, o[:], ps[:], AF.Silu, [kp], [ko])
                if oc < 8:
                    P.dma("sp", k.Q1_T[128 * oc:128 * oc + 128, t0:t0 + 512], o[:], reads=[ko], writes=[("Q1", ti, oc)])
                else:
                    r0 = 128 * (oc - 16)
                    P.dma("sp", k.G1_T[r0:r0 + 128, t0:t0 + 512], o[:], reads=[ko], writes=[("G1", ti, oc)])
            ps, kp = psr.next()
            for kc in range(8):
                mm(P, ps[0:16, :], W[:, kc, 2048:2064], xnT[:, kc, :], kc == 0, kc == 7, [kT, "W"], [kp])
            gl, kgl = glr.next()
            cp(P, "act", gl[:], ps[0:16, :], [kp], [kgl])
            P.dma("sp", k.GL_T[:, t0:t0 + 512], gl[:], reads=[kgl], writes=[("GL", ti)])
            for j in range(4):
                for half in range(2):
                    ps, kp = psr.next()
                    c0 = 1024 + 512 * half
                    for kc in range(8):
                        mm(P, ps[:], xnT[:, kc, 128 * j:128 * j + 128], W[:, kc, c0:c0 + 512], kc == 0, kc == 7,
                           [kT, "W"], [kp])
                    o, ko = ost.next()
                    cp(P, "act" if half else "dve", o[:], ps[:], [kp], [ko])
                    P.dma("sp", k.V1[t0 + 128 * j:t0 + 128 * j + 128, 512 * half:512 * half + 512], o[:], reads=[ko],
                          writes=[("V1", ti, j, half)])
        P.stage_end()


def stage5_in1(k):
    pass


def stage6_gla(k):
    P, nc = k.P, k.nc
    with contextlib.ExitStack() as st:
        A = lambda n, sh, dt=F32: st.enter_context(nc.sbuf_tensor(n, sh, dt))
        RG = lambda n, sh, dt, c: Ring(nc, st, n, sh, dt, c)
        Wo7, = load_weights(k, st, [("wo1", k.p["w_out1"], 8, D, "Wo")])
        fn7 = A("fnorm", [128, D])
        P.dma("sp", fn7[:], k.p["fnorm"][:, :], writes=["fnorm"])
        x7r = RG("x1in", [128, 4, D], F32, 1)
        y7r = RG("y1T", [128, 8, 512], BF16, 1)
        xr7 = RG("xres", [128, D], F32, 2)
        o7r = RG("outt", [128, D], F32, 2)
        junk7 = A("junk7", [128, D], BF16)
        ss7r = RG("ss7", [128, 1], F32, 2)
        pso7 = Ring(nc, st, "pso", [128, 512], F32, 2, psum=True)

        def out1_sequence(s0, S):
            for t0 in range(s0, s0 + S, 512):
                xt, kx = x7r.next()
                P.dma("sp", xt[:], k.X1[t0:t0 + 512, :].rearrange("(j p) d -> p j d", p=128), writes=[kx])
                yT, kyT = y7r.next()
                P.dma("sp", yT[:], k.Y1_T[:, t0:t0 + 512].rearrange("(kc p) t -> p kc t", p=128),
                      reads=[("Y1", h_, half_, t0) for h_ in range(4) for half_ in range(2)], writes=[kyT])
                for j in range(4):
                    xr, kxr = out_proj_tile(k, None, Wo7, (yT, kyT), (xt, kx), t0, j, pso7, xr7)
                    ss, kss = ss7r.next()
                    act(P, junk7[:], xr[:], AF.Square, [kxr], [kss, "junk7"], accum_out=ss[:, 0:1])
                    rsqrt(P, ss[:], ss[:], 1.0 / D, RMS_EPS, [kss], [kss])
                    o, ko = o7r.next()
                    stt(P, o[:], xr[:], ss[:, 0:1], fn7[:], ALU.mult, ALU.mult, [kxr, kss, "fnorm"], [ko])
                    P.dma("sp", k.y[t0 + 128 * j:t0 + 128 * j + 128, :], o[:], reads=[ko], writes=[("y", t0, j)])
        gu = A("g_gu", [16, 2, 512])
        P.dma("sp", gu[:], k.p["gate_up"][:, :, :], writes=["gu"])
        ngb = A("g_ngb", [128, 8])
        og = COLS["gb_0"][0]
        ts(P, "dve", ngb[:], k.cols[:, og:og + 8], -1.0, None, ALU.mult, None, ["cols"], ["ngb"])
        def make_lane(li):
            L = {}
            nm = lambda n: "%s_l%d" % (n, li)
            L["Sr"] = RG(nm("g_S"), [128, 256], F32, 2)
            L["Sbr"] = RG(nm("g_Sb"), [128, 256], BF16, 2)
            L["qr"] = RG(nm("g_q"), [128, 512], BF16, 2)
            L["kr"] = RG(nm("g_k"), [128, 512], BF16, 2)
            L["glr"] = RG(nm("g_gl"), [16, 512], F32, 2)
            L["vtr"] = RG(nm("g_v"), [128, 4, 256], BF16, 2)
            L["f"] = {n: RG(nm("gf_" + n), [128, 512], F32, 1) for n in ("e", "l", "cf", "ci", "Eq", "Ek", "Ee", "rstd")}
            L["dcr"] = RG(nm("g_dc"), [128, 4], F32, 2)
            L["ob"] = {n: RG(nm("go_" + n), [128, 512], BF16, 2) for n in ("qd", "kd", "ke")}
            L["attr"] = RG(nm("g_att"), [128, 256], BF16, 3)
            L["otr"] = RG(nm("g_ot"), [128, 2, 512], F32, 2)
            L["ofr"] = RG(nm("g_of"), [128, 2, 512], F32, 1)
            L["sqr"] = RG(nm("g_sq"), [128, 2, 512], F32, 1)
            L["gtr"] = RG(nm("g_gt"), [128, 2, 512], BF16, 1)
            L["yor"] = RG(nm("g_yo"), [128, 512], BF16, 2)
            L["psg"] = Ring(nc, st, nm("pgG"), [128, 512], F32, 3, psum=True)
            return L
        lanes = [make_lane(0), make_lane(1)]
        npass = [0]
        idb = k.cb["ident"]
        ones = k.cf["ones"]
        ev = [0]

        def evac_eng():
            ev[0] += 1
            return "act" if ev[0] % 2 else "dve"
        v3 = lambda a: a.rearrange("p (c j) -> p c j", j=128)
        seq_order = sorted(zip(k.seq_off, k.seqs), key=lambda a: -a[1])
        for si_, (s0, S) in enumerate(seq_order):
            if si_ > 0:
                out1_sequence(*seq_order[si_ - 1])
            ntile = S // 512
            for h in range(4):
                for z in range(2):
                    L = lanes[npass[0] % 2]
                    npass[0] += 1
                    Sr, Sbr, qr, kr, glr, vtr, f, dcr, ob = (L[n_] for n_ in ("Sr", "Sbr", "qr", "kr", "glr", "vtr", "f", "dcr", "ob"))
                    attr, otr, ofr, sqr, gtr, yor, psg = (L[n_] for n_ in ("attr", "otr", "ofr", "sqr", "gtr", "yor", "psg"))
                    S_, kS = Sr.next()
                    Sb, kSb = Sbr.next()
                    P.op("pool", (lambda a: lambda e: e.memset(a, 0.0))(S_[:]), writes=[kS])
                    P.op("pool", (lambda a: lambda e: e.memset(a, 0.0))(Sb[:]), writes=[kSb])
                    mask = k.cb["g_f" if z == 0 else "g_b"]
                    order = range(ntile) if z == 0 else range(ntile - 1, -1, -1)
                    for ti in order:
                        t0 = s0 + 512 * ti
                        qT, kq = qr.next(); kT, kk_ = kr.next(); gl, kgl = glr.next(); vt, kvt = vtr.next()
                        P.dma("sp", qT[:], k.Q1_T[128 * h:128 * h + 128, t0:t0 + 512], writes=[kq])
                        P.dma("sp", kT[:], k.Q1_T[512 + 128 * h:512 + 128 * h + 128, t0:t0 + 512], writes=[kk_])
                        P.dma("sp", gl[:], k.GL_T[:, t0:t0 + 512], writes=[kgl])
                        P.dma("sp", vt[:], k.V1[t0:t0 + 512, 256 * h:256 * h + 256].rearrange("(j p) d -> p j d", p=128),
                              writes=[kvt])
                        pz, kpz = psg.next()
                        mm(P, pz[:], gu[0:16, z, 128 * h:128 * h + 128], gl[0:16, :], True, True, ["gu", kgl], [kpz])
                        e_, ke_ = f["e"].next()
                        act(P, e_[:], pz[:], AF.Exp, [kpz, "ngb"], [ke_], scale=-1.0, bias=ngb[:, 4 * z + h:4 * z + h + 1])
                        l_, kl_ = f["l"].next()
                        act(P, l_[:], e_[:], AF.Ln, [ke_], [kl_], bias=1.0)
                        if GLA_CUT == 1:
                            continue
                        cf_, kcf = f["cf"].next()
                        P.op("dve", (lambda o, a, b: lambda e: e.tensor_tensor_scan(
                            out=o, data0=a, data1=b, initial=0.0, op0=ALU.mult, op1=ALU.add))(
                            cf_[:], k.cf["rmask128"][:], l_[:]), [kl_, "cf_rmask128"], [kcf])
                        tot = cf_[:, 127:512:128]
                        totb = tot.unsqueeze(2).to_broadcast([128, 4, 128])
                        if z == 0:
                            ci, kci = cf_, kcf
                        else:
                            ci, kci = f["ci"].next()
                            tt(P, "pool", ci[:], l_[:], cf_[:], ALU.subtract, [kl_, kcf], [kci])
                            tt(P, "dve", v3(ci[:]), v3(ci[:]), totb, ALU.add, [kci, kcf], [kci])
                        Eq, kEq = f["Eq"].next(); Ek, kEk = f["Ek"].next(); Ee, kEe = f["Ee"].next()
                        act(P, Eq[:], ci[:], AF.Exp, [kci], [kEq], scale=-1.0 / 16)
                        act(P, Ek[:], ci[:], AF.Exp, [kci], [kEk], scale=1.0 / 16)
                        dc, kdc = dcr.next()
                        act(P, dc[:], tot, AF.Exp, [kcf], [kdc], scale=-1.0 / 16)
                        tt(P, "dve", v3(Ee[:]), v3(Ek[:]), dc[:].unsqueeze(2).to_broadcast([128, 4, 128]), ALU.mult,
                           [kEk, kdc], [kEe])
                        qd, kqd = ob["qd"].next(); kd, kkd = ob["kd"].next(); ke, kke = ob["ke"].next()
                        tt(P, "pool", qd[:], qT[:], Eq[:], ALU.mult, [kq, kEq], [kqd])
                        tt(P, "dve", kd[:], kT[:], Ek[:], ALU.mult, [kk_, kEk], [kkd])
                        tt(P, "pool", ke[:], kT[:], Ee[:], ALU.mult, [kk_, kEe], [kke])
                        if GLA_CUT == 2:
                            continue
                        ot, kot = otr.next()
                        border = range(4) if z == 0 else range(3, -1, -1)
                        for b in border:
                            bc = slice(128 * b, 128 * b + 128)
                            pa, kpa = psg.next()
                            mm(P, pa[:, 0:128], kd[:, bc], qd[:, bc], True, True, [kkd, kqd], [kpa])
                            mm(P, pa[:, 128:256], ke[:, bc], idb[:], True, True, [kke, "cb_ident"], [kpa])
                            at, kat = attr.next()
                            tt(P, "dve", at[:], pa[:, 0:256], mask[:], ALU.mult, [kpa, "cb_g_f", "cb_g_b"], [kat])
                            keT = at[:, 128:256]
                            po, kpo = psg.next()
                            for half in range(2):
                                hc = slice(128 * half, 128 * half + 128)
                                mm(P, po[:, hc], vt[:, b, hc], at[:, 0:128], True, False, [kvt, kat], [kpo])
                                mm(P, po[:, hc], Sb[:, hc], qd[:, bc], False, True, [kSb, kqd], [kpo])
                            cp(P, "act", ot[:, :, bc], po[:, 0:256].rearrange("p (a t) -> p a t", a=2), [kpo], [kot])
                            pS, kpS = psg.next()
                            mm(P, pS[:, 0:256], keT, vt[:, b, :], True, True, [kat, kvt], [kpS])
                            stt(P, S_[:], S_[:], dc[:, b:b + 1], pS[:, 0:256], ALU.mult, ALU.add, [kpS, kdc, kS], [kS])
                            cp(P, "act", Sb[:], S_[:], [kS], [kSb])
                        if GLA_CUT == 3:
                            continue
                        if z == 0:
                            for half in range(2):
                                r0 = 256 * h + 128 * half
                                P.dma("sp", k.OF_T[r0:r0 + 128, t0:t0 + 512], ot[:, half, :], reads=[kot],
                                      writes=[("OF", h, half, t0)])
                            continue
                        of, kof = ofr.next(); gt, kgt = gtr.next()
                        for half in range(2):
                            r0 = 256 * h + 128 * half
                            P.dma("sp", of[:, half, :], k.OF_T[r0:r0 + 128, t0:t0 + 512], reads=[("OF", h, half, t0)],
                                  writes=[kof])
                            P.dma("sp", gt[:, half, :], k.G1_T[r0:r0 + 128, t0:t0 + 512], writes=[kgt])
                        tt(P, "pool", of[:], of[:], ot[:], ALU.add, [kof, kot], [kof])
                        if GLA_CUT == 4:
                            continue
                        sq, ksq = sqr.next()
                        act(P, sq[:], of[:], AF.Square, [kof], [ksq])
                        pn, kpn = psg.next()
                        mm(P, pn[:], ones[:], sq[:, 0, :], True, False, [ksq, "cf_ones"], [kpn])
                        mm(P, pn[:], ones[:], sq[:, 1, :], False, True, [ksq, "cf_ones"], [kpn])
                        rstd, krstd = f["rstd"].next()
                        if GLA_CUT == 5:
                            continue
                        rsqrt(P, rstd[:], pn[:], 1.0 / 256, RMS_EPS, [kpn], [krstd])
                        if GLA_CUT == 6:
                            continue
                        for half in range(2):
                            r0 = 256 * h + 128 * half
                            stt(P, of[:, half, :], of[:, half, :], col(k, "gnorm", half), rstd[:], ALU.mult, ALU.mult,
                                [kof, krstd, "cols"], [kof])
                            if GLA_CUT == 7:
                                continue
                            yo, kyo = yor.next()
                            tt(P, "dve", yo[:], of[:, half, :], gt[:, half, :], ALU.mult, [kof, kgt], [kyo])
                            P.dma("sp", k.Y1_T[r0:r0 + 128, t0:t0 + 512], yo[:], reads=[kyo], writes=[("Y1", h, half, t0)])
        out1_sequence(*seq_order[-1])
        P.stage_end()


def stage7_out1(k):
    return


_CACHE = {}


def kernel(**inputs):
    xp = np.asarray(inputs["x_prompt"], np.float32)
    xs = np.asarray(inputs["x_sample"], np.float32)
    B, S, _ = xp.shape
    DB, DS, _ = xs.shape
    n = NCORES
    pb, sbn = B // n, DB // n
    seqs = [S] * pb + [DS] * sbn
    key = tuple(seqs)
    if key not in _CACHE:
        _CACHE[key] = build(seqs)
    nc, k = _CACHE[key]
    consts = host_consts(max(seqs))
    params = host_params(inputs)
    shared = {"c_" + a: v for a, v in consts.items()}
    shared.update({"p_" + a: v for a, v in params.items()})
    in_maps = []
    for c in range(n):
        parts = [xp[c * pb + i] for i in range(pb)] + [xs[c * sbn + i] for i in range(sbn)]
        m = {"x": np.ascontiguousarray(np.concatenate(parts, axis=0))}
        m.update(shared)
        in_maps.append(m)
    res = run_bass_kernel_spmd(nc, in_maps, core_ids=list(range(n)))
    yp = np.empty_like(xp)
    ys = np.empty_like(xs)
    for c in range(n):
        y = np.asarray(res.results[c]["y"], np.float32)
        off = 0
        for i in range(pb):
            yp[c * pb + i] = y[off:off + S]
            off += S
        for i in range(sbn):
            ys[c * sbn + i] = y[off:off + DS]
            off += DS
    return (yp, ys)
```

```python
import contextlib
import numpy as np
import concourse.bass as bass
import concourse.mybir as mybir
from concourse.bass_utils import run_bass_kernel_spmd

F32 = mybir.dt.float32
BF16 = mybir.dt.bfloat16
AF = mybir.ActivationFunctionType
ALU = mybir.AluOpType

SAME_ENG_SYNC = True
LIST_SCHED = True
PE_FIXED = 30.0
ACT_FIXED = 120.0
PE_GHZ = 2.4
SCHED_XLAT = 180.0
PE_MODE_GROUP = True
PE_MODE_WINDOW = 12
PE_MODE_SLACK = 4000.0
SCHED_DEBUG = False
NDMASEM = 16
NCORES = 8
D = 1024
RW = 512
EVEN_SHIFT = 1664
EVEN_COLS = 4224
ODD_COLS = 3088
DECAY_C = 0.6065306597126334
GN_EPS = 64e-5
RMS_EPS = 1e-6


class Op:
    __slots__ = ("eng", "fn", "reads", "writes", "is_dma", "waits", "signal", "sig", "barrier", "deps", "cost",
                 "lat", "idx", "mode")


def _numel(ap):
    n = 1
    for d in ap.shape[1:]:
        n *= int(d)
    return n


class Prog:
    ENGS = ("pe", "act", "dve", "pool", "sp")

    def __init__(self, nc, stack):
        self.nc = nc
        self.ops = []
        self.sems = {e: stack.enter_context(nc.semaphore("s_" + e)) for e in ("pe", "act", "dve", "pool")}
        self.dsems = {q: [stack.enter_context(nc.semaphore("d_%s%d" % (q, i))) for i in range(NDMASEM)]
                      for q in ("sp", "pool", "act")}
        self.cnt = {e: 0 for e in ("pe", "act", "dve", "pool")}
        self.dcnt = {q: 0 for q in ("sp", "pool", "act")}
        self.last_writer = {}
        self.readers = {}
        self.seen = {e: {} for e in self.ENGS}
        self.last_op = {}
        self.recent_dma = {q: [] for q in ("sp", "pool", "act")}
        self.emitted = 0
        self.n_inst = 0

    def op(self, eng, fn, reads=(), writes=(), cost=500.0):
        o = Op()
        ex = [r for r in reads if isinstance(r, str) and (r.startswith("ps") or r.startswith("pg"))]
        if ex:
            reads = [r for r in reads if r not in ex]
            writes = list(writes) + ex
        o.eng = eng; o.fn = fn; o.reads = tuple(reads); o.writes = tuple(writes)
        o.is_dma = False; o.signal = False; o.sig = None; o.waits = []; o.barrier = False
        o.deps = []; o.cost = cost; o.lat = cost; o.mode = None
        self.ops.append(o)
        return o

    def dma(self, q, out, in_, reads=(), writes=()):
        o = self.op(q, lambda e: e.dma_start(out=out, in_=in_), reads, writes, cost=60.0)
        o.is_dma = True
        o.lat = 2200.0 + _numel(out) * 128 * 0.004
        return o

    def barrier(self):
        for e in self.ENGS:
            o = self.op(e, None)
            o.barrier = True

    def _deps(self, ops):
        for o in ops:
            if o.barrier:
                if o.eng == self.ENGS[-1]:
                    self.last_writer = {}
                    self.readers = {}
                continue
            deps = {}
            for k in o.reads:
                w = self.last_writer.get(k)
                if w is not None:
                    deps[id(w)] = (w, True)
            for k in o.writes:
                w = self.last_writer.get(k)
                if w is not None:
                    israw = isinstance(k, str) and (k.startswith("ps") or k.startswith("pg"))
                    if id(w) not in deps or israw:
                        deps[id(w)] = (w, israw or deps.get(id(w), (None, False))[1])
                for r in self.readers.get(k, ()):
                    if id(r) not in deps:
                        deps[id(r)] = (r, False)
            deps.pop(id(o), None)
            o.deps = list(deps.values())
            for k in o.reads:
                self.readers.setdefault(k, []).append(o)
            for k in o.writes:
                self.last_writer[k] = o
                self.readers[k] = []

    def _schedule(self, seg):
        import heapq
        n = len(seg)
        if n < 3 or not LIST_SCHED:
            return seg
        for i, o in enumerate(seg):
            o.idx = i
        inseg = set(id(o) for o in seg)
        succ = [[] for _ in range(n)]
        indeg = [0] * n
        for o in seg:
            for d, _ in o.deps:
                if id(d) in inseg:
                    succ[d.idx].append(o.idx)
                    indeg[o.idx] += 1
        rank = [0.0] * n
        for i in range(n - 1, -1, -1):
            m = 0.0
            for j in succ[i]:
                if rank[j] > m:
                    m = rank[j]
            rank[i] = seg[i].lat + m
        ready = [0.0] * n
        free = {e: 0.0 for e in self.ENGS}
        fut = {e: [] for e in self.ENGS}
        now = {e: [] for e in self.ENGS}
        for i in range(n):
            if indeg[i] == 0:
                heapq.heappush(fut[seg[i].eng], (0.0, i))
        out = []
        XLAT = SCHED_XLAT
        pe_mode = [None]
        while len(out) < n:
            best = None
            for e in self.ENGS:
                f, nw = fut[e], now[e]
                while f and f[0][0] <= free[e]:
                    t, i = heapq.heappop(f)
                    heapq.heappush(nw, (-rank[i], i))
                if nw:
                    cand = (free[e], 0, e)
                elif f:
                    cand = (f[0][0], 1, e)
                else:
                    continue
                if best is None or cand < best:
                    best = cand
            start, kind, e = best
            if kind == 0:
                if e == "pe" and PE_MODE_GROUP:
                    nw = now[e]
                    top_rank = -nw[0][0]
                    pick = None
                    cand_list = heapq.nsmallest(PE_MODE_WINDOW, nw)
                    for (nr, ii) in cand_list:
                        if seg[ii].mode == pe_mode[0] and (top_rank + nr) <= PE_MODE_SLACK:
                            pick = (nr, ii)
                            break
                    if pick is None:
                        _, i = heapq.heappop(nw)
                    else:
                        nw.remove(pick)
                        heapq.heapify(nw)
                        i = pick[1]
                    pe_mode[0] = seg[i].mode
                else:
                    _, i = heapq.heappop(now[e])
            else:
                _, i = heapq.heappop(fut[e])
                if e == "pe":
                    pe_mode[0] = seg[i].mode
            o = seg[i]
            free[e] = start + o.cost
            fin = start + o.lat
            out.append(o)
            for j in succ[i]:
                same = (seg[j].eng == e and not o.is_dma)
                r = (start + o.cost) if same else (fin + XLAT)
                if r > ready[j]:
                    ready[j] = r
                indeg[j] -= 1
                if indeg[j] == 0:
                    heapq.heappush(fut[seg[j].eng], (ready[j], j))
        if SCHED_DEBUG and n > 500:
            import collections
            load = collections.defaultdict(float)
            for o in seg:
                load[o.eng] += o.cost
            st = {}
            fr = {e: 0.0 for e in self.ENGS}
            pred = {}
            for o in out:
                t = fr[o.eng]
                p = None
                for d, _ in o.deps:
                    if id(d) in st:
                        same = (d.eng == o.eng and not d.is_dma)
                        r = st[id(d)] + (d.cost if same else d.lat + XLAT)
                        if r > t:
                            t, p = r, d
                st[id(o)] = t
                pred[id(o)] = p
                fr[o.eng] = t + o.cost
            last = max(out, key=lambda o: st[id(o)] + o.lat)
            print("SCHED seg n=%d makespan=%.0f loads=%s" % (n, st[id(last)] + last.lat, {e: int(v) for e, v in load.items()}))
            i = max(range(n), key=lambda q: rank[q])
            print("  pure DAG critical path length: %.0f" % rank[i])
            cp_ = collections.Counter(); cpt = collections.defaultdict(float)
            while True:
                o = seg[i]
                kk = (o.eng, str(o.writes[0] if o.writes else "-")[:7], o.is_dma)
                cp_[kk] += 1; cpt[kk] += o.lat
                if not succ[i]:
                    break
                i = max(succ[i], key=lambda q: rank[q])
            print("  DAG path:", sorted(((int(cpt[kk]), v, kk) for kk, v in cp_.items()), reverse=True)[:16])
            path = collections.Counter()
            tm = collections.defaultdict(float)
            o = last
            cnt = 0
            while o is not None and cnt < 100000:
                kk = (o.eng, str(o.writes[0] if o.writes else "-")[:6])
                path[kk] += 1
                tm[kk] += o.lat
                o = pred[id(o)]
                cnt += 1
            print("  critical path ops:", sorted(((v, int(tm[kk]), kk) for kk, v in path.items()), reverse=True)[:14])
        return out

    def _sync(self, ops):
        for o in ops:
            if o.barrier:
                if o.eng == self.ENGS[0]:
                    self._bar = [lo for lo in self.last_op.values() if lo is not None]
                    for q, lst in self.recent_dma.items():
                        self._bar.extend(lst)
                deps = [(d, True) for d in self._bar]
            else:
                deps = o.deps
            if o.is_dma:
                j = self.dcnt[o.eng]
                self.dcnt[o.eng] = j + 1
                s = j % NDMASEM
                o.sig = (("d", o.eng, s), 16 * (j // NDMASEM + 1))
                o.signal = True
                if j >= NDMASEM:
                    o.waits.append((("d", o.eng, s), 16 * (j // NDMASEM)))
            for d, israw in deps:
                if d.is_dma:
                    o.waits.append(d.sig)
                elif d.eng == o.eng and not o.is_dma:
                    if o.eng == "pe" or not SAME_ENG_SYNC or (not o.barrier and not israw):
                        continue
                    d.signal = True
                    o.waits.append(("c", d))
                else:
                    d.signal = True
                    o.waits.append(("c", d))
            if not o.barrier:
                if o.is_dma:
                    lst = self.recent_dma[o.eng]
                    lst.append(o)
                    if len(lst) > NDMASEM:
                        lst.pop(0)
                else:
                    self.last_op[o.eng] = o
        for o in ops:
            if o.signal and not o.is_dma:
                self.cnt[o.eng] += 1
                o.sig = (("c", o.eng), self.cnt[o.eng])

    def _sem(self, key):
        if key[0] == "c":
            return self.sems[key[1]]
        return self.dsems[key[1]][key[2]]

    def emit(self):
        ops = self.ops[self.emitted:]
        self.emitted = len(self.ops)
        self._deps(ops)
        ordered, seg = [], []
        for o in ops:
            if o.barrier:
                if seg:
                    ordered.extend(self._schedule(seg))
                    seg = []
                ordered.append(o)
            else:
                seg.append(o)
        if seg:
            ordered.extend(self._schedule(seg))
        ops = ordered
        self._sync(ops)
        per = {e: [o for o in ops if o.eng == e] for e in self.ENGS}
        self.n_inst += len(ops)
        with self.nc.Block() as block:
            def run(engname):
                def body(eng):
                    seen = self.seen[engname]
                    for o in per[engname]:
                        for w in o.waits:
                            if w[0] == "c":
                                key, val = w[1].sig
                            else:
                                key, val = w
                            if seen.get(key, 0) >= val:
                                continue
                            seen[key] = val
                            eng.wait_ge(self._sem(key), val)
                        if o.fn is None:
                            continue
                        ins = o.fn(eng)
                        if o.signal:
                            key, val = o.sig
                            ins.then_inc(self._sem(key), 16 if o.is_dma else 1)
                return body
            block.tensor(run("pe"))
            block.scalar(run("act"))
            block.vector(run("dve"))
            block.gpsimd(run("pool"))
            block.sync(run("sp"))

    def stage_end(self):
        self.barrier()
        self.emit()


class Ring:
    uid = 0
    def __init__(self, nc, st, name, shape, dt, n, psum=False):
        alloc = nc.psum_tensor if psum else nc.sbuf_tensor
        Ring.uid += 1
        self.tiles = [st.enter_context(alloc("%s_u%d_%d" % (name, Ring.uid, i), shape, dt)) for i in range(n)]
        self.keys = ["%s%d" % (name, i) for i in range(n)]
        self.i = -1

    def next(self):
        self.i = (self.i + 1) % len(self.tiles)
        return self.tiles[self.i], self.keys[self.i]


def _cols(v):
    v = np.asarray(v, np.float32)
    return np.ascontiguousarray(v.reshape(-1, 128).T)


COLS = {}


def _col_layout():
    off = 0
    for name, n in (("norm0", 8), ("norm1", 8), ("mup", 13), ("mun", 13), ("w0_0", 4), ("w0_1", 4), ("a0_0", 4),
                    ("a0_1", 4), ("k_k", 4), ("k_a", 4), ("r_k", 4), ("gn_w", 4), ("gn_b", 4), ("gnorm", 2),
                    ("gb_0", 4), ("gb_1", 4)):
        COLS[name] = (off, n)
        off += n
    return off


NCOLS = _col_layout()


def host_consts(smax):
    c = {}
    c["ident"] = np.eye(128, dtype=np.float32)
    R = np.zeros((128, 128), np.float32)
    for m in range(128):
        h, j = divmod(m, 64)
        k = h * 64 + (j + 32) % 64
        R[k, m] = 1.0
    c["rot"] = R
    inv = 10000.0 ** (-np.arange(0, 64, 2, dtype=np.float32) / 64.0)
    ang = np.arange(smax, dtype=np.float32)[None, :] * inv[:, None]
    cos, sin = np.cos(ang), np.sin(ang)
    c["cos"] = np.ascontiguousarray(np.tile(cos, (4, 1)).astype(np.float32))
    c["sin"] = np.ascontiguousarray(np.concatenate([-sin, sin, -sin, sin], 0).astype(np.float32))
    i = np.arange(128)[:, None]
    t = np.arange(128)[None, :]
    same = (i // 64) == (t // 64)
    strict = ((i < t) & same).astype(np.float32)
    incl = ((i <= t) & same).astype(np.float32)
    c["m_f"] = np.concatenate([strict, incl, strict.T], 1)
    c["m_b"] = np.concatenate([strict.T, incl.T, strict], 1)
    c["g_f"] = np.concatenate([(i <= t).astype(np.float32), np.ones((128, 128), np.float32)], 1)
    c["g_b"] = np.concatenate([(i >= t).astype(np.float32), np.ones((128, 128), np.float32)], 1)
    kl = np.arange(128)[:, None]
    ql = np.arange(256)[None, :]
    c["a_g"] = ((kl <= ql) & (ql <= kl + 128)).astype(np.float32)
    c["a_0"] = (np.arange(128)[None, :] <= np.arange(64)[:, None] + 64).astype(np.float32)
    bd = (np.arange(128)[:, None] // 64 == np.arange(128)[None, :] // 64).astype(np.float32)
    c["bd1"] = bd
    c["ones"] = np.ones((128, 128), np.float32)
    rm = np.ones((128, 512), np.float32)
    rm[:, ::64] = 0.0
    c["rmask"] = rm
    rm2 = np.ones((128, 512), np.float32)
    rm2[:, ::128] = 0.0
    c["rmask128"] = rm2
    return c


CONST_SHAPES = lambda smax: {"ident": [128, 128], "rot": [128, 128], "cos": [128, smax], "sin": [128, smax],
                             "m_f": [128, 384], "m_b": [128, 384],
                             "g_f": [128, 256], "g_b": [128, 256], "a_g": [128, 256], "a_0": [64, 128],
                             "bd1": [128, 128], "ones": [128, 128], "rmask": [128, 512],
                             "rmask128": [128, 512]}


def host_params(inp):
    g = lambda k: np.asarray(inp[k], np.float32)
    cols = np.zeros((128, NCOLS), np.float32)

    def put(name, v):
        o, n = COLS[name]
        cols[:, o:o + n] = _cols(v)
    put("norm0", g("even_norm")[0]); put("norm1", g("odd_norm")[0])
    put("mup", g("even_mu_prev")[0]); put("mun", g("even_mu_next")[0])
    for z in range(2):
        put("w0_%d" % z, g("rwkv_w0")[0, z]); put("a0_%d" % z, g("rwkv_a0")[0, z])
        put("gb_%d" % z, g("gla_gate_bias")[0, z])
    for nm, k in (("k_k", "rwkv_k_k"), ("k_a", "rwkv_k_a"), ("r_k", "rwkv_r_k"), ("gn_w", "rwkv_gn_w"),
                  ("gn_b", "rwkv_gn_b")):
        put(nm, g(k)[0])
    put("gnorm", g("gla_norm")[0])
    lora = np.zeros((128, 2, 512), np.float32)
    lora[0:64] = np.transpose(g("rwkv_w_up")[0], (1, 0, 2))
    lora[64:128] = np.transpose(g("rwkv_a_up")[0], (1, 0, 2))
    p = {"cols": cols, "lora": lora,
         "w_in0": g("even_w_in")[0], "w_out0": g("even_w_out")[0],
         "w_in1": g("odd_w_in")[0], "w_out1": g("odd_w_out")[0],
         "gate_up": np.ascontiguousarray(np.transpose(g("gla_gate_up")[0], (1, 0, 2))),
         "fnorm": np.ascontiguousarray(np.broadcast_to(g("final_norm")[None, :], (128, D)))}
    return p


PARAM_SHAPES = {"cols": [128, NCOLS], "lora": [128, 2, 512], "w_in0": [D, EVEN_COLS], "w_out0": [D, D],
                "w_in1": [D, ODD_COLS], "w_out1": [D, D], "gate_up": [16, 2, 512], "fnorm": [128, D]}


class K:
    pass


def mm(P, out, lhsT, rhs, start, stop, reads, writes):
    n = _numel(out)
    c = PE_FIXED + max(64, n) / PE_GHZ * (4.0 if lhsT.dtype == F32 else 1.0)
    o = P.op("pe", lambda e: e.matmul(out, lhsT=lhsT, rhs=rhs, start=start, stop=stop), reads, writes, cost=c)
    o.lat = c + 100.0
    rnd = lambda v: 32 if v <= 32 else (64 if v <= 64 else 128)
    o.mode = (rnd(int(lhsT.shape[0])), rnd(int(lhsT.shape[1])))


def act(P, out, in_, func, reads, writes, **kw):
    P.op("act", lambda e: e.activation(out=out, in_=in_, func=func, **kw), reads, writes,
         cost=ACT_FIXED + _numel(out) / 1.2)


def _vcost(eng, out):
    n = _numel(out)
    return (40.0 + n * 1.05) if eng == "dve" else (150.0 + n * 2.3)


def tt(P, eng, out, in0, in1, op, reads, writes):
    P.op(eng, lambda e: e.tensor_tensor(out=out, in0=in0, in1=in1, op=op), reads, writes, cost=_vcost(eng, out))


def ts(P, eng, out, in0, s1, s2, op0, op1, reads, writes):
    if op1 is None:
        P.op(eng, lambda e: e.tensor_scalar(out=out, in0=in0, scalar1=s1, scalar2=None, op0=op0), reads, writes,
             cost=_vcost(eng, out))
    else:
        P.op(eng, lambda e: e.tensor_scalar(out=out, in0=in0, scalar1=s1, scalar2=s2, op0=op0, op1=op1), reads, writes,
             cost=_vcost(eng, out))


def stt(P, out, in0, scalar, in1, op0, op1, reads, writes):
    P.op("dve", lambda e: e.scalar_tensor_tensor(out=out, in0=in0, scalar=scalar, in1=in1, op0=op0, op1=op1),
         reads, writes, cost=_vcost("dve", out))


def cp(P, eng, out, in_, reads, writes):
    if eng == "act":
        P.op("act", lambda e: e.activation(out=out, in_=in_, func=AF.Copy), reads, writes,
             cost=ACT_FIXED + _numel(out) / 1.2)
    else:
        P.op(eng, lambda e: e.tensor_copy(out=out, in_=in_), reads, writes, cost=_vcost(eng, out))


def rsqrt(P, out, in_, scale, bias, reads, writes):
    act(P, out, in_, AF.Ln, reads, writes, scale=scale, bias=bias)
    act(P, out, out, AF.Exp, writes, writes, scale=-0.5)


def col(k, name, j=0, n=1, rows=slice(0, 128)):
    o, _ = COLS[name]
    return k.cols[rows, o + j:o + j + n]


def build(seqs, n_stage=99, debug=()):
    nc = bass.Bass("TRN2", target_bir_lowering=False)
    k = K()
    k.nc = nc
    k.seqs = list(seqs)
    T = sum(seqs)
    k.T = T
    smax = max(seqs)
    k.seq_off = [sum(seqs[:i]) for i in range(len(seqs))]
    ein = lambda n, sh, dt=F32: nc.dram_tensor(n, sh, dt, kind="ExternalInput").ap()
    scr = lambda n, sh, dt=BF16: nc.dram_tensor(n, sh, dt, kind=("ExternalOutput" if n in debug else "Internal")).ap()
    k.x = ein("x", [T, D])
    k.y = nc.dram_tensor("y", [T, D], F32, kind="ExternalOutput").ap()
    k.c = {n: ein("c_" + n, sh) for n, sh in CONST_SHAPES(smax).items()}
    k.p = {n: ein("p_" + n, sh) for n, sh in PARAM_SHAPES.items()}
    k.RW_T = scr("RW_T", [EVEN_SHIFT, T])
    k.GA_T = scr("GA_T", [512, T])
    k.QK_T = scr("QK_T", [1024, T])
    k.VB = scr("VB", [T, 512])
    k.GB_T = scr("GB_T", [512, T])
    k.YF_T = scr("YF_T", [512, T], F32)
    k.Y0_T = scr("Y0_T", [1024, T])
    k.X1 = scr("X1", [T, D], F32)
    k.Q1_T = scr("Q1_T", [1024, T])
    k.GL_T = scr("GL_T", [16, T], F32)
    k.V1 = scr("V1", [T, D])
    k.G1_T = scr("G1_T", [D, T])
    k.OF_T = scr("OF_T", [D, T], F32)
    k.Y1_T = scr("Y1_T", [D, T])
    k.DEN = scr("DEN", [8, 512], F32)

    with contextlib.ExitStack() as gst:
        P = Prog(nc, gst)
        k.P = P
        sb = lambda n, sh, dt: gst.enter_context(nc.sbuf_tensor(n, sh, dt))
        k.cols = sb("cols", [128, NCOLS + 32], F32)
        k.cb = {}
        k.cf = {}
        for n in ("ident", "bd1", "ones", "rmask", "rmask128"):
            k.cf[n] = sb("cf_" + n, CONST_SHAPES(smax)[n], F32)
        BN = ("ident", "rot", "m_f", "m_b", "g_f", "g_b", "a_g", "a_0", "bd1")
        for n in BN:
            k.cb[n] = sb("cb_" + n, CONST_SHAPES(smax)[n], BF16)
        k.lora = sb("lora", [128, 2, 512], BF16)
        with contextlib.ExitStack() as st:
            tmp = st.enter_context(nc.sbuf_tensor("ctmp", [128, 2048], F32))
            P.dma("sp", k.cols[:, 0:NCOLS], k.p["cols"][:, :], writes=["cols"])
            for n in ("ident", "bd1", "ones", "rmask", "rmask128"):
                P.dma("sp", k.cf[n][:], k.c[n][:, :], writes=["cf_" + n])
            off = 0
            for n in BN:
                sh = CONST_SHAPES(smax)[n]
                P.dma("sp", tmp[0:sh[0], off:off + sh[1]], k.c[n][:, :], writes=["ctmp"])
                cp(P, "dve", k.cb[n][:], tmp[0:sh[0], off:off + sh[1]], ["ctmp"], ["cb_" + n])
                off += sh[1]
            P.stage_end()
            P.dma("sp", tmp[:, 0:1024], k.p["lora"].rearrange("p z c -> p (z c)"), writes=["ctmp2"])
            cp(P, "dve", k.lora[:].rearrange("p z c -> p (z c)"), tmp[:, 0:1024], ["ctmp2"], ["lora"])
            o_mup, o_mun, o_ka = COLS["mup"][0], COLS["mun"][0], COLS["k_a"][0]
            k.o_c0, k.o_omk = NCOLS, NCOLS + 13
            tt(P, "dve", k.cols[:, k.o_c0:k.o_c0 + 13], k.cols[:, o_mup:o_mup + 13], k.cols[:, o_mun:o_mun + 13],
               ALU.add, ["cols"], ["cols"])
            ts(P, "dve", k.cols[:, k.o_c0:k.o_c0 + 13], k.cols[:, k.o_c0:k.o_c0 + 13], -1.0, 1.0, ALU.mult, ALU.add,
               ["cols"], ["cols"])
            ts(P, "dve", k.cols[:, k.o_omk:k.o_omk + 4], k.cols[:, o_ka:o_ka + 4], -1.0, 1.0, ALU.mult, ALU.add,
               ["cols"], ["cols"])
            P.stage_end()
        stages = [stage1, stage2_attn, stage3_rwkv, stage4_out0, stage5_in1, stage6_gla, stage7_out1]
        for i, s in enumerate(stages):
            if i < n_stage:
                s(k)
        P.stage_end()
    k.n_inst = P.n_inst
    return nc, k


def load_weights(k, st, specs):
    P, nc = k.P, k.nc
    ws = [st.enter_context(nc.sbuf_tensor(name, [128, rows, ncol], BF16)) for name, dram, rows, ncol, key in specs]
    with contextlib.ExitStack() as inner:
        ring = Ring(nc, inner, "wstage", [128, 1056], F32, 3)
        for w, (name, dram, rows, ncol, key) in zip(ws, specs):
            v = dram.rearrange("(kc p) c -> p kc c", p=128)
            for kc in range(rows):
                for c0 in range(0, ncol, 1056):
                    c1 = min(ncol, c0 + 1056)
                    t, tk = ring.next()
                    P.dma("sp", t[:, 0:c1 - c0], v[:, kc, c0:c1], writes=[tk])
                    cp(P, "dve" if (c0 // 1056) % 2 else "pool", w[:, kc, c0:c1], t[:, 0:c1 - c0], [tk], [])
        P.stage_end()
    return ws


def rms_transpose(k, st, rings, src_ap, normcol, t0):
    P = k.P
    xt, kx = rings["x"].next()
    P.dma("sp", xt[:], src_ap[t0:t0 + 512, :].rearrange("(j p) d -> p j d", p=128), writes=[kx])
    return xt, kx


def rms_transpose_compute(k, rings, xt, kx, normname):
    P = k.P
    ss, kss = rings["ss"].next()
    junk, kj = rings["junk"].next()
    for j in range(4):
        act(P, junk[:], xt[:, j, :], AF.Square, [kx], [kss, kj], accum_out=ss[:, j:j + 1])
    rsqrt(P, ss[:, 0:4], ss[:, 0:4], 1.0 / D, RMS_EPS, [kss], [kss])
    xn, kxn = rings["xn"].next()
    for j in range(4):
        if j % 2 == 0:
            ts(P, "dve", xn[:, j, :], xt[:, j, :], ss[:, j:j + 1], None, ALU.mult, None, [kx, kss], [kxn])
        else:
            act(P, xn[:, j, :], xt[:, j, :], AF.Copy, [kx, kss], [kxn], scale=ss[:, j:j + 1])
    xnT, kT = rings["xnT"].next()
    for c in range(8):
        ps, kp = rings["pst"].next()
        for j in range(4):
            mm(P, ps[:, 128 * j:128 * j + 128], xn[:, j, 128 * c:128 * c + 128], k.cb["ident"][:], True, True,
               [kxn, "cb_ident"], [kp])
        if c % 2 == 0:
            act(P, xnT[:, c, :], ps[:], AF.Copy, [kp, "cols"], [kT], scale=col(k, normname, c))
        else:
            ts(P, "dve", xnT[:, c, :], ps[:], col(k, normname, c), None, ALU.mult, None, [kp, "cols"], [kT])
    return xnT, kT


def in_rings(k, st):
    nc = k.nc
    return {"x": Ring(nc, st, "xt", [128, 4, D], F32, 2), "ss": Ring(nc, st, "ss", [128, 4], F32, 2),
            "junk": Ring(nc, st, "junk", [128, D], BF16, 1), "xn": Ring(nc, st, "xn", [128, 4, D], BF16, 2),
            "xnT": Ring(nc, st, "xnT", [128, 8, 512], BF16, 2),
            "pst": Ring(nc, st, "pst", [128, 512], F32, 2, psum=True)}


def seq_pos(k, t0):
    for off, S in zip(k.seq_off, k.seqs):
        if off <= t0 < off + S:
            return t0 - off
    raise ValueError


def stage1(k):
    P, nc, T = k.P, k.nc, k.T
    with contextlib.ExitStack() as st:
        W, = load_weights(k, st, [("w0", k.p["w_in0"], 8, EVEN_COLS, "W")])
        R = in_rings(k, st)
        psr = Ring(nc, st, "ps1", [128, 512], F32, 4, psum=True)
        psrot = Ring(nc, st, "psrot", [128, 512], F32, 2, psum=True)
        ost = Ring(nc, st, "ost", [128, 512], BF16, 6)
        qraw = Ring(nc, st, "qraw", [128, 512], BF16, 2)
        ra = Ring(nc, st, "ra", [128, 512], F32, 2)
        rb = Ring(nc, st, "rb", [128, 512], F32, 2)
        cs = Ring(nc, st, "cs", [128, 2, 512], F32, 2)
        ntile = T // 512
        nxt = rms_transpose(k, st, R, k.x, "norm0", 0)
        for ti in range(ntile):
            t0 = ti * 512
            xt, kx = nxt
            if ti + 1 < ntile:
                nxt = rms_transpose(k, st, R, k.x, "norm0", t0 + 512)
            pos = seq_pos(k, t0)
            cst, kcs = cs.next()
            P.dma("sp", cst[:, 0, :], k.c["cos"][:, pos:pos + 512], writes=[kcs])
            P.dma("sp", cst[:, 1, :], k.c["sin"][:, pos:pos + 512], writes=[kcs])
            xnT, kT = rms_transpose_compute(k, R, xt, kx, "norm0")
            for oc in range(33):
                if 25 <= oc < 29:
                    j = oc - 25
                    ps, kp = psr.next()
                    for kc in range(8):
                        mm(P, ps[:], xnT[:, kc, 128 * j:128 * j + 128], W[:, kc, 3200:3712], kc == 0, kc == 7,
                           [kT, "W"], [kp])
                    o, ko = ost.next()
                    cp(P, "act" if j % 2 else "dve", o[:], ps[:], [kp], [ko])
                    P.dma("sp", k.VB[t0 + 128 * j:t0 + 128 * j + 128, :], o[:], reads=[ko], writes=[("VB", ti)])
                    continue
                ps, kp = psr.next()
                for kc in range(8):
                    mm(P, ps[:], W[:, kc, 128 * oc:128 * oc + 128], xnT[:, kc, :], kc == 0, kc == 7, [kT, "W"], [kp])
                o, ko = ost.next()
                if oc < 13:
                    cp(P, "act" if oc % 2 else "dve", o[:], ps[:], [kp], [ko])
                    P.dma("sp", k.RW_T[128 * oc:128 * oc + 128, t0:t0 + 512], o[:], reads=[ko], writes=[("RW_T", ti)])
                elif oc < 17 or oc >= 29:
                    act(P, o[:], ps[:], AF.Silu, [kp], [ko])
                    dst = k.GA_T if oc < 17 else k.GB_T
                    r0 = 128 * (oc - 13) if oc < 17 else 128 * (oc - 29)
                    P.dma("sp", dst[r0:r0 + 128, t0:t0 + 512], o[:], reads=[ko],
                          writes=[("GA_T" if oc < 17 else "GB_T", ti)])
                else:
                    qr, kq = qraw.next()
                    cp(P, "act", qr[:], ps[:], [kp], [kq])
                    pr, kpr = psrot.next()
                    mm(P, pr[:], k.cb["rot"][:], qr[:], True, True, [kq, "cb_rot"], [kpr])
                    a, ka = ra.next()
                    tt(P, "pool", a[:], qr[:], cst[:, 0, :], ALU.mult, [kq, kcs], [ka])
                    b, kb = rb.next()
                    tt(P, "dve", b[:], pr[:], cst[:, 1, :], ALU.mult, [kpr, kcs], [kb])
                    tt(P, "pool", o[:], a[:], b[:], ALU.add, [ka, kb], [ko])
                    r0 = 128 * (oc - 17)
                    P.dma("sp", k.QK_T[r0:r0 + 128, t0:t0 + 512], o[:], reads=[ko], writes=[("QK_T", ti)])
        P.stage_end()


class SRing:
    def __init__(self, items):
        self.items = items
        self.i = -1

    def next(self):
        self.i = (self.i + 1) % len(self.items)
        return self.items[self.i]


GLA_CUT = 0
PATTERNS = ((128, 1), (512, 4), (2048, 16))


def stage2_attn(k):
    P, nc = k.P, k.nc
    smax = max(k.seqs)
    with contextlib.ExitStack() as st:
        qsr = Ring(nc, st, "qs", [128, smax], BF16, 1)
        ksr = Ring(nc, st, "ks", [128, smax], BF16, 1)
        qdr = {d: Ring(nc, st, "qd%d" % d, [128, smax], BF16, 1) for d in (4, 16)}
        kdr = {d: Ring(nc, st, "kd%d" % d, [128, smax], BF16, 1) for d in (4, 16)}
        acc = [st.enter_context(nc.sbuf_tensor("acc%d" % h, [65, smax], F32)) for h in range(2)]
        vtr = Ring(nc, st, "vt", [128, 2, 65], BF16, 6)
        for t, key in zip(vtr.tiles, vtr.keys):
            P.op("pool", (lambda t: lambda e: e.memset(t[:], 1.0))(t), writes=[key])
        pst = [st.enter_context(nc.psum_tensor("psS%d" % i, [128, 512], F32)) for i in range(2)]
        psr = SRing([(pst[i], "psS%d" % i) for i in range(2)])
        po = [[(st.enter_context(nc.psum_tensor("psO%d_%d" % (h, b), [65, 128], F32)), "psO%d_%d" % (h, b))
               for b in range(2)] for h in range(2)]
        pdr = Ring(nc, st, "psD", [64, 512], F32, 2, psum=True)
        ptr_ = Ring(nc, st, "pt", [128, 256], BF16, 4)
        pmr = Ring(nc, st, "pm", [128, 256], BF16, 4)
        rcr = Ring(nc, st, "rc", [64, 512], F32, 2)
        tmr = Ring(nc, st, "tm", [64, 512], F32, 2)
        gr = Ring(nc, st, "gb", [64, 512], BF16, 2)
        outr = Ring(nc, st, "ao", [64, 512], BF16, 2)
        cnt = 0
        dslot = [0]
        for s0, S in zip(k.seq_off, k.seqs):
            nchunk = S // 128
            for hp in range(4):
                qs, kq = qsr.next()
                ks_, kk_ = ksr.next()
                P.dma("sp", qs[:, 0:S], k.QK_T[128 * hp:128 * hp + 128, s0:s0 + S], writes=[kq])
                P.dma("sp", ks_[:, 0:S], k.QK_T[512 + 128 * hp:512 + 128 * hp + 128, s0:s0 + S], writes=[kk_])
                qv, kv_ = {1: (qs, kq)}, {1: (ks_, kk_)}
                for d in (4, 16):
                    qd, kqd = qdr[d].next()
                    kd, kkd = kdr[d].next()
                    cp(P, "act", qd[:, 0:S].rearrange("p (r i) -> p r i", r=d),
                       qs[:, 0:S].rearrange("p (i r) -> p r i", r=d), [kq], [kqd])
                    cp(P, "dve", kd[:, 0:S].rearrange("p (r i) -> p r i", r=d),
                       ks_[:, 0:S].rearrange("p (i r) -> p r i", r=d), [kk_], [kkd])
                    qv[d] = (qd, kqd)
                    kv_[d] = (kd, kkd)
                for h in range(2):
                    P.op("pool", (lambda a: lambda e: e.memset(a, 0.0))(acc[h][:, 0:S]),
                         writes=[("acc", h, c) for c in range(nchunk)])
                for (win, d) in PATTERNS:
                    sub = S // d
                    nb = sub // 128
                    assert sub % 128 == 0 and nb >= 1
                    qt, kqt = qv[d]
                    kt, kkt = kv_[d]
                    for r in range(d):
                        for kb in range(nb + 1):
                            lo = max(0, 128 * kb - 64)
                            hi = min(sub, 128 * kb + 64)
                            nk = hi - lo
                            v, kv = vtr.next()
                            rows = k.VB[s0 + r + d * lo:s0 + r + d * (hi - 1) + 1:d, 128 * hp:128 * hp + 128]
                            P.dma("sp", v[0:nk, :, 0:64], rows.rearrange("p (h e) -> p h e", h=2), writes=[kv])
                            qb_lo = max(0, kb - 1)
                            qb_hi = min(nb - 1, kb)
                            nq = 128 * (qb_hi - qb_lo + 1)
                            if kb == 0:
                                mask = k.cb["a_0"][0:64, 0:128]
                            elif kb == nb:
                                mask = k.cb["a_g"][0:64, 0:128]
                            else:
                                mask = k.cb["a_g"][:, 0:256]
                            for h in range(2):
                                ph = slice(64 * h, 64 * h + 64)
                                keys = kt[ph, r * sub + lo:r * sub + hi]
                                qry = qt[ph, r * sub + 128 * qb_lo:r * sub + 128 * qb_lo + nq]
                                ps, kps = psr.next()
                                mm(P, ps[0:nk, 0:nq], keys, qry, True, True, [kqt, kkt], [kps])
                                e, ke = ptr_.next()
                                act(P, e[0:nk, 0:nq], ps[0:nk, 0:nq], AF.Exp, [kps], [ke], scale=0.125)
                                m, km = pmr.next()
                                cnt += 1
                                tt(P, "dve" if cnt % 3 else "pool", m[0:nk, 0:nq], e[0:nk, 0:nq], mask, ALU.mult,
                                   [ke, "cb_a_g", "cb_a_0"], [km])
                                for qb in range(qb_lo, qb_hi + 1):
                                    first = (kb == qb)
                                    pv, kpo = po[h][qb % 2]
                                    c0 = 128 * (qb - qb_lo)
                                    mm(P, pv[0:65, :], v[0:nk, h, 0:65], m[0:nk, c0:c0 + 128], first, not first,
                                       [kv, km], [kpo])
                                    if not first:
                                        a0 = r + d * 128 * qb
                                        pos = acc[h][0:65, a0:a0 + d * 127 + 1:d]
                                        ck = [("acc", h, c) for c in range(d * qb, d * qb + d)]
                                        tt(P, "dve", pos, pos, pv[0:65, :], ALU.add, [kpo] + ck, ck)
                for h in range(2):
                    for c0 in range(0, S, 512):
                        ck = [("acc", h, c) for c in range(c0 // 128, c0 // 128 + 4)]
                        pd, kpd = pdr.next()
                        mm(P, pd[0:64, :], k.cf["ones"][64:65, 0:64], acc[h][64:65, c0:c0 + 512], True, True,
                           ["cf_ones"] + ck, [kpd])
                        rc, krc = rcr.next()
                        act(P, rc[:], pd[0:64, :], AF.Ln, [kpd], [krc])
                        act(P, rc[:], rc[:], AF.Exp, [krc], [krc], scale=-1.0)
                        g, kg = gr.next()
                        r0 = 128 * hp + 64 * h
                        P.dma("sp", g[:], k.GB_T[r0:r0 + 64, s0 + c0:s0 + c0 + 512], writes=[kg])
                        tm, ktm = tmr.next()
                        tt(P, "pool", tm[:], acc[h][0:64, c0:c0 + 512], rc[:], ALU.mult, [krc] + ck, [ktm])
                        o, ko = outr.next()
                        tt(P, "dve", o[:], tm[:], g[:], ALU.mult, [ktm, kg], [ko])
                        P.dma("sp", k.Y0_T[512 + r0:512 + r0 + 64, s0 + c0:s0 + c0 + 512], o[:], reads=[ko],
                              writes=[("Y0_T", r0, c0)])
        P.stage_end()


def stage3_rwkv(k):
    P, nc = k.P, k.nc
    with contextlib.ExitStack() as st:
        A = lambda n, sh, dt=F32: st.enter_context(nc.sbuf_tensor(n, sh, dt))
        RG = lambda n, sh, dt, c: Ring(nc, st, n, sh, dt, c)
        raw = {n: RG("raw_" + n, [128, 514], BF16, 2) for n in ("r", "k", "v", "la")}
        f = {n: RG("f_" + n, [128, 512], F32, 1) for n in
             ("t1", "t2", "rs", "ks", "las", "sg", "lr", "kkr", "sq", "rn", "kk", "tka", "kd", "beta",
              "ci", "cf", "ce", "E1", "E2", "E3", "E4", "lro", "e1", "e2", "e3", "e4", "e6")}
        f["vs"] = RG("f_vs", [128, 512], F32, 3)
        prb = RG("prb", [128, 512], BF16, 3)
        twal = RG("twal", [128, 512], BF16, 1)
        pc = RG("pc", [128, 8], F32, 3)
        AR = RG("AR", [128, 4, 2, 128], BF16, 3)
        ob = {n: RG("o_" + n, [128, 512], BF16, 3) for n in ("Bt", "Kt", "Be", "Ke", "vb")}
        TB = RG("TB", [128, 512], BF16, 8)
        G13r = RG("G13r", [128, 384], BF16, 4)
        G2r = RG("G2r", [128, 256], BF16, 4)
        G13 = RG("G13", [128, 384], BF16, 16)
        G2 = RG("G2", [128, 256], BF16, 16)
        Z0 = RG("Z0", [128, 128], BF16, 8)
        ZP = RG("ZP", [128, 384], BF16, 16)
        Z6 = RG("Z6", [128, 128], BF16, 16)
        UT = RG("UT", [128, 2, 128], BF16, 3)
        for t_, k_ in zip(UT.tiles, UT.keys):
            P.op("pool", (lambda a: lambda e: e.memset(a, 0.0))(t_[:]), writes=[k_])
        AW = RG("AW", [128, 512], BF16, 2)
        YT = RG("YT", [128, 512], F32, 2)
        yf = RG("yf", [128, 512], F32, 2)
        gat = RG("gat", [128, 512], BF16, 2)
        yo = RG("yo", [128, 512], BF16, 2)
        ST2 = A("ST2", [128, 128])
        STb2 = A("STb2", [128, 128], BF16)
        pbank = [st.enter_context(nc.psum_tensor("pg%d" % i, [128, 512], F32)) for i in range(7)]
        psg = SRing([(pbank[i], "pg%d" % i) for i in range(4)])
        psc = SRing([(pbank[i], "pg%d" % i) for i in range(4, 6)])
        pse = SRing([(pbank[i], "pg%d" % i) for i in range(6, 7)])
        psy = st.enter_context(nc.psum_tensor("psy", [128, 512], F32))
        idb, bd1 = k.cb["ident"], k.cf["bd1"]
        bd1b = k.cb["bd1"]
        sqb = RG("sqb", [128, 512], BF16, 2)
        k.o_omk2 = k.o_omk + 4
        ts(P, "dve", k.cols[:, k.o_omk2:k.o_omk2 + 4], k.cols[:, k.o_omk:k.o_omk + 4], 2.0, None, ALU.mult, None,
           ["cols"], ["cols"])
        ev = [0]
        lvl = [0]

        def evac_eng():
            ev[0] += 1
            return "act" if ev[0] % 2 else "dve"

        def cpx(out, in_, reads, writes):
            cp(P, evac_eng(), out, in_, reads, writes)
        v3 = lambda a: a.rearrange("p (c j) -> p c j", j=64)
        v4 = lambda a: a.rearrange("p (b t) -> p b t", t=128)

        def elem(c):
            s0, S, hp, z, ti, ntile = c["item"]
            t0 = s0 + 512 * ti
            c["t0"] = t0
            lo_edge, hi_edge = (ti == 0), (ti == ntile - 1)
            rt = {}
            for n, r0 in (("r", 128 * hp), ("k", 512 + 128 * hp), ("v", 1024 + 128 * hp), ("la", 1536)):
                t, key = raw[n].next()
                if lo_edge:
                    P.op("pool", (lambda a: lambda e: e.memset(a, 0.0))(t[:, 0:1]), writes=[key])
                if hi_edge:
                    P.op("pool", (lambda a: lambda e: e.memset(a, 0.0))(t[:, 513:514]), writes=[key])
                c0 = 1 if lo_edge else 0
                c1 = 513 if hi_edge else 514
                P.dma("sp", t[:, c0:c1], k.RW_T[r0:r0 + 128, t0 - 1 + c0:t0 - 1 + c1], writes=[key])
                rt[n] = (t, key)
            yield
            sh = {}
            for n, mc, dst in (("r", hp, "rs"), ("k", 4 + hp, "ks"), ("v", 8 + hp, "vs"), ("la", 12, "las")):
                t, key = rt[n]
                t1, k1 = f["t1"].next()
                t2, k2 = f["t2"].next()
                o, ko = f[dst].next()
                act(P, t1[:], t[:, 0:512], AF.Copy, [key, "cols"], [k1], scale=col(k, "mup", mc))
                stt(P, t2[:], t[:, 2:514], col(k, "mun", mc), t1[:], ALU.mult, ALU.add, [key, k1, "cols"], [k2])
                stt(P, o[:], t[:, 1:513], k.cols[:, k.o_c0 + mc:k.o_c0 + mc + 1], t2[:], ALU.mult, ALU.add,
                    [key, k2, "cols"], [ko])
                sh[dst] = (o, ko)
                yield
            rs, krs = sh["rs"]; ks_, kks = sh["ks"]; vs, kvs = sh["vs"]; las, klas = sh["las"]
            c["vs"] = (vs, kvs)
            tw, ktw = twal.next()
            act(P, tw[0:64, :], las[0:64, :], AF.Tanh, [klas], [ktw])
            cp(P, "dve", tw[64:128, :], las[64:128, :], [klas], [ktw])
            hc = slice(128 * hp, 128 * hp + 128)

            def lora_sig(zz, wsel, dst, kdst, bias_name):
                rows = slice(0, 64) if wsel == "w" else slice(64, 128)
                pw, kpw = pse.next()
                mm(P, pw[:], k.lora[rows, zz, hc], tw[rows, :], True, True, [ktw, "lora"], [kpw])
                act(P, dst[:], pw[:], AF.Sigmoid, [kpw, "cols"], [kdst], bias=col(k, bias_name, hp))
            sg, ksg = f["sg"].next()
            lr, klr = f["lr"].next()
            lora_sig(z, "w", sg, ksg, "w0_%d" % z)
            lora_sig(z, "a", lr, klr, "a0_%d" % z)
            yield
            sq, ksq = sqb.next()
            act(P, sq[:], ks_[:], AF.Square, [kks, "cols"], [ksq], scale=col(k, "k_k", hp))
            rn, krn = f["rn"].next()
            pn, kpn = pse.next()
            mm(P, pn[:], bd1b[:], sq[:], True, True, [ksq, "cb_bd1"], [kpn])
            act(P, rn[:], pn[:], AF.Ln, [kpn], [krn], bias=1e-18)
            act(P, rn[:], rn[:], AF.Exp, [krn], [krn], scale=-0.5)
            kk, kkk = f["kk"].next()
            stt(P, kk[:], ks_[:], col(k, "k_k", hp), rn[:], ALU.mult, ALU.mult, [kks, krn, "cols"], [kkk])
            yield
            tka, ktka = f["tka"].next()
            ts(P, "dve", tka[:], lr[:], col(k, "k_a", hp), k.cols[:, k.o_omk + hp:k.o_omk + hp + 1],
               ALU.mult, ALU.add, [klr, "cols"], [ktka])
            kd, kkd = f["kd"].next()
            tt(P, "pool", kd[:], ks_[:], tka[:], ALU.mult, [kks, ktka], [kkd])
            beta, kbeta = f["beta"].next()
            tt(P, "pool", beta[:], kk[:], lr[:], ALU.mult, [kkk, klr], [kbeta])
            cf_, kcf = f["cf"].next()
            P.op("dve", (lambda o, a, b: lambda e: e.tensor_tensor_scan(
                out=o, data0=a, data1=b, initial=0.0, op0=ALU.mult, op1=ALU.add))(
                cf_[:], k.cf["rmask"][:], sg[:]), [ksg, "cf_rmask"], [kcf])
            tot = cf_[:, 63:512:64]
            totb = tot.unsqueeze(2).to_broadcast([128, 8, 64])
            if z == 0:
                ci, kci = cf_, kcf
            else:
                ci, kci = f["ci"].next()
                tt(P, "pool", ci[:], sg[:], cf_[:], ALU.subtract, [ksg, kcf], [kci])
                tt(P, "dve", v3(ci[:]), v3(ci[:]), totb, ALU.add, [kci, kcf], [kci])
            ce, kce = f["ce"].next()
            tt(P, "pool", ce[:], ci[:], sg[:], ALU.subtract, [kci, ksg], [kce])
            yield
            E1, kE1 = f["E1"].next(); E2, kE2 = f["E2"].next(); E3, kE3 = f["E3"].next(); E4, kE4 = f["E4"].next()
            act(P, E1[:], ce[:], AF.Exp, [kce], [kE1], scale=-DECAY_C)
            act(P, E2[:], ci[:], AF.Exp, [kci], [kE2], scale=DECAY_C)
            act(P, E3[:], ci[:], AF.Exp, [kci], [kE3], scale=-DECAY_C)
            pct, kpc = pc.next()
            act(P, pct[:], tot, AF.Exp, [kcf], [kpc], scale=-DECAY_C)
            c["pc"] = (pct, kpc)
            pcb = pct[:].unsqueeze(2).to_broadcast([128, 8, 64])
            tt(P, "dve", v3(E4[:]), v3(E2[:]), pcb, ALU.mult, [kE2, kpc], [kE4])
            yield
            art, kar = AR.next()
            stt(P, art[:, :, 0, :], v4(kk[:]), -1.0, v4(E1[:]), ALU.mult, ALU.mult, [kkk, kE1], [kar])
            tt(P, "pool", art[:, :, 1, :], v4(rs[:]), v4(E3[:]), ALU.mult, [krs, kE3], [kar])
            Bt, kBt = ob["Bt"].next(); Kt, kKt = ob["Kt"].next()
            Be, kBe = ob["Be"].next(); Ke, kKe = ob["Ke"].next(); vb, kvb = ob["vb"].next()
            tt(P, "pool", Bt[:], beta[:], E2[:], ALU.mult, [kbeta, kE2], [kBt])
            tt(P, "pool", Kt[:], kd[:], E2[:], ALU.mult, [kkd, kE2], [kKt])
            yield
            tt(P, "pool", Be[:], beta[:], E4[:], ALU.mult, [kbeta, kE4], [kBe])
            tt(P, "pool", Ke[:], kd[:], E4[:], ALU.mult, [kkd, kE4], [kKe])
            cp(P, "act", vb[:], vs[:], [kvs], [kvb])
            c.update(art=(art, kar), Bt=(Bt, kBt), Kt=(Kt, kKt), Be=(Be, kBe), Ke=(Ke, kKe), vb=(vb, kvb))
            if z == 1:
                yield
                lro, klro = f["lro"].next()
                lora_sig(0, "a", lro, klro, "a0_0")
                tt(P, "pool", lro[:], lro[:], lr[:], ALU.add, [klro, klr], [klro])
                ts(P, "dve", lro[:], lro[:], col(k, "k_a", hp), k.cols[:, k.o_omk2 + hp:k.o_omk2 + hp + 1],
                   ALU.mult, ALU.add, [klro, "cols"], [klro])
                tt(P, "pool", lro[:], lro[:], ks_[:], ALU.mult, [klro, kks], [klro])
                pr_, kpr_ = prb.next()
                stt(P, pr_[:], rs[:], col(k, "r_k", hp), lro[:], ALU.mult, ALU.mult, [krs, klro, "cols"], [kpr_])
                c["pr"] = (pr_, kpr_)

        def wave(c):
            s0, S, hp, z, ti, ntile = c["item"]
            mask = k.cb["m_f" if z == 0 else "m_b"]
            mkeys = ["cb_m_f", "cb_m_b"]
            art, kar = c["art"]; Bt, kBt = c["Bt"]; Kt, kKt = c["Kt"]
            Be, kBe = c["Be"]; Ke, kKe = c["Ke"]; vb, kvb = c["vb"]
            border = list(range(4)) if z == 0 else list(range(3, -1, -1))
            c["border"] = border
            aw, kaw = AW.next()
            c["aw"] = (aw, kaw)
            tb = {}
            for b in border:
                bc = slice(128 * b, 128 * b + 128)
                p_, kp_ = psg.next()
                for i, (src, skey) in enumerate(((vb, kvb), (Be, kBe), (Ke, kKe))):
                    mm(P, p_[:, 128 * i:128 * i + 128], src[:, bc], idb[:], True, True, [skey, "cb_ident"], [kp_])
                mm(P, p_[:, 384:512], vb[64:128, bc], idb[64:128, :], True, True, [kvb, "cb_ident"], [kp_])
                t, kt = TB.next()
                cpx(t[:], p_[:, 0:512], [kp_], [kt])
                tb[b] = (t, kt)
                yield
            c["tb"] = tb
            units = [(b, h) for b in border for h in range(2)]
            U = {}
            for ui, (b, h) in enumerate(units):
                bc = slice(128 * b, 128 * b + 128)
                ph = slice(64 * h, 64 * h + 64)
                arh = art[ph, b, :, :].rearrange("p a t -> p (a t)")
                p1, kp1 = psg.next()
                mm(P, p1[:, 0:256], Bt[ph, bc], arh, True, True, [kBt, kar], [kp1])
                mm(P, p1[:, 256:384], art[ph, b, 0, :], Bt[ph, bc], True, True, [kBt, kar], [kp1])
                g13, kg13 = G13.next()
                tt(P, "dve", g13[:], p1[:, 0:384], mask[:], ALU.mult, [kp1] + mkeys, [kg13])
                p2, kp2 = psg.next()
                mm(P, p2[:, 0:256], Kt[ph, bc], arh, True, True, [kKt, kar], [kp2])
                g2r, kg2r = G2r.next()
                cp(P, "act", g2r[:], p2[:, 0:256], [kp2], [kg2r])
                g2, kg2 = G2.next()
                tt(P, "pool", g2[:], g2r[:], mask[:, 0:256], ALU.mult, [kg2r] + mkeys, [kg2])
                U[(b, h)] = dict(g13=(g13, kg13), g2=(g2, kg2))
                yield
            for (b, h) in units:
                u = U[(b, h)]
                ph = slice(64 * h, 64 * h + 64)
                g2, kg2 = u["g2"]
                vt, kvt = tb[b]
                za = slice(0, 64) if h == 0 else slice(64, 128)
                zv = slice(64, 128) if h == 0 else slice(0, 64)
                p4, kp4 = psg.next()
                mm(P, p4[:, za], art[ph, b, 0, :], idb[ph, 64 * h:64 * h + 64], True, True, [kar, "cb_ident"], [kp4])
                mm(P, p4[:, zv], g2[:, 0:128], vt[:, 64 * h:64 * h + 64], True, True, [kg2, kvt], [kp4])
                z0, kz0 = Z0.next()
                cpx(z0[:], p4[:, 0:128], [kp4], [kz0])
                g13, kg13 = u["g13"]
                u["Z"] = (z0[:, 0:128], kz0)
                u["P"] = (g13[:, 0:128], kg13)
                u["PT"] = (g13[:, 256:384], kg13)
                if h == 1:
                    yield
            for j in range(6):
                last = (j == 5)
                for (b, h) in units:
                    u = U[(b, h)]
                    Zt, kZ = u["Z"]; Pt, kP = u["P"]; PTt, kPT = u["PT"]
                    ps, kps = psg.next()
                    if not last:
                        need_pt = (j < 4)
                        if j == 0:
                            mm(P, ps[:, 0:128], Pt, Zt, True, True, [kP, kZ], [kps])
                            mm(P, ps[:, 128:256], Pt, PTt, True, True, [kP, kPT], [kps])
                        elif need_pt:
                            mm(P, ps[:, 0:256], Pt, u["ZPT"], True, True, [kP, kZ], [kps])
                        else:
                            mm(P, ps[:, 0:128], Pt, Zt, True, True, [kP, kZ], [kps])
                        mm(P, ps[:, 256:384], PTt, Pt, True, True, [kP, kPT], [kps])
                        zn, kzn = ZP.next()
                        tt(P, "dve", zn[:, 0:128], ps[:, 0:128], Zt, ALU.add, [kps, kZ], [kzn])
                        lvl[0] += 1
                        if need_pt:
                            cp(P, "act" if lvl[0] % 8 else "dve", zn[:, 128:384], ps[:, 128:384], [kps], [kzn])
                        else:
                            cp(P, "act" if lvl[0] % 8 else "dve", zn[:, 256:384], ps[:, 256:384], [kps], [kzn])
                        u["Z"] = (zn[:, 0:128], kzn)
                        u["PT"] = (zn[:, 128:256], kzn)
                        u["P"] = (zn[:, 256:384], kzn)
                        u["ZPT"] = zn[:, 0:256]
                    else:
                        mm(P, ps[:, 0:128], Pt, Zt, True, True, [kP, kZ], [kps])
                        z6, kz6 = Z6.next()
                        tt(P, "dve", z6[:], ps[:, 0:128], Zt, ALU.add, [kps, kZ], [kz6])
                        u["z6"] = (z6, kz6)
                    if h == 1:
                        yield
            for (b, h) in units:
                u = U[(b, h)]
                ph = slice(64 * h, 64 * h + 64)
                bc = slice(128 * b, 128 * b + 128)
                z6, kz6 = u["z6"]
                p5, kp5 = psg.next()
                mm(P, p5[:, 0:128], z6[:], idb[:], True, True, [kz6, "cb_ident"], [kp5])
                cpx(aw[ph, bc], p5[ph, 0:128], [kp5], [kaw])
                if h == 1:
                    yield
            c["U"] = U

        def scan(c):
            s0, S, hp, z, ti, ntile = c["item"]
            t0 = c["t0"]
            first = (ti == 0) if z == 0 else (ti == ntile - 1)
            if first:
                P.op("pool", lambda e: e.memset(ST2[:], 0.0), writes=[("ST2", 0), ("ST2", 1)])
                P.op("pool", lambda e: e.memset(STb2[:], 0.0), writes=[("STb2", 0), ("STb2", 1)])
            art, kar = c["art"]
            pct, kpc = c["pc"]
            aw, kaw = c["aw"]
            U = c["U"]
            corder = (0, 1) if z == 0 else (1, 0)
            for b in c["border"]:
                bc = slice(128 * b, 128 * b + 128)
                tbt, ktb = c["tb"][b]
                vt, bet, ket = tbt[:, 0:128], tbt[:, 128:256], tbt[:, 256:384]
                ut, kut = UT.next()
                for cc in corder:
                    rows = slice(64 * cc, 64 * cc + 64)
                    tk = slice(128 * b + 64 * cc, 128 * b + 64 * cc + 64)
                    cidx = 2 * b + cc
                    hop = []
                    for h in range(2):
                        ph = slice(64 * h, 64 * h + 64)
                        pu, kpu = psc.next()
                        mm(P, pu[:, 0:64], aw[ph, bc], STb2[ph, 64 * h:64 * h + 64], True, True, [kaw, ("STb2", h)], [kpu])
                        hop.append((pu, kpu))
                    yield
                    hop2 = []
                    vtz = tbt[:, 384:512]
                    for h in range(2):
                        u = U[(b, h)]
                        z6, kz6 = u["z6"]
                        pu, kpu = hop[h]
                        if h == 0:
                            tt(P, "dve", ut[rows, 0, 0:64], pu[rows, 0:64], z6[rows, 64:128], ALU.add, [kpu, kz6], [kut])
                        else:
                            tt(P, "dve", ut[rows, 1, 64:128], pu[rows, 0:64], z6[rows, 0:64], ALU.add, [kpu, kz6], [kut])
                    gA, kgA = U[(b, 0)]["g13"]; g2A, kg2A = U[(b, 0)]["g2"]
                    gB, kgB = U[(b, 1)]["g13"]; g2B, kg2B = U[(b, 1)]["g2"]
                    cs_ = slice(128 + 64 * cc, 128 + 64 * cc + 64)
                    mm(P, psy[:, tk], STb2[:, :], art[:, b, 1, 64 * cc:64 * cc + 64], True, False,
                       [("STb2", 0), ("STb2", 1), kar], ["psy"])
                    mm(P, psy[0:64, tk], ut[rows, 0, 0:64], gA[rows, cs_], False, False, [kut, kgA], ["psy"])
                    mm(P, psy[0:64, tk], vt[rows, 0:64], g2A[rows, cs_], False, False, [ktb, kg2A], ["psy"])
                    mm(P, psy[:, tk], ut[rows, 1, :], gB[rows, cs_], False, False, [kut, kgB], ["psy"])
                    mm(P, psy[:, tk], vtz[rows, :], g2B[rows, cs_], False, True, [ktb, kg2B], ["psy"])
                    for h in range(2):
                        pss, kpss = psc.next()
                        if h == 0:
                            mm(P, pss[0:64, 0:64], bet[rows, 0:64], ut[rows, 0, 0:64], True, False, [ktb, kut], [kpss])
                            mm(P, pss[0:64, 0:64], ket[rows, 0:64], vt[rows, 0:64], False, True, [ktb], [kpss])
                        else:
                            mm(P, pss[:, 0:64], bet[rows, 0:128], ut[rows, 1, 64:128], True, False, [ktb, kut], [kpss])
                            mm(P, pss[:, 0:64], ket[rows, 0:128], vt[rows, 64:128], False, True, [ktb], [kpss])
                        hop2.append((pss, kpss))
                    yield
                    for h in range(2):
                        ph = slice(64 * h, 64 * h + 64)
                        pss, kpss = hop2[h]
                        sv = ST2[ph, 64 * h:64 * h + 64]
                        stt(P, sv, sv, pct[ph, cidx:cidx + 1], pss[ph, 0:64], ALU.mult, ALU.add,
                            [kpss, kpc, ("ST2", h)], [("ST2", h)])
                        cp(P, "act", STb2[ph, 64 * h:64 * h + 64], sv, [("ST2", h)], [("STb2", h)])
                    yield
            yt, kyt = YT.next()
            cp(P, "act", yt[:], psy[:], ["psy"], [kyt])
            rr = slice(128 * hp, 128 * hp + 128)
            if z == 0:
                P.dma("sp", k.YF_T[rr, t0:t0 + 512], yt[:], reads=[kyt], writes=[("YF", hp, t0)])
                return
            vs, kvs = c["vs"]
            pr_, kpr_ = c["pr"]
            yft, kyf = yf.next()
            P.dma("sp", yft[:], k.YF_T[rr, t0:t0 + 512], reads=[("YF", hp, t0)], writes=[kyf])
            gt, kgt = gat.next()
            P.dma("sp", gt[:], k.GA_T[rr, t0:t0 + 512], writes=[kgt])
            y, ky = f["e1"].next()
            tt(P, "pool", y[:], yt[:], yft[:], ALU.add, [kyt, kyf], [ky])
            d_, kd_ = f["e2"].next()
            sq2, ksq2 = f["e3"].next()
            rstd, krstd = f["e4"].next()
            pm, kpm = pse.next()
            mm(P, pm[:], bd1[:], y[:], True, True, [ky, "cf_bd1"], [kpm])
            stt(P, d_[:], pm[:], -1.0 / 64, y[:], ALU.mult, ALU.add, [kpm, ky], [kd_])
            sq2, ksq2 = sqb.next()
            act(P, sq2[:], d_[:], AF.Square, [kd_], [ksq2])
            yield
            pv_, kpv_ = pse.next()
            mm(P, pv_[:], bd1b[:], sq2[:], True, True, [ksq2, "cb_bd1"], [kpv_])
            rsqrt(P, rstd[:], pv_[:], 1.0 / 64, GN_EPS, [kpv_], [krstd])
            tt(P, "pool", d_[:], d_[:], rstd[:], ALU.mult, [kd_, krstd], [kd_])
            ts(P, "dve", d_[:], d_[:], col(k, "gn_w", hp), col(k, "gn_b", hp), ALU.mult, ALU.add, [kd_, "cols"], [kd_])
            yield
            bo, kbo = f["e6"].next()
            pb2, kpb2 = pse.next()
            mm(P, pb2[:], bd1b[:], pr_[:], True, True, [kpr_, "cb_bd1"], [kpb2])
            tt(P, "dve", bo[:], pb2[:], vs[:], ALU.mult, [kpb2, kvs], [kbo])
            tt(P, "pool", bo[:], bo[:], d_[:], ALU.add, [kbo, kd_], [kbo])
            o_, ko_ = yo.next()
            tt(P, "dve", o_[:], bo[:], gt[:], ALU.mult, [kbo, kgt], [ko_])
            P.dma("sp", k.Y0_T[rr, t0:t0 + 512], o_[:], reads=[ko_], writes=[("Y0a", hp, t0)])

        items = []
        for s0, S in zip(k.seq_off, k.seqs):
            ntile = S // 512
            for hp in range(4):
                for z in range(2):
                    order = range(ntile) if z == 0 else range(ntile - 1, -1, -1)
                    for ti in order:
                        items.append(dict(item=(s0, S, hp, z, ti, ntile)))
        n = len(items)
        for step in range(n + 2):
            gens = []
            if step < n:
                gens.append(elem(items[step]))
            if 0 <= step - 1 < n:
                gens.append(wave(items[step - 1]))
            if 0 <= step - 2 < n:
                gens.append(scan(items[step - 2]))
            while gens:
                for g in list(gens):
                    try:
                        next(g)
                    except StopIteration:
                        gens.remove(g)
            if step - 2 >= 0:
                items[step - 2].clear()
        P.stage_end()


def out_proj_tile(k, R, W, ysrc, xsrc, t0, j, pso, xres_ring, dst=None):
    P = k.P
    yT, kyT = ysrc
    xt, kx = xsrc
    if dst is None:
        xr, kxr = xres_ring.next()
    else:
        xr, kxr = dst
    for half in range(2):
        ps, kp = pso.next()
        for kc in range(8):
            mm(P, ps[:], yT[:, kc, 128 * j:128 * j + 128], W[:, kc, 512 * half:512 * half + 512], kc == 0, kc == 7,
               [kyT, "Wo"], [kp])
        tt(P, "dve", xr[:, 512 * half:512 * half + 512], ps[:], xt[:, j, 512 * half:512 * half + 512], ALU.add,
           [kp, kx], [kxr])
    return xr, kxr


def stage4_out0(k):
    P, nc, T = k.P, k.nc, k.T
    with contextlib.ExitStack() as st:
        Wo, W = load_weights(k, st, [("wo0", k.p["w_out0"], 8, D, "Wo"), ("w1", k.p["w_in1"], 8, ODD_COLS, "W")])
        R = in_rings(k, st)
        yr = Ring(nc, st, "y0T", [128, 8, 512], BF16, 2)
        x1r = Ring(nc, st, "x1t", [128, 4, D], F32, 2)
        pso = Ring(nc, st, "pso", [128, 512], F32, 2, psum=True)
        psr = Ring(nc, st, "ps1", [128, 512], F32, 4, psum=True)
        ost = Ring(nc, st, "ost", [128, 512], BF16, 6)
        glr = Ring(nc, st, "glt", [16, 512], F32, 2)
        qscale = 128.0 ** -0.5
        ntile = T // 512

        def loads(t0):
            xt, kx = R["x"].next()
            P.dma("sp", xt[:], k.x[t0:t0 + 512, :].rearrange("(j p) d -> p j d", p=128), writes=[kx])
            yT, kyT = yr.next()
            P.dma("sp", yT[:], k.Y0_T[:, t0:t0 + 512].rearrange("(kc p) t -> p kc t", p=128), writes=[kyT])
            return (xt, kx), (yT, kyT)
        nxt = loads(0)
        for ti in range(ntile):
            t0 = ti * 512
            xsrc, ysrc = nxt
            if ti + 1 < ntile:
                nxt = loads(t0 + 512)
            x1, kx1 = x1r.next()
            for j in range(4):
                xr, kxr = out_proj_tile(k, R, Wo, ysrc, xsrc, t0, j, pso, None, dst=(x1[:, j, :], kx1))
                P.dma("sp", k.X1[t0 + 128 * j:t0 + 128 * j + 128, :], xr, reads=[kxr], writes=[("X1", ti, j)])
            xnT, kT = rms_transpose_compute(k, R, x1, kx1, "norm1")
            for oc in list(range(8)) + list(range(16, 24)):
                ps, kp = psr.next()
                c0 = 128 * oc if oc < 8 else 2064 + 128 * (oc - 16)
                for kc in range(8):
                    mm(P, ps[:], W[:, kc, c0:c0 + 128], xnT[:, kc, :], kc == 0, kc == 7, [kT, "W"], [kp])
                o, ko = ost.next()
                if oc < 4:
                    act(P, o[:], ps[:], AF.Copy, [kp], [ko], scale=qscale)
                elif oc < 8:
                    cp(P, "dve", o[:], ps[:], [kp], [ko])
                else:
                    act(P, o[:], ps[:], AF.Silu, [kp], [ko])
                if oc < 8:
                    P.dma("sp", k.Q1_T[128 * oc:128 * oc + 128, t0:t0 + 512], o[:], reads=[ko], writes=[("Q1", ti, oc)])
                else:
                    r0 = 128 * (oc - 16)
                    P.dma("sp", k.G1_T[r0:r0 + 128, t0:t0 + 512], o[:], reads=[ko], writes=[("G1", ti, oc)])
            ps, kp = psr.next()
            for kc in range(8):
                mm(P, ps[0:16, :], W[:, kc, 2048:2064], xnT[:, kc, :], kc == 0, kc == 7, [kT, "W"], [kp])
            gl, kgl = glr.next()
            cp(P, "act", gl[:], ps[0:16, :], [kp], [kgl])
            P.dma("sp", k.GL_T[:, t0:t0 + 512], gl[:], reads=[kgl], writes=[("GL", ti)])
            for j in range(4):
                for half in range(2):
                    ps, kp = psr.next()
                    c0 = 1024 + 512 * half
                    for kc in range(8):
                        mm(P, ps[:], xnT[:, kc, 128 * j:128 * j + 128], W[:, kc, c0:c0 + 512], kc == 0, kc == 7,
                           [kT, "W"], [kp])
                    o, ko = ost.next()
                    cp(P, "act" if half else "dve", o[:], ps[:], [kp], [ko])
                    P.dma("sp", k.V1[t0 + 128 * j:t0 + 128 * j + 128, 512 * half:512 * half + 512], o[:], reads=[ko],
                          writes=[("V1", ti, j, half)])
        P.stage_end()


def stage5_in1(k):
    pass


def stage6_gla(k):
    P, nc = k.P, k.nc
    with contextlib.ExitStack() as st:
        A = lambda n, sh, dt=F32: st.enter_context(nc.sbuf_tensor(n, sh, dt))
        RG = lambda n, sh, dt, c: Ring(nc, st, n, sh, dt, c)
        Wo7, = load_weights(k, st, [("wo1", k.p["w_out1"], 8, D, "Wo")])
        fn7 = A("fnorm", [128, D])
        P.dma("sp", fn7[:], k.p["fnorm"][:, :], writes=["fnorm"])
        x7r = RG("x1in", [128, 4, D], F32, 1)
        y7r = RG("y1T", [128, 8, 512], BF16, 1)
        xr7 = RG("xres", [128, D], F32, 2)
        o7r = RG("outt", [128, D], F32, 2)
        junk7 = A("junk7", [128, D], BF16)
        ss7r = RG("ss7", [128, 1], F32, 2)
        pso7 = Ring(nc, st, "pso", [128, 512], F32, 2, psum=True)

        def out1_sequence(s0, S):
            for t0 in range(s0, s0 + S, 512):
                xt, kx = x7r.next()
                P.dma("sp", xt[:], k.X1[t0:t0 + 512, :].rearrange("(j p) d -> p j d", p=128), writes=[kx])
                yT, kyT = y7r.next()
                P.dma("sp", yT[:], k.Y1_T[:, t0:t0 + 512].rearrange("(kc p) t -> p kc t", p=128),
                      reads=[("Y1", h_, half_, t0) for h_ in range(4) for half_ in range(2)], writes=[kyT])
                for j in range(4):
                    xr, kxr = out_proj_tile(k, None, Wo7, (yT, kyT), (xt, kx), t0, j, pso7, xr7)
                    ss, kss = ss7r.next()
                    act(P, junk7[:], xr[:], AF.Square, [kxr], [kss, "junk7"], accum_out=ss[:, 0:1])
                    rsqrt(P, ss[:], ss[:], 1.0 / D, RMS_EPS, [kss], [kss])
                    o, ko = o7r.next()
                    stt(P, o[:], xr[:], ss[:, 0:1], fn7[:], ALU.mult, ALU.mult, [kxr, kss, "fnorm"], [ko])
                    P.dma("sp", k.y[t0 + 128 * j:t0 + 128 * j + 128, :], o[:], reads=[ko], writes=[("y", t0, j)])
        gu = A("g_gu", [16, 2, 512])
        P.dma("sp", gu[:], k.p["gate_up"][:, :, :], writes=["gu"])
        ngb = A("g_ngb", [128, 8])
        og = COLS["gb_0"][0]
        ts(P, "dve", ngb[:], k.cols[:, og:og + 8], -1.0, None, ALU.mult, None, ["cols"], ["ngb"])
        def make_lane(li):
            L = {}
            nm = lambda n: "%s_l%d" % (n, li)
            L["Sr"] = RG(nm("g_S"), [128, 256], F32, 2)
            L["Sbr"] = RG(nm("g_Sb"), [128, 256], BF16, 2)
            L["qr"] = RG(nm("g_q"), [128, 512], BF16, 2)
            L["kr"] = RG(nm("g_k"), [128, 512], BF16, 2)
            L["glr"] = RG(nm("g_gl"), [16, 512], F32, 2)
            L["vtr"] = RG(nm("g_v"), [128, 4, 256], BF16, 2)
            L["f"] = {n: RG(nm("gf_" + n), [128, 512], F32, 1) for n in ("e", "l", "cf", "ci", "Eq", "Ek", "Ee", "rstd")}
            L["dcr"] = RG(nm("g_dc"), [128, 4], F32, 2)
            L["ob"] = {n: RG(nm("go_" + n), [128, 512], BF16, 2) for n in ("qd", "kd", "ke")}
            L["attr"] = RG(nm("g_att"), [128, 256], BF16, 3)
            L["otr"] = RG(nm("g_ot"), [128, 2, 512], F32, 2)
            L["ofr"] = RG(nm("g_of"), [128, 2, 512], F32, 1)
            L["sqr"] = RG(nm("g_sq"), [128, 2, 512], F32, 1)
            L["gtr"] = RG(nm("g_gt"), [128, 2, 512], BF16, 1)
            L["yor"] = RG(nm("g_yo"), [128, 512], BF16, 2)
            L["psg"] = Ring(nc, st, nm("pgG"), [128, 512], F32, 3, psum=True)
            return L
        lanes = [make_lane(0), make_lane(1)]
        npass = [0]
        idb = k.cb["ident"]
        ones = k.cf["ones"]
        ev = [0]

        def evac_eng():
            ev[0] += 1
            return "act" if ev[0] % 2 else "dve"
        v3 = lambda a: a.rearrange("p (c j) -> p c j", j=128)
        seq_order = sorted(zip(k.seq_off, k.seqs), key=lambda a: -a[1])
        for si_, (s0, S) in enumerate(seq_order):
            if si_ > 0:
                out1_sequence(*seq_order[si_ - 1])
            ntile = S // 512
            for h in range(4):
                for z in range(2):
                    L = lanes[npass[0] % 2]
                    npass[0] += 1
                    Sr, Sbr, qr, kr, glr, vtr, f, dcr, ob = (L[n_] for n_ in ("Sr", "Sbr", "qr", "kr", "glr", "vtr", "f", "dcr", "ob"))
                    attr, otr, ofr, sqr, gtr, yor, psg = (L[n_] for n_ in ("attr", "otr", "ofr", "sqr", "gtr", "yor", "psg"))
                    S_, kS = Sr.next()
                    Sb, kSb = Sbr.next()
                    P.op("pool", (lambda a: lambda e: e.memset(a, 0.0))(S_[:]), writes=[kS])
                    P.op("pool", (lambda a: lambda e: e.memset(a, 0.0))(Sb[:]), writes=[kSb])
                    mask = k.cb["g_f" if z == 0 else "g_b"]
                    order = range(ntile) if z == 0 else range(ntile - 1, -1, -1)
                    for ti in order:
                        t0 = s0 + 512 * ti
                        qT, kq = qr.next(); kT, kk_ = kr.next(); gl, kgl = glr.next(); vt, kvt = vtr.next()
                        P.dma("sp", qT[:], k.Q1_T[128 * h:128 * h + 128, t0:t0 + 512], writes=[kq])
                        P.dma("sp", kT[:], k.Q1_T[512 + 128 * h:512 + 128 * h + 128, t0:t0 + 512], writes=[kk_])
                        P.dma("sp", gl[:], k.GL_T[:, t0:t0 + 512], writes=[kgl])
                        P.dma("sp", vt[:], k.V1[t0:t0 + 512, 256 * h:256 * h + 256].rearrange("(j p) d -> p j d", p=128),
                              writes=[kvt])
                        pz, kpz = psg.next()
                        mm(P, pz[:], gu[0:16, z, 128 * h:128 * h + 128], gl[0:16, :], True, True, ["gu", kgl], [kpz])
                        e_, ke_ = f["e"].next()
                        act(P, e_[:], pz[:], AF.Exp, [kpz, "ngb"], [ke_], scale=-1.0, bias=ngb[:, 4 * z + h:4 * z + h + 1])
                        l_, kl_ = f["l"].next()
                        act(P, l_[:], e_[:], AF.Ln, [ke_], [kl_], bias=1.0)
                        if GLA_CUT == 1:
                            continue
                        cf_, kcf = f["cf"].next()
                        P.op("dve", (lambda o, a, b: lambda e: e.tensor_tensor_scan(
                            out=o, data0=a, data1=b, initial=0.0, op0=ALU.mult, op1=ALU.add))(
                            cf_[:], k.cf["rmask128"][:], l_[:]), [kl_, "cf_rmask128"], [kcf])
                        tot = cf_[:, 127:512:128]
                        totb = tot.unsqueeze(2).to_broadcast([128, 4, 128])
                        if z == 0:
                            ci, kci = cf_, kcf
                        else:
                            ci, kci = f["ci"].next()
                            tt(P, "pool", ci[:], l_[:], cf_[:], ALU.subtract, [kl_, kcf], [kci])
                            tt(P, "dve", v3(ci[:]), v3(ci[:]), totb, ALU.add, [kci, kcf], [kci])
                        Eq, kEq = f["Eq"].next(); Ek, kEk = f["Ek"].next(); Ee, kEe = f["Ee"].next()
                        act(P, Eq[:], ci[:], AF.Exp, [kci], [kEq], scale=-1.0 / 16)
                        act(P, Ek[:], ci[:], AF.Exp, [kci], [kEk], scale=1.0 / 16)
                        dc, kdc = dcr.next()
                        act(P, dc[:], tot, AF.Exp, [kcf], [kdc], scale=-1.0 / 16)
                        tt(P, "dve", v3(Ee[:]), v3(Ek[:]), dc[:].unsqueeze(2).to_broadcast([128, 4, 128]), ALU.mult,
                           [kEk, kdc], [kEe])
                        qd, kqd = ob["qd"].next(); kd, kkd = ob["kd"].next(); ke, kke = ob["ke"].next()
                        tt(P, "pool", qd[:], qT[:], Eq[:], ALU.mult, [kq, kEq], [kqd])
                        tt(P, "dve", kd[:], kT[:], Ek[:], ALU.mult, [kk_, kEk], [kkd])
                        tt(P, "pool", ke[:], kT[:], Ee[:], ALU.mult, [kk_, kEe], [kke])
                        if GLA_CUT == 2:
                            continue
                        ot, kot = otr.next()
                        border = range(4) if z == 0 else range(3, -1, -1)
                        for b in border:
                            bc = slice(128 * b, 128 * b + 128)
                            pa, kpa = psg.next()
                            mm(P, pa[:, 0:128], kd[:, bc], qd[:, bc], True, True, [kkd, kqd], [kpa])
                            mm(P, pa[:, 128:256], ke[:, bc], idb[:], True, True, [kke, "cb_ident"], [kpa])
                            at, kat = attr.next()
                            tt(P, "dve", at[:], pa[:, 0:256], mask[:], ALU.mult, [kpa, "cb_g_f", "cb_g_b"], [kat])
                            keT = at[:, 128:256]
                            po, kpo = psg.next()
                            for half in range(2):
                                hc = slice(128 * half, 128 * half + 128)
                                mm(P, po[:, hc], vt[:, b, hc], at[:, 0:128], True, False, [kvt, kat], [kpo])
                                mm(P, po[:, hc], Sb[:, hc], qd[:, bc], False, True, [kSb, kqd], [kpo])
                            cp(P, "act", ot[:, :, bc], po[:, 0:256].rearrange("p (a t) -> p a t", a=2), [kpo], [kot])
                            pS, kpS = psg.next()
                            mm(P, pS[:, 0:256], keT, vt[:, b, :], True, True, [kat, kvt], [kpS])
                            stt(P, S_[:], S_[:], dc[:, b:b + 1], pS[:, 0:256], ALU.mult, ALU.add, [kpS, kdc, kS], [kS])
                            cp(P, "act", Sb[:], S_[:], [kS], [kSb])
                        if GLA_CUT == 3:
                            continue
                        if z == 0:
                            for half in range(2):
                                r0 = 256 * h + 128 * half
                                P.dma("sp", k.OF_T[r0:r0 + 128, t0:t0 + 512], ot[:, half, :], reads=[kot],
                                      writes=[("OF", h, half, t0)])
                            continue
                        of, kof = ofr.next(); gt, kgt = gtr.next()
                        for half in range(2):
                            r0 = 256 * h + 128 * half
                            P.dma("sp", of[:, half, :], k.OF_T[r0:r0 + 128, t0:t0 + 512], reads=[("OF", h, half, t0)],
                                  writes=[kof])
                            P.dma("sp", gt[:, half, :], k.G1_T[r0:r0 + 128, t0:t0 + 512], writes=[kgt])
                        tt(P, "pool", of[:], of[:], ot[:], ALU.add, [kof, kot], [kof])
                        if GLA_CUT == 4:
                            continue
                        sq, ksq = sqr.next()
                        act(P, sq[:], of[:], AF.Square, [kof], [ksq])
                        pn, kpn = psg.next()
                        mm(P, pn[:], ones[:], sq[:, 0, :], True, False, [ksq, "cf_ones"], [kpn])
                        mm(P, pn[:], ones[:], sq[:, 1, :], False, True, [ksq, "cf_ones"], [kpn])
                        rstd, krstd = f["rstd"].next()
                        if GLA_CUT == 5:
                            continue
                        rsqrt(P, rstd[:], pn[:], 1.0 / 256, RMS_EPS, [kpn], [krstd])
                        if GLA_CUT == 6:
                            continue
                        for half in range(2):
                            r0 = 256 * h + 128 * half
                            stt(P, of[:, half, :], of[:, half, :], col(k, "gnorm", half), rstd[:], ALU.mult, ALU.mult,
                                [kof, krstd, "cols"], [kof])
                            if GLA_CUT == 7:
                                continue
                            yo, kyo = yor.next()
                            tt(P, "dve", yo[:], of[:, half, :], gt[:, half, :], ALU.mult, [kof, kgt], [kyo])
                            P.dma("sp", k.Y1_T[r0:r0 + 128, t0:t0 + 512], yo[:], reads=[kyo], writes=[("Y1", h, half, t0)])
        out1_sequence(*seq_order[-1])
        P.stage_end()


def stage7_out1(k):
    return


_CACHE = {}


def kernel(**inputs):
    xp = np.asarray(inputs["x_prompt"], np.float32)
    xs = np.asarray(inputs["x_sample"], np.float32)
    B, S, _ = xp.shape
    DB, DS, _ = xs.shape
    n = NCORES
    pb, sbn = B // n, DB // n
    seqs = [S] * pb + [DS] * sbn
    key = tuple(seqs)
    if key not in _CACHE:
        _CACHE[key] = build(seqs)
    nc, k = _CACHE[key]
    consts = host_consts(max(seqs))
    params = host_params(inputs)
    shared = {"c_" + a: v for a, v in consts.items()}
    shared.update({"p_" + a: v for a, v in params.items()})
    in_maps = []
    for c in range(n):
        parts = [xp[c * pb + i] for i in range(pb)] + [xs[c * sbn + i] for i in range(sbn)]
        m = {"x": np.ascontiguousarray(np.concatenate(parts, axis=0))}
        m.update(shared)
        in_maps.append(m)
    res = run_bass_kernel_spmd(nc, in_maps, core_ids=list(range(n)))
    yp = np.empty_like(xp)
    ys = np.empty_like(xs)
    for c in range(n):
        y = np.asarray(res.results[c]["y"], np.float32)
        off = 0
        for i in range(pb):
            yp[c * pb + i] = y[off:off + S]
            off += S
        for i in range(sbn):
            ys[c * sbn + i] = y[off:off + DS]
            off += DS
    return (yp, ys)
```
